# Optimizing a Trainium2 kernel written in Bass

```python
import math
import jax, jax.numpy as jnp
from jax import lax
import numpy as np

D_MODEL = 1024
BATCH = 16
SEQ = 4096
DEPTH = 2

HEAD_DIM = 64
BRANCH_WIDTH = D_MODEL
N_BRANCH = 3
SSM_INNER = BRANCH_WIDTH
SSM_HEAD_DIM = 64
SSM_HEADS = SSM_INNER // SSM_HEAD_DIM
SSM_GROUPS = 4
SSM_HEADS_PER_GROUP = SSM_HEADS // SSM_GROUPS
SSM_STATE = 128
SSM_CONV_DIM = SSM_INNER + 2 * SSM_GROUPS * SSM_STATE
CONV_WIDTH = 4
SSD_CHUNK = 128
SWA_HEADS = BRANCH_WIDTH // HEAD_DIM
SWA_KV_HEADS = 4
SWA_GROUP = SWA_HEADS // SWA_KV_HEADS
SWA_WINDOW = 128
SWA_BLOCK = 128
FOX_HEADS = BRANCH_WIDTH // HEAD_DIM
FOX_BLOCK = 128
ROPE_THETA = 10000.0
NORM_EPS = 1e-6
IN_SIZES = (SSM_CONV_DIM, SSM_INNER, SSM_HEADS,
            SWA_HEADS * HEAD_DIM, SWA_KV_HEADS * HEAD_DIM, SWA_KV_HEADS * HEAD_DIM, BRANCH_WIDTH,
            FOX_HEADS * HEAD_DIM, FOX_HEADS * HEAD_DIM, FOX_HEADS * HEAD_DIM, FOX_HEADS, BRANCH_WIDTH,
            N_BRANCH * D_MODEL)
N_IN = sum(IN_SIZES)

kernel_name = "hybrid_ssd_swa_fox_gated_block"


def rms_norm(x, w):
    xf = x.astype(jnp.float32)
    y = xf * lax.rsqrt(jnp.mean(xf * xf, axis=-1, keepdims=True) + NORM_EPS)
    return (y * w.astype(jnp.float32)).astype(x.dtype)


def grouped_rms_norm(y, w, groups):
    b, s, d = y.shape
    yf = y.astype(jnp.float32).reshape(b, s, groups, d // groups)
    yf = yf * lax.rsqrt(jnp.mean(yf * yf, axis=-1, keepdims=True) + NORM_EPS)
    return (yf.reshape(b, s, d) * w.astype(jnp.float32)).astype(y.dtype)


def rope(x, cos, sin):
    half = x.shape[-1] // 2
    x1, x2 = x[..., :half], x[..., half:]
    c, s = cos[None, :, None, :], sin[None, :, None, :]
    return jnp.concatenate([x1 * c - x2 * s, x2 * c + x1 * s], axis=-1)


def causal_depthwise_conv(u, w, bias):
    c = u.shape[-1]
    out = lax.conv_general_dilated(u, w[:, None, :], window_strides=(1,),
                                   padding=[(CONV_WIDTH - 1, 0)],
                                   dimension_numbers=('NWC', 'WIO', 'NWC'),
                                   feature_group_count=c)
    return out + bias


def ssd_chunked_scan(xh, dt, a, bm, cm):
    b, s, g, r, p = xh.shape
    n = bm.shape[-1]
    nc, l = s // SSD_CHUNK, SSD_CHUNK
    dtype = xh.dtype
    x = xh.reshape(b, nc, l, g, r, p)
    dtc = dt.reshape(b, nc, l, g, r)
    x_dt = x * dtc[..., None]
    bc = bm.reshape(b, nc, l, g, n)
    cc = cm.reshape(b, nc, l, g, n)
    a_dt = jnp.transpose(dtc.astype(jnp.float32) * a.astype(jnp.float32), (0, 3, 4, 1, 2))
    a_cum = jnp.cumsum(a_dt, axis=-1)
    idx = jnp.arange(l)
    causal = idx[:, None] >= idx[None, :]
    seg = a_cum[..., :, None] - a_cum[..., None, :]
    decay = jnp.where(causal, jnp.exp(jnp.where(causal, seg, 0.0)), 0.0).astype(dtype)
    cb = jnp.einsum('bclgn,bcsgn->bgcls', cc, bc)
    y_diag = jnp.einsum('bgrcls,bcsgrp->bclgrp', cb[:, :, None] * decay, x_dt)
    decay_states = jnp.exp(a_cum[..., -1:] - a_cum).astype(dtype)
    states = jnp.einsum('bcsgn,bgrcs,bcsgrp->bcgrpn', bc, decay_states, x_dt)
    chunk_decay = jnp.exp(a_cum[..., -1]).astype(dtype)

    def step(h, inp):
        st, dec = inp
        return h * dec[..., None, None] + st, h

    h0 = jnp.zeros((b, g, r, p, n), dtype)
    _, prev = lax.scan(step, h0, (jnp.moveaxis(states, 1, 0), jnp.moveaxis(chunk_decay, -1, 0)))
    prev = jnp.moveaxis(prev, 0, 1)
    y_off = jnp.einsum('bclgn,bcgrpn,bgrcl->bclgrp', cc, prev, jnp.exp(a_cum).astype(dtype))
    return (y_diag + y_off).reshape(b, s, g, r, p)


def mamba2_branch(xbc, z, dt_raw, conv_w, conv_b, dt_bias, a_log, d_skip, norm_w):
    b, s, _ = xbc.shape
    xbc = jax.nn.silu(causal_depthwise_conv(xbc, conv_w, conv_b))
    gn = SSM_GROUPS * SSM_STATE
    xs, bm, cm = jnp.split(xbc, [SSM_INNER, SSM_INNER + gn], axis=-1)
    xh = xs.reshape(b, s, SSM_GROUPS, SSM_HEADS_PER_GROUP, SSM_HEAD_DIM)
    dt = jax.nn.softplus(dt_raw + dt_bias).reshape(b, s, SSM_GROUPS, SSM_HEADS_PER_GROUP)
    a = -jnp.exp(a_log).reshape(SSM_GROUPS, SSM_HEADS_PER_GROUP)
    y = ssd_chunked_scan(xh, dt, a,
                         bm.reshape(b, s, SSM_GROUPS, SSM_STATE),
                         cm.reshape(b, s, SSM_GROUPS, SSM_STATE))
    y = y + d_skip.reshape(SSM_GROUPS, SSM_HEADS_PER_GROUP)[:, :, None] * xh
    y = y.reshape(b, s, SSM_INNER) * jax.nn.silu(z)
    return grouped_rms_norm(y, norm_w, SSM_GROUPS)


def sliding_window_branch(q, k, v, z, sinks, cos, sin):
    b, s, _ = q.shape
    nb, blk = s // SWA_BLOCK, SWA_BLOCK
    q = rope(q.reshape(b, s, SWA_HEADS, HEAD_DIM), cos, sin)
    k = rope(k.reshape(b, s, SWA_KV_HEADS, HEAD_DIM), cos, sin)
    v = v.reshape(b, s, SWA_KV_HEADS, HEAD_DIM)
    qb = q.reshape(b, nb, blk, SWA_KV_HEADS, SWA_GROUP, HEAD_DIM)
    pad = ((0, 0), (blk, 0), (0, 0), (0, 0))
    kp, vp = jnp.pad(k, pad)[:, :s], jnp.pad(v, pad)[:, :s]
    kb = jnp.concatenate([kp.reshape(b, nb, blk, SWA_KV_HEADS, HEAD_DIM),
                          k.reshape(b, nb, blk, SWA_KV_HEADS, HEAD_DIM)], axis=2)
    vb = jnp.concatenate([vp.reshape(b, nb, blk, SWA_KV_HEADS, HEAD_DIM),
                          v.reshape(b, nb, blk, SWA_KV_HEADS, HEAD_DIM)], axis=2)
    scores = jnp.einsum('bnqkgd,bnskd->bnkgqs', qb, kb).astype(jnp.float32) * (HEAD_DIM ** -0.5)
    qi = jnp.arange(blk)[:, None]
    sj = jnp.arange(2 * blk)[None, :]
    diff = qi + blk - sj
    band = (diff >= 0) & (diff < SWA_WINDOW)
    key_pos = jnp.arange(nb)[:, None, None] * blk - blk + sj[None]
    mask = band[None] & (key_pos >= 0)
    scores = jnp.where(mask[None, :, None, None], scores, -jnp.inf)
    sink = jnp.broadcast_to(sinks.astype(jnp.float32).reshape(1, 1, SWA_KV_HEADS, SWA_GROUP, 1, 1),
                            scores.shape[:-1] + (1,))
    probs = jax.nn.softmax(jnp.concatenate([scores, sink], axis=-1), axis=-1)[..., :-1]
    out = jnp.einsum('bnkgqs,bnskd->bnqkgd', probs.astype(v.dtype), vb)
    return out.reshape(b, s, SWA_HEADS * HEAD_DIM) * jax.nn.silu(z)


def forgetting_attention_branch(q, k, v, f_logit, z, f_bias):
    b, s, _ = q.shape
    q = q.reshape(b, s, FOX_HEADS, HEAD_DIM)
    k = k.reshape(b, s, FOX_HEADS, HEAD_DIM)
    v = v.reshape(b, s, FOX_HEADS, HEAD_DIM)
    log_f = jax.nn.log_sigmoid((f_logit + f_bias).astype(jnp.float32))
    cum = jnp.transpose(jnp.cumsum(log_f, axis=1), (0, 2, 1))
    scale = HEAD_DIM ** -0.5
    outs = []
    for i in range(s // FOX_BLOCK):
        start, end = i * FOX_BLOCK, (i + 1) * FOX_BLOCK
        sc = jnp.einsum('bqhd,bkhd->bhqk', q[:, start:end], k[:, :end]).astype(jnp.float32) * scale
        sc = sc + cum[:, :, start:end, None] - cum[:, :, None, :end]
        causal = (start + jnp.arange(FOX_BLOCK))[:, None] >= jnp.arange(end)[None, :]
        sc = jnp.where(causal[None, None], sc, -jnp.inf)
        probs = jax.nn.softmax(sc, axis=-1).astype(v.dtype)
        outs.append(jnp.einsum('bhqk,bkhd->bqhd', probs, v[:, :end]))
    out = jnp.concatenate(outs, axis=1)
    return out.reshape(b, s, FOX_HEADS * HEAD_DIM) * jax.nn.silu(z)


def setup_inputs(seed: int = 0) -> dict:
    key = jax.random.key(seed)
    ks = jax.random.split(key, 16)
    L, D, W = DEPTH, D_MODEL, BRANCH_WIDTH
    x = jax.random.normal(ks[0], (BATCH, SEQ, D), jnp.float32)
    norm_w = 1.0 + 0.1 * jax.random.normal(ks[1], (L, D), jnp.float32)
    w_in = jax.random.normal(ks[2], (L, D, N_IN), jnp.float32) * D ** -0.5
    conv_w = jax.random.normal(ks[3], (L, CONV_WIDTH, SSM_CONV_DIM), jnp.float32) * CONV_WIDTH ** -0.5
    conv_b = 0.01 * jax.random.normal(ks[4], (L, SSM_CONV_DIM), jnp.float32)
    u = jax.random.uniform(ks[5], (L, SSM_HEADS), jnp.float32)
    dt0 = jnp.exp(u * (math.log(0.1) - math.log(0.001)) + math.log(0.001))
    dt_bias = dt0 + jnp.log(-jnp.expm1(-dt0))
    a_log = jnp.log(jax.random.uniform(ks[6], (L, SSM_HEADS), jnp.float32, minval=1.0, maxval=16.0))
    d_skip = 1.0 + 0.1 * jax.random.normal(ks[7], (L, SSM_HEADS), jnp.float32)
    ssm_norm_w = 1.0 + 0.1 * jax.random.normal(ks[8], (L, SSM_INNER), jnp.float32)
    sinks = 0.5 * jax.random.normal(ks[9], (L, SWA_HEADS), jnp.float32)
    f_bias = 3.0 + 0.5 * jax.random.normal(ks[10], (L, FOX_HEADS), jnp.float32)
    gate_bias = 0.1 * jax.random.normal(ks[11], (L, N_BRANCH, D), jnp.float32)
    w_proj = jax.random.normal(ks[12], (L, N_BRANCH, W, D), jnp.float32) * W ** -0.5
    w_out = jax.random.normal(ks[13], (L, D, D), jnp.float32) * D ** -0.5
    final_norm_w = 1.0 + 0.1 * jax.random.normal(ks[14], (D,), jnp.float32)
    return {"x": x, "norm_w": norm_w, "w_in": w_in, "conv_w": conv_w, "conv_b": conv_b,
            "dt_bias": dt_bias, "a_log": a_log, "d_skip": d_skip, "ssm_norm_w": ssm_norm_w,
            "sinks": sinks, "f_bias": f_bias, "gate_bias": gate_bias, "w_proj": w_proj,
            "w_out": w_out, "final_norm_w": final_norm_w}


def reference(x, norm_w, w_in, conv_w, conv_b, dt_bias, a_log, d_skip, ssm_norm_w,
              sinks, f_bias, gate_bias, w_proj, w_out, final_norm_w):
    b, s, d = x.shape
    pos = jnp.arange(s, dtype=jnp.float32)
    inv_freq = ROPE_THETA ** (-jnp.arange(0, HEAD_DIM, 2, dtype=jnp.float32) / HEAD_DIM)
    ang = pos[:, None] * inv_freq[None, :]
    cos, sin = jnp.cos(ang).astype(x.dtype), jnp.sin(ang).astype(x.dtype)
    split_at = [int(v) for v in np.cumsum(IN_SIZES)[:-1]]
    for layer in range(DEPTH):
        h = rms_norm(x, norm_w[layer])
        proj = jnp.einsum('bsd,de->bse', h, w_in[layer])
        (a_xbc, a_z, a_dt, b_q, b_k, b_v, b_z,
         c_q, c_k, c_v, c_f, c_z, gates) = jnp.split(proj, split_at, axis=-1)
        y_a = mamba2_branch(a_xbc, a_z, a_dt, conv_w[layer], conv_b[layer], dt_bias[layer],
                            a_log[layer], d_skip[layer], ssm_norm_w[layer])
        y_b = sliding_window_branch(b_q, b_k, b_v, b_z, sinks[layer], cos, sin)
        y_c = forgetting_attention_branch(c_q, c_k, c_v, c_f, c_z, f_bias[layer])
        ys = jnp.stack([y_a, y_b, y_c], axis=2)
        branch = jnp.einsum('bsiw,iwd->bsid', ys, w_proj[layer])
        g = jax.nn.sigmoid(gates.reshape(b, s, N_BRANCH, d) + gate_bias[layer])
        merged = jnp.sum(g * branch, axis=2)
        x = x + jnp.einsum('bsd,de->bse', merged, w_out[layer])
    return rms_norm(x, final_norm_w)
```

```python
import math
from contextlib import ExitStack

import numpy as np
import concourse.bass as bass
import concourse.mybir as mybir
from concourse.bass_utils import run_bass_kernel_spmd

F32 = mybir.dt.float32
BF16 = mybir.dt.bfloat16
AF = mybir.ActivationFunctionType
ALU = mybir.AluOpType

D = 1024
KC = 8
DEPTH = 2
NCORES = 8
EPS = 1e-6
ENGS = ("pe", "act", "dve", "pool", "sp")

A0, A_N = 0, 3088
B0, B_N = 3088, 3840
C0, C_G = 6928, 2056
G0, G_N = 6928 + 2 * 2056, 3072
NT = G0 + G_N
SWA_QORDER = [0, 4, 1, 5, 2, 6, 3, 7, 8, 12, 9, 13, 10, 14, 11, 15]


def _col_order():
    o = {}
    off = 0
    names = [("a_xbc", 2048), ("a_z", 1024), ("a_dt", 16), ("b_q", 1024), ("b_k", 256), ("b_v", 256),
             ("b_z", 1024), ("c_q", 1024), ("c_k", 1024), ("c_v", 1024), ("c_f", 16), ("c_z", 1024),
             ("gates", 3072)]
    for n, s in names:
        o[n] = off
        off += s
    cols = []
    cols += list(range(o["a_xbc"], o["a_xbc"] + 2048))
    cols += list(range(o["a_z"], o["a_z"] + 1024))
    cols += list(range(o["a_dt"], o["a_dt"] + 16))
    assert len(cols) == A_N
    q = [o["b_q"] + h * 64 + d for h in SWA_QORDER for d in range(64)]
    qs = [o["b_q"] + h * 64 + (d + 32) % 64 for h in SWA_QORDER for d in range(64)]
    k = [o["b_k"] + h * 64 + d for h in range(4) for d in range(64)]
    ks = [o["b_k"] + h * 64 + (d + 32) % 64 for h in range(4) for d in range(64)]
    cols += q + k + qs + ks
    cols += list(range(o["b_v"], o["b_v"] + 256))
    cols += list(range(o["b_z"], o["b_z"] + 1024))
    assert len(cols) == B0 + B_N
    for hg in range(2):
        for nm in ("c_q", "c_k", "c_v", "c_z"):
            cols += list(range(o[nm] + hg * 512, o[nm] + hg * 512 + 512))
        cols += list(range(o["c_f"] + hg * 8, o["c_f"] + hg * 8 + 8))
    assert len(cols) == G0
    cols += list(range(o["gates"], o["gates"] + 3072))
    assert len(cols) == NT
    return np.array(cols, dtype=np.int64)


class Prog:
    def __init__(self, nc, stack, same_engine_sync=True):
        self.nc = nc
        self.stack = stack
        self.same = same_engine_sync
        self.ops = []
        self.last_w = {}
        self.readers = {}
        self.eng_sem = {e: stack.enter_context(nc.semaphore("s_" + e)) for e in ENGS}
        self.cnt = {e: 0 for e in ENGS}
        self.dsem = {}
        self.dcnt = {}
        self.known = {e: {} for e in ENGS}
        self.done_ops = 0

    def op(self, eng, meth, reads=(), writes=(), dma=None, **kw):
        idx = len(self.ops)
        deps = set()
        for k in reads:
            if k in self.last_w:
                deps.add(self.last_w[k])
        for k in writes:
            if k in self.last_w:
                deps.add(self.last_w[k])
            for r in self.readers.get(k, ()):
                deps.add(r)
        deps.discard(idx)
        for k in reads:
            self.readers.setdefault(k, []).append(idx)
        for k in writes:
            self.last_w[k] = idx
            self.readers[k] = []
        self.ops.append(dict(eng=eng, meth=meth, kw=kw, deps=deps, dma=dma, sig=False, ev=None))
        return idx

    def emit(self, final=False):
        nc, ops = self.nc, self.ops
        import os as _os
        if _os.environ.get("OPS_LIMIT"):
            del ops[int(_os.environ["OPS_LIMIT"]):]
        lo = self.done_ops
        new = range(lo, len(ops))
        for i in new:
            o = ops[i]
            if o["dma"] is not None:
                o["sig"] = True
            for d in o["deps"]:
                od = ops[d]
                if d < lo:
                    continue
                if od["dma"] is not None or od["eng"] != o["eng"] or o["dma"] is not None:
                    od["sig"] = True
                elif self.same and o["eng"] != "pe":
                    od["sig"] = True
        for i in new:
            o = ops[i]
            if o["dma"] is not None:
                k = o["dma"]
                if k not in self.dsem:
                    self.dsem[k] = self.stack.enter_context(nc.semaphore("d_" + str(k)))
                    self.dcnt[k] = 0
                self.dcnt[k] += 16
                o["ev"] = (self.dsem[k], self.dcnt[k], "d_" + str(k))
            elif o["sig"]:
                self.cnt[o["eng"]] += 1
                o["ev"] = (self.eng_sem[o["eng"]], self.cnt[o["eng"]], o["eng"])
        per_eng = {e: [] for e in ENGS}
        for i in new:
            per_eng[ops[i]["eng"]].append(i)
        same = self.same

        def body(ename):
            def f(eng):
                kn = self.known[ename]
                for i in per_eng[ename]:
                    o = ops[i]
                    need = {}
                    for d in o["deps"]:
                        if d < lo:
                            continue
                        od = ops[d]
                        ev = od["ev"]
                        if ev is None:
                            continue
                        sem, val, name = ev
                        if (od["dma"] is None and od["eng"] == ename and o["dma"] is None
                                and (ename == "pe" or not same)):
                            continue
                        if kn.get(name, 0) >= val:
                            continue
                        if name not in need or need[name][1] < val:
                            need[name] = (sem, val)
                    for name, (sem, val) in need.items():
                        eng.wait_ge(sem, val)
                        kn[name] = val
                    ins = getattr(eng, o["meth"])(**o["kw"])
                    if o["ev"] is not None:
                        sem, val, name = o["ev"]
                        ins.then_inc(sem, 16 if o["dma"] is not None else 1)
                if ename == "sp":
                    for k, s in self.dsem.items():
                        if kn.get("d_" + str(k), 0) < self.dcnt[k]:
                            eng.wait_ge(s, self.dcnt[k])
                            kn["d_" + str(k)] = self.dcnt[k]
            return f

        with nc.Block() as block:
            block.tensor(body("pe"))
            block.scalar(body("act"))
            block.vector(body("dve"))
            block.gpsimd(body("pool"))
            block.sync(body("sp"))
        self.done_ops = len(ops)


class Builder:
    def __init__(self, S, NSEQ, debug=False, layers=DEPTH, phases="ABCD"):
        self.S, self.NSEQ, self.debug, self.layers, self.phases = S, NSEQ, debug, layers, phases
        self.NCH = S // 128
        self.NTL = S // 512
        nc = self.nc = bass.Bass("TRN2", target_bir_lowering=False)
        L = DEPTH
        okind = "ExternalOutput" if debug else "Internal"
        self.x = nc.dram_tensor("x", [NSEQ, S, D], F32, kind="ExternalInput").ap()
        self.win = nc.dram_tensor("win", [L, 128, KC, NT], F32, kind="ExternalInput").ap()
        self.wproj = nc.dram_tensor("wproj", [L, 3, 128, KC, D], F32, kind="ExternalInput").ap()
        self.wout = nc.dram_tensor("wout", [L, 128, KC, D], F32, kind="ExternalInput").ap()
        self.ppart = nc.dram_tensor("ppart", [128, L * (8 + 64 + 16)], F32, kind="ExternalInput").ap()
        self.prow = nc.dram_tensor("prow", [L * (80 + 1024 + 3072) + 1024], F32, kind="ExternalInput").ap()
        self.rope = nc.dram_tensor("rope", [2, 128, S], F32, kind="ExternalInput").ap()
        self.out = nc.dram_tensor("out", [NSEQ, S, D], F32, kind="ExternalOutput").ap()
        self.Y = nc.dram_tensor("ybr", [3, NSEQ, S, D], BF16, kind=okind).ap()
        self.X1 = nc.dram_tensor("x1", [NSEQ, S, D], F32, kind=okind).ap()
        self.pools = {"gen": list(range(8))}
        self.prr = {}

    def psb(self, n=1, pool="gen"):
        banks = self.pools[pool]
        r = self.prr.get(pool, 0)
        if n == 2:
            assert len(banks) % 2 == 0
            if r % 2:
                r += 1
            b = banks[r % len(banks)]
            self.prr[pool] = r + 2
            return b
        b = banks[r % len(banks)]
        self.prr[pool] = r + 1
        return b

    def pk(self, b, n=1):
        return ["ps%d" % (b + i) for i in range(n)]

    def build(self):
        nc = self.nc
        with ExitStack() as gst:
            self.P = P = Prog(nc, gst)
            sb = lambda name, shape, dt: gst.enter_context(nc.sbuf_tensor(name, shape, dt))
            self.ps = gst.enter_context(nc.psum_tensor("ps", [128, 8, 512], F32))
            self.ident = sb("ident", [128, 128], BF16)
            self.identf = sb("identf", [128, 128], F32)
            self.tri = sb("tri", [128, 128], F32)
            self.elast = sb("elast", [128, 128], F32)
            self.onesf = sb("onesf", [128, 128], F32)
            self.mle = sb("mle", [128, 128], BF16)
            self.mgt = sb("mgt", [128, 128], BF16)
            self.mlef = sb("mlef", [128, 128], F32)
            self.sel = sb("sel", [8, 8, 65], BF16)
            self.ppt = sb("ppt", [128, DEPTH * 88], F32)
            self.prw = sb("prw", [128, DEPTH * 80], F32)
            self.abc = sb("abc", [128, DEPTH * 16], F32)
            self.esink = sb("esink", [128, DEPTH * 16], F32)
            self.xt = [sb("xt%d" % i, [128, D], F32) for i in range(2)]
            self.xn = [sb("xn%d" % i, [128, D], BF16) for i in range(2)]
            self.junk = [sb("junk%d" % i, [128, D], BF16) for i in range(2)]
            self.st4 = [sb("st4_%d" % i, [128, 4], F32) for i in range(2)]
            self.xslot = 0
            self.setup_consts()
            import os as _os
            if _os.environ.get("SETUP_LIMIT"):
                lim = int(_os.environ["SETUP_LIMIT"])
                del P.ops[lim:]
            P.emit()
            for l in range(self.layers):
                if "A" in self.phases:
                    self.phase_A(l)
                if "B" in self.phases:
                    self.phase_B(l)
                if "C" in self.phases:
                    for hg in range(2):
                        self.phase_C(l, hg)
                if "D" in self.phases:
                    self.phase_D(l)
        return nc

    def setup_consts(self):
        P = self.P
        P.op("pool", "memset", writes=["ident"], ap=self.ident[:], constant=1.0)
        P.op("pool", "affine_select", reads=["ident"], writes=["ident"], out=self.ident[:], in_=self.ident[:],
             pattern=[[-1, 128]], compare_op=ALU.is_equal, fill=0.0, base=0, channel_multiplier=1)
        P.op("pool", "memset", writes=["identf"], ap=self.identf[:], constant=1.0)
        P.op("pool", "affine_select", reads=["identf"], writes=["identf"], out=self.identf[:], in_=self.identf[:],
             pattern=[[-1, 128]], compare_op=ALU.is_equal, fill=0.0, base=0, channel_multiplier=1)
        P.op("pool", "memset", writes=["tri"], ap=self.tri[:], constant=1.0)
        P.op("pool", "affine_select", reads=["tri"], writes=["tri"], out=self.tri[:], in_=self.tri[:],
             pattern=[[1, 128]], compare_op=ALU.is_ge, fill=0.0, base=0, channel_multiplier=-1)
        P.op("pool", "memset", writes=["mlef"], ap=self.mlef[:], constant=1.0)
        P.op("pool", "affine_select", reads=["mlef"], writes=["mlef"], out=self.mlef[:], in_=self.mlef[:],
             pattern=[[1, 128]], compare_op=ALU.is_ge, fill=0.0, base=0, channel_multiplier=-1)
        P.op("pool", "memset", writes=["mle"], ap=self.mle[:], constant=1.0)
        P.op("pool", "affine_select", reads=["mle"], writes=["mle"], out=self.mle[:], in_=self.mle[:],
             pattern=[[1, 128]], compare_op=ALU.is_ge, fill=0.0, base=0, channel_multiplier=-1)
        P.op("pool", "memset", writes=["mgt"], ap=self.mgt[:], constant=1.0)
        P.op("pool", "affine_select", reads=["mgt"], writes=["mgt"], out=self.mgt[:], in_=self.mgt[:],
             pattern=[[-1, 128]], compare_op=ALU.is_gt, fill=0.0, base=0, channel_multiplier=1)
        P.op("pool", "memset", writes=["elast"], ap=self.elast[:], constant=1.0)
        P.op("pool", "affine_select", reads=["elast"], writes=["elast"], out=self.elast[:], in_=self.elast[:],
             pattern=[[0, 128]], compare_op=ALU.is_equal, fill=0.0, base=-127, channel_multiplier=1)
        P.op("pool", "memset", writes=["onesf"], ap=self.onesf[:], constant=1.0)
        P.op("pool", "memset", writes=["sel"], ap=self.sel[:], constant=8.0)
        P.op("pool", "affine_select", reads=["sel"], writes=["sel"], out=self.sel[:], in_=self.sel[:],
             pattern=[[1, 8], [0, 65]], compare_op=ALU.is_equal, fill=0.0, base=0, channel_multiplier=-1)
        P.op("pool", "affine_select", reads=["sel"], writes=["sel"], out=self.sel[:], in_=self.sel[:],
             pattern=[[0, 8], [1, 65]], compare_op=ALU.is_equal, fill=0.0, base=-64, channel_multiplier=0)
        P.op("sp", "dma_start", writes=["ppt"], dma="ppt", out=self.ppt[:], in_=self.ppart)
        for l in range(DEPTH):
            P.op("sp", "dma_start", writes=["prw"], dma="prw", out=self.prw[:, l * 80:(l + 1) * 80],
                 in_=self.prow[l * 4176:l * 4176 + 80].partition_broadcast(128))
        for l in range(DEPTH):
            P.op("act", "activation", reads=["prw"], writes=["abc"], out=self.abc[:, l * 16:(l + 1) * 16],
                 in_=self.prw[:, l * 80 + 16:l * 80 + 32], func=AF.Exp)
            P.op("dve", "tensor_scalar", reads=["abc"], writes=["abc"], out=self.abc[:, l * 16:(l + 1) * 16],
                 in0=self.abc[:, l * 16:(l + 1) * 16], scalar1=-1.0, scalar2=None, op0=ALU.mult)
            P.op("act", "activation", reads=["prw"], writes=["esink"], out=self.esink[:, l * 16:(l + 1) * 16],
                 in_=self.prw[:, l * 80 + 48:l * 80 + 64], func=AF.Exp)

    def nw(self, l):
        return self.ppt[:, l * 88:l * 88 + 8]

    def convw(self, l, b, k):
        o = l * 88 + 8 + b * 4 + k
        return self.ppt[:, o:o + 1]

    def convb(self, l, b):
        o = l * 88 + 72 + b
        return self.ppt[:, o:o + 1]

    def rowp(self, l, i):
        return self.prw[:, l * 80 + i * 16:l * 80 + (i + 1) * 16]

    def xsrc(self, l, s, c):
        src = self.x if l == 0 else self.X1
        return src[s, c * 128:(c + 1) * 128, :], ("xin" if l == 0 else "x1_%d_%d" % (s, c))

    def h_chunk(self, l, s, c, hT, hkey, col0):
        P = self.P
        sl = self.xslot
        self.xslot ^= 1
        xt, xn, junk, st4 = self.xt[sl], self.xn[sl], self.junk[sl], self.st4[sl]
        src, skey = self.xsrc(l, s, c)
        P.op("sp", "dma_start", reads=[skey], writes=["xt%d" % sl], dma="xt%d" % sl, out=xt[:], in_=src)
        P.op("act", "activation", reads=["xt%d" % sl], writes=["junk%d" % sl, "st4_%d" % sl],
             out=junk[:], in_=xt[:], func=AF.Square, accum_out=st4[:, 0:1])
        P.op("act", "activation", reads=["st4_%d" % sl, "epsb"], writes=["st4_%d" % sl], out=st4[:, 1:2], in_=st4[:, 0:1],
             func=AF.Ln, scale=1.0 / D, bias=self.epsb[:])
        P.op("act", "activation", reads=["st4_%d" % sl], writes=["st4_%d" % sl], out=st4[:, 2:3], in_=st4[:, 1:2],
             func=AF.Exp, scale=-0.5)
        P.op("act", "activation", reads=["xt%d" % sl, "st4_%d" % sl], writes=["xn%d" % sl], out=xn[:], in_=xt[:],
             func=AF.Identity, scale=st4[:, 2:3])
        b = self.psb()
        ptb = self.ps[:, b, :].bitcast(BF16).rearrange("p (k t) -> p k t", k=8)
        for kc in range(KC):
            P.op("pe", "transpose", reads=["xn%d" % sl, "ident"], writes=self.pk(b), out=ptb[:, kc, :],
                 in_=xn[:, kc * 128:(kc + 1) * 128], identity=self.ident[:])
        P.op("dve", "tensor_tensor", reads=self.pk(b) + ["ppt"], writes=[hkey],
             out=hT[:, :, col0:col0 + 128], in0=ptb,
             in1=self.nw(l).unsqueeze(2).broadcast_to([128, 8, 128]), op=ALU.mult)
        return sl

    def load_w(self, wt, key, l, c0, n, src=None):
        P = self.P
        src = self.win[l] if src is None else src
        step = 1024
        for kc in range(KC):
            for o in range(0, n, step):
                m = min(step, n - o)
                P.op("pool", "dma_start", writes=[key], dma=key, out=wt[:, kc, o:o + m],
                     in_=src[:, kc, c0 + o:c0 + o + m])

    def proj_tm(self, dst_bank, hT, hkey, col0, wt, wkey, wc0, n, poff=0):
        P = self.P
        for kc in range(KC):
            P.op("pe", "matmul", reads=[hkey, wkey], writes=self.pk(dst_bank),
                 out=self.ps[:, dst_bank, poff:poff + n], lhsT=hT[:, kc, col0:col0 + 128],
                 rhs=wt[:, kc, wc0:wc0 + n], start=(kc == 0), stop=(kc == KC - 1))

    def proj_fm(self, dst_bank, hT, hkey, ntok, wt, wkey, wc0, m, first=True):
        P = self.P
        for kc in range(KC):
            P.op("pe", "matmul", reads=[hkey, wkey], writes=self.pk(dst_bank),
                 out=self.ps[0:m, dst_bank, 0:ntok], lhsT=wt[:, kc, wc0:wc0 + m],
                 rhs=hT[:, kc, 0:ntok], start=(first and kc == 0), stop=(kc == KC - 1))

    def phase_A(self, l):
        nc, P, S = self.nc, self.P, self.S
        with ExitStack() as st:
            self.uid = getattr(self, "uid", 0) + 1
            sb = lambda name, shape, dt, _u=self.uid: st.enter_context(nc.sbuf_tensor("%s_u%d" % (name, _u), shape, dt))
            wa = sb("wa", [128, KC, A_N], BF16)
            hT = sb("hT", [128, KC, 512], BF16)
            Ub = [sb("Ub%d" % i, [128, 515], F32) for i in range(2)]
            Ucar = sb("Ucar", [128, 16, 3], F32)
            junkA = sb("junkA", [128, 256], BF16)
            acc = [sb("acc%d" % i, [128, 512], F32) for i in range(2)]
            xsT = sb("xsT", [128, 8, 512], BF16)
            BT = sb("BT", [128, 4, 512], BF16)
            CT = sb("CT", [128, 4, 512], BF16)
            H = sb("H", [128, 16, 64], F32)
            Hb = sb("Hb", [128, 16, 64], BF16)
            Htmp = sb("Htmp", [128, 16, 64], F32)
            sm = sb("sm", [128, 12, 16], F32)
            rhsall = sb("rhsall", [128, 16, 128], F32)
            Eh = sb("Eh", [128, 16, 128], F32)
            dec = Eh
            cbm = sb("cbm", [128, 4, 128], F32)
            MT = sb("MT", [128, 16, 128], BF16)
            xstm = sb("xstm", [128, 16, 64], F32)
            xdt = sb("xdt", [128, 16, 64], BF16)
            xw = sb("xw", [128, 16, 64], BF16)
            Btm = sb("Btm", [128, 4, 128], BF16)
            sz = sb("sz", [128, D], F32)
            y1 = sb("y1", [128, 16, 64], F32)
            y2 = sb("y2", [128, 16, 64], F32)
            yo = [sb("yo%d" % i, [128, D], BF16) for i in range(2)]
            snw = sb("snw", [128, D], F32)
            self.epsb = sb("epsb", [128, 1], F32)
            P.op("dve", "memset", writes=["epsb"], ap=self.epsb[:], constant=EPS)
            self.load_w(wa, "wa", l, A0, A_N)
            P.op("sp", "dma_start", writes=["snw"], dma="snw", out=snw[:],
                 in_=self.prow[l * 4176 + 80:l * 4176 + 80 + 1024].partition_broadcast(128))
            ps = self.ps
            for s in range(self.NSEQ):
                P.op("dve", "memset", writes=["H"], ap=H[:], constant=0.0)
                P.op("pool", "memset", writes=["Ucar%d" % b for b in range(16)], ap=Ucar[:], constant=0.0)
                for t in range(self.NTL):
                    for c4 in range(4):
                        self.h_chunk(l, s, t * 4 + c4, hT, "hT", c4 * 128)
                    for b in range(16):
                        pb = self.psb()
                        self.proj_fm(pb, hT, "hT", 512, wa, "wa", b * 128, 128)
                        ukey = "Ub%d" % (b % 2)
                        U_ = Ub[b % 2]
                        P.op("pool", "tensor_copy", reads=["Ucar%d" % b], writes=[ukey], out=U_[:, 0:3], in_=Ucar[:, b, :])
                        P.op("act", "activation", reads=self.pk(pb), writes=[ukey], out=U_[:, 3:515],
                             in_=ps[:, pb, :], func=AF.Copy)
                        a = acc[b % 2]
                        akey = "acc%d" % (b % 2)
                        P.op("dve", "tensor_scalar", reads=[ukey, "ppt"], writes=[akey], out=a[:], in0=U_[:, 0:512],
                             scalar1=self.convw(l, b, 0), scalar2=None, op0=ALU.mult)
                        for k in range(1, 4):
                            P.op("dve", "scalar_tensor_tensor", reads=[ukey, akey, "ppt"], writes=[akey], out=a[:],
                                 in0=U_[:, k:k + 512], scalar=self.convw(l, b, k), in1=a[:],
                                 op0=ALU.mult, op1=ALU.add)
                        if b < 8:
                            dst, dkey = xsT[:, b, :], "xsT"
                        elif b < 12:
                            dst, dkey = BT[:, b - 8, :], "BT"
                        else:
                            dst, dkey = CT[:, b - 12, :], "CT"
                        P.op("act", "activation", reads=[akey, "ppt"], writes=[dkey], out=dst, in_=a[:],
                             func=AF.Silu, bias=self.convb(l, b))
                        P.op("pool", "tensor_copy", reads=[ukey], writes=["Ucar%d" % b], out=Ucar[:, b, :], in_=U_[:, 512:515])
                    for c4 in range(4):
                        c = t * 4 + c4
                        cs = slice(c4 * 128, (c4 + 1) * 128)
                        pd = self.psb()
                        self.proj_tm(pd, hT, "hT", c4 * 128, wa, "wa", 3072, 16)
                        dtr, dt_, adt, acum, nacum, lastbc, dS, ea, cd, dtS, e1 = [sm[:, i, :] for i in range(11)]
                        P.op("dve", "tensor_tensor", reads=self.pk(pd) + ["prw"], writes=["sm0"], out=dtr,
                             in0=ps[:, pd, 0:16], in1=self.rowp(l, 0), op=ALU.add)
                        P.op("act", "activation", reads=["sm0"], writes=["sm10"], out=e1, in_=dtr, func=AF.Exp)
                        P.op("act", "activation", reads=["sm10"], writes=["sm1"], out=dt_, in_=e1, func=AF.Ln, bias=1.0)
                        P.op("dve", "tensor_tensor", reads=["sm1", "abc"], writes=["sm2"], out=adt, in0=dt_,
                             in1=self.abc[:, l * 16:(l + 1) * 16], op=ALU.mult)
                        pa = self.psb()
                        P.op("pe", "matmul", reads=["tri", "sm2"], writes=self.pk(pa), out=ps[:, pa, 0:16],
                             lhsT=self.tri[:], rhs=adt, start=True, stop=True)
                        P.op("dve", "tensor_copy", reads=self.pk(pa), writes=["sm3"], out=acum, in_=ps[:, pa, 0:16])
                        P.op("pe", "matmul", reads=["elast", "sm3"], writes=self.pk(pa), out=ps[:, pa, 16:32],
                             lhsT=self.elast[:], rhs=acum, start=True, stop=True)
                        P.op("dve", "tensor_copy", reads=self.pk(pa), writes=["sm5"], out=lastbc, in_=ps[:, pa, 16:32])
                        P.op("dve", "tensor_tensor", reads=["sm5", "sm3"], writes=["sm6"], out=dS, in0=lastbc, in1=acum,
                             op=ALU.subtract)
                        P.op("act", "activation", reads=["sm6"], writes=["sm6"], out=dS, in_=dS, func=AF.Exp)
                        P.op("act", "activation", reads=["sm3"], writes=["sm7"], out=ea, in_=acum, func=AF.Exp)
                        P.op("act", "activation", reads=["sm5"], writes=["sm8"], out=cd, in_=lastbc, func=AF.Exp)
                        P.op("dve", "tensor_tensor", reads=["sm1", "sm6"], writes=["sm9"], out=dtS, in0=dt_, in1=dS,
                             op=ALU.mult)
                        P.op("dve", "tensor_tensor", reads=["tri", "sm2"], writes=["rhsall"], out=rhsall[:],
                             in0=self.tri[:].unsqueeze(1).broadcast_to([128, 16, 128]),
                             in1=adt.unsqueeze(2).broadcast_to([128, 16, 128]), op=ALU.mult)
                        for g in range(4):
                            pg = self.psb()
                            P.op("pe", "matmul", reads=["onesf", "rhsall"], writes=self.pk(pg),
                                 out=ps[:, pg, :], lhsT=self.onesf[:],
                                 rhs=rhsall[:, 4 * g:4 * g + 4, :], start=True, stop=True)
                            for r in range(4):
                                h = 4 * g + r
                                P.op("dve", "tensor_scalar", reads=self.pk(pg) + ["sm3"], writes=["Eh%d" % g],
                                     out=Eh[:, h, :], in0=ps[:, pg, r * 128:(r + 1) * 128],
                                     scalar1=acum[:, h:h + 1], scalar2=0.0, op0=ALU.subtract, op1=ALU.min)
                            P.op("act", "activation", reads=["Eh%d" % g], writes=["Eh%d" % g, "dec%d" % g],
                                 out=dec[:, 4 * g:4 * g + 4, :], in_=Eh[:, 4 * g:4 * g + 4, :], func=AF.Exp)
                        pc = self.psb()
                        for g in range(4):
                            P.op("pe", "matmul", reads=["BT", "CT"], writes=self.pk(pc),
                                 out=ps[:, pc, g * 128:(g + 1) * 128], lhsT=BT[:, g, cs], rhs=CT[:, g, cs],
                                 start=True, stop=True)
                        P.op("dve", "tensor_tensor", reads=self.pk(pc) + ["mlef"], writes=["cbm"], out=cbm[:],
                             in0=ps[:, pc, :].rearrange("p (g l) -> p g l", g=4),
                             in1=self.mlef[:].unsqueeze(1).broadcast_to([128, 4, 128]), op=ALU.mult)
                        for g in range(4):
                            P.op("pool", "tensor_tensor", reads=["dec%d" % g, "Eh%d" % g, "cbm"], writes=["MT%d" % g],
                                 out=MT[:, 4 * g:4 * g + 4, :], in0=dec[:, 4 * g:4 * g + 4, :],
                                 in1=cbm[:, g, :].unsqueeze(1).broadcast_to([128, 4, 128]), op=ALU.mult)
                        px = self.psb()
                        pxb = ps[:, px, :].bitcast(BF16).rearrange("p (k t) -> p k t", k=8)
                        for b in range(8):
                            P.op("pe", "transpose", reads=["xsT", "ident"], writes=self.pk(px), out=pxb[:, b, :],
                                 in_=xsT[:, b, cs], identity=self.ident[:])
                        pxv = ps[:, px, :].bitcast(BF16).rearrange("p (h d) -> p h d", h=16)
                        P.op("act", "activation", reads=self.pk(px), writes=["xstm"], out=xstm[:], in_=pxv, func=AF.Copy)
                        P.op("dve", "tensor_tensor", reads=["xstm", "sm1"], writes=["xdt"], out=xdt[:], in0=xstm[:],
                             in1=dt_.unsqueeze(2).broadcast_to([128, 16, 64]), op=ALU.mult)
                        P.op("pool", "tensor_tensor", reads=["xstm", "sm9"], writes=["xw"], out=xw[:], in0=xstm[:],
                             in1=dtS.unsqueeze(2).broadcast_to([128, 16, 64]), op=ALU.mult)
                        pbt = self.psb()
                        pbb = ps[:, pbt, 0:256].bitcast(BF16).rearrange("p (k t) -> p k t", k=4)
                        for g in range(4):
                            P.op("pe", "transpose", reads=["BT", "ident"], writes=self.pk(pbt), out=pbb[:, g, :],
                                 in_=BT[:, g, cs], identity=self.ident[:])
                        P.op("act", "activation", reads=self.pk(pbt), writes=["Btm"], out=Btm[:], in_=pbb, func=AF.Copy)
                        P.op("act", "activation", reads=["H"], writes=["Hb"], out=Hb[:], in_=H[:], func=AF.Copy)
                        po = self.psb(2)
                        for g in range(4):
                            P.op("pe", "matmul", reads=["CT", "Hb"], writes=self.pk(po, 2),
                                 out=ps[:, po + g // 2, (g % 2) * 256:(g % 2) * 256 + 256],
                                 lhsT=CT[:, g, cs], rhs=Hb[:, 4 * g:4 * g + 4, :], start=True, stop=True)
                        poall = ps[:, po:po + 2, :].rearrange("p b (h d) -> p (b h) d", d=64)
                        P.op("dve", "tensor_tensor", reads=self.pk(po, 2) + ["sm7"], writes=["y1"], out=y1[:], in0=poall,
                             in1=ea.unsqueeze(2).broadcast_to([128, 16, 64]), op=ALU.mult)
                        pst = self.psb(2)
                        for g in range(4):
                            P.op("pe", "matmul", reads=["Btm", "xw"], writes=self.pk(pst, 2),
                                 out=ps[:, pst + g // 2, (g % 2) * 256:(g % 2) * 256 + 256],
                                 lhsT=Btm[:, g, :], rhs=xw[:, 4 * g:4 * g + 4, :], start=True, stop=True)
                        pstall = ps[:, pst:pst + 2, :].rearrange("p b (h d) -> p (b h) d", d=64)
                        P.op("dve", "tensor_tensor", reads=["H", "sm8"], writes=["Htmp"], out=Htmp[:], in0=H[:],
                             in1=cd.unsqueeze(2).broadcast_to([128, 16, 64]), op=ALU.mult)
                        P.op("dve", "tensor_tensor", reads=self.pk(pst, 2) + ["Htmp"], writes=["H"], out=H[:], in0=pstall,
                             in1=Htmp[:], op=ALU.add)
                        pyd = self.psb(2)
                        for h in range(16):
                            P.op("pe", "matmul", reads=["MT%d" % (h // 4), "xdt"], writes=self.pk(pyd, 2),
                                 out=ps[:, pyd + h // 8, (h % 8) * 64:(h % 8) * 64 + 64],
                                 lhsT=MT[:, h, :], rhs=xdt[:, h, :], start=True, stop=True)
                        pydall = ps[:, pyd:pyd + 2, :].rearrange("p b (h d) -> p (b h) d", d=64)
                        P.op("dve", "tensor_tensor", reads=self.pk(pyd, 2) + ["y1"], writes=["y1"], out=y1[:], in0=pydall,
                             in1=y1[:], op=ALU.add)
                        P.op("pool", "tensor_tensor", reads=["xstm", "prw"], writes=["y2"], out=y2[:], in0=xstm[:],
                             in1=self.rowp(l, 2).unsqueeze(2).broadcast_to([128, 16, 64]), op=ALU.mult)
                        P.op("pool", "tensor_tensor", reads=["y1", "y2"], writes=["y2"], out=y2[:], in0=y1[:], in1=y2[:],
                             op=ALU.add)
                        pz = self.psb(2)
                        for n in range(2):
                            self.proj_tm(pz + n, hT, "hT", c4 * 128, wa, "wa", 2048 + n * 512, 512)
                        P.op("act", "activation", reads=self.pk(pz, 2), writes=["sz"], out=sz[:],
                             in_=ps[:, pz:pz + 2, :].rearrange("p b n -> p (b n)"), func=AF.Silu)
                        y2f = y2[:].rearrange("p h d -> p (h d)")
                        P.op("dve", "tensor_tensor", reads=["y2", "sz"], writes=["y2"], out=y2f, in0=y2f, in1=sz[:],
                             op=ALU.mult)
                        ss = sm[:, 11, 0:4]
                        for g in range(4):
                            P.op("act", "activation", reads=["y2"], writes=["junkA", "sm11"], out=junkA[:],
                                 in_=y2f[:, g * 256:(g + 1) * 256], func=AF.Square, accum_out=sm[:, 11, g:g + 1])
                        P.op("act", "activation", reads=["sm11", "epsb"], writes=["sm11"], out=sm[:, 11, 4:8], in_=ss, func=AF.Ln,
                             scale=1.0 / 256, bias=self.epsb[:])
                        P.op("act", "activation", reads=["sm11"], writes=["sm11"], out=sm[:, 11, 8:12], in_=sm[:, 11, 4:8],
                             func=AF.Exp, scale=-0.5)
                        P.op("dve", "tensor_tensor", reads=["y2", "sm11"], writes=["y2"],
                             out=y2[:].rearrange("p (g r) d -> p g (r d)", g=4),
                             in0=y2[:].rearrange("p (g r) d -> p g (r d)", g=4),
                             in1=sm[:, 11, 8:12].unsqueeze(2).broadcast_to([128, 4, 256]), op=ALU.mult)
                        yq = yo[c % 2]
                        P.op("pool", "tensor_tensor", reads=["y2", "snw"], writes=["yo%d" % (c % 2)], out=yq[:], in0=y2f,
                             in1=snw[:], op=ALU.mult)
                        P.op("pool", "dma_start", reads=["yo%d" % (c % 2)], writes=["Y0_%d_%d" % (s, c)],
                             dma="yo%d" % (c % 2), out=self.Y[0, s, c * 128:(c + 1) * 128, :], in_=yq[:])
            P.emit()

    def phase_B(self, l):
        nc, P, S, NCH = self.nc, self.P, self.S, self.NCH
        with ExitStack() as st:
            self.uid = getattr(self, "uid", 0) + 1
            sb = lambda name, shape, dt, _u=self.uid: st.enter_context(nc.sbuf_tensor("%s_u%d" % (name, _u), shape, dt))
            wb = sb("wb", [128, KC, B_N], BF16)
            hT = sb("hT", [128, KC, 512], BF16)
            cosT = sb("cosT", [128, 512], F32)
            sinS = sb("sinS", [128, 512], F32)
            t1 = [sb("t1_%d" % i, [128, 512], F32) for i in range(2)]
            t2 = [sb("t2_%d" % i, [128, 512], F32) for i in range(2)]
            qrT = sb("qrT", [128, 8, 512], BF16)
            krT = sb("krT", [128, 2, S], BF16)
            V = sb("V", [128, NCH, 4, 65], BF16)
            sz = sb("sz", [128, D], F32)
            Pc = [sb("Pc%d" % i, [128, 512], BF16) for i in range(2)]
            Pp = [sb("Pp%d" % i, [128, 512], BF16) for i in range(2)]
            den = sb("den", [128, 16], F32)
            yf = sb("yf", [128, 16, 64], F32)
            yo = [sb("yo%d" % i, [128, D], BF16) for i in range(2)]
            self.epsb = sb("epsb", [128, 1], F32)
            P.op("dve", "memset", writes=["epsb"], ap=self.epsb[:], constant=EPS)
            self.load_w(wb, "wb", l, B0, B_N)
            P.op("pool", "memset", writes=["V%d" % i for i in range(NCH)], ap=V[:], constant=1.0)
            ps = self.ps
            for s in range(self.NSEQ):
                for t in range(self.NTL):
                    ts = slice(t * 512, (t + 1) * 512)
                    for c4 in range(4):
                        self.h_chunk(l, s, t * 4 + c4, hT, "hT", c4 * 128)
                    P.op("sp", "dma_start", writes=["cosT"], dma="cosT", out=cosT[:], in_=self.rope[0, :, ts])
                    P.op("sp", "dma_start", writes=["sinS"], dma="sinS", out=sinS[:], in_=self.rope[1, :, ts])
                    for b in range(10):
                        c0 = b * 128 if b < 8 else 1024 + (b - 8) * 128
                        c1 = 1280 + c0
                        pq = self.psb()
                        self.proj_fm(pq, hT, "hT", 512, wb, "wb", c0, 128)
                        pqs = self.psb()
                        self.proj_fm(pqs, hT, "hT", 512, wb, "wb", c1, 128)
                        i2 = b % 2
                        P.op("dve", "tensor_tensor", reads=self.pk(pq) + ["cosT"], writes=["t1_%d" % i2], out=t1[i2][:],
                             in0=ps[:, pq, :], in1=cosT[:], op=ALU.mult)
                        P.op("dve", "tensor_tensor", reads=self.pk(pqs) + ["sinS"], writes=["t2_%d" % i2], out=t2[i2][:],
                             in0=ps[:, pqs, :], in1=sinS[:], op=ALU.mult)
                        if b < 8:
                            dst, dkey = qrT[:, b, :], "qrT"
                        else:
                            dst, dkey = krT[:, b - 8, ts], "krT%d" % t
                        P.op("pool", "tensor_tensor", reads=["t1_%d" % i2, "t2_%d" % i2], writes=[dkey], out=dst,
                             in0=t1[i2][:], in1=t2[i2][:], op=ALU.add)
                    for c4 in range(4):
                        c = t * 4 + c4
                        cs = slice(c4 * 128, (c4 + 1) * 128)
                        pv = self.psb()
                        self.proj_tm(pv, hT, "hT", c4 * 128, wb, "wb", 2560, 256)
                        P.op("act", "activation", reads=self.pk(pv), writes=["V%d" % c], out=V[:, c, :, 0:64],
                             in_=ps[:, pv, 0:256].rearrange("p (h d) -> p h d", h=4), func=AF.Copy)
                        pz = self.psb(2)
                        for n in range(2):
                            self.proj_tm(pz + n, hT, "hT", c4 * 128, wb, "wb", 2816 + n * 512, 512)
                        P.op("act", "activation", reads=self.pk(pz, 2), writes=["sz"], out=sz[:],
                             in_=ps[:, pz:pz + 2, :].rearrange("p b n -> p (b n)"), func=AF.Silu)
                        pos = []
                        for kv in range(4):
                            half = slice((kv % 2) * 64, (kv % 2) * 64 + 64)
                            blk0 = (kv // 2) * 4
                            qv = qrT[half, blk0:blk0 + 4, cs]
                            i2 = kv % 2
                            psc = self.psb()
                            P.op("pe", "matmul", reads=["krT%d" % t, "qrT"], writes=self.pk(psc),
                                 out=ps[:, psc, :].rearrange("p (a q) -> p a q", a=4),
                                 lhsT=krT[half, kv // 2, c * 128:(c + 1) * 128], rhs=qv, start=True, stop=True)
                            P.op("act", "activation", reads=self.pk(psc), writes=["Pc%d" % i2], out=Pc[i2][:],
                                 in_=ps[:, psc, :], func=AF.Exp, scale=0.125)
                            P.op("pool", "tensor_tensor", reads=["Pc%d" % i2, "mle"], writes=["Pc%d" % i2],
                                 out=Pc[i2][:].rearrange("p (a q) -> p a q", a=4),
                                 in0=Pc[i2][:].rearrange("p (a q) -> p a q", a=4),
                                 in1=self.mle[:].unsqueeze(1).broadcast_to([128, 4, 128]), op=ALU.mult)
                            if c > 0:
                                psp = self.psb()
                                P.op("pe", "matmul", reads=["krT%d" % ((c - 1) // 4), "qrT"], writes=self.pk(psp),
                                     out=ps[:, psp, :].rearrange("p (a q) -> p a q", a=4),
                                     lhsT=krT[half, kv // 2, (c - 1) * 128:c * 128], rhs=qv, start=True, stop=True)
                                P.op("act", "activation", reads=self.pk(psp), writes=["Pp%d" % i2], out=Pp[i2][:],
                                     in_=ps[:, psp, :], func=AF.Exp, scale=0.125)
                                P.op("pool", "tensor_tensor", reads=["Pp%d" % i2, "mgt"], writes=["Pp%d" % i2],
                                     out=Pp[i2][:].rearrange("p (a q) -> p a q", a=4),
                                     in0=Pp[i2][:].rearrange("p (a q) -> p a q", a=4),
                                     in1=self.mgt[:].unsqueeze(1).broadcast_to([128, 4, 128]), op=ALU.mult)
                            po = self.psb()
                            pos.append(po)
                            for a in range(4):
                                if c > 0:
                                    P.op("pe", "matmul", reads=["Pp%d" % i2, "V%d" % (c - 1)], writes=self.pk(po),
                                         out=ps[:, po, a * 65:(a + 1) * 65], lhsT=Pp[i2][:, a * 128:(a + 1) * 128],
                                         rhs=V[:, c - 1, kv, :], start=True, stop=False)
                                P.op("pe", "matmul", reads=["Pc%d" % i2, "V%d" % c], writes=self.pk(po),
                                     out=ps[:, po, a * 65:(a + 1) * 65], lhsT=Pc[i2][:, a * 128:(a + 1) * 128],
                                     rhs=V[:, c, kv, :], start=(c == 0), stop=True)
                            pov = ps[:, po, 0:260].rearrange("p (a e) -> p a e", a=4)
                            P.op("dve", "tensor_tensor", reads=self.pk(po) + ["esink"], writes=["den%d" % kv],
                                 out=den[:, 4 * kv:4 * kv + 4].unsqueeze(2), in0=pov[:, :, 64:65],
                                 in1=self.esink[:, l * 16 + 4 * kv:l * 16 + 4 * kv + 4].unsqueeze(2), op=ALU.add)
                            P.op("dve", "reciprocal", reads=["den%d" % kv], writes=["den%d" % kv],
                                 out=den[:, 4 * kv:4 * kv + 4], in_=den[:, 4 * kv:4 * kv + 4])
                            P.op("dve", "tensor_tensor", reads=self.pk(po) + ["den%d" % kv], writes=["yf%d" % kv],
                                 out=yf[:, 4 * kv:4 * kv + 4, :], in0=pov[:, :, 0:64],
                                 in1=den[:, 4 * kv:4 * kv + 4].unsqueeze(2).broadcast_to([128, 4, 64]), op=ALU.mult)
                        yq = yo[c % 2]
                        P.op("pool", "tensor_tensor", reads=["yf%d" % k for k in range(4)] + ["sz"],
                             writes=["yo%d" % (c % 2)], out=yq[:], in0=yf[:].rearrange("p h d -> p (h d)"), in1=sz[:],
                             op=ALU.mult)
                        P.op("pool", "dma_start", reads=["yo%d" % (c % 2)], writes=["Y1_%d_%d" % (s, c)],
                             dma="yo%d" % (c % 2), out=self.Y[1, s, c * 128:(c + 1) * 128, :], in_=yq[:])
            P.emit()

    def phase_C(self, l, hg):
        nc, P, S, NCH = self.nc, self.P, self.S, self.NCH
        with ExitStack() as st:
            self.uid = getattr(self, "uid", 0) + 1
            sb = lambda name, shape, dt, _u=self.uid: st.enter_context(nc.sbuf_tensor("%s_u%d" % (name, _u), shape, dt))
            wc = sb("wc", [128, KC, C_G], BF16)
            hT = sb("hT", [128, KC, 512], BF16)
            KT = sb("KT", [65, 8, S], BF16)
            V = sb("V", [128, NCH, 8, 65], BF16)
            QT = sb("QT", [65, 8, 512], BF16)
            NC_ = sb("NC", [128, NCH, 8], F32)
            fsm = sb("fsm", [128, 4, 8], F32)
            cumT = sb("cumT", [8, 512], BF16)
            sz = sb("sz", [128, 4, 512], F32)
            PT = [sb("PT%d" % i, [128, 512], BF16) for i in range(3)]
            rd = [sb("rd%d" % i, [128, 4], F32) for i in range(2)]
            yn = [sb("yn%d" % i, [128, 4, 64], F32) for i in range(2)]
            yo = [sb("yo%d" % i, [128, 4, 512], BF16) for i in range(2)]
            self.epsb = sb("epsb", [128, 1], F32)
            P.op("dve", "memset", writes=["epsb"], ap=self.epsb[:], constant=EPS)
            self.load_w(wc, "wc", l, C0 + hg * C_G, C_G)
            P.op("pool", "memset", writes=["V%d" % i for i in range(NCH)], ap=V[:], constant=1.0)
            P.op("pool", "memset", writes=["KT%d" % i for i in range(self.NTL)], ap=KT[64:65, :, :], constant=1.0)
            ps = self.ps
            self.pools = {"gen": [0, 1, 2], "ct": [3], "acc": [4, 5], "sc": [6, 7]}
            self.prr = {}
            fb = self.rowp(l, 4)[:, hg * 8:hg * 8 + 8]
            pti = 0
            for s in range(self.NSEQ):
                for t in range(self.NTL):
                    ts = slice(t * 512, (t + 1) * 512)
                    for c4 in range(4):
                        self.h_chunk(l, s, t * 4 + c4, hT, "hT", c4 * 128)
                    for h in range(8):
                        pkb = self.psb()
                        self.proj_fm(pkb, hT, "hT", 512, wc, "wc", 512 + h * 64, 64)
                        P.op("act", "activation", reads=self.pk(pkb), writes=["KT%d" % t], out=KT[0:64, h, ts],
                             in_=ps[0:64, pkb, :], func=AF.Copy)
                    pct = self.psb(pool="ct")
                    for c4 in range(4):
                        c = t * 4 + c4
                        pf = self.psb()
                        self.proj_tm(pf, hT, "hT", c4 * 128, wc, "wc", 2048, 8)
                        f0, f1, f2 = fsm[:, 0, :], fsm[:, 1, :], fsm[:, 2, :]
                        P.op("dve", "tensor_tensor", reads=self.pk(pf) + ["prw"], writes=["f0"], out=f0,
                             in0=ps[:, pf, 0:8], in1=fb, op=ALU.add)
                        P.op("act", "activation", reads=["f0"], writes=["f1"], out=f1, in_=f0, func=AF.Exp, scale=-1.0)
                        P.op("act", "activation", reads=["f1"], writes=["f2"], out=f2, in_=f1, func=AF.Ln, bias=1.0)
                        P.op("pe", "matmul", reads=["tri", "f2"], writes=self.pk(pf), out=ps[:, pf, 8:16],
                             lhsT=self.tri[:], rhs=f2, start=True, stop=(c == 0))
                        if c > 0:
                            P.op("pe", "matmul", reads=["elast", "NC%d" % (c - 1)], writes=self.pk(pf), out=ps[:, pf, 8:16],
                                 lhsT=self.elast[:], rhs=NC_[:, c - 1, :], start=False, stop=True)
                        P.op("dve", "tensor_copy", reads=self.pk(pf), writes=["NC%d" % c], out=NC_[:, c, :], in_=ps[:, pf, 8:16])
                        P.op("pe", "transpose", reads=["NC%d" % c, "identf"], writes=self.pk(pct),
                             out=ps[0:8, pct, c4 * 128:(c4 + 1) * 128], in_=NC_[:, c, :], identity=self.identf[:])
                        pv = self.psb()
                        self.proj_tm(pv, hT, "hT", c4 * 128, wc, "wc", 1024, 512)
                        P.op("act", "activation", reads=self.pk(pv), writes=["V%d" % c], out=V[:, c, :, 0:64],
                             in_=ps[:, pv, :].rearrange("p (h d) -> p h d", h=8), func=AF.Copy)
                        pz = self.psb()
                        self.proj_tm(pz, hT, "hT", c4 * 128, wc, "wc", 1536, 512)
                        P.op("act", "activation", reads=self.pk(pz), writes=["sz%d" % c4], out=sz[:, c4, :],
                             in_=ps[:, pz, :], func=AF.Silu)
                    P.op("dve", "tensor_scalar", reads=self.pk(pct), writes=["cumT"], out=cumT[:], in0=ps[0:8, pct, :],
                         scalar1=-1.0, scalar2=None, op0=ALU.mult)
                    yq = yo[t % 2]
                    ykey = "yo%d" % (t % 2)
                    for h in range(8):
                        pqb = self.psb()
                        for kc in range(KC):
                            P.op("pe", "matmul", reads=["hT", "wc"], writes=self.pk(pqb), out=ps[0:64, pqb, :],
                                 lhsT=wc[:, kc, h * 64:(h + 1) * 64], rhs=hT[:, kc, :], start=(kc == 0),
                                 stop=(kc == KC - 1))
                        P.op("pe", "matmul", reads=["sel", "cumT"], writes=self.pk(pqb), out=ps[64:65, pqb, :],
                             lhsT=self.sel[:, h, 64:65], rhs=cumT[:], start=True, stop=True)
                        P.op("act", "activation", reads=self.pk(pqb), writes=["QT%d" % h], out=QT[:, h, :],
                             in_=ps[0:65, pqb, :], func=AF.Copy)
                        po = self.psb(pool="acc")
                        pov = ps[:, po, 0:260].rearrange("p (a e) -> p a e", a=4)
                        nkb = 4 * t + 4
                        first = True
                        for j in range(nkb):
                            jj = j - 4 * t
                            q0 = max(jj, 0)
                            nq = 4 - q0
                            psc = self.psb(pool="sc")
                            P.op("pe", "matmul", reads=["KT%d" % (j // 4), "QT%d" % h], writes=self.pk(psc),
                                 out=ps[:, psc, 0:nq * 128], lhsT=KT[:, h, j * 128:(j + 1) * 128],
                                 rhs=QT[:, h, q0 * 128:512], start=True, stop=True)
                            pt = PT[pti % 3]
                            ptk = "PT%d" % (pti % 3)
                            pti += 1
                            P.op("act", "activation", reads=self.pk(psc) + ["NC%d" % j], writes=[ptk], out=pt[:, 0:nq * 128],
                                 in_=ps[:, psc, 0:nq * 128], func=AF.Exp, scale=0.125, bias=NC_[:, j, h:h + 1])
                            if jj >= 0:
                                P.op("pool", "tensor_tensor", reads=[ptk, "mle"], writes=[ptk], out=pt[:, 0:128],
                                     in0=pt[:, 0:128], in1=self.mle[:], op=ALU.mult)
                            for a in range(q0, 4):
                                P.op("pe", "matmul", reads=[ptk, "V%d" % j], writes=self.pk(po), out=pov[:, a, :],
                                     lhsT=pt[:, (a - q0) * 128:(a - q0 + 1) * 128], rhs=V[:, j, h, :],
                                     start=first, stop=(j == nkb - 1), skip_group_check=True)
                                first = False
                        r_, y_ = rd[h % 2], yn[h % 2]
                        P.op("dve", "reciprocal", reads=self.pk(po), writes=["rd%d" % (h % 2)], out=r_[:].unsqueeze(2),
                             in_=pov[:, :, 64:65])
                        P.op("dve", "tensor_tensor", reads=self.pk(po) + ["rd%d" % (h % 2)], writes=["yn%d" % (h % 2)],
                             out=y_[:], in0=pov[:, :, 0:64], in1=r_[:].unsqueeze(2).broadcast_to([128, 4, 64]),
                             op=ALU.mult)
                        P.op("pool", "tensor_tensor", reads=["yn%d" % (h % 2)] + ["sz%d" % i for i in range(4)],
                             writes=[ykey], out=yq[:, :, h * 64:(h + 1) * 64], in0=y_[:],
                             in1=sz[:, :, h * 64:(h + 1) * 64], op=ALU.mult)
                    for c4 in range(4):
                        c = t * 4 + c4
                        P.op("pool", "dma_start", reads=[ykey], writes=["Y2_%d_%d_%d" % (s, c, hg)], dma=ykey,
                             out=self.Y[2, s, c * 128:(c + 1) * 128, hg * 512:(hg + 1) * 512], in_=yq[:, c4, :])
            P.emit()
            self.pools = {"gen": list(range(8))}
            self.prr = {}

    def phase_D(self, l):
        nc, P, S = self.nc, self.P, self.S
        last = (l == self.layers - 1)
        with ExitStack() as st:
            self.uid = getattr(self, "uid", 0) + 1
            sb = lambda name, shape, dt, _u=self.uid: st.enter_context(nc.sbuf_tensor("%s_u%d" % (name, _u), shape, dt))
            wg = sb("wg", [128, KC, G_N], BF16)
            wp = [sb("wp%d" % i, [128, KC, D], BF16) for i in range(3)]
            wo = sb("wo", [128, KC, D], BF16)
            hT = sb("hT", [128, KC, 128], BF16)
            gbb = sb("gbb", [128, 3 * D], F32)
            fnw = sb("fnw", [128, D], F32)
            yt = [[sb("yt%d_%d" % (b, i), [128, D], BF16) for i in range(2)] for b in range(3)]
            ybT = sb("ybT", [128, KC, 128], BF16)
            gs = sb("gs", [128, D], F32)
            tmp = sb("tmp", [128, D], F32)
            mg = sb("mg", [128, D], F32)
            mgb = sb("mgb", [128, D], BF16)
            mT = sb("mT", [128, KC, 128], BF16)
            xo = [sb("xo%d" % i, [128, D], F32) for i in range(2)]
            fs = sb("fs", [128, 4], F32)
            self.epsb = sb("epsb", [128, 1], F32)
            P.op("dve", "memset", writes=["epsb"], ap=self.epsb[:], constant=EPS)
            self.load_w(wg, "wg", l, G0, G_N)
            for b in range(3):
                self.load_w(wp[b], "wp%d" % b, l, 0, D, src=self.wproj[l, b])
            self.load_w(wo, "wo", l, 0, D, src=self.wout[l])
            P.op("sp", "dma_start", writes=["gbb"], dma="gbb", out=gbb[:],
                 in_=self.prow[l * 4176 + 1104:l * 4176 + 1104 + 3072].partition_broadcast(128))
            P.op("sp", "dma_start", writes=["fnw"], dma="fnw", out=fnw[:],
                 in_=self.prow[DEPTH * 4176:DEPTH * 4176 + 1024].partition_broadcast(128))
            ps = self.ps
            for s in range(self.NSEQ):
                for c in range(self.NCH):
                    i2 = c % 2
                    for b in range(3):
                        rk = ["Y%d_%d_%d" % (b, s, c)] if b < 2 else ["Y2_%d_%d_%d" % (s, c, g) for g in range(2)]
                        P.op("sp", "dma_start", reads=rk, writes=["yt%d_%d" % (b, i2)], dma="yt%d_%d" % (b, i2),
                             out=yt[b][i2][:], in_=self.Y[b, s, c * 128:(c + 1) * 128, :])
                    sl = self.h_chunk(l, s, c, hT, "hT", 0)
                    for b in range(3):
                        pg = self.psb(2)
                        for n in range(2):
                            self.proj_tm(pg + n, hT, "hT", 0, wg, "wg", b * D + n * 512, 512)
                        P.op("dve", "tensor_tensor", reads=self.pk(pg, 2) + ["gbb"], writes=["gs"], out=gs[:],
                             in0=ps[:, pg:pg + 2, :].rearrange("p b n -> p (b n)"), in1=gbb[:, b * D:(b + 1) * D],
                             op=ALU.add)
                        P.op("act", "activation", reads=["gs"], writes=["gs"], out=gs[:], in_=gs[:], func=AF.Sigmoid)
                        pt_ = self.psb()
                        ptb = ps[:, pt_, :].bitcast(BF16).rearrange("p (k t) -> p k t", k=8)
                        for kc in range(KC):
                            P.op("pe", "transpose", reads=["yt%d_%d" % (b, i2), "ident"], writes=self.pk(pt_),
                                 out=ptb[:, kc, :], in_=yt[b][i2][:, kc * 128:(kc + 1) * 128], identity=self.ident[:])
                        P.op("act", "activation", reads=self.pk(pt_), writes=["ybT"], out=ybT[:], in_=ptb, func=AF.Copy)
                        pb = self.psb(2)
                        for n in range(2):
                            for kc in range(KC):
                                P.op("pe", "matmul", reads=["ybT", "wp%d" % b], writes=self.pk(pb + n),
                                     out=ps[:, pb + n, :], lhsT=ybT[:, kc, :], rhs=wp[b][:, kc, n * 512:(n + 1) * 512],
                                     start=(kc == 0), stop=(kc == KC - 1))
                        pball = ps[:, pb:pb + 2, :].rearrange("p b n -> p (b n)")
                        if b == 0:
                            P.op("dve", "tensor_tensor", reads=self.pk(pb, 2) + ["gs"], writes=["mg"], out=mg[:], in0=pball,
                                 in1=gs[:], op=ALU.mult)
                        else:
                            P.op("dve", "tensor_tensor", reads=self.pk(pb, 2) + ["gs"], writes=["tmp"], out=tmp[:],
                                 in0=pball, in1=gs[:], op=ALU.mult)
                            P.op("pool", "tensor_tensor", reads=["tmp", "mg"], writes=["mg"], out=mg[:], in0=tmp[:],
                                 in1=mg[:], op=ALU.add)
                    P.op("act", "activation", reads=["mg"], writes=["mgb"], out=mgb[:], in_=mg[:], func=AF.Copy)
                    pt_ = self.psb()
                    ptb = ps[:, pt_, :].bitcast(BF16).rearrange("p (k t) -> p k t", k=8)
                    for kc in range(KC):
                        P.op("pe", "transpose", reads=["mgb", "ident"], writes=self.pk(pt_), out=ptb[:, kc, :],
                             in_=mgb[:, kc * 128:(kc + 1) * 128], identity=self.ident[:])
                    P.op("dve", "tensor_copy", reads=self.pk(pt_), writes=["mT"], out=mT[:], in_=ptb)
                    po = self.psb(2)
                    for n in range(2):
                        for kc in range(KC):
                            P.op("pe", "matmul", reads=["mT", "wo"], writes=self.pk(po + n), out=ps[:, po + n, :],
                                 lhsT=mT[:, kc, :], rhs=wo[:, kc, n * 512:(n + 1) * 512], start=(kc == 0),
                                 stop=(kc == KC - 1))
                    xq = xo[i2]
                    P.op("dve", "tensor_tensor", reads=self.pk(po, 2) + ["xt%d" % sl], writes=["xo%d" % i2], out=xq[:],
                         in0=ps[:, po:po + 2, :].rearrange("p b n -> p (b n)"), in1=self.xt[sl][:], op=ALU.add)
                    if not last:
                        P.op("pool", "dma_start", reads=["xo%d" % i2], writes=["x1_%d_%d" % (s, c)], dma="xo%d" % i2,
                             out=self.X1[s, c * 128:(c + 1) * 128, :], in_=xq[:])
                    else:
                        P.op("act", "activation", reads=["xo%d" % i2], writes=["junkD", "fs"], out=self.junk[0][:],
                             in_=xq[:], func=AF.Square, accum_out=fs[:, 0:1])
                        P.op("act", "activation", reads=["fs", "epsb"], writes=["fs"], out=fs[:, 1:2], in_=fs[:, 0:1], func=AF.Ln,
                             scale=1.0 / D, bias=self.epsb[:])
                        P.op("act", "activation", reads=["fs"], writes=["fs"], out=fs[:, 2:3], in_=fs[:, 1:2], func=AF.Exp,
                             scale=-0.5)
                        P.op("dve", "scalar_tensor_tensor", reads=["xo%d" % i2, "fs", "fnw"], writes=["xo%d" % i2],
                             out=xq[:], in0=xq[:], scalar=fs[:, 2:3], in1=fnw[:], op0=ALU.mult, op1=ALU.mult)
                        P.op("pool", "dma_start", reads=["xo%d" % i2], writes=["out_%d_%d" % (s, c)], dma="xo%d" % i2,
                             out=self.out[s, c * 128:(c + 1) * 128, :], in_=xq[:])
            P.emit()


def host_layout(S, norm_w, w_in, conv_w, conv_b, dt_bias, a_log, d_skip, ssm_norm_w, sinks, f_bias, gate_bias,
                w_proj, w_out, final_norm_w):
    L = DEPTH
    cols = _col_order()
    win = np.ascontiguousarray(
        np.asarray(w_in, np.float32)[:, :, cols].reshape(L, KC, 128, NT).transpose(0, 2, 1, 3))
    wproj = np.ascontiguousarray(np.asarray(w_proj, np.float32).reshape(L, 3, KC, 128, D).transpose(0, 1, 3, 2, 4))
    wout = np.ascontiguousarray(np.asarray(w_out, np.float32).reshape(L, KC, 128, D).transpose(0, 2, 1, 3))
    ppart = np.zeros((128, L * 88), np.float32)
    prow = np.zeros((L * 4176 + 1024,), np.float32)
    for l in range(L):
        ppart[:, l * 88:l * 88 + 8] = np.asarray(norm_w[l]).reshape(KC, 128).T
        cw = np.asarray(conv_w[l]).reshape(4, 16, 128)
        ppart[:, l * 88 + 8:l * 88 + 72] = cw.transpose(2, 1, 0).reshape(128, 64)
        ppart[:, l * 88 + 72:l * 88 + 88] = np.asarray(conv_b[l]).reshape(16, 128).T
        o = l * 4176
        prow[o:o + 16] = dt_bias[l]
        prow[o + 16:o + 32] = a_log[l]
        prow[o + 32:o + 48] = d_skip[l]
        prow[o + 48:o + 64] = sinks[l]
        prow[o + 64:o + 80] = f_bias[l]
        prow[o + 80:o + 1104] = ssm_norm_w[l]
        prow[o + 1104:o + 4176] = np.asarray(gate_bias[l]).reshape(-1)
    prow[L * 4176:] = final_norm_w
    pos = np.arange(S, dtype=np.float32)
    inv = (np.float32(10000.0) ** (-np.arange(0, 64, 2, dtype=np.float32) / np.float32(64))).astype(np.float32)
    ang = (pos[:, None] * inv[None, :]).astype(np.float32)
    cos, sin = np.cos(ang).astype(np.float32), np.sin(ang).astype(np.float32)
    rope = np.zeros((2, 128, S), np.float32)
    for p in range(128):
        rope[0, p] = cos[:, p % 32]
        rope[1, p] = sin[:, p % 32] * (-1.0 if (p % 64) < 32 else 1.0)
    return dict(win=win, wproj=wproj, wout=wout, ppart=ppart, prow=prow, rope=rope)


_NC_CACHE = {}


def kernel(x, norm_w, w_in, conv_w, conv_b, dt_bias, a_log, d_skip, ssm_norm_w, sinks, f_bias, gate_bias,
           w_proj, w_out, final_norm_w):
    x = np.asarray(x, np.float32)
    B, S, _ = x.shape
    nseq = B // NCORES
    shared = host_layout(S, norm_w, w_in, conv_w, conv_b, dt_bias, a_log, d_skip, ssm_norm_w, sinks, f_bias,
                         gate_bias, w_proj, w_out, final_norm_w)
    key = (S, nseq)
    if key not in _NC_CACHE:
        _NC_CACHE[key] = Builder(S, nseq).build()
    nc = _NC_CACHE[key]
    in_maps = []
    for c in range(NCORES):
        m = dict(shared)
        m["x"] = np.ascontiguousarray(x[c * nseq:(c + 1) * nseq])
        in_maps.append(m)
    res = run_bass_kernel_spmd(nc, in_maps, core_ids=list(range(NCORES)))
    return np.concatenate([r["out"] for r in res.results], axis=0).astype(np.float32)
```

```python
import math
from contextlib import ExitStack

import numpy as np
import concourse.bass as bass
import concourse.mybir as mybir
from concourse.bass_utils import run_bass_kernel_spmd

F32 = mybir.dt.float32
BF16 = mybir.dt.bfloat16
AF = mybir.ActivationFunctionType
ALU = mybir.AluOpType

D = 1024
KC = 8
DEPTH = 2
NCORES = 8
EPS = 1e-6
ENGS = ("pe", "act", "dve", "pool", "sp")

A0, A_N = 0, 3088
B0, B_N = 3088, 3840
C0, C_G = 6928, 2056
G0, G_N = 6928 + 2 * 2056, 3072
NT = G0 + G_N
SWA_QORDER = [0, 4, 1, 5, 2, 6, 3, 7, 8, 12, 9, 13, 10, 14, 11, 15]


def _col_order():
    o = {}
    off = 0
    names = [("a_xbc", 2048), ("a_z", 1024), ("a_dt", 16), ("b_q", 1024), ("b_k", 256), ("b_v", 256),
             ("b_z", 1024), ("c_q", 1024), ("c_k", 1024), ("c_v", 1024), ("c_f", 16), ("c_z", 1024),
             ("gates", 3072)]
    for n, s in names:
        o[n] = off
        off += s
    cols = []
    cols += list(range(o["a_xbc"], o["a_xbc"] + 2048))
    cols += list(range(o["a_z"], o["a_z"] + 1024))
    cols += list(range(o["a_dt"], o["a_dt"] + 16))
    assert len(cols) == A_N
    q = [o["b_q"] + h * 64 + d for h in SWA_QORDER for d in range(64)]
    qs = [o["b_q"] + h * 64 + (d + 32) % 64 for h in SWA_QORDER for d in range(64)]
    k = [o["b_k"] + h * 64 + d for h in range(4) for d in range(64)]
    ks = [o["b_k"] + h * 64 + (d + 32) % 64 for h in range(4) for d in range(64)]
    cols += q + k + qs + ks
    cols += list(range(o["b_v"], o["b_v"] + 256))
    cols += list(range(o["b_z"], o["b_z"] + 1024))
    assert len(cols) == B0 + B_N
    for hg in range(2):
        for nm in ("c_q", "c_k", "c_v", "c_z"):
            cols += list(range(o[nm] + hg * 512, o[nm] + hg * 512 + 512))
        cols += list(range(o["c_f"] + hg * 8, o["c_f"] + hg * 8 + 8))
    assert len(cols) == G0
    cols += list(range(o["gates"], o["gates"] + 3072))
    assert len(cols) == NT
    return np.array(cols, dtype=np.int64)


class Prog:
    def __init__(self, nc, stack, same_engine_sync=True):
        self.nc = nc
        self.stack = stack
        self.same = same_engine_sync
        self.ops = []
        self.last_w = {}
        self.readers = {}
        self.eng_sem = {e: stack.enter_context(nc.semaphore("s_" + e)) for e in ENGS}
        self.cnt = {e: 0 for e in ENGS}
        self.dsem = {}
        self.dcnt = {}
        self.known = {e: {} for e in ENGS}
        self.done_ops = 0

    def op(self, eng, meth, reads=(), writes=(), dma=None, **kw):
        idx = len(self.ops)
        deps = set()
        for k in reads:
            if k in self.last_w:
                deps.add(self.last_w[k])
        for k in writes:
            if k in self.last_w:
                deps.add(self.last_w[k])
            for r in self.readers.get(k, ()):
                deps.add(r)
        deps.discard(idx)
        best = {}
        for d in deps:
            od = self.ops[d]
            kk = ("d", od["dma"]) if od["dma"] is not None else ("e", od["eng"])
            if kk not in best or best[kk] < d:
                best[kk] = d
        deps = set(best.values())
        for k in reads:
            self.readers.setdefault(k, []).append(idx)
        for k in writes:
            self.last_w[k] = idx
            self.readers[k] = []
        self.ops.append(dict(eng=eng, meth=meth, kw=kw, deps=deps, dma=dma, sig=False, ev=None))
        return idx

    def emit(self, final=False):
        nc, ops = self.nc, self.ops
        import os as _os
        if _os.environ.get("OPS_LIMIT"):
            del ops[int(_os.environ["OPS_LIMIT"]):]
        lo = self.done_ops
        new = range(lo, len(ops))
        for i in new:
            o = ops[i]
            if o["dma"] is not None:
                o["sig"] = True
            for d in o["deps"]:
                od = ops[d]
                if d < lo:
                    continue
                if od["dma"] is not None or od["eng"] != o["eng"] or o["dma"] is not None:
                    od["sig"] = True
                elif self.same and o["eng"] != "pe":
                    od["sig"] = True
        for i in new:
            o = ops[i]
            if o["dma"] is not None:
                k = o["dma"]
                if k not in self.dsem:
                    self.dsem[k] = self.stack.enter_context(nc.semaphore("d_" + str(k)))
                    self.dcnt[k] = 0
                self.dcnt[k] += 16
                o["ev"] = (self.dsem[k], self.dcnt[k], "d_" + str(k))
            elif o["sig"]:
                self.cnt[o["eng"]] += 1
                o["ev"] = (self.eng_sem[o["eng"]], self.cnt[o["eng"]], o["eng"])
        per_eng = {e: [] for e in ENGS}
        for i in new:
            per_eng[ops[i]["eng"]].append(i)
        same = self.same

        def body(ename):
            def f(eng):
                kn = self.known[ename]
                for i in per_eng[ename]:
                    o = ops[i]
                    need = {}
                    for d in o["deps"]:
                        if d < lo:
                            continue
                        od = ops[d]
                        ev = od["ev"]
                        if ev is None:
                            continue
                        sem, val, name = ev
                        if (od["dma"] is None and od["eng"] == ename and o["dma"] is None
                                and (ename == "pe" or not same)):
                            continue
                        if kn.get(name, 0) >= val:
                            continue
                        if name not in need or need[name][1] < val:
                            need[name] = (sem, val)
                    for name, (sem, val) in need.items():
                        eng.wait_ge(sem, val)
                        kn[name] = val
                    ins = getattr(eng, o["meth"])(**o["kw"])
                    if o["ev"] is not None:
                        sem, val, name = o["ev"]
                        ins.then_inc(sem, 16 if o["dma"] is not None else 1)
                if ename == "sp":
                    for k, s in self.dsem.items():
                        if kn.get("d_" + str(k), 0) < self.dcnt[k]:
                            eng.wait_ge(s, self.dcnt[k])
                            kn["d_" + str(k)] = self.dcnt[k]
            return f

        with nc.Block() as block:
            block.tensor(body("pe"))
            block.scalar(body("act"))
            block.vector(body("dve"))
            block.gpsimd(body("pool"))
            block.sync(body("sp"))
        self.done_ops = len(ops)


class Builder:
    def __init__(self, S, NSEQ, debug=False, layers=DEPTH, phases="ABCD"):
        self.S, self.NSEQ, self.debug, self.layers, self.phases = S, NSEQ, debug, layers, phases
        self.NCH = S // 128
        self.NTL = S // 512
        nc = self.nc = bass.Bass("TRN2", target_bir_lowering=False)
        L = DEPTH
        okind = "ExternalOutput" if debug else "Internal"
        self.x = nc.dram_tensor("x", [NSEQ, S, D], F32, kind="ExternalInput").ap()
        self.win = nc.dram_tensor("win", [L, 128, KC, NT], F32, kind="ExternalInput").ap()
        self.wproj = nc.dram_tensor("wproj", [L, 3, 128, KC, D], F32, kind="ExternalInput").ap()
        self.wout = nc.dram_tensor("wout", [L, 128, KC, D], F32, kind="ExternalInput").ap()
        self.ppart = nc.dram_tensor("ppart", [128, L * (8 + 64 + 16)], F32, kind="ExternalInput").ap()
        self.prow = nc.dram_tensor("prow", [L * (80 + 1024 + 3072) + 1024], F32, kind="ExternalInput").ap()
        self.rope = nc.dram_tensor("rope", [2, 128, S], F32, kind="ExternalInput").ap()
        self.out = nc.dram_tensor("out", [NSEQ, S, D], F32, kind="ExternalOutput").ap()
        self.Y = nc.dram_tensor("ybr", [3, NSEQ, S, D], BF16, kind=okind).ap()
        self.X1 = nc.dram_tensor("x1", [NSEQ, S, D], F32, kind=okind).ap()
        self.pools = {"gen": list(range(8))}
        self.prr = {}

    def psb(self, n=1, pool="gen"):
        banks = self.pools[pool]
        r = self.prr.get(pool, 0)
        if n == 2:
            assert len(banks) % 2 == 0
            if r % 2:
                r += 1
            b = banks[r % len(banks)]
            self.prr[pool] = r + 2
            return b
        b = banks[r % len(banks)]
        self.prr[pool] = r + 1
        return b

    def pk(self, b, n=1):
        return ["ps%d" % (b + i) for i in range(n)]

    def build(self):
        nc = self.nc
        with ExitStack() as gst:
            self.P = P = Prog(nc, gst)
            sb = lambda name, shape, dt: gst.enter_context(nc.sbuf_tensor(name, shape, dt))
            self.ps = gst.enter_context(nc.psum_tensor("ps", [128, 8, 512], F32))
            self.ident = sb("ident", [128, 128], BF16)
            self.identf = sb("identf", [128, 128], F32)
            self.tri = sb("tri", [128, 128], F32)
            self.elast = sb("elast", [128, 128], F32)
            self.onesf = sb("onesf", [128, 128], F32)
            self.mle = sb("mle", [128, 128], BF16)
            self.mgt = sb("mgt", [128, 128], BF16)
            self.mlef = sb("mlef", [128, 128], F32)
            self.sel = sb("sel", [8, 8, 65], BF16)
            self.ppt = sb("ppt", [128, DEPTH * 88], F32)
            self.prw = sb("prw", [128, DEPTH * 80], F32)
            self.abc = sb("abc", [128, DEPTH * 16], F32)
            self.esink = sb("esink", [128, DEPTH * 16], F32)
            self.xt = [sb("xt%d" % i, [128, D], F32) for i in range(2)]
            self.xn = [sb("xn%d" % i, [128, D], BF16) for i in range(2)]
            self.junk = [sb("junk%d" % i, [128, D], BF16) for i in range(2)]
            self.st4 = [sb("st4_%d" % i, [128, 4], F32) for i in range(2)]
            self.xslot = 0
            self.setup_consts()
            import os as _os
            if _os.environ.get("SETUP_LIMIT"):
                lim = int(_os.environ["SETUP_LIMIT"])
                del P.ops[lim:]
            P.emit()
            for l in range(self.layers):
                if "A" in self.phases:
                    self.phase_A(l)
                if "B" in self.phases:
                    self.phase_B(l)
                if "C" in self.phases:
                    for hg in range(2):
                        self.phase_C(l, hg)
                if "D" in self.phases:
                    self.phase_D(l)
        return nc

    def setup_consts(self):
        P = self.P
        P.op("pool", "memset", writes=["ident"], ap=self.ident[:], constant=1.0)
        P.op("pool", "affine_select", reads=["ident"], writes=["ident"], out=self.ident[:], in_=self.ident[:],
             pattern=[[-1, 128]], compare_op=ALU.is_equal, fill=0.0, base=0, channel_multiplier=1)
        P.op("pool", "memset", writes=["identf"], ap=self.identf[:], constant=1.0)
        P.op("pool", "affine_select", reads=["identf"], writes=["identf"], out=self.identf[:], in_=self.identf[:],
             pattern=[[-1, 128]], compare_op=ALU.is_equal, fill=0.0, base=0, channel_multiplier=1)
        P.op("pool", "memset", writes=["tri"], ap=self.tri[:], constant=1.0)
        P.op("pool", "affine_select", reads=["tri"], writes=["tri"], out=self.tri[:], in_=self.tri[:],
             pattern=[[1, 128]], compare_op=ALU.is_ge, fill=0.0, base=0, channel_multiplier=-1)
        P.op("pool", "memset", writes=["mlef"], ap=self.mlef[:], constant=1.0)
        P.op("pool", "affine_select", reads=["mlef"], writes=["mlef"], out=self.mlef[:], in_=self.mlef[:],
             pattern=[[1, 128]], compare_op=ALU.is_ge, fill=0.0, base=0, channel_multiplier=-1)
        P.op("pool", "memset", writes=["mle"], ap=self.mle[:], constant=1.0)
        P.op("pool", "affine_select", reads=["mle"], writes=["mle"], out=self.mle[:], in_=self.mle[:],
             pattern=[[1, 128]], compare_op=ALU.is_ge, fill=0.0, base=0, channel_multiplier=-1)
        P.op("pool", "memset", writes=["mgt"], ap=self.mgt[:], constant=1.0)
        P.op("pool", "affine_select", reads=["mgt"], writes=["mgt"], out=self.mgt[:], in_=self.mgt[:],
             pattern=[[-1, 128]], compare_op=ALU.is_gt, fill=0.0, base=0, channel_multiplier=1)
        P.op("pool", "memset", writes=["elast"], ap=self.elast[:], constant=1.0)
        P.op("pool", "affine_select", reads=["elast"], writes=["elast"], out=self.elast[:], in_=self.elast[:],
             pattern=[[0, 128]], compare_op=ALU.is_equal, fill=0.0, base=-127, channel_multiplier=1)
        P.op("pool", "memset", writes=["onesf"], ap=self.onesf[:], constant=1.0)
        P.op("pool", "memset", writes=["sel"], ap=self.sel[:], constant=8.0)
        P.op("pool", "affine_select", reads=["sel"], writes=["sel"], out=self.sel[:], in_=self.sel[:],
             pattern=[[1, 8], [0, 65]], compare_op=ALU.is_equal, fill=0.0, base=0, channel_multiplier=-1)
        P.op("pool", "affine_select", reads=["sel"], writes=["sel"], out=self.sel[:], in_=self.sel[:],
             pattern=[[0, 8], [1, 65]], compare_op=ALU.is_equal, fill=0.0, base=-64, channel_multiplier=0)
        P.op("sp", "dma_start", writes=["ppt"], dma="ppt", out=self.ppt[:], in_=self.ppart)
        for l in range(DEPTH):
            P.op("sp", "dma_start", writes=["prw"], dma="prw", out=self.prw[:, l * 80:(l + 1) * 80],
                 in_=self.prow[l * 4176:l * 4176 + 80].partition_broadcast(128))
        for l in range(DEPTH):
            P.op("act", "activation", reads=["prw"], writes=["abc"], out=self.abc[:, l * 16:(l + 1) * 16],
                 in_=self.prw[:, l * 80 + 16:l * 80 + 32], func=AF.Exp)
            P.op("dve", "tensor_scalar", reads=["abc"], writes=["abc"], out=self.abc[:, l * 16:(l + 1) * 16],
                 in0=self.abc[:, l * 16:(l + 1) * 16], scalar1=-1.0, scalar2=None, op0=ALU.mult)
            P.op("act", "activation", reads=["prw"], writes=["esink"], out=self.esink[:, l * 16:(l + 1) * 16],
                 in_=self.prw[:, l * 80 + 48:l * 80 + 64], func=AF.Exp)

    def nw(self, l):
        return self.ppt[:, l * 88:l * 88 + 8]

    def convw(self, l, b, k):
        o = l * 88 + 8 + b * 4 + k
        return self.ppt[:, o:o + 1]

    def convb(self, l, b):
        o = l * 88 + 72 + b
        return self.ppt[:, o:o + 1]

    def rowp(self, l, i):
        return self.prw[:, l * 80 + i * 16:l * 80 + (i + 1) * 16]

    def xsrc(self, l, s, c):
        src = self.x if l == 0 else self.X1
        return src[s, c * 128:(c + 1) * 128, :], ("xin" if l == 0 else "x1_%d_%d" % (s, c))

    def h_chunk(self, l, s, c, hT, hkey, col0):
        P = self.P
        sl = self.xslot
        self.xslot ^= 1
        xt, xn, junk, st4 = self.xt[sl], self.xn[sl], self.junk[sl], self.st4[sl]
        src, skey = self.xsrc(l, s, c)
        P.op("sp", "dma_start", reads=[skey], writes=["xt%d" % sl], dma="xt%d" % sl, out=xt[:], in_=src)
        P.op("act", "activation", reads=["xt%d" % sl], writes=["junk%d" % sl, "st4_%d" % sl],
             out=junk[:], in_=xt[:], func=AF.Square, accum_out=st4[:, 0:1])
        P.op("act", "activation", reads=["st4_%d" % sl, "epsb"], writes=["st4_%d" % sl], out=st4[:, 1:2], in_=st4[:, 0:1],
             func=AF.Ln, scale=1.0 / D, bias=self.epsb[:])
        P.op("act", "activation", reads=["st4_%d" % sl], writes=["st4_%d" % sl], out=st4[:, 2:3], in_=st4[:, 1:2],
             func=AF.Exp, scale=-0.5)
        P.op("act", "activation", reads=["xt%d" % sl, "st4_%d" % sl], writes=["xn%d" % sl], out=xn[:], in_=xt[:],
             func=AF.Identity, scale=st4[:, 2:3])
        b = self.psb()
        ptb = self.ps[:, b, :].bitcast(BF16).rearrange("p (k t) -> p k t", k=8)
        for kc in range(KC):
            P.op("pe", "transpose", reads=["xn%d" % sl, "ident"], writes=self.pk(b), out=ptb[:, kc, :],
                 in_=xn[:, kc * 128:(kc + 1) * 128], identity=self.ident[:])
        P.op("dve", "tensor_tensor", reads=self.pk(b) + ["ppt"], writes=[hkey],
             out=hT[:, :, col0:col0 + 128], in0=ptb,
             in1=self.nw(l).unsqueeze(2).broadcast_to([128, 8, 128]), op=ALU.mult)
        return sl

    def load_w(self, wt, key, l, c0, n, src=None):
        P = self.P
        src = self.win[l] if src is None else src
        step = 1024
        for kc in range(KC):
            for o in range(0, n, step):
                m = min(step, n - o)
                P.op("pool", "dma_start", writes=[key], dma=key, out=wt[:, kc, o:o + m],
                     in_=src[:, kc, c0 + o:c0 + o + m])

    def proj_tm(self, dst_bank, hT, hkey, col0, wt, wkey, wc0, n, poff=0):
        P = self.P
        for kc in range(KC):
            P.op("pe", "matmul", reads=[hkey, wkey], writes=self.pk(dst_bank),
                 out=self.ps[:, dst_bank, poff:poff + n], lhsT=hT[:, kc, col0:col0 + 128],
                 rhs=wt[:, kc, wc0:wc0 + n], start=(kc == 0), stop=(kc == KC - 1))

    def proj_fm(self, dst_bank, hT, hkey, ntok, wt, wkey, wc0, m, first=True):
        P = self.P
        for kc in range(KC):
            P.op("pe", "matmul", reads=[hkey, wkey], writes=self.pk(dst_bank),
                 out=self.ps[0:m, dst_bank, 0:ntok], lhsT=wt[:, kc, wc0:wc0 + m],
                 rhs=hT[:, kc, 0:ntok], start=(first and kc == 0), stop=(kc == KC - 1))

    def phase_A(self, l):
        nc, P, S = self.nc, self.P, self.S
        with ExitStack() as st:
            self.uid = getattr(self, "uid", 0) + 1
            sb = lambda name, shape, dt, _u=self.uid: st.enter_context(nc.sbuf_tensor("%s_u%d" % (name, _u), shape, dt))
            wa = sb("wa", [128, KC, A_N], BF16)
            hT = sb("hT", [128, KC, 512], BF16)
            Ub = [sb("Ub%d" % i, [128, 515], F32) for i in range(2)]
            Ucar = sb("Ucar", [128, 16, 3], F32)
            junkA = sb("junkA", [128, 256], BF16)
            acc = [sb("acc%d" % i, [128, 512], F32) for i in range(2)]
            xsT = sb("xsT", [128, 8, 512], BF16)
            BT = sb("BT", [128, 4, 512], BF16)
            CT = sb("CT", [128, 4, 512], BF16)
            H = sb("H", [128, 16, 64], F32)
            Hb = sb("Hb", [128, 16, 64], BF16)
            Htmp = sb("Htmp", [128, 16, 64], F32)
            sm = sb("sm", [128, 12, 16], F32)
            rhsall = sb("rhsall", [128, 16, 128], F32)
            Eh = sb("Eh", [128, 16, 128], F32)
            dec = Eh
            cbm = sb("cbm", [128, 4, 128], F32)
            MT = sb("MT", [128, 16, 128], BF16)
            xstm = sb("xstm", [128, 16, 64], F32)
            xdt = sb("xdt", [128, 16, 64], BF16)
            xw = sb("xw", [128, 16, 64], BF16)
            Btm = sb("Btm", [128, 4, 128], BF16)
            sz = sb("sz", [128, D], F32)
            y1 = sb("y1", [128, 16, 64], F32)
            y2 = sb("y2", [128, 16, 64], F32)
            yo = [sb("yo%d" % i, [128, D], BF16) for i in range(2)]
            snw = sb("snw", [128, D], F32)
            self.epsb = sb("epsb", [128, 1], F32)
            P.op("dve", "memset", writes=["epsb"], ap=self.epsb[:], constant=EPS)
            self.load_w(wa, "wa", l, A0, A_N)
            P.op("sp", "dma_start", writes=["snw"], dma="snw", out=snw[:],
                 in_=self.prow[l * 4176 + 80:l * 4176 + 80 + 1024].partition_broadcast(128))
            ps = self.ps
            for s in range(self.NSEQ):
                P.op("dve", "memset", writes=["H"], ap=H[:], constant=0.0)
                P.op("pool", "memset", writes=["Ucar%d" % b for b in range(16)], ap=Ucar[:], constant=0.0)
                for t in range(self.NTL):
                    for c4 in range(4):
                        self.h_chunk(l, s, t * 4 + c4, hT, "hT", c4 * 128)
                    for b in range(16):
                        pb = self.psb()
                        self.proj_fm(pb, hT, "hT", 512, wa, "wa", b * 128, 128)
                        ukey = "Ub%d" % (b % 2)
                        U_ = Ub[b % 2]
                        P.op("pool", "tensor_copy", reads=["Ucar%d" % b], writes=[ukey], out=U_[:, 0:3], in_=Ucar[:, b, :])
                        P.op("act", "activation", reads=self.pk(pb), writes=[ukey], out=U_[:, 3:515],
                             in_=ps[:, pb, :], func=AF.Copy)
                        a = acc[b % 2]
                        akey = "acc%d" % (b % 2)
                        P.op("dve", "tensor_scalar", reads=[ukey, "ppt"], writes=[akey], out=a[:], in0=U_[:, 0:512],
                             scalar1=self.convw(l, b, 0), scalar2=None, op0=ALU.mult)
                        for k in range(1, 4):
                            P.op("dve", "scalar_tensor_tensor", reads=[ukey, akey, "ppt"], writes=[akey], out=a[:],
                                 in0=U_[:, k:k + 512], scalar=self.convw(l, b, k), in1=a[:],
                                 op0=ALU.mult, op1=ALU.add)
                        if b < 8:
                            dst, dkey = xsT[:, b, :], "xsT"
                        elif b < 12:
                            dst, dkey = BT[:, b - 8, :], "BT"
                        else:
                            dst, dkey = CT[:, b - 12, :], "CT"
                        P.op("act", "activation", reads=[akey, "ppt"], writes=[dkey], out=dst, in_=a[:],
                             func=AF.Silu, bias=self.convb(l, b))
                        P.op("pool", "tensor_copy", reads=[ukey], writes=["Ucar%d" % b], out=Ucar[:, b, :], in_=U_[:, 512:515])
                    for c4 in range(4):
                        c = t * 4 + c4
                        cs = slice(c4 * 128, (c4 + 1) * 128)
                        pd = self.psb()
                        self.proj_tm(pd, hT, "hT", c4 * 128, wa, "wa", 3072, 16)
                        dtr, dt_, adt, acum, nacum, lastbc, dS, ea, cd, dtS, e1 = [sm[:, i, :] for i in range(11)]
                        P.op("dve", "tensor_tensor", reads=self.pk(pd) + ["prw"], writes=["sm0"], out=dtr,
                             in0=ps[:, pd, 0:16], in1=self.rowp(l, 0), op=ALU.add)
                        P.op("act", "activation", reads=["sm0"], writes=["sm10"], out=e1, in_=dtr, func=AF.Exp)
                        P.op("act", "activation", reads=["sm10"], writes=["sm1"], out=dt_, in_=e1, func=AF.Ln, bias=1.0)
                        P.op("dve", "tensor_tensor", reads=["sm1", "abc"], writes=["sm2"], out=adt, in0=dt_,
                             in1=self.abc[:, l * 16:(l + 1) * 16], op=ALU.mult)
                        pa = self.psb()
                        P.op("pe", "matmul", reads=["tri", "sm2"], writes=self.pk(pa), out=ps[:, pa, 0:16],
                             lhsT=self.tri[:], rhs=adt, start=True, stop=True)
                        P.op("dve", "tensor_copy", reads=self.pk(pa), writes=["sm3"], out=acum, in_=ps[:, pa, 0:16])
                        P.op("pe", "matmul", reads=["elast", "sm3"], writes=self.pk(pa), out=ps[:, pa, 16:32],
                             lhsT=self.elast[:], rhs=acum, start=True, stop=True)
                        P.op("dve", "tensor_copy", reads=self.pk(pa), writes=["sm5"], out=lastbc, in_=ps[:, pa, 16:32])
                        P.op("dve", "tensor_tensor", reads=["sm5", "sm3"], writes=["sm6"], out=dS, in0=lastbc, in1=acum,
                             op=ALU.subtract)
                        P.op("act", "activation", reads=["sm6"], writes=["sm6"], out=dS, in_=dS, func=AF.Exp)
                        P.op("act", "activation", reads=["sm3"], writes=["sm7"], out=ea, in_=acum, func=AF.Exp)
                        P.op("act", "activation", reads=["sm5"], writes=["sm8"], out=cd, in_=lastbc, func=AF.Exp)
                        P.op("dve", "tensor_tensor", reads=["sm1", "sm6"], writes=["sm9"], out=dtS, in0=dt_, in1=dS,
                             op=ALU.mult)
                        P.op("dve", "tensor_tensor", reads=["tri", "sm2"], writes=["rhsall"], out=rhsall[:],
                             in0=self.tri[:].unsqueeze(1).broadcast_to([128, 16, 128]),
                             in1=adt.unsqueeze(2).broadcast_to([128, 16, 128]), op=ALU.mult)
                        for g in range(4):
                            pg = self.psb()
                            P.op("pe", "matmul", reads=["onesf", "rhsall"], writes=self.pk(pg),
                                 out=ps[:, pg, :], lhsT=self.onesf[:],
                                 rhs=rhsall[:, 4 * g:4 * g + 4, :], start=True, stop=True)
                            for r in range(4):
                                h = 4 * g + r
                                P.op("dve", "tensor_scalar", reads=self.pk(pg) + ["sm3"], writes=["Eh%d" % g],
                                     out=Eh[:, h, :], in0=ps[:, pg, r * 128:(r + 1) * 128],
                                     scalar1=acum[:, h:h + 1], scalar2=0.0, op0=ALU.subtract, op1=ALU.min)
                            P.op("act", "activation", reads=["Eh%d" % g], writes=["Eh%d" % g, "dec%d" % g],
                                 out=dec[:, 4 * g:4 * g + 4, :], in_=Eh[:, 4 * g:4 * g + 4, :], func=AF.Exp)
                        pc = self.psb()
                        for g in range(4):
                            P.op("pe", "matmul", reads=["BT", "CT"], writes=self.pk(pc),
                                 out=ps[:, pc, g * 128:(g + 1) * 128], lhsT=BT[:, g, cs], rhs=CT[:, g, cs],
                                 start=True, stop=True)
                        P.op("dve", "tensor_tensor", reads=self.pk(pc) + ["mlef"], writes=["cbm"], out=cbm[:],
                             in0=ps[:, pc, :].rearrange("p (g l) -> p g l", g=4),
                             in1=self.mlef[:].unsqueeze(1).broadcast_to([128, 4, 128]), op=ALU.mult)
                        for g in range(4):
                            P.op("pool", "tensor_tensor", reads=["dec%d" % g, "Eh%d" % g, "cbm"], writes=["MT%d" % g],
                                 out=MT[:, 4 * g:4 * g + 4, :], in0=dec[:, 4 * g:4 * g + 4, :],
                                 in1=cbm[:, g, :].unsqueeze(1).broadcast_to([128, 4, 128]), op=ALU.mult)
                        px = self.psb()
                        pxb = ps[:, px, :].bitcast(BF16).rearrange("p (k t) -> p k t", k=8)
                        for b in range(8):
                            P.op("pe", "transpose", reads=["xsT", "ident"], writes=self.pk(px), out=pxb[:, b, :],
                                 in_=xsT[:, b, cs], identity=self.ident[:])
                        pxv = ps[:, px, :].bitcast(BF16).rearrange("p (h d) -> p h d", h=16)
                        P.op("act", "activation", reads=self.pk(px), writes=["xstm"], out=xstm[:], in_=pxv, func=AF.Copy)
                        P.op("dve", "tensor_tensor", reads=["xstm", "sm1"], writes=["xdt"], out=xdt[:], in0=xstm[:],
                             in1=dt_.unsqueeze(2).broadcast_to([128, 16, 64]), op=ALU.mult)
                        P.op("pool", "tensor_tensor", reads=["xstm", "sm9"], writes=["xw"], out=xw[:], in0=xstm[:],
                             in1=dtS.unsqueeze(2).broadcast_to([128, 16, 64]), op=ALU.mult)
                        pbt = self.psb()
                        pbb = ps[:, pbt, 0:256].bitcast(BF16).rearrange("p (k t) -> p k t", k=4)
                        for g in range(4):
                            P.op("pe", "transpose", reads=["BT", "ident"], writes=self.pk(pbt), out=pbb[:, g, :],
                                 in_=BT[:, g, cs], identity=self.ident[:])
                        P.op("act", "activation", reads=self.pk(pbt), writes=["Btm"], out=Btm[:], in_=pbb, func=AF.Copy)
                        P.op("act", "activation", reads=["H"], writes=["Hb"], out=Hb[:], in_=H[:], func=AF.Copy)
                        po = self.psb(2)
                        for g in range(4):
                            P.op("pe", "matmul", reads=["CT", "Hb"], writes=self.pk(po, 2),
                                 out=ps[:, po + g // 2, (g % 2) * 256:(g % 2) * 256 + 256],
                                 lhsT=CT[:, g, cs], rhs=Hb[:, 4 * g:4 * g + 4, :], start=True, stop=True)
                        poall = ps[:, po:po + 2, :].rearrange("p b (h d) -> p (b h) d", d=64)
                        P.op("dve", "tensor_tensor", reads=self.pk(po, 2) + ["sm7"], writes=["y1"], out=y1[:], in0=poall,
                             in1=ea.unsqueeze(2).broadcast_to([128, 16, 64]), op=ALU.mult)
                        pst = self.psb(2)
                        for g in range(4):
                            P.op("pe", "matmul", reads=["Btm", "xw"], writes=self.pk(pst, 2),
                                 out=ps[:, pst + g // 2, (g % 2) * 256:(g % 2) * 256 + 256],
                                 lhsT=Btm[:, g, :], rhs=xw[:, 4 * g:4 * g + 4, :], start=True, stop=True)
                        pstall = ps[:, pst:pst + 2, :].rearrange("p b (h d) -> p (b h) d", d=64)
                        P.op("dve", "tensor_tensor", reads=["H", "sm8"], writes=["Htmp"], out=Htmp[:], in0=H[:],
                             in1=cd.unsqueeze(2).broadcast_to([128, 16, 64]), op=ALU.mult)
                        P.op("dve", "tensor_tensor", reads=self.pk(pst, 2) + ["Htmp"], writes=["H"], out=H[:], in0=pstall,
                             in1=Htmp[:], op=ALU.add)
                        pyd = self.psb(2)
                        for h in range(16):
                            P.op("pe", "matmul", reads=["MT%d" % (h // 4), "xdt"], writes=self.pk(pyd, 2),
                                 out=ps[:, pyd + h // 8, (h % 8) * 64:(h % 8) * 64 + 64],
                                 lhsT=MT[:, h, :], rhs=xdt[:, h, :], start=True, stop=True)
                        pydall = ps[:, pyd:pyd + 2, :].rearrange("p b (h d) -> p (b h) d", d=64)
                        P.op("dve", "tensor_tensor", reads=self.pk(pyd, 2) + ["y1"], writes=["y1"], out=y1[:], in0=pydall,
                             in1=y1[:], op=ALU.add)
                        P.op("pool", "tensor_tensor", reads=["xstm", "prw"], writes=["y2"], out=y2[:], in0=xstm[:],
                             in1=self.rowp(l, 2).unsqueeze(2).broadcast_to([128, 16, 64]), op=ALU.mult)
                        P.op("pool", "tensor_tensor", reads=["y1", "y2"], writes=["y2"], out=y2[:], in0=y1[:], in1=y2[:],
                             op=ALU.add)
                        pz = self.psb(2)
                        for n in range(2):
                            self.proj_tm(pz + n, hT, "hT", c4 * 128, wa, "wa", 2048 + n * 512, 512)
                        P.op("act", "activation", reads=self.pk(pz, 2), writes=["sz"], out=sz[:],
                             in_=ps[:, pz:pz + 2, :].rearrange("p b n -> p (b n)"), func=AF.Silu)
                        y2f = y2[:].rearrange("p h d -> p (h d)")
                        P.op("dve", "tensor_tensor", reads=["y2", "sz"], writes=["y2"], out=y2f, in0=y2f, in1=sz[:],
                             op=ALU.mult)
                        ss = sm[:, 11, 0:4]
                        for g in range(4):
                            P.op("act", "activation", reads=["y2"], writes=["junkA", "sm11"], out=junkA[:],
                                 in_=y2f[:, g * 256:(g + 1) * 256], func=AF.Square, accum_out=sm[:, 11, g:g + 1])
                        P.op("act", "activation", reads=["sm11", "epsb"], writes=["sm11"], out=sm[:, 11, 4:8], in_=ss, func=AF.Ln,
                             scale=1.0 / 256, bias=self.epsb[:])
                        P.op("act", "activation", reads=["sm11"], writes=["sm11"], out=sm[:, 11, 8:12], in_=sm[:, 11, 4:8],
                             func=AF.Exp, scale=-0.5)
                        P.op("dve", "tensor_tensor", reads=["y2", "sm11"], writes=["y2"],
                             out=y2[:].rearrange("p (g r) d -> p g (r d)", g=4),
                             in0=y2[:].rearrange("p (g r) d -> p g (r d)", g=4),
                             in1=sm[:, 11, 8:12].unsqueeze(2).broadcast_to([128, 4, 256]), op=ALU.mult)
                        yq = yo[c % 2]
                        P.op("pool", "tensor_tensor", reads=["y2", "snw"], writes=["yo%d" % (c % 2)], out=yq[:], in0=y2f,
                             in1=snw[:], op=ALU.mult)
                        P.op("pool", "dma_start", reads=["yo%d" % (c % 2)], writes=["Y0_%d_%d" % (s, c)],
                             dma="yo%d" % (c % 2), out=self.Y[0, s, c * 128:(c + 1) * 128, :], in_=yq[:])
            P.emit()

    def phase_B(self, l):
        nc, P, S, NCH = self.nc, self.P, self.S, self.NCH
        with ExitStack() as st:
            self.uid = getattr(self, "uid", 0) + 1
            sb = lambda name, shape, dt, _u=self.uid: st.enter_context(nc.sbuf_tensor("%s_u%d" % (name, _u), shape, dt))
            wb = sb("wb", [128, KC, B_N], BF16)
            hT = sb("hT", [128, KC, 512], BF16)
            cosT = sb("cosT", [128, 512], F32)
            sinS = sb("sinS", [128, 512], F32)
            t1 = [sb("t1_%d" % i, [128, 512], F32) for i in range(2)]
            t2 = [sb("t2_%d" % i, [128, 512], F32) for i in range(2)]
            qrT = sb("qrT", [128, 8, 512], BF16)
            krT = sb("krT", [128, 2, S], BF16)
            V = sb("V", [128, NCH, 4, 65], BF16)
            sz = sb("sz", [128, D], F32)
            Pc = [sb("Pc%d" % i, [128, 512], BF16) for i in range(2)]
            Pp = [sb("Pp%d" % i, [128, 512], BF16) for i in range(2)]
            den = sb("den", [128, 16], F32)
            yf = sb("yf", [128, 16, 64], F32)
            yo = [sb("yo%d" % i, [128, D], BF16) for i in range(2)]
            self.epsb = sb("epsb", [128, 1], F32)
            P.op("dve", "memset", writes=["epsb"], ap=self.epsb[:], constant=EPS)
            self.load_w(wb, "wb", l, B0, B_N)
            P.op("pool", "memset", writes=["V%d" % i for i in range(NCH)], ap=V[:], constant=1.0)
            ps = self.ps
            for s in range(self.NSEQ):
                for t in range(self.NTL):
                    ts = slice(t * 512, (t + 1) * 512)
                    for c4 in range(4):
                        self.h_chunk(l, s, t * 4 + c4, hT, "hT", c4 * 128)
                    P.op("sp", "dma_start", writes=["cosT"], dma="cosT", out=cosT[:], in_=self.rope[0, :, ts])
                    P.op("sp", "dma_start", writes=["sinS"], dma="sinS", out=sinS[:], in_=self.rope[1, :, ts])
                    for b in range(10):
                        c0 = b * 128 if b < 8 else 1024 + (b - 8) * 128
                        c1 = 1280 + c0
                        pq = self.psb()
                        self.proj_fm(pq, hT, "hT", 512, wb, "wb", c0, 128)
                        pqs = self.psb()
                        self.proj_fm(pqs, hT, "hT", 512, wb, "wb", c1, 128)
                        i2 = b % 2
                        P.op("dve", "tensor_tensor", reads=self.pk(pq) + ["cosT"], writes=["t1_%d" % i2], out=t1[i2][:],
                             in0=ps[:, pq, :], in1=cosT[:], op=ALU.mult)
                        P.op("dve", "tensor_tensor", reads=self.pk(pqs) + ["sinS"], writes=["t2_%d" % i2], out=t2[i2][:],
                             in0=ps[:, pqs, :], in1=sinS[:], op=ALU.mult)
                        if b < 8:
                            dst, dkey = qrT[:, b, :], "qrT"
                        else:
                            dst, dkey = krT[:, b - 8, ts], "krT%d" % t
                        P.op("pool", "tensor_tensor", reads=["t1_%d" % i2, "t2_%d" % i2], writes=[dkey], out=dst,
                             in0=t1[i2][:], in1=t2[i2][:], op=ALU.add)
                    for c4 in range(4):
                        c = t * 4 + c4
                        cs = slice(c4 * 128, (c4 + 1) * 128)
                        pv = self.psb()
                        self.proj_tm(pv, hT, "hT", c4 * 128, wb, "wb", 2560, 256)
                        P.op("act", "activation", reads=self.pk(pv), writes=["V%d" % c], out=V[:, c, :, 0:64],
                             in_=ps[:, pv, 0:256].rearrange("p (h d) -> p h d", h=4), func=AF.Copy)
                        pz = self.psb(2)
                        for n in range(2):
                            self.proj_tm(pz + n, hT, "hT", c4 * 128, wb, "wb", 2816 + n * 512, 512)
                        P.op("act", "activation", reads=self.pk(pz, 2), writes=["sz"], out=sz[:],
                             in_=ps[:, pz:pz + 2, :].rearrange("p b n -> p (b n)"), func=AF.Silu)
                        pos = []
                        for kv in range(4):
                            half = slice((kv % 2) * 64, (kv % 2) * 64 + 64)
                            blk0 = (kv // 2) * 4
                            qv = qrT[half, blk0:blk0 + 4, cs]
                            i2 = kv % 2
                            psc = self.psb()
                            P.op("pe", "matmul", reads=["krT%d" % t, "qrT"], writes=self.pk(psc),
                                 out=ps[:, psc, :].rearrange("p (a q) -> p a q", a=4),
                                 lhsT=krT[half, kv // 2, c * 128:(c + 1) * 128], rhs=qv, start=True, stop=True)
                            P.op("act", "activation", reads=self.pk(psc), writes=["Pc%d" % i2], out=Pc[i2][:],
                                 in_=ps[:, psc, :], func=AF.Exp, scale=0.125)
                            P.op("pool", "tensor_tensor", reads=["Pc%d" % i2, "mle"], writes=["Pc%d" % i2],
                                 out=Pc[i2][:].rearrange("p (a q) -> p a q", a=4),
                                 in0=Pc[i2][:].rearrange("p (a q) -> p a q", a=4),
                                 in1=self.mle[:].unsqueeze(1).broadcast_to([128, 4, 128]), op=ALU.mult)
                            if c > 0:
                                psp = self.psb()
                                P.op("pe", "matmul", reads=["krT%d" % ((c - 1) // 4), "qrT"], writes=self.pk(psp),
                                     out=ps[:, psp, :].rearrange("p (a q) -> p a q", a=4),
                                     lhsT=krT[half, kv // 2, (c - 1) * 128:c * 128], rhs=qv, start=True, stop=True)
                                P.op("act", "activation", reads=self.pk(psp), writes=["Pp%d" % i2], out=Pp[i2][:],
                                     in_=ps[:, psp, :], func=AF.Exp, scale=0.125)
                                P.op("pool", "tensor_tensor", reads=["Pp%d" % i2, "mgt"], writes=["Pp%d" % i2],
                                     out=Pp[i2][:].rearrange("p (a q) -> p a q", a=4),
                                     in0=Pp[i2][:].rearrange("p (a q) -> p a q", a=4),
                                     in1=self.mgt[:].unsqueeze(1).broadcast_to([128, 4, 128]), op=ALU.mult)
                            po = self.psb()
                            pos.append(po)
                            for a in range(4):
                                if c > 0:
                                    P.op("pe", "matmul", reads=["Pp%d" % i2, "V%d" % (c - 1)], writes=self.pk(po),
                                         out=ps[:, po, a * 65:(a + 1) * 65], lhsT=Pp[i2][:, a * 128:(a + 1) * 128],
                                         rhs=V[:, c - 1, kv, :], start=True, stop=False)
                                P.op("pe", "matmul", reads=["Pc%d" % i2, "V%d" % c], writes=self.pk(po),
                                     out=ps[:, po, a * 65:(a + 1) * 65], lhsT=Pc[i2][:, a * 128:(a + 1) * 128],
                                     rhs=V[:, c, kv, :], start=(c == 0), stop=True)
                            pov = ps[:, po, 0:260].rearrange("p (a e) -> p a e", a=4)
                            P.op("dve", "tensor_tensor", reads=self.pk(po) + ["esink"], writes=["den%d" % kv],
                                 out=den[:, 4 * kv:4 * kv + 4].unsqueeze(2), in0=pov[:, :, 64:65],
                                 in1=self.esink[:, l * 16 + 4 * kv:l * 16 + 4 * kv + 4].unsqueeze(2), op=ALU.add)
                            P.op("dve", "reciprocal", reads=["den%d" % kv], writes=["den%d" % kv],
                                 out=den[:, 4 * kv:4 * kv + 4], in_=den[:, 4 * kv:4 * kv + 4])
                            P.op("dve", "tensor_tensor", reads=self.pk(po) + ["den%d" % kv], writes=["yf%d" % kv],
                                 out=yf[:, 4 * kv:4 * kv + 4, :], in0=pov[:, :, 0:64],
                                 in1=den[:, 4 * kv:4 * kv + 4].unsqueeze(2).broadcast_to([128, 4, 64]), op=ALU.mult)
                        yq = yo[c % 2]
                        P.op("pool", "tensor_tensor", reads=["yf%d" % k for k in range(4)] + ["sz"],
                             writes=["yo%d" % (c % 2)], out=yq[:], in0=yf[:].rearrange("p h d -> p (h d)"), in1=sz[:],
                             op=ALU.mult)
                        P.op("pool", "dma_start", reads=["yo%d" % (c % 2)], writes=["Y1_%d_%d" % (s, c)],
                             dma="yo%d" % (c % 2), out=self.Y[1, s, c * 128:(c + 1) * 128, :], in_=yq[:])
            P.emit()

    def phase_C(self, l, hg):
        nc, P, S, NCH = self.nc, self.P, self.S, self.NCH
        with ExitStack() as st:
            self.uid = getattr(self, "uid", 0) + 1
            sb = lambda name, shape, dt, _u=self.uid: st.enter_context(nc.sbuf_tensor("%s_u%d" % (name, _u), shape, dt))
            wc = sb("wc", [128, KC, C_G], BF16)
            hT = sb("hT", [128, KC, 512], BF16)
            KT = sb("KT", [65, 8, S], BF16)
            V = sb("V", [128, NCH, 8, 65], BF16)
            QT = sb("QT", [65, 8, 512], BF16)
            NC_ = sb("NC", [128, NCH, 8], F32)
            fsm = sb("fsm", [128, 4, 8], F32)
            cumT = sb("cumT", [8, 512], BF16)
            sz = sb("sz", [128, 4, 512], F32)
            PT = [sb("PT%d" % i, [128, 512], BF16) for i in range(4)]
            rd = [sb("rd%d" % i, [128, 4], F32) for i in range(2)]
            yn = [sb("yn%d" % i, [128, 4, 64], F32) for i in range(2)]
            yo = [sb("yo%d" % i, [128, 4, 512], BF16) for i in range(2)]
            self.epsb = sb("epsb", [128, 1], F32)
            P.op("dve", "memset", writes=["epsb"], ap=self.epsb[:], constant=EPS)
            self.load_w(wc, "wc", l, C0 + hg * C_G, C_G)
            P.op("pool", "memset", writes=["V%d" % i for i in range(NCH)], ap=V[:], constant=1.0)
            P.op("pool", "memset", writes=["KT%d" % i for i in range(self.NTL)], ap=KT[64:65, :, :], constant=1.0)
            ps = self.ps
            self.pools = {"gen": [0, 1], "ct": [2], "acc": [3, 4], "sc": [5, 6, 7]}
            self.prr = {}
            fb = self.rowp(l, 4)[:, hg * 8:hg * 8 + 8]
            pti = 0
            for s in range(self.NSEQ):
                for t in range(self.NTL):
                    ts = slice(t * 512, (t + 1) * 512)
                    for c4 in range(4):
                        self.h_chunk(l, s, t * 4 + c4, hT, "hT", c4 * 128)
                    for h in range(8):
                        pkb = self.psb()
                        self.proj_fm(pkb, hT, "hT", 512, wc, "wc", 512 + h * 64, 64)
                        P.op("act", "activation", reads=self.pk(pkb), writes=["KT%d" % t], out=KT[0:64, h, ts],
                             in_=ps[0:64, pkb, :], func=AF.Copy)
                    pct = self.psb(pool="ct")
                    for c4 in range(4):
                        c = t * 4 + c4
                        pf = self.psb()
                        self.proj_tm(pf, hT, "hT", c4 * 128, wc, "wc", 2048, 8)
                        f0, f1, f2 = fsm[:, 0, :], fsm[:, 1, :], fsm[:, 2, :]
                        P.op("dve", "tensor_tensor", reads=self.pk(pf) + ["prw"], writes=["f0"], out=f0,
                             in0=ps[:, pf, 0:8], in1=fb, op=ALU.add)
                        P.op("act", "activation", reads=["f0"], writes=["f1"], out=f1, in_=f0, func=AF.Exp, scale=-1.0)
                        P.op("act", "activation", reads=["f1"], writes=["f2"], out=f2, in_=f1, func=AF.Ln, bias=1.0)
                        P.op("pe", "matmul", reads=["tri", "f2"], writes=self.pk(pf), out=ps[:, pf, 8:16],
                             lhsT=self.tri[:], rhs=f2, start=True, stop=(c == 0))
                        if c > 0:
                            P.op("pe", "matmul", reads=["elast", "NC%d" % (c - 1)], writes=self.pk(pf), out=ps[:, pf, 8:16],
                                 lhsT=self.elast[:], rhs=NC_[:, c - 1, :], start=False, stop=True)
                        P.op("dve", "tensor_copy", reads=self.pk(pf), writes=["NC%d" % c], out=NC_[:, c, :], in_=ps[:, pf, 8:16])
                        P.op("pe", "transpose", reads=["NC%d" % c, "identf"], writes=self.pk(pct),
                             out=ps[0:8, pct, c4 * 128:(c4 + 1) * 128], in_=NC_[:, c, :], identity=self.identf[:])
                        pv = self.psb()
                        self.proj_tm(pv, hT, "hT", c4 * 128, wc, "wc", 1024, 512)
                        P.op("act", "activation", reads=self.pk(pv), writes=["V%d" % c], out=V[:, c, :, 0:64],
                             in_=ps[:, pv, :].rearrange("p (h d) -> p h d", h=8), func=AF.Copy)
                        pz = self.psb()
                        self.proj_tm(pz, hT, "hT", c4 * 128, wc, "wc", 1536, 512)
                        P.op("act", "activation", reads=self.pk(pz), writes=["sz%d" % c4], out=sz[:, c4, :],
                             in_=ps[:, pz, :], func=AF.Silu)
                    P.op("dve", "tensor_scalar", reads=self.pk(pct), writes=["cumT"], out=cumT[:], in0=ps[0:8, pct, :],
                         scalar1=-1.0, scalar2=None, op0=ALU.mult)
                    yq = yo[t % 2]
                    ykey = "yo%d" % (t % 2)
                    for h in range(8):
                        pqb = self.psb()
                        for kc in range(KC):
                            P.op("pe", "matmul", reads=["hT", "wc"], writes=self.pk(pqb), out=ps[0:64, pqb, :],
                                 lhsT=wc[:, kc, h * 64:(h + 1) * 64], rhs=hT[:, kc, :], start=(kc == 0),
                                 stop=(kc == KC - 1))
                        P.op("pe", "matmul", reads=["sel", "cumT"], writes=self.pk(pqb), out=ps[64:65, pqb, :],
                             lhsT=self.sel[:, h, 64:65], rhs=cumT[:], start=True, stop=True)
                        P.op("act", "activation", reads=self.pk(pqb), writes=["QT%d" % h], out=QT[:, h, :],
                             in_=ps[0:65, pqb, :], func=AF.Copy)
                    nkb = 4 * t + 4

                    def qk(h, j):
                        q0 = max(j - 4 * t, 0)
                        nq = 4 - q0
                        psc = self.psb(pool="sc")
                        P.op("pe", "matmul", reads=["KT%d" % (j // 4), "QT%d" % h], writes=self.pk(psc),
                             out=ps[:, psc, 0:nq * 128], lhsT=KT[:, h, j * 128:(j + 1) * 128],
                             rhs=QT[:, h, q0 * 128:512], start=True, stop=True)
                        return psc

                    seq = [(2 * hp + e, j) for hp in range(4) for j in range(nkb) for e in range(2)]
                    DIST = 2
                    pend = [qk(*seq[i]) for i in range(min(DIST, len(seq)))]
                    pos = {}
                    for i, (h, j) in enumerate(seq):
                        psc = pend.pop(0)
                        if i + DIST < len(seq):
                            pend.append(qk(*seq[i + DIST]))
                        if j == 0:
                            pos[h] = self.psb(pool="acc")
                        po = pos[h]
                        pov = ps[:, po, 0:260].rearrange("p (a e) -> p a e", a=4)
                        jj = j - 4 * t
                        q0 = max(jj, 0)
                        nq = 4 - q0
                        pt = PT[pti % 4]
                        ptk = "PT%d" % (pti % 4)
                        pti += 1
                        P.op("act", "activation", reads=self.pk(psc) + ["NC%d" % j], writes=[ptk], out=pt[:, 0:nq * 128],
                             in_=ps[:, psc, 0:nq * 128], func=AF.Exp, scale=0.125, bias=NC_[:, j, h:h + 1])
                        if jj >= 0:
                            P.op("pool", "tensor_tensor", reads=[ptk, "mle"], writes=[ptk], out=pt[:, 0:128],
                                 in0=pt[:, 0:128], in1=self.mle[:], op=ALU.mult)
                        for a in range(q0, 4):
                            P.op("pe", "matmul", reads=[ptk, "V%d" % j], writes=self.pk(po), out=pov[:, a, :],
                                 lhsT=pt[:, (a - q0) * 128:(a - q0 + 1) * 128], rhs=V[:, j, h, :],
                                 start=(j == 0 and a == 0), stop=(j == nkb - 1), skip_group_check=True)
                        if j == nkb - 1:
                            r_, y_ = rd[h % 2], yn[h % 2]
                            P.op("dve", "reciprocal", reads=self.pk(po), writes=["rd%d" % (h % 2)], out=r_[:].unsqueeze(2),
                                 in_=pov[:, :, 64:65])
                            P.op("dve", "tensor_tensor", reads=self.pk(po) + ["rd%d" % (h % 2)], writes=["yn%d" % (h % 2)],
                                 out=y_[:], in0=pov[:, :, 0:64], in1=r_[:].unsqueeze(2).broadcast_to([128, 4, 64]),
                                 op=ALU.mult)
                            P.op("pool", "tensor_tensor", reads=["yn%d" % (h % 2)] + ["sz%d" % i for i in range(4)],
                                 writes=[ykey], out=yq[:, :, h * 64:(h + 1) * 64], in0=y_[:],
                                 in1=sz[:, :, h * 64:(h + 1) * 64], op=ALU.mult)
                    for c4 in range(4):
                        c = t * 4 + c4
                        P.op("pool", "dma_start", reads=[ykey], writes=["Y2_%d_%d_%d" % (s, c, hg)], dma=ykey,
                             out=self.Y[2, s, c * 128:(c + 1) * 128, hg * 512:(hg + 1) * 512], in_=yq[:, c4, :])
            P.emit()
            self.pools = {"gen": list(range(8))}
            self.prr = {}

    def phase_D(self, l):
        nc, P, S = self.nc, self.P, self.S
        last = (l == self.layers - 1)
        with ExitStack() as st:
            self.uid = getattr(self, "uid", 0) + 1
            sb = lambda name, shape, dt, _u=self.uid: st.enter_context(nc.sbuf_tensor("%s_u%d" % (name, _u), shape, dt))
            wg = sb("wg", [128, KC, G_N], BF16)
            wp = [sb("wp%d" % i, [128, KC, D], BF16) for i in range(3)]
            wo = sb("wo", [128, KC, D], BF16)
            hT = sb("hT", [128, KC, 128], BF16)
            gbb = sb("gbb", [128, 3 * D], F32)
            fnw = sb("fnw", [128, D], F32)
            yt = [[sb("yt%d_%d" % (b, i), [128, D], BF16) for i in range(2)] for b in range(3)]
            ybT = sb("ybT", [128, KC, 128], BF16)
            gs = sb("gs", [128, D], F32)
            tmp = sb("tmp", [128, D], F32)
            mg = sb("mg", [128, D], F32)
            mgb = sb("mgb", [128, D], BF16)
            mT = sb("mT", [128, KC, 128], BF16)
            xo = [sb("xo%d" % i, [128, D], F32) for i in range(2)]
            fs = sb("fs", [128, 4], F32)
            self.epsb = sb("epsb", [128, 1], F32)
            P.op("dve", "memset", writes=["epsb"], ap=self.epsb[:], constant=EPS)
            self.load_w(wg, "wg", l, G0, G_N)
            for b in range(3):
                self.load_w(wp[b], "wp%d" % b, l, 0, D, src=self.wproj[l, b])
            self.load_w(wo, "wo", l, 0, D, src=self.wout[l])
            P.op("sp", "dma_start", writes=["gbb"], dma="gbb", out=gbb[:],
                 in_=self.prow[l * 4176 + 1104:l * 4176 + 1104 + 3072].partition_broadcast(128))
            P.op("sp", "dma_start", writes=["fnw"], dma="fnw", out=fnw[:],
                 in_=self.prow[DEPTH * 4176:DEPTH * 4176 + 1024].partition_broadcast(128))
            ps = self.ps
            for s in range(self.NSEQ):
                for c in range(self.NCH):
                    i2 = c % 2
                    for b in range(3):
                        rk = ["Y%d_%d_%d" % (b, s, c)] if b < 2 else ["Y2_%d_%d_%d" % (s, c, g) for g in range(2)]
                        P.op("sp", "dma_start", reads=rk, writes=["yt%d_%d" % (b, i2)], dma="yt%d_%d" % (b, i2),
                             out=yt[b][i2][:], in_=self.Y[b, s, c * 128:(c + 1) * 128, :])
                    sl = self.h_chunk(l, s, c, hT, "hT", 0)
                    for b in range(3):
                        pg = self.psb(2)
                        for n in range(2):
                            self.proj_tm(pg + n, hT, "hT", 0, wg, "wg", b * D + n * 512, 512)
                        P.op("dve", "tensor_tensor", reads=self.pk(pg, 2) + ["gbb"], writes=["gs"], out=gs[:],
                             in0=ps[:, pg:pg + 2, :].rearrange("p b n -> p (b n)"), in1=gbb[:, b * D:(b + 1) * D],
                             op=ALU.add)
                        P.op("act", "activation", reads=["gs"], writes=["gs"], out=gs[:], in_=gs[:], func=AF.Sigmoid)
                        pt_ = self.psb()
                        ptb = ps[:, pt_, :].bitcast(BF16).rearrange("p (k t) -> p k t", k=8)
                        for kc in range(KC):
                            P.op("pe", "transpose", reads=["yt%d_%d" % (b, i2), "ident"], writes=self.pk(pt_),
                                 out=ptb[:, kc, :], in_=yt[b][i2][:, kc * 128:(kc + 1) * 128], identity=self.ident[:])
                        P.op("act", "activation", reads=self.pk(pt_), writes=["ybT"], out=ybT[:], in_=ptb, func=AF.Copy)
                        pb = self.psb(2)
                        for n in range(2):
                            for kc in range(KC):
                                P.op("pe", "matmul", reads=["ybT", "wp%d" % b], writes=self.pk(pb + n),
                                     out=ps[:, pb + n, :], lhsT=ybT[:, kc, :], rhs=wp[b][:, kc, n * 512:(n + 1) * 512],
                                     start=(kc == 0), stop=(kc == KC - 1))
                        pball = ps[:, pb:pb + 2, :].rearrange("p b n -> p (b n)")
                        if b == 0:
                            P.op("dve", "tensor_tensor", reads=self.pk(pb, 2) + ["gs"], writes=["mg"], out=mg[:], in0=pball,
                                 in1=gs[:], op=ALU.mult)
                        else:
                            P.op("dve", "tensor_tensor", reads=self.pk(pb, 2) + ["gs"], writes=["tmp"], out=tmp[:],
                                 in0=pball, in1=gs[:], op=ALU.mult)
                            P.op("pool", "tensor_tensor", reads=["tmp", "mg"], writes=["mg"], out=mg[:], in0=tmp[:],
                                 in1=mg[:], op=ALU.add)
                    P.op("act", "activation", reads=["mg"], writes=["mgb"], out=mgb[:], in_=mg[:], func=AF.Copy)
                    pt_ = self.psb()
                    ptb = ps[:, pt_, :].bitcast(BF16).rearrange("p (k t) -> p k t", k=8)
                    for kc in range(KC):
                        P.op("pe", "transpose", reads=["mgb", "ident"], writes=self.pk(pt_), out=ptb[:, kc, :],
                             in_=mgb[:, kc * 128:(kc + 1) * 128], identity=self.ident[:])
                    P.op("dve", "tensor_copy", reads=self.pk(pt_), writes=["mT"], out=mT[:], in_=ptb)
                    po = self.psb(2)
                    for n in range(2):
                        for kc in range(KC):
                            P.op("pe", "matmul", reads=["mT", "wo"], writes=self.pk(po + n), out=ps[:, po + n, :],
                                 lhsT=mT[:, kc, :], rhs=wo[:, kc, n * 512:(n + 1) * 512], start=(kc == 0),
                                 stop=(kc == KC - 1))
                    xq = xo[i2]
                    P.op("dve", "tensor_tensor", reads=self.pk(po, 2) + ["xt%d" % sl], writes=["xo%d" % i2], out=xq[:],
                         in0=ps[:, po:po + 2, :].rearrange("p b n -> p (b n)"), in1=self.xt[sl][:], op=ALU.add)
                    if not last:
                        P.op("pool", "dma_start", reads=["xo%d" % i2], writes=["x1_%d_%d" % (s, c)], dma="xo%d" % i2,
                             out=self.X1[s, c * 128:(c + 1) * 128, :], in_=xq[:])
                    else:
                        P.op("act", "activation", reads=["xo%d" % i2], writes=["junkD", "fs"], out=self.junk[0][:],
                             in_=xq[:], func=AF.Square, accum_out=fs[:, 0:1])
                        P.op("act", "activation", reads=["fs", "epsb"], writes=["fs"], out=fs[:, 1:2], in_=fs[:, 0:1], func=AF.Ln,
                             scale=1.0 / D, bias=self.epsb[:])
                        P.op("act", "activation", reads=["fs"], writes=["fs"], out=fs[:, 2:3], in_=fs[:, 1:2], func=AF.Exp,
                             scale=-0.5)
                        P.op("dve", "scalar_tensor_tensor", reads=["xo%d" % i2, "fs", "fnw"], writes=["xo%d" % i2],
                             out=xq[:], in0=xq[:], scalar=fs[:, 2:3], in1=fnw[:], op0=ALU.mult, op1=ALU.mult)
                        P.op("pool", "dma_start", reads=["xo%d" % i2], writes=["out_%d_%d" % (s, c)], dma="xo%d" % i2,
                             out=self.out[s, c * 128:(c + 1) * 128, :], in_=xq[:])
            P.emit()


def host_layout(S, norm_w, w_in, conv_w, conv_b, dt_bias, a_log, d_skip, ssm_norm_w, sinks, f_bias, gate_bias,
                w_proj, w_out, final_norm_w):
    L = DEPTH
    cols = _col_order()
    win = np.ascontiguousarray(
        np.asarray(w_in, np.float32)[:, :, cols].reshape(L, KC, 128, NT).transpose(0, 2, 1, 3))
    wproj = np.ascontiguousarray(np.asarray(w_proj, np.float32).reshape(L, 3, KC, 128, D).transpose(0, 1, 3, 2, 4))
    wout = np.ascontiguousarray(np.asarray(w_out, np.float32).reshape(L, KC, 128, D).transpose(0, 2, 1, 3))
    ppart = np.zeros((128, L * 88), np.float32)
    prow = np.zeros((L * 4176 + 1024,), np.float32)
    for l in range(L):
        ppart[:, l * 88:l * 88 + 8] = np.asarray(norm_w[l]).reshape(KC, 128).T
        cw = np.asarray(conv_w[l]).reshape(4, 16, 128)
        ppart[:, l * 88 + 8:l * 88 + 72] = cw.transpose(2, 1, 0).reshape(128, 64)
        ppart[:, l * 88 + 72:l * 88 + 88] = np.asarray(conv_b[l]).reshape(16, 128).T
        o = l * 4176
        prow[o:o + 16] = dt_bias[l]
        prow[o + 16:o + 32] = a_log[l]
        prow[o + 32:o + 48] = d_skip[l]
        prow[o + 48:o + 64] = sinks[l]
        prow[o + 64:o + 80] = f_bias[l]
        prow[o + 80:o + 1104] = ssm_norm_w[l]
        prow[o + 1104:o + 4176] = np.asarray(gate_bias[l]).reshape(-1)
    prow[L * 4176:] = final_norm_w
    pos = np.arange(S, dtype=np.float32)
    inv = (np.float32(10000.0) ** (-np.arange(0, 64, 2, dtype=np.float32) / np.float32(64))).astype(np.float32)
    ang = (pos[:, None] * inv[None, :]).astype(np.float32)
    cos, sin = np.cos(ang).astype(np.float32), np.sin(ang).astype(np.float32)
    rope = np.zeros((2, 128, S), np.float32)
    for p in range(128):
        rope[0, p] = cos[:, p % 32]
        rope[1, p] = sin[:, p % 32] * (-1.0 if (p % 64) < 32 else 1.0)
    return dict(win=win, wproj=wproj, wout=wout, ppart=ppart, prow=prow, rope=rope)


_NC_CACHE = {}


def kernel(x, norm_w, w_in, conv_w, conv_b, dt_bias, a_log, d_skip, ssm_norm_w, sinks, f_bias, gate_bias,
           w_proj, w_out, final_norm_w):
    x = np.asarray(x, np.float32)
    B, S, _ = x.shape
    nseq = B // NCORES
    shared = host_layout(S, norm_w, w_in, conv_w, conv_b, dt_bias, a_log, d_skip, ssm_norm_w, sinks, f_bias,
                         gate_bias, w_proj, w_out, final_norm_w)
    key = (S, nseq)
    if key not in _NC_CACHE:
        _NC_CACHE[key] = Builder(S, nseq).build()
    nc = _NC_CACHE[key]
    in_maps = []
    for c in range(NCORES):
        m = dict(shared)
        m["x"] = np.ascontiguousarray(x[c * nseq:(c + 1) * nseq])
        in_maps.append(m)
    res = run_bass_kernel_spmd(nc, in_maps, core_ids=list(range(NCORES)))
    return np.concatenate([r["out"] for r in res.results], axis=0).astype(np.float32)
```

```python
import math
from contextlib import ExitStack

import numpy as np
import concourse.bass as bass
import concourse.mybir as mybir
from concourse.bass_utils import run_bass_kernel_spmd

F32 = mybir.dt.float32
BF16 = mybir.dt.bfloat16
AF = mybir.ActivationFunctionType
ALU = mybir.AluOpType

D = 1024
KC = 8
DEPTH = 2
NCORES = 8
EPS = 1e-6
ENGS = ("pe", "act", "dve", "pool", "sp")

A0, A_N = 0, 3088
B0, B_N = 3088, 3840
C0, C_G = 6928, 2056
G0, G_N = 6928 + 2 * 2056, 3072
NT = G0 + G_N
SWA_QORDER = [0, 4, 1, 5, 2, 6, 3, 7, 8, 12, 9, 13, 10, 14, 11, 15]


def _col_order():
    o = {}
    off = 0
    names = [("a_xbc", 2048), ("a_z", 1024), ("a_dt", 16), ("b_q", 1024), ("b_k", 256), ("b_v", 256),
             ("b_z", 1024), ("c_q", 1024), ("c_k", 1024), ("c_v", 1024), ("c_f", 16), ("c_z", 1024),
             ("gates", 3072)]
    for n, s in names:
        o[n] = off
        off += s
    cols = []
    cols += list(range(o["a_xbc"], o["a_xbc"] + 2048))
    cols += list(range(o["a_z"], o["a_z"] + 1024))
    cols += list(range(o["a_dt"], o["a_dt"] + 16))
    assert len(cols) == A_N
    q = [o["b_q"] + h * 64 + d for h in SWA_QORDER for d in range(64)]
    qs = [o["b_q"] + h * 64 + (d + 32) % 64 for h in SWA_QORDER for d in range(64)]
    k = [o["b_k"] + h * 64 + d for h in range(4) for d in range(64)]
    ks = [o["b_k"] + h * 64 + (d + 32) % 64 for h in range(4) for d in range(64)]
    cols += q + k + qs + ks
    cols += list(range(o["b_v"], o["b_v"] + 256))
    cols += list(range(o["b_z"], o["b_z"] + 1024))
    assert len(cols) == B0 + B_N
    for hg in range(2):
        for nm in ("c_q", "c_k", "c_v", "c_z"):
            cols += list(range(o[nm] + hg * 512, o[nm] + hg * 512 + 512))
        cols += list(range(o["c_f"] + hg * 8, o["c_f"] + hg * 8 + 8))
    assert len(cols) == G0
    cols += list(range(o["gates"], o["gates"] + 3072))
    assert len(cols) == NT
    return np.array(cols, dtype=np.int64)


class Prog:
    def __init__(self, nc, stack, same_engine_sync=True):
        self.nc = nc
        self.stack = stack
        self.same = same_engine_sync
        self.ops = []
        self.last_w = {}
        self.readers = {}
        self.eng_sem = {e: stack.enter_context(nc.semaphore("s_" + e)) for e in ENGS}
        self.cnt = {e: 0 for e in ENGS}
        self.dsem = {}
        self.dcnt = {}
        self.known = {e: {} for e in ENGS}
        self.done_ops = 0

    def op(self, eng, meth, reads=(), writes=(), dma=None, **kw):
        idx = len(self.ops)
        deps = set()
        for k in reads:
            if k in self.last_w:
                deps.add(self.last_w[k])
        for k in writes:
            if k in self.last_w:
                deps.add(self.last_w[k])
            for r in self.readers.get(k, ()):
                deps.add(r)
        deps.discard(idx)
        best = {}
        for d in deps:
            od = self.ops[d]
            kk = ("d", od["dma"]) if od["dma"] is not None else ("e", od["eng"])
            if kk not in best or best[kk] < d:
                best[kk] = d
        deps = set(best.values())
        for k in reads:
            self.readers.setdefault(k, []).append(idx)
        for k in writes:
            self.last_w[k] = idx
            self.readers[k] = []
        self.ops.append(dict(eng=eng, meth=meth, kw=kw, deps=deps, dma=dma, sig=False, ev=None))
        return idx

    def emit(self, final=False):
        nc, ops = self.nc, self.ops
        import os as _os
        if _os.environ.get("OPS_LIMIT"):
            del ops[int(_os.environ["OPS_LIMIT"]):]
        lo = self.done_ops
        new = range(lo, len(ops))
        for i in new:
            o = ops[i]
            if o["dma"] is not None:
                o["sig"] = True
            for d in o["deps"]:
                od = ops[d]
                if d < lo:
                    continue
                if od["dma"] is not None or od["eng"] != o["eng"] or o["dma"] is not None:
                    od["sig"] = True
                elif self.same and o["eng"] != "pe":
                    od["sig"] = True
        for i in new:
            o = ops[i]
            if o["dma"] is not None:
                k = o["dma"]
                if k not in self.dsem:
                    self.dsem[k] = self.stack.enter_context(nc.semaphore("d_" + str(k)))
                    self.dcnt[k] = 0
                self.dcnt[k] += 16
                o["ev"] = (self.dsem[k], self.dcnt[k], "d_" + str(k))
            elif o["sig"]:
                self.cnt[o["eng"]] += 1
                o["ev"] = (self.eng_sem[o["eng"]], self.cnt[o["eng"]], o["eng"])
        per_eng = {e: [] for e in ENGS}
        for i in new:
            per_eng[ops[i]["eng"]].append(i)
        same = self.same

        def body(ename):
            def f(eng):
                kn = self.known[ename]
                for i in per_eng[ename]:
                    o = ops[i]
                    need = {}
                    for d in o["deps"]:
                        if d < lo:
                            continue
                        od = ops[d]
                        ev = od["ev"]
                        if ev is None:
                            continue
                        sem, val, name = ev
                        if (od["dma"] is None and od["eng"] == ename and o["dma"] is None
                                and (ename == "pe" or not same)):
                            continue
                        if kn.get(name, 0) >= val:
                            continue
                        if name not in need or need[name][1] < val:
                            need[name] = (sem, val)
                    for name, (sem, val) in need.items():
                        eng.wait_ge(sem, val)
                        kn[name] = val
                    ins = getattr(eng, o["meth"])(**o["kw"])
                    if o["ev"] is not None:
                        sem, val, name = o["ev"]
                        ins.then_inc(sem, 16 if o["dma"] is not None else 1)
                if ename == "sp":
                    for k, s in self.dsem.items():
                        if kn.get("d_" + str(k), 0) < self.dcnt[k]:
                            eng.wait_ge(s, self.dcnt[k])
                            kn["d_" + str(k)] = self.dcnt[k]
            return f

        with nc.Block() as block:
            block.tensor(body("pe"))
            block.scalar(body("act"))
            block.vector(body("dve"))
            block.gpsimd(body("pool"))
            block.sync(body("sp"))
        self.done_ops = len(ops)


class Builder:
    def __init__(self, S, NSEQ, debug=False, layers=DEPTH, phases="ABCD"):
        self.S, self.NSEQ, self.debug, self.layers, self.phases = S, NSEQ, debug, layers, phases
        self.NCH = S // 128
        self.NTL = S // 512
        nc = self.nc = bass.Bass("TRN2", target_bir_lowering=False)
        L = DEPTH
        okind = "ExternalOutput" if debug else "Internal"
        self.x = nc.dram_tensor("x", [NSEQ, S, D], F32, kind="ExternalInput").ap()
        self.win = nc.dram_tensor("win", [L, 128, KC, NT], F32, kind="ExternalInput").ap()
        self.wproj = nc.dram_tensor("wproj", [L, 3, 128, KC, D], F32, kind="ExternalInput").ap()
        self.wout = nc.dram_tensor("wout", [L, 128, KC, D], F32, kind="ExternalInput").ap()
        self.ppart = nc.dram_tensor("ppart", [128, L * (8 + 64 + 16)], F32, kind="ExternalInput").ap()
        self.prow = nc.dram_tensor("prow", [L * (80 + 1024 + 3072) + 1024], F32, kind="ExternalInput").ap()
        self.rope = nc.dram_tensor("rope", [2, 128, S], F32, kind="ExternalInput").ap()
        self.out = nc.dram_tensor("out", [NSEQ, S, D], F32, kind="ExternalOutput").ap()
        self.Y = nc.dram_tensor("ybr", [3, NSEQ, S, D], BF16, kind=okind).ap()
        self.X1 = nc.dram_tensor("x1", [NSEQ, S, D], F32, kind=okind).ap()
        self.Y2T = nc.dram_tensor("y2t", [NSEQ, D, S], BF16, kind=okind).ap()
        self.pools = {"gen": list(range(8))}
        self.prr = {}

    def psb(self, n=1, pool="gen"):
        banks = self.pools[pool]
        r = self.prr.get(pool, 0)
        if n == 2:
            assert len(banks) % 2 == 0
            if r % 2:
                r += 1
            b = banks[r % len(banks)]
            self.prr[pool] = r + 2
            return b
        b = banks[r % len(banks)]
        self.prr[pool] = r + 1
        return b

    def pk(self, b, n=1):
        return ["ps%d" % (b + i) for i in range(n)]

    def build(self):
        nc = self.nc
        with ExitStack() as gst:
            self.P = P = Prog(nc, gst)
            sb = lambda name, shape, dt: gst.enter_context(nc.sbuf_tensor(name, shape, dt))
            self.ps = gst.enter_context(nc.psum_tensor("ps", [128, 8, 512], F32))
            self.ident = sb("ident", [128, 128], BF16)
            self.identf = sb("identf", [128, 128], F32)
            self.tri = sb("tri", [128, 128], F32)
            self.elast = sb("elast", [128, 128], F32)
            self.onesf = sb("onesf", [128, 128], F32)
            self.onesb = sb("onesb", [128, 64], BF16)
            self.mle = sb("mle", [128, 128], BF16)
            self.mgt = sb("mgt", [128, 128], BF16)
            self.mlef = sb("mlef", [128, 128], F32)
            self.sel = sb("sel", [8, 8, 65], BF16)
            self.ppt = sb("ppt", [128, DEPTH * 88], F32)
            self.prw = sb("prw", [128, DEPTH * 80], F32)
            self.abc = sb("abc", [128, DEPTH * 16], F32)
            self.esink = sb("esink", [128, DEPTH * 16], F32)
            self.xt = [sb("xt%d" % i, [128, D], F32) for i in range(2)]
            self.xn = [sb("xn%d" % i, [128, D], BF16) for i in range(2)]
            self.junk = [sb("junk%d" % i, [128, D], BF16) for i in range(2)]
            self.st4 = [sb("st4_%d" % i, [128, 4], F32) for i in range(2)]
            self.xslot = 0
            self.setup_consts()
            import os as _os
            if _os.environ.get("SETUP_LIMIT"):
                lim = int(_os.environ["SETUP_LIMIT"])
                del P.ops[lim:]
            P.emit()
            for l in range(self.layers):
                if "A" in self.phases:
                    self.phase_A(l)
                if "B" in self.phases:
                    self.phase_B(l)
                if "C" in self.phases:
                    for hg in range(2):
                        self.phase_C(l, hg)
                if "D" in self.phases:
                    self.phase_D(l)
        return nc

    def setup_consts(self):
        P = self.P
        P.op("pool", "memset", writes=["ident"], ap=self.ident[:], constant=1.0)
        P.op("pool", "affine_select", reads=["ident"], writes=["ident"], out=self.ident[:], in_=self.ident[:],
             pattern=[[-1, 128]], compare_op=ALU.is_equal, fill=0.0, base=0, channel_multiplier=1)
        P.op("pool", "memset", writes=["identf"], ap=self.identf[:], constant=1.0)
        P.op("pool", "affine_select", reads=["identf"], writes=["identf"], out=self.identf[:], in_=self.identf[:],
             pattern=[[-1, 128]], compare_op=ALU.is_equal, fill=0.0, base=0, channel_multiplier=1)
        P.op("pool", "memset", writes=["tri"], ap=self.tri[:], constant=1.0)
        P.op("pool", "affine_select", reads=["tri"], writes=["tri"], out=self.tri[:], in_=self.tri[:],
             pattern=[[1, 128]], compare_op=ALU.is_ge, fill=0.0, base=0, channel_multiplier=-1)
        P.op("pool", "memset", writes=["mlef"], ap=self.mlef[:], constant=1.0)
        P.op("pool", "affine_select", reads=["mlef"], writes=["mlef"], out=self.mlef[:], in_=self.mlef[:],
             pattern=[[1, 128]], compare_op=ALU.is_ge, fill=0.0, base=0, channel_multiplier=-1)
        P.op("pool", "memset", writes=["mle"], ap=self.mle[:], constant=1.0)
        P.op("pool", "affine_select", reads=["mle"], writes=["mle"], out=self.mle[:], in_=self.mle[:],
             pattern=[[1, 128]], compare_op=ALU.is_ge, fill=0.0, base=0, channel_multiplier=-1)
        P.op("pool", "memset", writes=["mgt"], ap=self.mgt[:], constant=1.0)
        P.op("pool", "affine_select", reads=["mgt"], writes=["mgt"], out=self.mgt[:], in_=self.mgt[:],
             pattern=[[-1, 128]], compare_op=ALU.is_gt, fill=0.0, base=0, channel_multiplier=1)
        P.op("pool", "memset", writes=["elast"], ap=self.elast[:], constant=1.0)
        P.op("pool", "affine_select", reads=["elast"], writes=["elast"], out=self.elast[:], in_=self.elast[:],
             pattern=[[0, 128]], compare_op=ALU.is_equal, fill=0.0, base=-127, channel_multiplier=1)
        P.op("pool", "memset", writes=["onesf"], ap=self.onesf[:], constant=1.0)
        P.op("pool", "memset", writes=["onesb"], ap=self.onesb[:], constant=1.0)
        P.op("pool", "memset", writes=["sel"], ap=self.sel[:], constant=8.0)
        P.op("pool", "affine_select", reads=["sel"], writes=["sel"], out=self.sel[:], in_=self.sel[:],
             pattern=[[1, 8], [0, 65]], compare_op=ALU.is_equal, fill=0.0, base=0, channel_multiplier=-1)
        P.op("pool", "affine_select", reads=["sel"], writes=["sel"], out=self.sel[:], in_=self.sel[:],
             pattern=[[0, 8], [1, 65]], compare_op=ALU.is_equal, fill=0.0, base=-64, channel_multiplier=0)
        P.op("sp", "dma_start", writes=["ppt"], dma="ppt", out=self.ppt[:], in_=self.ppart)
        for l in range(DEPTH):
            P.op("sp", "dma_start", writes=["prw"], dma="prw", out=self.prw[:, l * 80:(l + 1) * 80],
                 in_=self.prow[l * 4176:l * 4176 + 80].partition_broadcast(128))
        for l in range(DEPTH):
            P.op("act", "activation", reads=["prw"], writes=["abc"], out=self.abc[:, l * 16:(l + 1) * 16],
                 in_=self.prw[:, l * 80 + 16:l * 80 + 32], func=AF.Exp)
            P.op("dve", "tensor_scalar", reads=["abc"], writes=["abc"], out=self.abc[:, l * 16:(l + 1) * 16],
                 in0=self.abc[:, l * 16:(l + 1) * 16], scalar1=-1.0, scalar2=None, op0=ALU.mult)
            P.op("act", "activation", reads=["prw"], writes=["esink"], out=self.esink[:, l * 16:(l + 1) * 16],
                 in_=self.prw[:, l * 80 + 48:l * 80 + 64], func=AF.Exp)

    def nw(self, l):
        return self.ppt[:, l * 88:l * 88 + 8]

    def convw(self, l, b, k):
        o = l * 88 + 8 + b * 4 + k
        return self.ppt[:, o:o + 1]

    def convb(self, l, b):
        o = l * 88 + 72 + b
        return self.ppt[:, o:o + 1]

    def rowp(self, l, i):
        return self.prw[:, l * 80 + i * 16:l * 80 + (i + 1) * 16]

    def xsrc(self, l, s, c):
        src = self.x if l == 0 else self.X1
        return src[s, c * 128:(c + 1) * 128, :], ("xin" if l == 0 else "x1_%d_%d" % (s, c))

    def h_chunk(self, l, s, c, hT, hkey, col0):
        P = self.P
        sl = self.xslot
        self.xslot ^= 1
        xt, xn, junk, st4 = self.xt[sl], self.xn[sl], self.junk[sl], self.st4[sl]
        src, skey = self.xsrc(l, s, c)
        P.op("sp", "dma_start", reads=[skey], writes=["xt%d" % sl], dma="xt%d" % sl, out=xt[:], in_=src)
        P.op("act", "activation", reads=["xt%d" % sl], writes=["junk%d" % sl, "st4_%d" % sl],
             out=junk[:], in_=xt[:], func=AF.Square, accum_out=st4[:, 0:1])
        P.op("act", "activation", reads=["st4_%d" % sl, "epsb"], writes=["st4_%d" % sl], out=st4[:, 1:2], in_=st4[:, 0:1],
             func=AF.Ln, scale=1.0 / D, bias=self.epsb[:])
        P.op("act", "activation", reads=["st4_%d" % sl], writes=["st4_%d" % sl], out=st4[:, 2:3], in_=st4[:, 1:2],
             func=AF.Exp, scale=-0.5)
        P.op("act", "activation", reads=["xt%d" % sl, "st4_%d" % sl], writes=["xn%d" % sl], out=xn[:], in_=xt[:],
             func=AF.Identity, scale=st4[:, 2:3])
        b = self.psb()
        ptb = self.ps[:, b, :].bitcast(BF16).rearrange("p (k t) -> p k t", k=8)
        for kc in range(KC):
            P.op("pe", "transpose", reads=["xn%d" % sl, "ident"], writes=self.pk(b), out=ptb[:, kc, :],
                 in_=xn[:, kc * 128:(kc + 1) * 128], identity=self.ident[:])
        P.op("dve", "tensor_tensor", reads=self.pk(b) + ["ppt"], writes=[hkey],
             out=hT[:, :, col0:col0 + 128], in0=ptb,
             in1=self.nw(l).unsqueeze(2).broadcast_to([128, 8, 128]), op=ALU.mult)
        return sl

    def load_w(self, wt, key, l, c0, n, src=None):
        P = self.P
        src = self.win[l] if src is None else src
        step = 1024
        for kc in range(KC):
            for o in range(0, n, step):
                m = min(step, n - o)
                P.op("pool", "dma_start", writes=[key], dma=key, out=wt[:, kc, o:o + m],
                     in_=src[:, kc, c0 + o:c0 + o + m])

    def proj_tm(self, dst_bank, hT, hkey, col0, wt, wkey, wc0, n, poff=0):
        P = self.P
        for kc in range(KC):
            P.op("pe", "matmul", reads=[hkey, wkey], writes=self.pk(dst_bank),
                 out=self.ps[:, dst_bank, poff:poff + n], lhsT=hT[:, kc, col0:col0 + 128],
                 rhs=wt[:, kc, wc0:wc0 + n], start=(kc == 0), stop=(kc == KC - 1))

    def proj_fm(self, dst_bank, hT, hkey, ntok, wt, wkey, wc0, m, first=True):
        P = self.P
        for kc in range(KC):
            P.op("pe", "matmul", reads=[hkey, wkey], writes=self.pk(dst_bank),
                 out=self.ps[0:m, dst_bank, 0:ntok], lhsT=wt[:, kc, wc0:wc0 + m],
                 rhs=hT[:, kc, 0:ntok], start=(first and kc == 0), stop=(kc == KC - 1))

    def phase_A(self, l):
        nc, P, S = self.nc, self.P, self.S
        with ExitStack() as st:
            self.uid = getattr(self, "uid", 0) + 1
            sb = lambda name, shape, dt, _u=self.uid: st.enter_context(nc.sbuf_tensor("%s_u%d" % (name, _u), shape, dt))
            wa = sb("wa", [128, KC, A_N], BF16)
            hT = sb("hT", [128, KC, 512], BF16)
            Ub = [sb("Ub%d" % i, [128, 515], F32) for i in range(2)]
            Ucar = sb("Ucar", [128, 16, 3], F32)
            junkA = sb("junkA", [128, 256], BF16)
            acc = [sb("acc%d" % i, [128, 512], F32) for i in range(2)]
            xsT = sb("xsT", [128, 8, 512], BF16)
            BT = sb("BT", [128, 4, 512], BF16)
            CT = sb("CT", [128, 4, 512], BF16)
            H = sb("H", [128, 16, 64], F32)
            Hb = sb("Hb", [128, 16, 64], BF16)
            Htmp = sb("Htmp", [128, 16, 64], F32)
            sm = sb("sm", [128, 12, 16], F32)
            rhsall = sb("rhsall", [128, 16, 128], F32)
            Eh = sb("Eh", [128, 16, 128], F32)
            dec = Eh
            cbm = sb("cbm", [128, 4, 128], F32)
            MT = sb("MT", [128, 16, 128], BF16)
            xstm = sb("xstm", [128, 16, 64], F32)
            xdt = sb("xdt", [128, 16, 64], BF16)
            xw = sb("xw", [128, 16, 64], BF16)
            Btm = sb("Btm", [128, 4, 128], BF16)
            sz = sb("sz", [128, D], F32)
            y1 = sb("y1", [128, 16, 64], F32)
            y2 = sb("y2", [128, 16, 64], F32)
            yo = [sb("yo%d" % i, [128, D], BF16) for i in range(2)]
            snw = sb("snw", [128, D], F32)
            self.epsb = sb("epsb", [128, 1], F32)
            P.op("dve", "memset", writes=["epsb"], ap=self.epsb[:], constant=EPS)
            self.load_w(wa, "wa", l, A0, A_N)
            P.op("sp", "dma_start", writes=["snw"], dma="snw", out=snw[:],
                 in_=self.prow[l * 4176 + 80:l * 4176 + 80 + 1024].partition_broadcast(128))
            ps = self.ps
            for s in range(self.NSEQ):
                P.op("dve", "memset", writes=["H"], ap=H[:], constant=0.0)
                P.op("pool", "memset", writes=["Ucar%d" % b for b in range(16)], ap=Ucar[:], constant=0.0)
                for t in range(self.NTL):
                    for c4 in range(4):
                        self.h_chunk(l, s, t * 4 + c4, hT, "hT", c4 * 128)
                    for b in range(16):
                        pb = self.psb()
                        self.proj_fm(pb, hT, "hT", 512, wa, "wa", b * 128, 128)
                        ukey = "Ub%d" % (b % 2)
                        U_ = Ub[b % 2]
                        P.op("pool", "tensor_copy", reads=["Ucar%d" % b], writes=[ukey], out=U_[:, 0:3], in_=Ucar[:, b, :])
                        P.op("act", "activation", reads=self.pk(pb), writes=[ukey], out=U_[:, 3:515],
                             in_=ps[:, pb, :], func=AF.Copy)
                        a = acc[b % 2]
                        akey = "acc%d" % (b % 2)
                        P.op("dve", "tensor_scalar", reads=[ukey, "ppt"], writes=[akey], out=a[:], in0=U_[:, 0:512],
                             scalar1=self.convw(l, b, 0), scalar2=None, op0=ALU.mult)
                        for k in range(1, 4):
                            P.op("dve", "scalar_tensor_tensor", reads=[ukey, akey, "ppt"], writes=[akey], out=a[:],
                                 in0=U_[:, k:k + 512], scalar=self.convw(l, b, k), in1=a[:],
                                 op0=ALU.mult, op1=ALU.add)
                        if b < 8:
                            dst, dkey = xsT[:, b, :], "xsT"
                        elif b < 12:
                            dst, dkey = BT[:, b - 8, :], "BT"
                        else:
                            dst, dkey = CT[:, b - 12, :], "CT"
                        P.op("act", "activation", reads=[akey, "ppt"], writes=[dkey], out=dst, in_=a[:],
                             func=AF.Silu, bias=self.convb(l, b))
                        P.op("pool", "tensor_copy", reads=[ukey], writes=["Ucar%d" % b], out=Ucar[:, b, :], in_=U_[:, 512:515])
                    for c4 in range(4):
                        c = t * 4 + c4
                        cs = slice(c4 * 128, (c4 + 1) * 128)
                        pd = self.psb()
                        self.proj_tm(pd, hT, "hT", c4 * 128, wa, "wa", 3072, 16)
                        dtr, dt_, adt, acum, nacum, lastbc, dS, ea, cd, dtS, e1 = [sm[:, i, :] for i in range(11)]
                        P.op("dve", "tensor_tensor", reads=self.pk(pd) + ["prw"], writes=["sm0"], out=dtr,
                             in0=ps[:, pd, 0:16], in1=self.rowp(l, 0), op=ALU.add)
                        P.op("act", "activation", reads=["sm0"], writes=["sm10"], out=e1, in_=dtr, func=AF.Exp)
                        P.op("act", "activation", reads=["sm10"], writes=["sm1"], out=dt_, in_=e1, func=AF.Ln, bias=1.0)
                        P.op("dve", "tensor_tensor", reads=["sm1", "abc"], writes=["sm2"], out=adt, in0=dt_,
                             in1=self.abc[:, l * 16:(l + 1) * 16], op=ALU.mult)
                        pa = self.psb()
                        P.op("pe", "matmul", reads=["tri", "sm2"], writes=self.pk(pa), out=ps[:, pa, 0:16],
                             lhsT=self.tri[:], rhs=adt, start=True, stop=True)
                        P.op("dve", "tensor_copy", reads=self.pk(pa), writes=["sm3"], out=acum, in_=ps[:, pa, 0:16])
                        P.op("pe", "matmul", reads=["elast", "sm3"], writes=self.pk(pa), out=ps[:, pa, 16:32],
                             lhsT=self.elast[:], rhs=acum, start=True, stop=True)
                        P.op("dve", "tensor_copy", reads=self.pk(pa), writes=["sm5"], out=lastbc, in_=ps[:, pa, 16:32])
                        P.op("dve", "tensor_tensor", reads=["sm5", "sm3"], writes=["sm6"], out=dS, in0=lastbc, in1=acum,
                             op=ALU.subtract)
                        P.op("act", "activation", reads=["sm6"], writes=["sm6"], out=dS, in_=dS, func=AF.Exp)
                        P.op("act", "activation", reads=["sm3"], writes=["sm7"], out=ea, in_=acum, func=AF.Exp)
                        P.op("act", "activation", reads=["sm5"], writes=["sm8"], out=cd, in_=lastbc, func=AF.Exp)
                        P.op("dve", "tensor_tensor", reads=["sm1", "sm6"], writes=["sm9"], out=dtS, in0=dt_, in1=dS,
                             op=ALU.mult)
                        P.op("dve", "tensor_tensor", reads=["tri", "sm2"], writes=["rhsall"], out=rhsall[:],
                             in0=self.tri[:].unsqueeze(1).broadcast_to([128, 16, 128]),
                             in1=adt.unsqueeze(2).broadcast_to([128, 16, 128]), op=ALU.mult)
                        for g in range(4):
                            pg = self.psb()
                            P.op("pe", "matmul", reads=["onesf", "rhsall"], writes=self.pk(pg),
                                 out=ps[:, pg, :], lhsT=self.onesf[:],
                                 rhs=rhsall[:, 4 * g:4 * g + 4, :], start=True, stop=True)
                            for r in range(4):
                                h = 4 * g + r
                                P.op("dve", "tensor_scalar", reads=self.pk(pg) + ["sm3"], writes=["Eh%d" % g],
                                     out=Eh[:, h, :], in0=ps[:, pg, r * 128:(r + 1) * 128],
                                     scalar1=acum[:, h:h + 1], scalar2=0.0, op0=ALU.subtract, op1=ALU.min)
                            P.op("act", "activation", reads=["Eh%d" % g], writes=["Eh%d" % g, "dec%d" % g],
                                 out=dec[:, 4 * g:4 * g + 4, :], in_=Eh[:, 4 * g:4 * g + 4, :], func=AF.Exp)
                        pc = self.psb()
                        for g in range(4):
                            P.op("pe", "matmul", reads=["BT", "CT"], writes=self.pk(pc),
                                 out=ps[:, pc, g * 128:(g + 1) * 128], lhsT=BT[:, g, cs], rhs=CT[:, g, cs],
                                 start=True, stop=True)
                        P.op("dve", "tensor_tensor", reads=self.pk(pc) + ["mlef"], writes=["cbm"], out=cbm[:],
                             in0=ps[:, pc, :].rearrange("p (g l) -> p g l", g=4),
                             in1=self.mlef[:].unsqueeze(1).broadcast_to([128, 4, 128]), op=ALU.mult)
                        for g in range(4):
                            P.op("pool", "tensor_tensor", reads=["dec%d" % g, "Eh%d" % g, "cbm"], writes=["MT%d" % g],
                                 out=MT[:, 4 * g:4 * g + 4, :], in0=dec[:, 4 * g:4 * g + 4, :],
                                 in1=cbm[:, g, :].unsqueeze(1).broadcast_to([128, 4, 128]), op=ALU.mult)
                        px = self.psb()
                        pxb = ps[:, px, :].bitcast(BF16).rearrange("p (k t) -> p k t", k=8)
                        for b in range(8):
                            P.op("pe", "transpose", reads=["xsT", "ident"], writes=self.pk(px), out=pxb[:, b, :],
                                 in_=xsT[:, b, cs], identity=self.ident[:])
                        pxv = ps[:, px, :].bitcast(BF16).rearrange("p (h d) -> p h d", h=16)
                        P.op("act", "activation", reads=self.pk(px), writes=["xstm"], out=xstm[:], in_=pxv, func=AF.Copy)
                        P.op("dve", "tensor_tensor", reads=["xstm", "sm1"], writes=["xdt"], out=xdt[:], in0=xstm[:],
                             in1=dt_.unsqueeze(2).broadcast_to([128, 16, 64]), op=ALU.mult)
                        P.op("pool", "tensor_tensor", reads=["xstm", "sm9"], writes=["xw"], out=xw[:], in0=xstm[:],
                             in1=dtS.unsqueeze(2).broadcast_to([128, 16, 64]), op=ALU.mult)
                        pbt = self.psb()
                        pbb = ps[:, pbt, 0:256].bitcast(BF16).rearrange("p (k t) -> p k t", k=4)
                        for g in range(4):
                            P.op("pe", "transpose", reads=["BT", "ident"], writes=self.pk(pbt), out=pbb[:, g, :],
                                 in_=BT[:, g, cs], identity=self.ident[:])
                        P.op("act", "activation", reads=self.pk(pbt), writes=["Btm"], out=Btm[:], in_=pbb, func=AF.Copy)
                        P.op("act", "activation", reads=["H"], writes=["Hb"], out=Hb[:], in_=H[:], func=AF.Copy)
                        po = self.psb(2)
                        for g in range(4):
                            P.op("pe", "matmul", reads=["CT", "Hb"], writes=self.pk(po, 2),
                                 out=ps[:, po + g // 2, (g % 2) * 256:(g % 2) * 256 + 256],
                                 lhsT=CT[:, g, cs], rhs=Hb[:, 4 * g:4 * g + 4, :], start=True, stop=True)
                        poall = ps[:, po:po + 2, :].rearrange("p b (h d) -> p (b h) d", d=64)
                        P.op("dve", "tensor_tensor", reads=self.pk(po, 2) + ["sm7"], writes=["y1"], out=y1[:], in0=poall,
                             in1=ea.unsqueeze(2).broadcast_to([128, 16, 64]), op=ALU.mult)
                        pst = self.psb(2)
                        for g in range(4):
                            P.op("pe", "matmul", reads=["Btm", "xw"], writes=self.pk(pst, 2),
                                 out=ps[:, pst + g // 2, (g % 2) * 256:(g % 2) * 256 + 256],
                                 lhsT=Btm[:, g, :], rhs=xw[:, 4 * g:4 * g + 4, :], start=True, stop=True)
                        pstall = ps[:, pst:pst + 2, :].rearrange("p b (h d) -> p (b h) d", d=64)
                        P.op("dve", "tensor_tensor", reads=["H", "sm8"], writes=["Htmp"], out=Htmp[:], in0=H[:],
                             in1=cd.unsqueeze(2).broadcast_to([128, 16, 64]), op=ALU.mult)
                        P.op("dve", "tensor_tensor", reads=self.pk(pst, 2) + ["Htmp"], writes=["H"], out=H[:], in0=pstall,
                             in1=Htmp[:], op=ALU.add)
                        pyd = self.psb(2)
                        for h in range(16):
                            P.op("pe", "matmul", reads=["MT%d" % (h // 4), "xdt"], writes=self.pk(pyd, 2),
                                 out=ps[:, pyd + h // 8, (h % 8) * 64:(h % 8) * 64 + 64],
                                 lhsT=MT[:, h, :], rhs=xdt[:, h, :], start=True, stop=True)
                        pydall = ps[:, pyd:pyd + 2, :].rearrange("p b (h d) -> p (b h) d", d=64)
                        P.op("dve", "tensor_tensor", reads=self.pk(pyd, 2) + ["y1"], writes=["y1"], out=y1[:], in0=pydall,
                             in1=y1[:], op=ALU.add)
                        P.op("pool", "tensor_tensor", reads=["xstm", "prw"], writes=["y2"], out=y2[:], in0=xstm[:],
                             in1=self.rowp(l, 2).unsqueeze(2).broadcast_to([128, 16, 64]), op=ALU.mult)
                        P.op("pool", "tensor_tensor", reads=["y1", "y2"], writes=["y2"], out=y2[:], in0=y1[:], in1=y2[:],
                             op=ALU.add)
                        pz = self.psb(2)
                        for n in range(2):
                            self.proj_tm(pz + n, hT, "hT", c4 * 128, wa, "wa", 2048 + n * 512, 512)
                        P.op("act", "activation", reads=self.pk(pz, 2), writes=["sz"], out=sz[:],
                             in_=ps[:, pz:pz + 2, :].rearrange("p b n -> p (b n)"), func=AF.Silu)
                        y2f = y2[:].rearrange("p h d -> p (h d)")
                        P.op("dve", "tensor_tensor", reads=["y2", "sz"], writes=["y2"], out=y2f, in0=y2f, in1=sz[:],
                             op=ALU.mult)
                        ss = sm[:, 11, 0:4]
                        for g in range(4):
                            P.op("act", "activation", reads=["y2"], writes=["junkA", "sm11"], out=junkA[:],
                                 in_=y2f[:, g * 256:(g + 1) * 256], func=AF.Square, accum_out=sm[:, 11, g:g + 1])
                        P.op("act", "activation", reads=["sm11", "epsb"], writes=["sm11"], out=sm[:, 11, 4:8], in_=ss, func=AF.Ln,
                             scale=1.0 / 256, bias=self.epsb[:])
                        P.op("act", "activation", reads=["sm11"], writes=["sm11"], out=sm[:, 11, 8:12], in_=sm[:, 11, 4:8],
                             func=AF.Exp, scale=-0.5)
                        P.op("dve", "tensor_tensor", reads=["y2", "sm11"], writes=["y2"],
                             out=y2[:].rearrange("p (g r) d -> p g (r d)", g=4),
                             in0=y2[:].rearrange("p (g r) d -> p g (r d)", g=4),
                             in1=sm[:, 11, 8:12].unsqueeze(2).broadcast_to([128, 4, 256]), op=ALU.mult)
                        yq = yo[c % 2]
                        P.op("pool", "tensor_tensor", reads=["y2", "snw"], writes=["yo%d" % (c % 2)], out=yq[:], in0=y2f,
                             in1=snw[:], op=ALU.mult)
                        P.op("pool", "dma_start", reads=["yo%d" % (c % 2)], writes=["Y0_%d_%d" % (s, c)],
                             dma="yo%d" % (c % 2), out=self.Y[0, s, c * 128:(c + 1) * 128, :], in_=yq[:])
            P.emit()

    def phase_B(self, l):
        nc, P, S, NCH = self.nc, self.P, self.S, self.NCH
        with ExitStack() as st:
            self.uid = getattr(self, "uid", 0) + 1
            sb = lambda name, shape, dt, _u=self.uid: st.enter_context(nc.sbuf_tensor("%s_u%d" % (name, _u), shape, dt))
            wb = sb("wb", [128, KC, B_N], BF16)
            hT = sb("hT", [128, KC, 512], BF16)
            cosT = sb("cosT", [128, 512], F32)
            sinS = sb("sinS", [128, 512], F32)
            t1 = [sb("t1_%d" % i, [128, 512], F32) for i in range(2)]
            t2 = [sb("t2_%d" % i, [128, 512], F32) for i in range(2)]
            qrT = sb("qrT", [128, 8, 512], BF16)
            krT = sb("krT", [128, 2, S], BF16)
            V = sb("V", [128, NCH, 4, 65], BF16)
            sz = sb("sz", [128, D], F32)
            Pc = [sb("Pc%d" % i, [128, 512], BF16) for i in range(2)]
            Pp = [sb("Pp%d" % i, [128, 512], BF16) for i in range(2)]
            den = sb("den", [128, 16], F32)
            yf = sb("yf", [128, 16, 64], F32)
            yo = [sb("yo%d" % i, [128, D], BF16) for i in range(2)]
            self.epsb = sb("epsb", [128, 1], F32)
            P.op("dve", "memset", writes=["epsb"], ap=self.epsb[:], constant=EPS)
            self.load_w(wb, "wb", l, B0, B_N)
            P.op("pool", "memset", writes=["V%d" % i for i in range(NCH)], ap=V[:], constant=1.0)
            ps = self.ps
            for s in range(self.NSEQ):
                for t in range(self.NTL):
                    ts = slice(t * 512, (t + 1) * 512)
                    for c4 in range(4):
                        self.h_chunk(l, s, t * 4 + c4, hT, "hT", c4 * 128)
                    P.op("sp", "dma_start", writes=["cosT"], dma="cosT", out=cosT[:], in_=self.rope[0, :, ts])
                    P.op("sp", "dma_start", writes=["sinS"], dma="sinS", out=sinS[:], in_=self.rope[1, :, ts])
                    for b in range(10):
                        c0 = b * 128 if b < 8 else 1024 + (b - 8) * 128
                        c1 = 1280 + c0
                        pq = self.psb()
                        self.proj_fm(pq, hT, "hT", 512, wb, "wb", c0, 128)
                        pqs = self.psb()
                        self.proj_fm(pqs, hT, "hT", 512, wb, "wb", c1, 128)
                        i2 = b % 2
                        P.op("dve", "tensor_tensor", reads=self.pk(pq) + ["cosT"], writes=["t1_%d" % i2], out=t1[i2][:],
                             in0=ps[:, pq, :], in1=cosT[:], op=ALU.mult)
                        P.op("dve", "tensor_tensor", reads=self.pk(pqs) + ["sinS"], writes=["t2_%d" % i2], out=t2[i2][:],
                             in0=ps[:, pqs, :], in1=sinS[:], op=ALU.mult)
                        if b < 8:
                            dst, dkey = qrT[:, b, :], "qrT"
                        else:
                            dst, dkey = krT[:, b - 8, ts], "krT%d" % t
                        P.op("pool", "tensor_tensor", reads=["t1_%d" % i2, "t2_%d" % i2], writes=[dkey], out=dst,
                             in0=t1[i2][:], in1=t2[i2][:], op=ALU.add)
                    for c4 in range(4):
                        c = t * 4 + c4
                        cs = slice(c4 * 128, (c4 + 1) * 128)
                        pv = self.psb()
                        self.proj_tm(pv, hT, "hT", c4 * 128, wb, "wb", 2560, 256)
                        P.op("act", "activation", reads=self.pk(pv), writes=["V%d" % c], out=V[:, c, :, 0:64],
                             in_=ps[:, pv, 0:256].rearrange("p (h d) -> p h d", h=4), func=AF.Copy)
                        pz = self.psb(2)
                        for n in range(2):
                            self.proj_tm(pz + n, hT, "hT", c4 * 128, wb, "wb", 2816 + n * 512, 512)
                        P.op("act", "activation", reads=self.pk(pz, 2), writes=["sz"], out=sz[:],
                             in_=ps[:, pz:pz + 2, :].rearrange("p b n -> p (b n)"), func=AF.Silu)
                        pos = []
                        for kv in range(4):
                            half = slice((kv % 2) * 64, (kv % 2) * 64 + 64)
                            blk0 = (kv // 2) * 4
                            qv = qrT[half, blk0:blk0 + 4, cs]
                            i2 = kv % 2
                            psc = self.psb()
                            P.op("pe", "matmul", reads=["krT%d" % t, "qrT"], writes=self.pk(psc),
                                 out=ps[:, psc, :].rearrange("p (a q) -> p a q", a=4),
                                 lhsT=krT[half, kv // 2, c * 128:(c + 1) * 128], rhs=qv, start=True, stop=True)
                            P.op("act", "activation", reads=self.pk(psc), writes=["Pc%d" % i2], out=Pc[i2][:],
                                 in_=ps[:, psc, :], func=AF.Exp, scale=0.125)
                            P.op("pool", "tensor_tensor", reads=["Pc%d" % i2, "mle"], writes=["Pc%d" % i2],
                                 out=Pc[i2][:].rearrange("p (a q) -> p a q", a=4),
                                 in0=Pc[i2][:].rearrange("p (a q) -> p a q", a=4),
                                 in1=self.mle[:].unsqueeze(1).broadcast_to([128, 4, 128]), op=ALU.mult)
                            if c > 0:
                                psp = self.psb()
                                P.op("pe", "matmul", reads=["krT%d" % ((c - 1) // 4), "qrT"], writes=self.pk(psp),
                                     out=ps[:, psp, :].rearrange("p (a q) -> p a q", a=4),
                                     lhsT=krT[half, kv // 2, (c - 1) * 128:c * 128], rhs=qv, start=True, stop=True)
                                P.op("act", "activation", reads=self.pk(psp), writes=["Pp%d" % i2], out=Pp[i2][:],
                                     in_=ps[:, psp, :], func=AF.Exp, scale=0.125)
                                P.op("pool", "tensor_tensor", reads=["Pp%d" % i2, "mgt"], writes=["Pp%d" % i2],
                                     out=Pp[i2][:].rearrange("p (a q) -> p a q", a=4),
                                     in0=Pp[i2][:].rearrange("p (a q) -> p a q", a=4),
                                     in1=self.mgt[:].unsqueeze(1).broadcast_to([128, 4, 128]), op=ALU.mult)
                            po = self.psb()
                            pos.append(po)
                            for a in range(4):
                                if c > 0:
                                    P.op("pe", "matmul", reads=["Pp%d" % i2, "V%d" % (c - 1)], writes=self.pk(po),
                                         out=ps[:, po, a * 65:(a + 1) * 65], lhsT=Pp[i2][:, a * 128:(a + 1) * 128],
                                         rhs=V[:, c - 1, kv, :], start=True, stop=False)
                                P.op("pe", "matmul", reads=["Pc%d" % i2, "V%d" % c], writes=self.pk(po),
                                     out=ps[:, po, a * 65:(a + 1) * 65], lhsT=Pc[i2][:, a * 128:(a + 1) * 128],
                                     rhs=V[:, c, kv, :], start=(c == 0), stop=True)
                            pov = ps[:, po, 0:260].rearrange("p (a e) -> p a e", a=4)
                            P.op("dve", "tensor_tensor", reads=self.pk(po) + ["esink"], writes=["den%d" % kv],
                                 out=den[:, 4 * kv:4 * kv + 4].unsqueeze(2), in0=pov[:, :, 64:65],
                                 in1=self.esink[:, l * 16 + 4 * kv:l * 16 + 4 * kv + 4].unsqueeze(2), op=ALU.add)
                            P.op("dve", "reciprocal", reads=["den%d" % kv], writes=["den%d" % kv],
                                 out=den[:, 4 * kv:4 * kv + 4], in_=den[:, 4 * kv:4 * kv + 4])
                            P.op("dve", "tensor_tensor", reads=self.pk(po) + ["den%d" % kv], writes=["yf%d" % kv],
                                 out=yf[:, 4 * kv:4 * kv + 4, :], in0=pov[:, :, 0:64],
                                 in1=den[:, 4 * kv:4 * kv + 4].unsqueeze(2).broadcast_to([128, 4, 64]), op=ALU.mult)
                        yq = yo[c % 2]
                        P.op("pool", "tensor_tensor", reads=["yf%d" % k for k in range(4)] + ["sz"],
                             writes=["yo%d" % (c % 2)], out=yq[:], in0=yf[:].rearrange("p h d -> p (h d)"), in1=sz[:],
                             op=ALU.mult)
                        P.op("pool", "dma_start", reads=["yo%d" % (c % 2)], writes=["Y1_%d_%d" % (s, c)],
                             dma="yo%d" % (c % 2), out=self.Y[1, s, c * 128:(c + 1) * 128, :], in_=yq[:])
            P.emit()

    def phase_C(self, l, hg):
        nc, P, S, NCH = self.nc, self.P, self.S, self.NCH
        with ExitStack() as st:
            self.uid = getattr(self, "uid", 0) + 1
            sb = lambda name, shape, dt, _u=self.uid: st.enter_context(nc.sbuf_tensor("%s_u%d" % (name, _u), shape, dt))
            wc = sb("wc", [128, KC, C_G], BF16)
            hT = sb("hT", [128, KC, 512], BF16)
            KT = sb("KT", [72, 8, S], BF16)
            V = sb("V", [128, NCH, 8, 65], BF16)
            QT = sb("QT", [72, 8, 512], BF16)
            NC_ = sb("NC", [128, NCH, 8], F32)
            fsm = sb("fsm", [128, 4, 8], F32)
            cumT = sb("cumT", [8, 512], BF16)
            szT = sb("szT", [64, 8, 512], F32)
            PT = [sb("PT%d" % i, [128, 512], BF16) for i in range(4)]
            rd = [sb("rd%d" % i, [65, 512], BF16) for i in range(2)]
            rdf = [sb("rdf%d" % i, [65, 512], F32) for i in range(2)]
            yn = [sb("yn%d" % i, [64, 512], F32) for i in range(2)]
            yo = [sb("yo0", [64, 8, 512], BF16)] * 2
            self.epsb = sb("epsb", [128, 1], F32)
            P.op("dve", "memset", writes=["epsb"], ap=self.epsb[:], constant=EPS)
            self.load_w(wc, "wc", l, C0 + hg * C_G, C_G)
            P.op("pool", "memset", writes=["V%d" % i for i in range(NCH)], ap=V[:], constant=1.0)
            P.op("pool", "memset", writes=["KT%d" % i for i in range(self.NTL)], ap=KT[64:72, :, :], constant=1.0)
            P.op("pool", "affine_select", reads=["KT%d" % i for i in range(self.NTL)],
                 writes=["KT%d" % i for i in range(self.NTL)], out=KT[64:72, :, :], in_=KT[64:72, :, :],
                 pattern=[[1, 8], [0, S]], compare_op=ALU.is_equal, fill=0.0, base=0, channel_multiplier=-1)
            ps = self.ps
            self.pools = {"gen": [0, 1], "ct": [2], "acc": [2, 3, 4], "sc": [5, 6, 7]}
            self.prr = {}
            fb = self.rowp(l, 4)[:, hg * 8:hg * 8 + 8]
            pti = 0
            for s in range(self.NSEQ):
                for t in range(self.NTL):
                    ts = slice(t * 512, (t + 1) * 512)
                    for c4 in range(4):
                        self.h_chunk(l, s, t * 4 + c4, hT, "hT", c4 * 128)
                    for hp in range(4):
                        pkb = self.psb()
                        self.proj_fm(pkb, hT, "hT", 512, wc, "wc", 512 + hp * 128, 128)
                        for e in range(2):
                            P.op("dve", "tensor_copy", reads=self.pk(pkb), writes=["KT%d" % t], out=KT[0:64, 2 * hp + e, ts],
                                 in_=ps[64 * e:64 * e + 64, pkb, :])
                    pct = self.psb(pool="ct")
                    for c4 in range(4):
                        c = t * 4 + c4
                        pf = self.psb()
                        self.proj_tm(pf, hT, "hT", c4 * 128, wc, "wc", 2048, 8)
                        f0, f1, f2 = fsm[:, 0, :], fsm[:, 1, :], fsm[:, 2, :]
                        P.op("dve", "tensor_tensor", reads=self.pk(pf) + ["prw"], writes=["f0"], out=f0,
                             in0=ps[:, pf, 0:8], in1=fb, op=ALU.add)
                        P.op("act", "activation", reads=["f0"], writes=["f1"], out=f1, in_=f0, func=AF.Exp, scale=-1.0)
                        P.op("act", "activation", reads=["f1"], writes=["f2"], out=f2, in_=f1, func=AF.Ln, bias=1.0)
                        P.op("pe", "matmul", reads=["tri", "f2"], writes=self.pk(pf), out=ps[:, pf, 8:16],
                             lhsT=self.tri[:], rhs=f2, start=True, stop=(c == 0))
                        if c > 0:
                            P.op("pe", "matmul", reads=["elast", "NC%d" % (c - 1)], writes=self.pk(pf), out=ps[:, pf, 8:16],
                                 lhsT=self.elast[:], rhs=NC_[:, c - 1, :], start=False, stop=True)
                        P.op("dve", "tensor_copy", reads=self.pk(pf), writes=["NC%d" % c], out=NC_[:, c, :], in_=ps[:, pf, 8:16])
                        P.op("pe", "transpose", reads=["NC%d" % c, "identf"], writes=self.pk(pct),
                             out=ps[0:8, pct, c4 * 128:(c4 + 1) * 128], in_=NC_[:, c, :], identity=self.identf[:])
                        pv = self.psb()
                        self.proj_tm(pv, hT, "hT", c4 * 128, wc, "wc", 1024, 512)
                        P.op("act", "activation", reads=self.pk(pv), writes=["V%d" % c], out=V[:, c, :, 0:64],
                             in_=ps[:, pv, :].rearrange("p (h d) -> p h d", h=8), func=AF.Copy)
                    P.op("dve", "tensor_scalar", reads=self.pk(pct), writes=["QT%d" % h for h in range(8)],
                         out=QT[64:72, :, :], in0=ps[0:8, pct, :].unsqueeze(1).broadcast_to([8, 8, 512]),
                         scalar1=-8.0, scalar2=None, op0=ALU.mult)
                    yq = yo[0]
                    ykey = "yo0"
                    for hp in range(4):
                        pz = self.psb()
                        self.proj_fm(pz, hT, "hT", 512, wc, "wc", 1536 + hp * 128, 128)
                        for e in range(2):
                            P.op("act", "activation", reads=self.pk(pz), writes=["szT%d" % (2 * hp + e)],
                                 out=szT[:, 2 * hp + e, :], in_=ps[64 * e:64 * e + 64, pz, :], func=AF.Silu)
                    for hp in range(4):
                        pqb = self.psb()
                        self.proj_fm(pqb, hT, "hT", 512, wc, "wc", hp * 128, 128)
                        for e in range(2):
                            P.op("dve", "tensor_copy", reads=self.pk(pqb), writes=["QT%d" % (2 * hp + e)],
                                 out=QT[0:64, 2 * hp + e, :], in_=ps[64 * e:64 * e + 64, pqb, :])
                    nkb = 4 * t + 4

                    def qk(h, j):
                        q0 = max(j - 4 * t, 0)
                        nq = 4 - q0
                        psc = self.psb(pool="sc")
                        P.op("pe", "matmul", reads=["KT%d" % (j // 4), "QT%d" % h], writes=self.pk(psc),
                             out=ps[:, psc, 0:nq * 128], lhsT=KT[:, h, j * 128:(j + 1) * 128],
                             rhs=QT[:, h, q0 * 128:512], start=True, stop=True)
                        return psc

                    seq = [(2 * hp + e, j) for hp in range(4) for j in range(nkb) for e in range(2)]
                    DIST = 2
                    pend = [qk(*seq[i]) for i in range(min(DIST, len(seq)))]
                    pos = {}
                    for i, (h, j) in enumerate(seq):
                        psc = pend.pop(0)
                        if i + DIST < len(seq):
                            pend.append(qk(*seq[i + DIST]))
                        if j == 0:
                            pos[h] = self.psb(pool="acc")
                        po = pos[h]
                        jj = j - 4 * t
                        q0 = max(jj, 0)
                        nq = 4 - q0
                        pt = PT[pti % 4]
                        ptk = "PT%d" % (pti % 4)
                        pti += 1
                        P.op("act", "activation", reads=self.pk(psc) + ["NC%d" % j], writes=[ptk], out=pt[:, 0:nq * 128],
                             in_=ps[:, psc, 0:nq * 128], func=AF.Exp, scale=0.125, bias=NC_[:, j, h:h + 1])
                        if jj >= 0:
                            P.op("pool", "tensor_tensor", reads=[ptk, "mle"], writes=[ptk], out=pt[:, 0:128],
                                 in0=pt[:, 0:128], in1=self.mle[:], op=ALU.mult)
                        P.op("pe", "matmul", reads=[ptk, "V%d" % j], writes=self.pk(po), out=ps[0:65, po, q0 * 128:512],
                             lhsT=V[:, j, h, :], rhs=pt[:, 0:nq * 128], start=(j == 0), stop=(j == nkb - 1))
                        if j == nkb - 1:
                            r_, y_ = rd[h % 2], yn[h % 2]
                            rk, yk = "rd%d" % (h % 2), "yn%d" % (h % 2)
                            rf_ = rdf[h % 2]
                            P.op("act", "activation", reads=self.pk(po), writes=[rk + "f"], out=rf_[64:65, :],
                                 in_=ps[64:65, po, :], func=AF.Ln)
                            P.op("act", "activation", reads=[rk + "f"], writes=[rk], out=r_[64:65, :],
                                 in_=rf_[64:65, :], func=AF.Exp, scale=-1.0)
                            pbc = self.psb()
                            P.op("pe", "matmul", reads=[rk, "onesb"], writes=self.pk(pbc), out=ps[0:64, pbc, :],
                                 lhsT=self.onesb[64:65, 0:64], rhs=r_[64:65, :], start=True, stop=True)
                            P.op("dve", "tensor_tensor", reads=self.pk(pbc) + ["szT%d" % h], writes=[yk], out=y_[:],
                                 in0=ps[0:64, pbc, :], in1=szT[:, h, :], op=ALU.mult)
                            P.op("dve", "tensor_tensor", reads=self.pk(po) + [yk], writes=[ykey], out=yq[:, h, :],
                                 in0=ps[0:64, po, :], in1=y_[:], op=ALU.mult)
                    P.op("pool", "dma_start", reads=[ykey], writes=["Y2T_%d_%d_%d" % (s, t, hg)], dma=ykey,
                         out=self.Y2T[s, hg * 512:(hg + 1) * 512, ts].rearrange("(h d) t -> d h t", d=64), in_=yq[:])
            P.emit()
            self.pools = {"gen": list(range(8))}
            self.prr = {}

    def phase_D(self, l):
        nc, P, S = self.nc, self.P, self.S
        last = (l == self.layers - 1)
        with ExitStack() as st:
            self.uid = getattr(self, "uid", 0) + 1
            sb = lambda name, shape, dt, _u=self.uid: st.enter_context(nc.sbuf_tensor("%s_u%d" % (name, _u), shape, dt))
            wg = sb("wg", [128, KC, G_N], BF16)
            wp = [sb("wp%d" % i, [128, KC, D], BF16) for i in range(3)]
            wo = sb("wo", [128, KC, D], BF16)
            hT = sb("hT", [128, KC, 128], BF16)
            gbb = sb("gbb", [128, 3 * D], F32)
            fnw = sb("fnw", [128, D], F32)
            yt = [[sb("yt%d_%d" % (b, i), [128, D], BF16) for i in range(2)] for b in range(2)]
            ybTc = [sb("ybTc%d" % i, [128, KC, 128], BF16) for i in range(2)]
            ybT = sb("ybT", [128, KC, 128], BF16)
            gs = sb("gs", [128, D], F32)
            tmp = sb("tmp", [128, D], F32)
            mg = sb("mg", [128, D], F32)
            mgb = sb("mgb", [128, D], BF16)
            mT = sb("mT", [128, KC, 128], BF16)
            xo = [sb("xo%d" % i, [128, D], F32) for i in range(2)]
            fs = sb("fs", [128, 4], F32)
            self.epsb = sb("epsb", [128, 1], F32)
            P.op("dve", "memset", writes=["epsb"], ap=self.epsb[:], constant=EPS)
            self.load_w(wg, "wg", l, G0, G_N)
            for b in range(3):
                self.load_w(wp[b], "wp%d" % b, l, 0, D, src=self.wproj[l, b])
            self.load_w(wo, "wo", l, 0, D, src=self.wout[l])
            P.op("sp", "dma_start", writes=["gbb"], dma="gbb", out=gbb[:],
                 in_=self.prow[l * 4176 + 1104:l * 4176 + 1104 + 3072].partition_broadcast(128))
            P.op("sp", "dma_start", writes=["fnw"], dma="fnw", out=fnw[:],
                 in_=self.prow[DEPTH * 4176:DEPTH * 4176 + 1024].partition_broadcast(128))
            ps = self.ps
            for s in range(self.NSEQ):
                for c in range(self.NCH):
                    i2 = c % 2
                    for b in range(2):
                        rk = ["Y%d_%d_%d" % (b, s, c)]
                        P.op("sp", "dma_start", reads=rk, writes=["yt%d_%d" % (b, i2)], dma="yt%d_%d" % (b, i2),
                             out=yt[b][i2][:], in_=self.Y[b, s, c * 128:(c + 1) * 128, :])
                    P.op("sp", "dma_start", reads=["Y2T_%d_%d_%d" % (s, c // 4, g) for g in range(2)],
                         writes=["ybTc%d" % i2], dma="ybTc%d" % i2, out=ybTc[i2][:],
                         in_=self.Y2T[s, :, c * 128:(c + 1) * 128].rearrange("(kc p) t -> p kc t", p=128))
                    sl = self.h_chunk(l, s, c, hT, "hT", 0)
                    for b in range(3):
                        pg = self.psb(2)
                        for n in range(2):
                            self.proj_tm(pg + n, hT, "hT", 0, wg, "wg", b * D + n * 512, 512)
                        P.op("dve", "tensor_tensor", reads=self.pk(pg, 2) + ["gbb"], writes=["gs"], out=gs[:],
                             in0=ps[:, pg:pg + 2, :].rearrange("p b n -> p (b n)"), in1=gbb[:, b * D:(b + 1) * D],
                             op=ALU.add)
                        P.op("act", "activation", reads=["gs"], writes=["gs"], out=gs[:], in_=gs[:], func=AF.Sigmoid)
                        if b < 2:
                            pt_ = self.psb()
                            ptb = ps[:, pt_, :].bitcast(BF16).rearrange("p (k t) -> p k t", k=8)
                            for kc in range(KC):
                                P.op("pe", "transpose", reads=["yt%d_%d" % (b, i2), "ident"], writes=self.pk(pt_),
                                     out=ptb[:, kc, :], in_=yt[b][i2][:, kc * 128:(kc + 1) * 128], identity=self.ident[:])
                            P.op("act", "activation", reads=self.pk(pt_), writes=["ybT"], out=ybT[:], in_=ptb, func=AF.Copy)
                            ysrc, ykey_ = ybT, "ybT"
                        else:
                            ysrc, ykey_ = ybTc[i2], "ybTc%d" % i2
                        pb = self.psb(2)
                        for n in range(2):
                            for kc in range(KC):
                                P.op("pe", "matmul", reads=[ykey_, "wp%d" % b], writes=self.pk(pb + n),
                                     out=ps[:, pb + n, :], lhsT=ysrc[:, kc, :], rhs=wp[b][:, kc, n * 512:(n + 1) * 512],
                                     start=(kc == 0), stop=(kc == KC - 1))
                        pball = ps[:, pb:pb + 2, :].rearrange("p b n -> p (b n)")
                        if b == 0:
                            P.op("dve", "tensor_tensor", reads=self.pk(pb, 2) + ["gs"], writes=["mg"], out=mg[:], in0=pball,
                                 in1=gs[:], op=ALU.mult)
                        else:
                            P.op("dve", "tensor_tensor", reads=self.pk(pb, 2) + ["gs"], writes=["tmp"], out=tmp[:],
                                 in0=pball, in1=gs[:], op=ALU.mult)
                            P.op("pool", "tensor_tensor", reads=["tmp", "mg"], writes=["mg"], out=mg[:], in0=tmp[:],
                                 in1=mg[:], op=ALU.add)
                    P.op("act", "activation", reads=["mg"], writes=["mgb"], out=mgb[:], in_=mg[:], func=AF.Copy)
                    pt_ = self.psb()
                    ptb = ps[:, pt_, :].bitcast(BF16).rearrange("p (k t) -> p k t", k=8)
                    for kc in range(KC):
                        P.op("pe", "transpose", reads=["mgb", "ident"], writes=self.pk(pt_), out=ptb[:, kc, :],
                             in_=mgb[:, kc * 128:(kc + 1) * 128], identity=self.ident[:])
                    P.op("dve", "tensor_copy", reads=self.pk(pt_), writes=["mT"], out=mT[:], in_=ptb)
                    po = self.psb(2)
                    for n in range(2):
                        for kc in range(KC):
                            P.op("pe", "matmul", reads=["mT", "wo"], writes=self.pk(po + n), out=ps[:, po + n, :],
                                 lhsT=mT[:, kc, :], rhs=wo[:, kc, n * 512:(n + 1) * 512], start=(kc == 0),
                                 stop=(kc == KC - 1))
                    xq = xo[i2]
                    P.op("dve", "tensor_tensor", reads=self.pk(po, 2) + ["xt%d" % sl], writes=["xo%d" % i2], out=xq[:],
                         in0=ps[:, po:po + 2, :].rearrange("p b n -> p (b n)"), in1=self.xt[sl][:], op=ALU.add)
                    if not last:
                        P.op("pool", "dma_start", reads=["xo%d" % i2], writes=["x1_%d_%d" % (s, c)], dma="xo%d" % i2,
                             out=self.X1[s, c * 128:(c + 1) * 128, :], in_=xq[:])
                    else:
                        P.op("act", "activation", reads=["xo%d" % i2], writes=["junkD", "fs"], out=self.junk[0][:],
                             in_=xq[:], func=AF.Square, accum_out=fs[:, 0:1])
                        P.op("act", "activation", reads=["fs", "epsb"], writes=["fs"], out=fs[:, 1:2], in_=fs[:, 0:1], func=AF.Ln,
                             scale=1.0 / D, bias=self.epsb[:])
                        P.op("act", "activation", reads=["fs"], writes=["fs"], out=fs[:, 2:3], in_=fs[:, 1:2], func=AF.Exp,
                             scale=-0.5)
                        P.op("dve", "scalar_tensor_tensor", reads=["xo%d" % i2, "fs", "fnw"], writes=["xo%d" % i2],
                             out=xq[:], in0=xq[:], scalar=fs[:, 2:3], in1=fnw[:], op0=ALU.mult, op1=ALU.mult)
                        P.op("pool", "dma_start", reads=["xo%d" % i2], writes=["out_%d_%d" % (s, c)], dma="xo%d" % i2,
                             out=self.out[s, c * 128:(c + 1) * 128, :], in_=xq[:])
            P.emit()


def host_layout(S, norm_w, w_in, conv_w, conv_b, dt_bias, a_log, d_skip, ssm_norm_w, sinks, f_bias, gate_bias,
                w_proj, w_out, final_norm_w):
    L = DEPTH
    cols = _col_order()
    win = np.ascontiguousarray(
        np.asarray(w_in, np.float32)[:, :, cols].reshape(L, KC, 128, NT).transpose(0, 2, 1, 3))
    wproj = np.ascontiguousarray(np.asarray(w_proj, np.float32).reshape(L, 3, KC, 128, D).transpose(0, 1, 3, 2, 4))
    wout = np.ascontiguousarray(np.asarray(w_out, np.float32).reshape(L, KC, 128, D).transpose(0, 2, 1, 3))
    ppart = np.zeros((128, L * 88), np.float32)
    prow = np.zeros((L * 4176 + 1024,), np.float32)
    for l in range(L):
        ppart[:, l * 88:l * 88 + 8] = np.asarray(norm_w[l]).reshape(KC, 128).T
        cw = np.asarray(conv_w[l]).reshape(4, 16, 128)
        ppart[:, l * 88 + 8:l * 88 + 72] = cw.transpose(2, 1, 0).reshape(128, 64)
        ppart[:, l * 88 + 72:l * 88 + 88] = np.asarray(conv_b[l]).reshape(16, 128).T
        o = l * 4176
        prow[o:o + 16] = dt_bias[l]
        prow[o + 16:o + 32] = a_log[l]
        prow[o + 32:o + 48] = d_skip[l]
        prow[o + 48:o + 64] = sinks[l]
        prow[o + 64:o + 80] = f_bias[l]
        prow[o + 80:o + 1104] = ssm_norm_w[l]
        prow[o + 1104:o + 4176] = np.asarray(gate_bias[l]).reshape(-1)
    prow[L * 4176:] = final_norm_w
    pos = np.arange(S, dtype=np.float32)
    inv = (np.float32(10000.0) ** (-np.arange(0, 64, 2, dtype=np.float32) / np.float32(64))).astype(np.float32)
    ang = (pos[:, None] * inv[None, :]).astype(np.float32)
    cos, sin = np.cos(ang).astype(np.float32), np.sin(ang).astype(np.float32)
    rope = np.zeros((2, 128, S), np.float32)
    for p in range(128):
        rope[0, p] = cos[:, p % 32]
        rope[1, p] = sin[:, p % 32] * (-1.0 if (p % 64) < 32 else 1.0)
    return dict(win=win, wproj=wproj, wout=wout, ppart=ppart, prow=prow, rope=rope)


_NC_CACHE = {}


def kernel(x, norm_w, w_in, conv_w, conv_b, dt_bias, a_log, d_skip, ssm_norm_w, sinks, f_bias, gate_bias,
           w_proj, w_out, final_norm_w):
    x = np.asarray(x, np.float32)
    B, S, _ = x.shape
    nseq = B // NCORES
    shared = host_layout(S, norm_w, w_in, conv_w, conv_b, dt_bias, a_log, d_skip, ssm_norm_w, sinks, f_bias,
                         gate_bias, w_proj, w_out, final_norm_w)
    key = (S, nseq)
    if key not in _NC_CACHE:
        _NC_CACHE[key] = Builder(S, nseq).build()
    nc = _NC_CACHE[key]
    in_maps = []
    for c in range(NCORES):
        m = dict(shared)
        m["x"] = np.ascontiguousarray(x[c * nseq:(c + 1) * nseq])
        in_maps.append(m)
    res = run_bass_kernel_spmd(nc, in_maps, core_ids=list(range(NCORES)))
    return np.concatenate([r["out"] for r in res.results], axis=0).astype(np.float32)
```

```python
import math
from contextlib import ExitStack

import numpy as np
import concourse.bass as bass
import concourse.mybir as mybir
from concourse.bass_utils import run_bass_kernel_spmd

F32 = mybir.dt.float32
BF16 = mybir.dt.bfloat16
AF = mybir.ActivationFunctionType
ALU = mybir.AluOpType

D = 1024
KC = 8
DEPTH = 2
NCORES = 8
EPS = 1e-6
ENGS = ("pe", "act", "dve", "pool", "sp")

A0, A_N = 0, 3088
B0, B_N = 3088, 3840
C0, C_G = 6928, 2056
G0, G_N = 6928 + 2 * 2056, 3072
NT = G0 + G_N
SWA_QORDER = [0, 4, 1, 5, 2, 6, 3, 7, 8, 12, 9, 13, 10, 14, 11, 15]


def _col_order():
    o = {}
    off = 0
    names = [("a_xbc", 2048), ("a_z", 1024), ("a_dt", 16), ("b_q", 1024), ("b_k", 256), ("b_v", 256),
             ("b_z", 1024), ("c_q", 1024), ("c_k", 1024), ("c_v", 1024), ("c_f", 16), ("c_z", 1024),
             ("gates", 3072)]
    for n, s in names:
        o[n] = off
        off += s
    cols = []
    cols += list(range(o["a_xbc"], o["a_xbc"] + 2048))
    cols += list(range(o["a_z"], o["a_z"] + 1024))
    cols += list(range(o["a_dt"], o["a_dt"] + 16))
    assert len(cols) == A_N
    q = [o["b_q"] + h * 64 + d for h in SWA_QORDER for d in range(64)]
    qs = [o["b_q"] + h * 64 + (d + 32) % 64 for h in SWA_QORDER for d in range(64)]
    k = [o["b_k"] + h * 64 + d for h in range(4) for d in range(64)]
    ks = [o["b_k"] + h * 64 + (d + 32) % 64 for h in range(4) for d in range(64)]
    cols += q + k + qs + ks
    cols += list(range(o["b_v"], o["b_v"] + 256))
    cols += list(range(o["b_z"], o["b_z"] + 1024))
    assert len(cols) == B0 + B_N
    for hg in range(2):
        for nm in ("c_q", "c_k", "c_v", "c_z"):
            cols += list(range(o[nm] + hg * 512, o[nm] + hg * 512 + 512))
        cols += list(range(o["c_f"] + hg * 8, o["c_f"] + hg * 8 + 8))
    assert len(cols) == G0
    cols += list(range(o["gates"], o["gates"] + 3072))
    assert len(cols) == NT
    return np.array(cols, dtype=np.int64)


class Prog:
    def __init__(self, nc, stack, same_engine_sync=True):
        self.nc = nc
        self.stack = stack
        self.same = same_engine_sync
        self.ops = []
        self.last_w = {}
        self.readers = {}
        self.eng_sem = {e: stack.enter_context(nc.semaphore("s_" + e)) for e in ENGS}
        self.cnt = {e: 0 for e in ENGS}
        self.dsem = {}
        self.dcnt = {}
        self.known = {e: {} for e in ENGS}
        self.done_ops = 0

    def op(self, eng, meth, reads=(), writes=(), dma=None, **kw):
        idx = len(self.ops)
        deps = set()
        for k in reads:
            if k in self.last_w:
                deps.add(self.last_w[k])
        for k in writes:
            if k in self.last_w:
                deps.add(self.last_w[k])
            for r in self.readers.get(k, ()):
                deps.add(r)
        deps.discard(idx)
        best = {}
        for d in deps:
            od = self.ops[d]
            kk = ("d", od["dma"]) if od["dma"] is not None else ("e", od["eng"])
            if kk not in best or best[kk] < d:
                best[kk] = d
        deps = set(best.values())
        for k in reads:
            self.readers.setdefault(k, []).append(idx)
        for k in writes:
            self.last_w[k] = idx
            self.readers[k] = []
        self.ops.append(dict(eng=eng, meth=meth, kw=kw, deps=deps, dma=dma, sig=False, ev=None))
        return idx

    def emit(self, final=False):
        nc, ops = self.nc, self.ops
        import os as _os
        if _os.environ.get("OPS_LIMIT"):
            del ops[int(_os.environ["OPS_LIMIT"]):]
        lo = self.done_ops
        new = range(lo, len(ops))
        for i in new:
            o = ops[i]
            if o["dma"] is not None:
                o["sig"] = True
            for d in o["deps"]:
                od = ops[d]
                if d < lo:
                    continue
                if od["dma"] is not None or od["eng"] != o["eng"] or o["dma"] is not None:
                    od["sig"] = True
                elif self.same and o["eng"] != "pe":
                    od["sig"] = True
        for i in new:
            o = ops[i]
            if o["dma"] is not None:
                k = o["dma"]
                if k not in self.dsem:
                    self.dsem[k] = self.stack.enter_context(nc.semaphore("d_" + str(k)))
                    self.dcnt[k] = 0
                self.dcnt[k] += 16
                o["ev"] = (self.dsem[k], self.dcnt[k], "d_" + str(k))
            elif o["sig"]:
                self.cnt[o["eng"]] += 1
                o["ev"] = (self.eng_sem[o["eng"]], self.cnt[o["eng"]], o["eng"])
        per_eng = {e: [] for e in ENGS}
        for i in new:
            per_eng[ops[i]["eng"]].append(i)
        same = self.same

        def body(ename):
            def f(eng):
                kn = self.known[ename]
                for i in per_eng[ename]:
                    o = ops[i]
                    need = {}
                    for d in o["deps"]:
                        if d < lo:
                            continue
                        od = ops[d]
                        ev = od["ev"]
                        if ev is None:
                            continue
                        sem, val, name = ev
                        if (od["dma"] is None and od["eng"] == ename and o["dma"] is None
                                and (ename == "pe" or not same)):
                            continue
                        if kn.get(name, 0) >= val:
                            continue
                        if name not in need or need[name][1] < val:
                            need[name] = (sem, val)
                    for name, (sem, val) in need.items():
                        eng.wait_ge(sem, val)
                        kn[name] = val
                    ins = getattr(eng, o["meth"])(**o["kw"])
                    if o["ev"] is not None:
                        sem, val, name = o["ev"]
                        ins.then_inc(sem, 16 if o["dma"] is not None else 1)
                if ename == "sp":
                    for k, s in self.dsem.items():
                        if kn.get("d_" + str(k), 0) < self.dcnt[k]:
                            eng.wait_ge(s, self.dcnt[k])
                            kn["d_" + str(k)] = self.dcnt[k]
            return f

        with nc.Block() as block:
            block.tensor(body("pe"))
            block.scalar(body("act"))
            block.vector(body("dve"))
            block.gpsimd(body("pool"))
            block.sync(body("sp"))
        self.done_ops = len(ops)


class Builder:
    def __init__(self, S, NSEQ, debug=False, layers=DEPTH, phases="ABCD"):
        self.S, self.NSEQ, self.debug, self.layers, self.phases = S, NSEQ, debug, layers, phases
        self.NCH = S // 128
        self.NTL = S // 512
        nc = self.nc = bass.Bass("TRN2", target_bir_lowering=False)
        L = DEPTH
        okind = "ExternalOutput" if debug else "Internal"
        self.x = nc.dram_tensor("x", [NSEQ, S, D], F32, kind="ExternalInput").ap()
        self.win = nc.dram_tensor("win", [L, 128, KC, NT], F32, kind="ExternalInput").ap()
        self.wproj = nc.dram_tensor("wproj", [L, 3, 128, KC, D], F32, kind="ExternalInput").ap()
        self.wout = nc.dram_tensor("wout", [L, 128, KC, D], F32, kind="ExternalInput").ap()
        self.ppart = nc.dram_tensor("ppart", [128, L * (8 + 64 + 16)], F32, kind="ExternalInput").ap()
        self.prow = nc.dram_tensor("prow", [L * (80 + 1024 + 3072) + 1024], F32, kind="ExternalInput").ap()
        self.rope = nc.dram_tensor("rope", [2, 128, S], F32, kind="ExternalInput").ap()
        self.out = nc.dram_tensor("out", [NSEQ, S, D], F32, kind="ExternalOutput").ap()
        self.Y = nc.dram_tensor("ybr", [3, NSEQ, S, D], BF16, kind=okind).ap()
        self.X1 = nc.dram_tensor("x1", [NSEQ, S, D], F32, kind=okind).ap()
        self.Y2T = nc.dram_tensor("y2t", [NSEQ, D, S], BF16, kind=okind).ap()
        self.pools = {"gen": list(range(8))}
        self.prr = {}

    def psb(self, n=1, pool="gen"):
        banks = self.pools[pool]
        r = self.prr.get(pool, 0)
        if n == 2:
            assert len(banks) % 2 == 0
            if r % 2:
                r += 1
            b = banks[r % len(banks)]
            self.prr[pool] = r + 2
            return b
        b = banks[r % len(banks)]
        self.prr[pool] = r + 1
        return b

    def pk(self, b, n=1):
        return ["ps%d" % (b + i) for i in range(n)]

    def build(self):
        nc = self.nc
        with ExitStack() as gst:
            self.P = P = Prog(nc, gst)
            sb = lambda name, shape, dt: gst.enter_context(nc.sbuf_tensor(name, shape, dt))
            self.ps = gst.enter_context(nc.psum_tensor("ps", [128, 8, 512], F32))
            self.ident = sb("ident", [128, 128], BF16)
            self.identf = sb("identf", [128, 128], F32)
            self.tri = sb("tri", [128, 128], F32)
            self.elast = sb("elast", [128, 128], F32)
            self.onesf = sb("onesf", [128, 128], F32)
            self.onesb = sb("onesb", [128, 64], BF16)
            self.mle = sb("mle", [128, 128], BF16)
            self.mgt = sb("mgt", [128, 128], BF16)
            self.mlef = sb("mlef", [128, 128], F32)
            self.sel = sb("sel", [8, 8, 65], BF16)
            self.ppt = sb("ppt", [128, DEPTH * 88], F32)
            self.prw = sb("prw", [128, DEPTH * 80], F32)
            self.abc = sb("abc", [128, DEPTH * 16], F32)
            self.esink = sb("esink", [128, DEPTH * 16], F32)
            self.xt = [sb("xt%d" % i, [128, D], F32) for i in range(2)]
            self.xn = [sb("xn%d" % i, [128, D], BF16) for i in range(2)]
            self.junk = [sb("junk%d" % i, [128, D], BF16) for i in range(2)]
            self.st4 = [sb("st4_%d" % i, [128, 4], F32) for i in range(2)]
            self.xslot = 0
            self.setup_consts()
            import os as _os
            if _os.environ.get("SETUP_LIMIT"):
                lim = int(_os.environ["SETUP_LIMIT"])
                del P.ops[lim:]
            P.emit()
            for l in range(self.layers):
                if "A" in self.phases:
                    self.phase_A(l)
                if "B" in self.phases:
                    self.phase_B(l)
                if "C" in self.phases:
                    for hg in range(2):
                        self.phase_C(l, hg)
                if "D" in self.phases:
                    self.phase_D(l)
        return nc

    def setup_consts(self):
        P = self.P
        P.op("pool", "memset", writes=["ident"], ap=self.ident[:], constant=1.0)
        P.op("pool", "affine_select", reads=["ident"], writes=["ident"], out=self.ident[:], in_=self.ident[:],
             pattern=[[-1, 128]], compare_op=ALU.is_equal, fill=0.0, base=0, channel_multiplier=1)
        P.op("pool", "memset", writes=["identf"], ap=self.identf[:], constant=1.0)
        P.op("pool", "affine_select", reads=["identf"], writes=["identf"], out=self.identf[:], in_=self.identf[:],
             pattern=[[-1, 128]], compare_op=ALU.is_equal, fill=0.0, base=0, channel_multiplier=1)
        P.op("pool", "memset", writes=["tri"], ap=self.tri[:], constant=1.0)
        P.op("pool", "affine_select", reads=["tri"], writes=["tri"], out=self.tri[:], in_=self.tri[:],
             pattern=[[1, 128]], compare_op=ALU.is_ge, fill=0.0, base=0, channel_multiplier=-1)
        P.op("pool", "memset", writes=["mlef"], ap=self.mlef[:], constant=1.0)
        P.op("pool", "affine_select", reads=["mlef"], writes=["mlef"], out=self.mlef[:], in_=self.mlef[:],
             pattern=[[1, 128]], compare_op=ALU.is_ge, fill=0.0, base=0, channel_multiplier=-1)
        P.op("pool", "memset", writes=["mle"], ap=self.mle[:], constant=1.0)
        P.op("pool", "affine_select", reads=["mle"], writes=["mle"], out=self.mle[:], in_=self.mle[:],
             pattern=[[1, 128]], compare_op=ALU.is_ge, fill=0.0, base=0, channel_multiplier=-1)
        P.op("pool", "memset", writes=["mgt"], ap=self.mgt[:], constant=1.0)
        P.op("pool", "affine_select", reads=["mgt"], writes=["mgt"], out=self.mgt[:], in_=self.mgt[:],
             pattern=[[-1, 128]], compare_op=ALU.is_gt, fill=0.0, base=0, channel_multiplier=1)
        P.op("pool", "memset", writes=["elast"], ap=self.elast[:], constant=1.0)
        P.op("pool", "affine_select", reads=["elast"], writes=["elast"], out=self.elast[:], in_=self.elast[:],
             pattern=[[0, 128]], compare_op=ALU.is_equal, fill=0.0, base=-127, channel_multiplier=1)
        P.op("pool", "memset", writes=["onesf"], ap=self.onesf[:], constant=1.0)
        P.op("pool", "memset", writes=["onesb"], ap=self.onesb[:], constant=1.0)
        P.op("pool", "memset", writes=["sel"], ap=self.sel[:], constant=8.0)
        P.op("pool", "affine_select", reads=["sel"], writes=["sel"], out=self.sel[:], in_=self.sel[:],
             pattern=[[1, 8], [0, 65]], compare_op=ALU.is_equal, fill=0.0, base=0, channel_multiplier=-1)
        P.op("pool", "affine_select", reads=["sel"], writes=["sel"], out=self.sel[:], in_=self.sel[:],
             pattern=[[0, 8], [1, 65]], compare_op=ALU.is_equal, fill=0.0, base=-64, channel_multiplier=0)
        P.op("sp", "dma_start", writes=["ppt"], dma="ppt", out=self.ppt[:], in_=self.ppart)
        for l in range(DEPTH):
            P.op("sp", "dma_start", writes=["prw"], dma="prw", out=self.prw[:, l * 80:(l + 1) * 80],
                 in_=self.prow[l * 4176:l * 4176 + 80].partition_broadcast(128))
        for l in range(DEPTH):
            P.op("act", "activation", reads=["prw"], writes=["abc"], out=self.abc[:, l * 16:(l + 1) * 16],
                 in_=self.prw[:, l * 80 + 16:l * 80 + 32], func=AF.Exp)
            P.op("dve", "tensor_scalar", reads=["abc"], writes=["abc"], out=self.abc[:, l * 16:(l + 1) * 16],
                 in0=self.abc[:, l * 16:(l + 1) * 16], scalar1=-1.0, scalar2=None, op0=ALU.mult)
            P.op("act", "activation", reads=["prw"], writes=["esink"], out=self.esink[:, l * 16:(l + 1) * 16],
                 in_=self.prw[:, l * 80 + 48:l * 80 + 64], func=AF.Exp)

    def nw(self, l):
        return self.ppt[:, l * 88:l * 88 + 8]

    def convw(self, l, b, k):
        o = l * 88 + 8 + b * 4 + k
        return self.ppt[:, o:o + 1]

    def convb(self, l, b):
        o = l * 88 + 72 + b
        return self.ppt[:, o:o + 1]

    def rowp(self, l, i):
        return self.prw[:, l * 80 + i * 16:l * 80 + (i + 1) * 16]

    def xsrc(self, l, s, c):
        src = self.x if l == 0 else self.X1
        return src[s, c * 128:(c + 1) * 128, :], ("xin" if l == 0 else "x1_%d_%d" % (s, c))

    def h_chunk(self, l, s, c, hT, hkey, col0, slot=None):
        P = self.P
        if slot is None:
            sl = self.xslot
            self.xslot ^= 1
        else:
            sl = slot
        xt, xn, junk, st4 = self.xt[sl], self.xn[sl], self.junk[sl], self.st4[sl]
        src, skey = self.xsrc(l, s, c)
        P.op("sp", "dma_start", reads=[skey], writes=["xt%d" % sl], dma="xt%d" % sl, out=xt[:], in_=src)
        P.op("act", "activation", reads=["xt%d" % sl], writes=["junk%d" % sl, "st4_%d" % sl],
             out=junk[:], in_=xt[:], func=AF.Square, accum_out=st4[:, 0:1])
        P.op("act", "activation", reads=["st4_%d" % sl, "epsb"], writes=["st4_%d" % sl], out=st4[:, 1:2], in_=st4[:, 0:1],
             func=AF.Ln, scale=1.0 / D, bias=self.epsb[:])
        P.op("act", "activation", reads=["st4_%d" % sl], writes=["st4_%d" % sl], out=st4[:, 2:3], in_=st4[:, 1:2],
             func=AF.Exp, scale=-0.5)
        P.op("act", "activation", reads=["xt%d" % sl, "st4_%d" % sl], writes=["xn%d" % sl], out=xn[:], in_=xt[:],
             func=AF.Identity, scale=st4[:, 2:3])
        b = self.psb()
        ptb = self.ps[:, b, :].bitcast(BF16).rearrange("p (k t) -> p k t", k=8)
        for kc in range(KC):
            P.op("pe", "transpose", reads=["xn%d" % sl, "ident"], writes=self.pk(b), out=ptb[:, kc, :],
                 in_=xn[:, kc * 128:(kc + 1) * 128], identity=self.ident[:])
        P.op("dve", "tensor_tensor", reads=self.pk(b) + ["ppt"], writes=[hkey],
             out=hT[:, :, col0:col0 + 128], in0=ptb,
             in1=self.nw(l).unsqueeze(2).broadcast_to([128, 8, 128]), op=ALU.mult)
        return sl

    def load_w(self, wt, key, l, c0, n, src=None):
        P = self.P
        src = self.win[l] if src is None else src
        step = 1024
        for kc in range(KC):
            for o in range(0, n, step):
                m = min(step, n - o)
                P.op("pool", "dma_start", writes=[key], dma=key, out=wt[:, kc, o:o + m],
                     in_=src[:, kc, c0 + o:c0 + o + m])

    def proj_tm(self, dst_bank, hT, hkey, col0, wt, wkey, wc0, n, poff=0):
        P = self.P
        for kc in range(KC):
            P.op("pe", "matmul", reads=[hkey, wkey], writes=self.pk(dst_bank),
                 out=self.ps[:, dst_bank, poff:poff + n], lhsT=hT[:, kc, col0:col0 + 128],
                 rhs=wt[:, kc, wc0:wc0 + n], start=(kc == 0), stop=(kc == KC - 1))

    def proj_fm(self, dst_bank, hT, hkey, ntok, wt, wkey, wc0, m, first=True):
        P = self.P
        for kc in range(KC):
            P.op("pe", "matmul", reads=[hkey, wkey], writes=self.pk(dst_bank),
                 out=self.ps[0:m, dst_bank, 0:ntok], lhsT=wt[:, kc, wc0:wc0 + m],
                 rhs=hT[:, kc, 0:ntok], start=(first and kc == 0), stop=(kc == KC - 1))

    def phase_A(self, l):
        nc, P, S = self.nc, self.P, self.S
        with ExitStack() as st:
            self.uid = getattr(self, "uid", 0) + 1
            sb = lambda name, shape, dt, _u=self.uid: st.enter_context(nc.sbuf_tensor("%s_u%d" % (name, _u), shape, dt))
            wa = sb("wa", [128, KC, A_N], BF16)
            hT = sb("hT", [128, KC, 512], BF16)
            Ub = [sb("Ub%d" % i, [128, 515], F32) for i in range(2)]
            Ucar = sb("Ucar", [128, 16, 3], F32)
            junkA = sb("junkA", [128, 256], BF16)
            acc = [sb("acc%d" % i, [128, 512], F32) for i in range(2)]
            xsT = sb("xsT", [128, 8, 512], BF16)
            BT = sb("BT", [128, 4, 512], BF16)
            CT = sb("CT", [128, 4, 512], BF16)
            H = sb("H", [128, 16, 64], F32)
            Hb = sb("Hb", [128, 16, 64], BF16)
            Htmp = sb("Htmp", [128, 16, 64], F32)
            sm = sb("sm", [128, 12, 16], F32)
            rhsall = sb("rhsall", [128, 16, 128], F32)
            Eh = sb("Eh", [128, 16, 128], F32)
            dec = Eh
            cbm = sb("cbm", [128, 4, 128], F32)
            MT = sb("MT", [128, 16, 128], BF16)
            xstm = sb("xstm", [128, 16, 64], F32)
            xdt = sb("xdt", [128, 16, 64], BF16)
            xw = sb("xw", [128, 16, 64], BF16)
            Btm = sb("Btm", [128, 4, 128], BF16)
            sz = sb("sz", [128, D], F32)
            y1 = sb("y1", [128, 16, 64], F32)
            y2 = sb("y2", [128, 16, 64], F32)
            yo = [sb("yo%d" % i, [128, D], BF16) for i in range(2)]
            snw = sb("snw", [128, D], F32)
            self.epsb = sb("epsb", [128, 1], F32)
            P.op("dve", "memset", writes=["epsb"], ap=self.epsb[:], constant=EPS)
            self.load_w(wa, "wa", l, A0, A_N)
            P.op("sp", "dma_start", writes=["snw"], dma="snw", out=snw[:],
                 in_=self.prow[l * 4176 + 80:l * 4176 + 80 + 1024].partition_broadcast(128))
            ps = self.ps
            for s in range(self.NSEQ):
                P.op("dve", "memset", writes=["H"], ap=H[:], constant=0.0)
                P.op("pool", "memset", writes=["Ucar%d" % b for b in range(16)], ap=Ucar[:], constant=0.0)
                for t in range(self.NTL):
                    for c4 in range(4):
                        self.h_chunk(l, s, t * 4 + c4, hT, "hT", c4 * 128)
                    for b in range(16):
                        pb = self.psb()
                        self.proj_fm(pb, hT, "hT", 512, wa, "wa", b * 128, 128)
                        ukey = "Ub%d" % (b % 2)
                        U_ = Ub[b % 2]
                        P.op("pool", "tensor_copy", reads=["Ucar%d" % b], writes=[ukey], out=U_[:, 0:3], in_=Ucar[:, b, :])
                        P.op("act", "activation", reads=self.pk(pb), writes=[ukey], out=U_[:, 3:515],
                             in_=ps[:, pb, :], func=AF.Copy)
                        a = acc[b % 2]
                        akey = "acc%d" % (b % 2)
                        P.op("dve", "tensor_scalar", reads=[ukey, "ppt"], writes=[akey], out=a[:], in0=U_[:, 0:512],
                             scalar1=self.convw(l, b, 0), scalar2=None, op0=ALU.mult)
                        for k in range(1, 4):
                            P.op("dve", "scalar_tensor_tensor", reads=[ukey, akey, "ppt"], writes=[akey], out=a[:],
                                 in0=U_[:, k:k + 512], scalar=self.convw(l, b, k), in1=a[:],
                                 op0=ALU.mult, op1=ALU.add)
                        if b < 8:
                            dst, dkey = xsT[:, b, :], "xsT"
                        elif b < 12:
                            dst, dkey = BT[:, b - 8, :], "BT"
                        else:
                            dst, dkey = CT[:, b - 12, :], "CT"
                        P.op("act", "activation", reads=[akey, "ppt"], writes=[dkey], out=dst, in_=a[:],
                             func=AF.Silu, bias=self.convb(l, b))
                        P.op("pool", "tensor_copy", reads=[ukey], writes=["Ucar%d" % b], out=Ucar[:, b, :], in_=U_[:, 512:515])
                    for c4 in range(4):
                        c = t * 4 + c4
                        cs = slice(c4 * 128, (c4 + 1) * 128)
                        pd = self.psb()
                        self.proj_tm(pd, hT, "hT", c4 * 128, wa, "wa", 3072, 16)
                        dtr, dt_, adt, acum, nacum, lastbc, dS, ea, cd, dtS, e1 = [sm[:, i, :] for i in range(11)]
                        P.op("dve", "tensor_tensor", reads=self.pk(pd) + ["prw"], writes=["sm0"], out=dtr,
                             in0=ps[:, pd, 0:16], in1=self.rowp(l, 0), op=ALU.add)
                        P.op("act", "activation", reads=["sm0"], writes=["sm10"], out=e1, in_=dtr, func=AF.Exp)
                        P.op("act", "activation", reads=["sm10"], writes=["sm1"], out=dt_, in_=e1, func=AF.Ln, bias=1.0)
                        P.op("dve", "tensor_tensor", reads=["sm1", "abc"], writes=["sm2"], out=adt, in0=dt_,
                             in1=self.abc[:, l * 16:(l + 1) * 16], op=ALU.mult)
                        pa = self.psb()
                        P.op("pe", "matmul", reads=["tri", "sm2"], writes=self.pk(pa), out=ps[:, pa, 0:16],
                             lhsT=self.tri[:], rhs=adt, start=True, stop=True)
                        P.op("dve", "tensor_copy", reads=self.pk(pa), writes=["sm3"], out=acum, in_=ps[:, pa, 0:16])
                        P.op("pe", "matmul", reads=["elast", "sm3"], writes=self.pk(pa), out=ps[:, pa, 16:32],
                             lhsT=self.elast[:], rhs=acum, start=True, stop=True)
                        P.op("dve", "tensor_copy", reads=self.pk(pa), writes=["sm5"], out=lastbc, in_=ps[:, pa, 16:32])
                        P.op("dve", "tensor_tensor", reads=["sm5", "sm3"], writes=["sm6"], out=dS, in0=lastbc, in1=acum,
                             op=ALU.subtract)
                        P.op("act", "activation", reads=["sm6"], writes=["sm6"], out=dS, in_=dS, func=AF.Exp)
                        P.op("act", "activation", reads=["sm3"], writes=["sm7"], out=ea, in_=acum, func=AF.Exp)
                        P.op("act", "activation", reads=["sm5"], writes=["sm8"], out=cd, in_=lastbc, func=AF.Exp)
                        P.op("dve", "tensor_tensor", reads=["sm1", "sm6"], writes=["sm9"], out=dtS, in0=dt_, in1=dS,
                             op=ALU.mult)
                        P.op("dve", "tensor_tensor", reads=["tri", "sm2"], writes=["rhsall"], out=rhsall[:],
                             in0=self.tri[:].unsqueeze(1).broadcast_to([128, 16, 128]),
                             in1=adt.unsqueeze(2).broadcast_to([128, 16, 128]), op=ALU.mult)
                        for g in range(4):
                            pg = self.psb()
                            P.op("pe", "matmul", reads=["onesf", "rhsall"], writes=self.pk(pg),
                                 out=ps[:, pg, :], lhsT=self.onesf[:],
                                 rhs=rhsall[:, 4 * g:4 * g + 4, :], start=True, stop=True)
                            for r in range(4):
                                h = 4 * g + r
                                P.op("dve", "tensor_scalar", reads=self.pk(pg) + ["sm3"], writes=["Eh%d" % g],
                                     out=Eh[:, h, :], in0=ps[:, pg, r * 128:(r + 1) * 128],
                                     scalar1=acum[:, h:h + 1], scalar2=0.0, op0=ALU.subtract, op1=ALU.min)
                            P.op("act", "activation", reads=["Eh%d" % g], writes=["Eh%d" % g, "dec%d" % g],
                                 out=dec[:, 4 * g:4 * g + 4, :], in_=Eh[:, 4 * g:4 * g + 4, :], func=AF.Exp)
                        pc = self.psb()
                        for g in range(4):
                            P.op("pe", "matmul", reads=["BT", "CT"], writes=self.pk(pc),
                                 out=ps[:, pc, g * 128:(g + 1) * 128], lhsT=BT[:, g, cs], rhs=CT[:, g, cs],
                                 start=True, stop=True)
                        P.op("dve", "tensor_tensor", reads=self.pk(pc) + ["mlef"], writes=["cbm"], out=cbm[:],
                             in0=ps[:, pc, :].rearrange("p (g l) -> p g l", g=4),
                             in1=self.mlef[:].unsqueeze(1).broadcast_to([128, 4, 128]), op=ALU.mult)
                        for g in range(4):
                            P.op("pool", "tensor_tensor", reads=["dec%d" % g, "Eh%d" % g, "cbm"], writes=["MT%d" % g],
                                 out=MT[:, 4 * g:4 * g + 4, :], in0=dec[:, 4 * g:4 * g + 4, :],
                                 in1=cbm[:, g, :].unsqueeze(1).broadcast_to([128, 4, 128]), op=ALU.mult)
                        px = self.psb()
                        pxb = ps[:, px, :].bitcast(BF16).rearrange("p (k t) -> p k t", k=8)
                        for b in range(8):
                            P.op("pe", "transpose", reads=["xsT", "ident"], writes=self.pk(px), out=pxb[:, b, :],
                                 in_=xsT[:, b, cs], identity=self.ident[:])
                        pxv = ps[:, px, :].bitcast(BF16).rearrange("p (h d) -> p h d", h=16)
                        P.op("act", "activation", reads=self.pk(px), writes=["xstm"], out=xstm[:], in_=pxv, func=AF.Copy)
                        P.op("dve", "tensor_tensor", reads=["xstm", "sm1"], writes=["xdt"], out=xdt[:], in0=xstm[:],
                             in1=dt_.unsqueeze(2).broadcast_to([128, 16, 64]), op=ALU.mult)
                        P.op("pool", "tensor_tensor", reads=["xstm", "sm9"], writes=["xw"], out=xw[:], in0=xstm[:],
                             in1=dtS.unsqueeze(2).broadcast_to([128, 16, 64]), op=ALU.mult)
                        pbt = self.psb()
                        pbb = ps[:, pbt, 0:256].bitcast(BF16).rearrange("p (k t) -> p k t", k=4)
                        for g in range(4):
                            P.op("pe", "transpose", reads=["BT", "ident"], writes=self.pk(pbt), out=pbb[:, g, :],
                                 in_=BT[:, g, cs], identity=self.ident[:])
                        P.op("act", "activation", reads=self.pk(pbt), writes=["Btm"], out=Btm[:], in_=pbb, func=AF.Copy)
                        P.op("act", "activation", reads=["H"], writes=["Hb"], out=Hb[:], in_=H[:], func=AF.Copy)
                        po = self.psb(2)
                        for g in range(4):
                            P.op("pe", "matmul", reads=["CT", "Hb"], writes=self.pk(po, 2),
                                 out=ps[:, po + g // 2, (g % 2) * 256:(g % 2) * 256 + 256],
                                 lhsT=CT[:, g, cs], rhs=Hb[:, 4 * g:4 * g + 4, :], start=True, stop=True)
                        poall = ps[:, po:po + 2, :].rearrange("p b (h d) -> p (b h) d", d=64)
                        P.op("dve", "tensor_tensor", reads=self.pk(po, 2) + ["sm7"], writes=["y1"], out=y1[:], in0=poall,
                             in1=ea.unsqueeze(2).broadcast_to([128, 16, 64]), op=ALU.mult)
                        pst = self.psb(2)
                        for g in range(4):
                            P.op("pe", "matmul", reads=["Btm", "xw"], writes=self.pk(pst, 2),
                                 out=ps[:, pst + g // 2, (g % 2) * 256:(g % 2) * 256 + 256],
                                 lhsT=Btm[:, g, :], rhs=xw[:, 4 * g:4 * g + 4, :], start=True, stop=True)
                        pstall = ps[:, pst:pst + 2, :].rearrange("p b (h d) -> p (b h) d", d=64)
                        P.op("dve", "tensor_tensor", reads=["H", "sm8"], writes=["Htmp"], out=Htmp[:], in0=H[:],
                             in1=cd.unsqueeze(2).broadcast_to([128, 16, 64]), op=ALU.mult)
                        P.op("dve", "tensor_tensor", reads=self.pk(pst, 2) + ["Htmp"], writes=["H"], out=H[:], in0=pstall,
                             in1=Htmp[:], op=ALU.add)
                        pyd = self.psb(2)
                        for h in range(16):
                            P.op("pe", "matmul", reads=["MT%d" % (h // 4), "xdt"], writes=self.pk(pyd, 2),
                                 out=ps[:, pyd + h // 8, (h % 8) * 64:(h % 8) * 64 + 64],
                                 lhsT=MT[:, h, :], rhs=xdt[:, h, :], start=True, stop=True)
                        pydall = ps[:, pyd:pyd + 2, :].rearrange("p b (h d) -> p (b h) d", d=64)
                        P.op("dve", "tensor_tensor", reads=self.pk(pyd, 2) + ["y1"], writes=["y1"], out=y1[:], in0=pydall,
                             in1=y1[:], op=ALU.add)
                        P.op("pool", "tensor_tensor", reads=["xstm", "prw"], writes=["y2"], out=y2[:], in0=xstm[:],
                             in1=self.rowp(l, 2).unsqueeze(2).broadcast_to([128, 16, 64]), op=ALU.mult)
                        P.op("pool", "tensor_tensor", reads=["y1", "y2"], writes=["y2"], out=y2[:], in0=y1[:], in1=y2[:],
                             op=ALU.add)
                        pz = self.psb(2)
                        for n in range(2):
                            self.proj_tm(pz + n, hT, "hT", c4 * 128, wa, "wa", 2048 + n * 512, 512)
                        P.op("act", "activation", reads=self.pk(pz, 2), writes=["sz"], out=sz[:],
                             in_=ps[:, pz:pz + 2, :].rearrange("p b n -> p (b n)"), func=AF.Silu)
                        y2f = y2[:].rearrange("p h d -> p (h d)")
                        P.op("dve", "tensor_tensor", reads=["y2", "sz"], writes=["y2"], out=y2f, in0=y2f, in1=sz[:],
                             op=ALU.mult)
                        ss = sm[:, 11, 0:4]
                        for g in range(4):
                            P.op("act", "activation", reads=["y2"], writes=["junkA", "sm11"], out=junkA[:],
                                 in_=y2f[:, g * 256:(g + 1) * 256], func=AF.Square, accum_out=sm[:, 11, g:g + 1])
                        P.op("act", "activation", reads=["sm11", "epsb"], writes=["sm11"], out=sm[:, 11, 4:8], in_=ss, func=AF.Ln,
                             scale=1.0 / 256, bias=self.epsb[:])
                        P.op("act", "activation", reads=["sm11"], writes=["sm11"], out=sm[:, 11, 8:12], in_=sm[:, 11, 4:8],
                             func=AF.Exp, scale=-0.5)
                        P.op("dve", "tensor_tensor", reads=["y2", "sm11"], writes=["y2"],
                             out=y2[:].rearrange("p (g r) d -> p g (r d)", g=4),
                             in0=y2[:].rearrange("p (g r) d -> p g (r d)", g=4),
                             in1=sm[:, 11, 8:12].unsqueeze(2).broadcast_to([128, 4, 256]), op=ALU.mult)
                        yq = yo[c % 2]
                        P.op("pool", "tensor_tensor", reads=["y2", "snw"], writes=["yo%d" % (c % 2)], out=yq[:], in0=y2f,
                             in1=snw[:], op=ALU.mult)
                        P.op("pool", "dma_start", reads=["yo%d" % (c % 2)], writes=["Y0_%d_%d" % (s, c)],
                             dma="yo%d" % (c % 2), out=self.Y[0, s, c * 128:(c + 1) * 128, :], in_=yq[:])
            P.emit()

    def phase_B(self, l):
        nc, P, S, NCH = self.nc, self.P, self.S, self.NCH
        with ExitStack() as st:
            self.uid = getattr(self, "uid", 0) + 1
            sb = lambda name, shape, dt, _u=self.uid: st.enter_context(nc.sbuf_tensor("%s_u%d" % (name, _u), shape, dt))
            wb = sb("wb", [128, KC, B_N], BF16)
            hT = sb("hT", [128, KC, 512], BF16)
            cosT = sb("cosT", [128, 512], F32)
            sinS = sb("sinS", [128, 512], F32)
            t1 = [sb("t1_%d" % i, [128, 512], F32) for i in range(2)]
            t2 = [sb("t2_%d" % i, [128, 512], F32) for i in range(2)]
            qrT = sb("qrT", [128, 8, 512], BF16)
            krT = sb("krT", [128, 2, S], BF16)
            V = sb("V", [128, NCH, 4, 65], BF16)
            sz = sb("sz", [128, D], F32)
            Pc = [sb("Pc%d" % i, [128, 512], BF16) for i in range(2)]
            Pp = [sb("Pp%d" % i, [128, 512], BF16) for i in range(2)]
            den = sb("den", [128, 16], F32)
            yf = sb("yf", [128, 16, 64], F32)
            yo = [sb("yo%d" % i, [128, D], BF16) for i in range(2)]
            self.epsb = sb("epsb", [128, 1], F32)
            P.op("dve", "memset", writes=["epsb"], ap=self.epsb[:], constant=EPS)
            self.load_w(wb, "wb", l, B0, B_N)
            P.op("pool", "memset", writes=["V%d" % i for i in range(NCH)], ap=V[:], constant=1.0)
            ps = self.ps
            for s in range(self.NSEQ):
                for t in range(self.NTL):
                    ts = slice(t * 512, (t + 1) * 512)
                    for c4 in range(4):
                        self.h_chunk(l, s, t * 4 + c4, hT, "hT", c4 * 128)
                    P.op("sp", "dma_start", writes=["cosT"], dma="cosT", out=cosT[:], in_=self.rope[0, :, ts])
                    P.op("sp", "dma_start", writes=["sinS"], dma="sinS", out=sinS[:], in_=self.rope[1, :, ts])
                    for b in range(10):
                        c0 = b * 128 if b < 8 else 1024 + (b - 8) * 128
                        c1 = 1280 + c0
                        pq = self.psb()
                        self.proj_fm(pq, hT, "hT", 512, wb, "wb", c0, 128)
                        pqs = self.psb()
                        self.proj_fm(pqs, hT, "hT", 512, wb, "wb", c1, 128)
                        i2 = b % 2
                        P.op("dve", "tensor_tensor", reads=self.pk(pq) + ["cosT"], writes=["t1_%d" % i2], out=t1[i2][:],
                             in0=ps[:, pq, :], in1=cosT[:], op=ALU.mult)
                        P.op("dve", "tensor_tensor", reads=self.pk(pqs) + ["sinS"], writes=["t2_%d" % i2], out=t2[i2][:],
                             in0=ps[:, pqs, :], in1=sinS[:], op=ALU.mult)
                        if b < 8:
                            dst, dkey = qrT[:, b, :], "qrT"
                        else:
                            dst, dkey = krT[:, b - 8, ts], "krT%d" % t
                        P.op("pool", "tensor_tensor", reads=["t1_%d" % i2, "t2_%d" % i2], writes=[dkey], out=dst,
                             in0=t1[i2][:], in1=t2[i2][:], op=ALU.add)
                    for c4 in range(4):
                        c = t * 4 + c4
                        cs = slice(c4 * 128, (c4 + 1) * 128)
                        pv = self.psb()
                        self.proj_tm(pv, hT, "hT", c4 * 128, wb, "wb", 2560, 256)
                        P.op("act", "activation", reads=self.pk(pv), writes=["V%d" % c], out=V[:, c, :, 0:64],
                             in_=ps[:, pv, 0:256].rearrange("p (h d) -> p h d", h=4), func=AF.Copy)
                        pz = self.psb(2)
                        for n in range(2):
                            self.proj_tm(pz + n, hT, "hT", c4 * 128, wb, "wb", 2816 + n * 512, 512)
                        P.op("act", "activation", reads=self.pk(pz, 2), writes=["sz"], out=sz[:],
                             in_=ps[:, pz:pz + 2, :].rearrange("p b n -> p (b n)"), func=AF.Silu)
                        pos = []
                        for kv in range(4):
                            half = slice((kv % 2) * 64, (kv % 2) * 64 + 64)
                            blk0 = (kv // 2) * 4
                            qv = qrT[half, blk0:blk0 + 4, cs]
                            i2 = kv % 2
                            psc = self.psb()
                            P.op("pe", "matmul", reads=["krT%d" % t, "qrT"], writes=self.pk(psc),
                                 out=ps[:, psc, :].rearrange("p (a q) -> p a q", a=4),
                                 lhsT=krT[half, kv // 2, c * 128:(c + 1) * 128], rhs=qv, start=True, stop=True)
                            P.op("act", "activation", reads=self.pk(psc), writes=["Pc%d" % i2], out=Pc[i2][:],
                                 in_=ps[:, psc, :], func=AF.Exp, scale=0.125)
                            P.op("pool", "tensor_tensor", reads=["Pc%d" % i2, "mle"], writes=["Pc%d" % i2],
                                 out=Pc[i2][:].rearrange("p (a q) -> p a q", a=4),
                                 in0=Pc[i2][:].rearrange("p (a q) -> p a q", a=4),
                                 in1=self.mle[:].unsqueeze(1).broadcast_to([128, 4, 128]), op=ALU.mult)
                            if c > 0:
                                psp = self.psb()
                                P.op("pe", "matmul", reads=["krT%d" % ((c - 1) // 4), "qrT"], writes=self.pk(psp),
                                     out=ps[:, psp, :].rearrange("p (a q) -> p a q", a=4),
                                     lhsT=krT[half, kv // 2, (c - 1) * 128:c * 128], rhs=qv, start=True, stop=True)
                                P.op("act", "activation", reads=self.pk(psp), writes=["Pp%d" % i2], out=Pp[i2][:],
                                     in_=ps[:, psp, :], func=AF.Exp, scale=0.125)
                                P.op("pool", "tensor_tensor", reads=["Pp%d" % i2, "mgt"], writes=["Pp%d" % i2],
                                     out=Pp[i2][:].rearrange("p (a q) -> p a q", a=4),
                                     in0=Pp[i2][:].rearrange("p (a q) -> p a q", a=4),
                                     in1=self.mgt[:].unsqueeze(1).broadcast_to([128, 4, 128]), op=ALU.mult)
                            po = self.psb()
                            pos.append(po)
                            for a in range(4):
                                if c > 0:
                                    P.op("pe", "matmul", reads=["Pp%d" % i2, "V%d" % (c - 1)], writes=self.pk(po),
                                         out=ps[:, po, a * 65:(a + 1) * 65], lhsT=Pp[i2][:, a * 128:(a + 1) * 128],
                                         rhs=V[:, c - 1, kv, :], start=True, stop=False)
                                P.op("pe", "matmul", reads=["Pc%d" % i2, "V%d" % c], writes=self.pk(po),
                                     out=ps[:, po, a * 65:(a + 1) * 65], lhsT=Pc[i2][:, a * 128:(a + 1) * 128],
                                     rhs=V[:, c, kv, :], start=(c == 0), stop=True)
                            pov = ps[:, po, 0:260].rearrange("p (a e) -> p a e", a=4)
                            P.op("dve", "tensor_tensor", reads=self.pk(po) + ["esink"], writes=["den%d" % kv],
                                 out=den[:, 4 * kv:4 * kv + 4].unsqueeze(2), in0=pov[:, :, 64:65],
                                 in1=self.esink[:, l * 16 + 4 * kv:l * 16 + 4 * kv + 4].unsqueeze(2), op=ALU.add)
                            P.op("dve", "reciprocal", reads=["den%d" % kv], writes=["den%d" % kv],
                                 out=den[:, 4 * kv:4 * kv + 4], in_=den[:, 4 * kv:4 * kv + 4])
                            P.op("dve", "tensor_tensor", reads=self.pk(po) + ["den%d" % kv], writes=["yf%d" % kv],
                                 out=yf[:, 4 * kv:4 * kv + 4, :], in0=pov[:, :, 0:64],
                                 in1=den[:, 4 * kv:4 * kv + 4].unsqueeze(2).broadcast_to([128, 4, 64]), op=ALU.mult)
                        yq = yo[c % 2]
                        P.op("pool", "tensor_tensor", reads=["yf%d" % k for k in range(4)] + ["sz"],
                             writes=["yo%d" % (c % 2)], out=yq[:], in0=yf[:].rearrange("p h d -> p (h d)"), in1=sz[:],
                             op=ALU.mult)
                        P.op("pool", "dma_start", reads=["yo%d" % (c % 2)], writes=["Y1_%d_%d" % (s, c)],
                             dma="yo%d" % (c % 2), out=self.Y[1, s, c * 128:(c + 1) * 128, :], in_=yq[:])
            P.emit()

    def phase_C(self, l, hg):
        nc, P, S, NCH = self.nc, self.P, self.S, self.NCH
        with ExitStack() as st:
            self.uid = getattr(self, "uid", 0) + 1
            sb = lambda name, shape, dt, _u=self.uid: st.enter_context(nc.sbuf_tensor("%s_u%d" % (name, _u), shape, dt))
            wc = sb("wc", [128, KC, C_G], BF16)
            hT = sb("hT", [128, KC, 512], BF16)
            KT = sb("KT", [72, 8, S], BF16)
            V = sb("V", [128, NCH, 8, 65], BF16)
            QT = sb("QT", [72, 8, 512], BF16)
            NC_ = sb("NC", [128, NCH, 8], F32)
            fsm = sb("fsm", [128, 4, 8], F32)
            cumT = sb("cumT", [8, 512], BF16)
            szT = sb("szT", [64, 8, 512], F32)
            PT = [sb("PT%d" % i, [128, 512], BF16) for i in range(4)]
            rd = [sb("rd%d" % i, [65, 512], BF16) for i in range(2)]
            rdf = [sb("rdf%d" % i, [65, 512], F32) for i in range(2)]
            yn = [sb("yn%d" % i, [64, 512], F32) for i in range(2)]
            yo = [sb("yo0", [64, 8, 512], BF16)] * 2
            self.epsb = sb("epsb", [128, 1], F32)
            P.op("dve", "memset", writes=["epsb"], ap=self.epsb[:], constant=EPS)
            self.load_w(wc, "wc", l, C0 + hg * C_G, C_G)
            P.op("pool", "memset", writes=["V%d" % i for i in range(NCH)], ap=V[:], constant=1.0)
            P.op("pool", "memset", writes=["KT%d" % i for i in range(self.NTL)], ap=KT[64:72, :, :], constant=1.0)
            P.op("pool", "affine_select", reads=["KT%d" % i for i in range(self.NTL)],
                 writes=["KT%d" % i for i in range(self.NTL)], out=KT[64:72, :, :], in_=KT[64:72, :, :],
                 pattern=[[1, 8], [0, S]], compare_op=ALU.is_equal, fill=0.0, base=0, channel_multiplier=-1)
            ps = self.ps
            self.pools = {"gen": [0, 1], "ct": [2], "acc": [2, 3, 4], "sc": [5, 6, 7]}
            self.prr = {}
            fb = self.rowp(l, 4)[:, hg * 8:hg * 8 + 8]
            pti = 0
            for s in range(self.NSEQ):
                for t in range(self.NTL):
                    ts = slice(t * 512, (t + 1) * 512)
                    for c4 in range(4):
                        self.h_chunk(l, s, t * 4 + c4, hT, "hT", c4 * 128)
                    for hp in range(4):
                        pkb = self.psb()
                        self.proj_fm(pkb, hT, "hT", 512, wc, "wc", 512 + hp * 128, 128)
                        for e in range(2):
                            P.op("dve", "tensor_copy", reads=self.pk(pkb), writes=["KT%d" % t], out=KT[0:64, 2 * hp + e, ts],
                                 in_=ps[64 * e:64 * e + 64, pkb, :])
                    pct = self.psb(pool="ct")
                    for c4 in range(4):
                        c = t * 4 + c4
                        pf = self.psb()
                        self.proj_tm(pf, hT, "hT", c4 * 128, wc, "wc", 2048, 8)
                        f0, f1, f2 = fsm[:, 0, :], fsm[:, 1, :], fsm[:, 2, :]
                        P.op("dve", "tensor_tensor", reads=self.pk(pf) + ["prw"], writes=["f0"], out=f0,
                             in0=ps[:, pf, 0:8], in1=fb, op=ALU.add)
                        P.op("act", "activation", reads=["f0"], writes=["f1"], out=f1, in_=f0, func=AF.Exp, scale=-1.0)
                        P.op("act", "activation", reads=["f1"], writes=["f2"], out=f2, in_=f1, func=AF.Ln, bias=1.0)
                        P.op("pe", "matmul", reads=["tri", "f2"], writes=self.pk(pf), out=ps[:, pf, 8:16],
                             lhsT=self.tri[:], rhs=f2, start=True, stop=(c == 0))
                        if c > 0:
                            P.op("pe", "matmul", reads=["elast", "NC%d" % (c - 1)], writes=self.pk(pf), out=ps[:, pf, 8:16],
                                 lhsT=self.elast[:], rhs=NC_[:, c - 1, :], start=False, stop=True)
                        P.op("dve", "tensor_copy", reads=self.pk(pf), writes=["NC%d" % c], out=NC_[:, c, :], in_=ps[:, pf, 8:16])
                        P.op("pe", "transpose", reads=["NC%d" % c, "identf"], writes=self.pk(pct),
                             out=ps[0:8, pct, c4 * 128:(c4 + 1) * 128], in_=NC_[:, c, :], identity=self.identf[:])
                        pv = self.psb()
                        self.proj_tm(pv, hT, "hT", c4 * 128, wc, "wc", 1024, 512)
                        P.op("act", "activation", reads=self.pk(pv), writes=["V%d" % c], out=V[:, c, :, 0:64],
                             in_=ps[:, pv, :].rearrange("p (h d) -> p h d", h=8), func=AF.Copy)
                    P.op("dve", "tensor_scalar", reads=self.pk(pct), writes=["QT%d" % h for h in range(8)],
                         out=QT[64:72, :, :], in0=ps[0:8, pct, :].unsqueeze(1).broadcast_to([8, 8, 512]),
                         scalar1=-8.0, scalar2=None, op0=ALU.mult)
                    yq = yo[0]
                    ykey = "yo0"
                    for hp in range(4):
                        pz = self.psb()
                        self.proj_fm(pz, hT, "hT", 512, wc, "wc", 1536 + hp * 128, 128)
                        for e in range(2):
                            P.op("act", "activation", reads=self.pk(pz), writes=["szT%d" % (2 * hp + e)],
                                 out=szT[:, 2 * hp + e, :], in_=ps[64 * e:64 * e + 64, pz, :], func=AF.Silu)
                    for hp in range(4):
                        pqb = self.psb()
                        self.proj_fm(pqb, hT, "hT", 512, wc, "wc", hp * 128, 128)
                        for e in range(2):
                            P.op("dve", "tensor_copy", reads=self.pk(pqb), writes=["QT%d" % (2 * hp + e)],
                                 out=QT[0:64, 2 * hp + e, :], in_=ps[64 * e:64 * e + 64, pqb, :])
                    nkb = 4 * t + 4

                    def qk(h, j):
                        q0 = max(j - 4 * t, 0)
                        nq = 4 - q0
                        psc = self.psb(pool="sc")
                        P.op("pe", "matmul", reads=["KT%d" % (j // 4), "QT%d" % h], writes=self.pk(psc),
                             out=ps[:, psc, 0:nq * 128], lhsT=KT[:, h, j * 128:(j + 1) * 128],
                             rhs=QT[:, h, q0 * 128:512], start=True, stop=True)
                        return psc

                    seq = [(2 * hp + e, j) for hp in range(4) for j in range(nkb) for e in range(2)]
                    DIST = 2
                    pend = [qk(*seq[i]) for i in range(min(DIST, len(seq)))]
                    pos = {}
                    for i, (h, j) in enumerate(seq):
                        psc = pend.pop(0)
                        if i + DIST < len(seq):
                            pend.append(qk(*seq[i + DIST]))
                        if j == 0:
                            pos[h] = self.psb(pool="acc")
                        po = pos[h]
                        jj = j - 4 * t
                        q0 = max(jj, 0)
                        nq = 4 - q0
                        pt = PT[pti % 4]
                        ptk = "PT%d" % (pti % 4)
                        pti += 1
                        P.op("act", "activation", reads=self.pk(psc) + ["NC%d" % j], writes=[ptk], out=pt[:, 0:nq * 128],
                             in_=ps[:, psc, 0:nq * 128], func=AF.Exp, scale=0.125, bias=NC_[:, j, h:h + 1])
                        if jj >= 0:
                            P.op("pool", "tensor_tensor", reads=[ptk, "mle"], writes=[ptk], out=pt[:, 0:128],
                                 in0=pt[:, 0:128], in1=self.mle[:], op=ALU.mult)
                        P.op("pe", "matmul", reads=[ptk, "V%d" % j], writes=self.pk(po), out=ps[0:65, po, q0 * 128:512],
                             lhsT=V[:, j, h, :], rhs=pt[:, 0:nq * 128], start=(j == 0), stop=(j == nkb - 1))
                        if j == nkb - 1:
                            r_, y_ = rd[h % 2], yn[h % 2]
                            rk, yk = "rd%d" % (h % 2), "yn%d" % (h % 2)
                            rf_ = rdf[h % 2]
                            P.op("act", "activation", reads=self.pk(po), writes=[rk + "f"], out=rf_[64:65, :],
                                 in_=ps[64:65, po, :], func=AF.Ln)
                            P.op("act", "activation", reads=[rk + "f"], writes=[rk], out=r_[64:65, :],
                                 in_=rf_[64:65, :], func=AF.Exp, scale=-1.0)
                            pbc = self.psb()
                            P.op("pe", "matmul", reads=[rk, "onesb"], writes=self.pk(pbc), out=ps[0:64, pbc, :],
                                 lhsT=self.onesb[64:65, 0:64], rhs=r_[64:65, :], start=True, stop=True)
                            P.op("dve", "tensor_tensor", reads=self.pk(pbc) + ["szT%d" % h], writes=[yk], out=y_[:],
                                 in0=ps[0:64, pbc, :], in1=szT[:, h, :], op=ALU.mult)
                            P.op("dve", "tensor_tensor", reads=self.pk(po) + [yk], writes=[ykey], out=yq[:, h, :],
                                 in0=ps[0:64, po, :], in1=y_[:], op=ALU.mult)
                    P.op("pool", "dma_start", reads=[ykey], writes=["Y2T_%d_%d_%d" % (s, t, hg)], dma=ykey,
                         out=self.Y2T[s, hg * 512:(hg + 1) * 512, ts].rearrange("(h d) t -> d h t", d=64), in_=yq[:])
            P.emit()
            self.pools = {"gen": list(range(8))}
            self.prr = {}

    def phase_D(self, l):
        nc, P, S = self.nc, self.P, self.S
        last = (l == self.layers - 1)
        NS = self.NSEQ
        with ExitStack() as st:
            self.uid = getattr(self, "uid", 0) + 1
            sb = lambda name, shape, dt, _u=self.uid: st.enter_context(nc.sbuf_tensor("%s_u%d" % (name, _u), shape, dt))
            wg = sb("wg", [128, KC, G_N], BF16)
            wp = [sb("wp%d" % i, [128, KC, D], BF16) for i in range(3)]
            wo = sb("wo", [128, KC, D], BF16)
            gbb = sb("gbb", [128, 3 * D], F32)
            fnw = sb("fnw", [128, D], F32) if last else None
            nb = max(NS, 2)
            yt = [[sb("yt%d_%d" % (b, i), [128, D], BF16) for i in range(nb)] for b in range(2)]
            ybTc = [sb("ybTc%d" % i, [128, KC, 128], BF16) for i in range(nb)]
            hTs = [sb("hT%d" % i, [128, KC, 128], BF16) for i in range(nb)]
            ybTs = [sb("ybT%d" % i, [128, KC, 128], BF16) for i in range(nb)]
            gss = [sb("gs%d" % i, [128, D], F32) for i in range(nb)]
            mgs = [sb("mg%d" % i, [128, D], F32) for i in range(nb)]
            mgbs = [sb("mgb%d" % i, [128, D], BF16) for i in range(nb)]
            mTs = [sb("mT%d" % i, [128, KC, 128], BF16) for i in range(nb)]
            xo = [sb("xo%d" % i, [128, D], F32) for i in range(nb)]
            fss = [sb("fs%d" % i, [128, 4], F32) for i in range(nb)]
            self.epsb = sb("epsb", [128, 1], F32)
            P.op("dve", "memset", writes=["epsb"], ap=self.epsb[:], constant=EPS)
            self.load_w(wg, "wg", l, G0, G_N)
            for b in range(3):
                self.load_w(wp[b], "wp%d" % b, l, 0, D, src=self.wproj[l, b])
            self.load_w(wo, "wo", l, 0, D, src=self.wout[l])
            P.op("sp", "dma_start", writes=["gbb"], dma="gbb", out=gbb[:],
                 in_=self.prow[l * 4176 + 1104:l * 4176 + 1104 + 3072].partition_broadcast(128))
            if last:
                P.op("sp", "dma_start", writes=["fnw"], dma="fnw", out=fnw[:],
                     in_=self.prow[DEPTH * 4176:DEPTH * 4176 + 1024].partition_broadcast(128))
            ps = self.ps

            def chunk_gen(s, c, k):
                hT, ybT, gs, mg, mgb, mT, xq, fs = hTs[k], ybTs[k], gss[k], mgs[k], mgbs[k], mTs[k], xo[k], fss[k]
                K_ = lambda n: "%s%d" % (n, k)
                for b in range(2):
                    P.op("sp", "dma_start", reads=["Y%d_%d_%d" % (b, s, c)], writes=["yt%d_%d" % (b, k)],
                         dma="yt%d_%d" % (b, k), out=yt[b][k][:], in_=self.Y[b, s, c * 128:(c + 1) * 128, :])
                P.op("sp", "dma_start", reads=["Y2T_%d_%d_%d" % (s, c // 4, g) for g in range(2)],
                     writes=[K_("ybTc")], dma=K_("ybTc"), out=ybTc[k][:],
                     in_=self.Y2T[s, :, c * 128:(c + 1) * 128].rearrange("(kc p) t -> p kc t", p=128))
                sl = self.h_chunk(l, s, c, hT, K_("hT"), 0, slot=k % 2)
                yield
                for b in range(3):
                    pg = self.psb(2)
                    for n in range(2):
                        self.proj_tm(pg + n, hT, K_("hT"), 0, wg, "wg", b * D + n * 512, 512)
                    P.op("dve", "tensor_tensor", reads=self.pk(pg, 2) + ["gbb"], writes=[K_("gs")], out=gs[:],
                         in0=ps[:, pg:pg + 2, :].rearrange("p b n -> p (b n)"), in1=gbb[:, b * D:(b + 1) * D],
                         op=ALU.add)
                    P.op("act", "activation", reads=[K_("gs")], writes=[K_("gs")], out=gs[:], in_=gs[:], func=AF.Sigmoid)
                    if b < 2:
                        pt_ = self.psb()
                        ptb = ps[:, pt_, :].bitcast(BF16).rearrange("p (k t) -> p k t", k=8)
                        for kc in range(KC):
                            P.op("pe", "transpose", reads=["yt%d_%d" % (b, k), "ident"], writes=self.pk(pt_),
                                 out=ptb[:, kc, :], in_=yt[b][k][:, kc * 128:(kc + 1) * 128], identity=self.ident[:])
                        P.op("act", "activation", reads=self.pk(pt_), writes=[K_("ybT")], out=ybT[:], in_=ptb, func=AF.Copy)
                        ysrc, ykey_ = ybT, K_("ybT")
                    else:
                        ysrc, ykey_ = ybTc[k], K_("ybTc")
                    yield
                    pb = self.psb(2)
                    for n in range(2):
                        for kc in range(KC):
                            P.op("pe", "matmul", reads=[ykey_, "wp%d" % b], writes=self.pk(pb + n),
                                 out=ps[:, pb + n, :], lhsT=ysrc[:, kc, :], rhs=wp[b][:, kc, n * 512:(n + 1) * 512],
                                 start=(kc == 0), stop=(kc == KC - 1))
                    pball = ps[:, pb:pb + 2, :].rearrange("p b n -> p (b n)")
                    if b == 0:
                        P.op("dve", "tensor_tensor", reads=self.pk(pb, 2) + [K_("gs")], writes=[K_("mg")], out=mg[:],
                             in0=pball, in1=gs[:], op=ALU.mult)
                    else:
                        P.op("dve", "tensor_tensor", reads=self.pk(pb, 2) + [K_("gs")], writes=[K_("gs")], out=gs[:],
                             in0=pball, in1=gs[:], op=ALU.mult)
                        P.op("pool", "tensor_tensor", reads=[K_("gs"), K_("mg")], writes=[K_("mg")], out=mg[:],
                             in0=gs[:], in1=mg[:], op=ALU.add)
                    yield
                P.op("act", "activation", reads=[K_("mg")], writes=[K_("mgb")], out=mgb[:], in_=mg[:], func=AF.Copy)
                pt_ = self.psb()
                ptb = ps[:, pt_, :].bitcast(BF16).rearrange("p (k t) -> p k t", k=8)
                for kc in range(KC):
                    P.op("pe", "transpose", reads=[K_("mgb"), "ident"], writes=self.pk(pt_), out=ptb[:, kc, :],
                         in_=mgb[:, kc * 128:(kc + 1) * 128], identity=self.ident[:])
                P.op("dve", "tensor_copy", reads=self.pk(pt_), writes=[K_("mT")], out=mT[:], in_=ptb)
                yield
                po = self.psb(2)
                for n in range(2):
                    for kc in range(KC):
                        P.op("pe", "matmul", reads=[K_("mT"), "wo"], writes=self.pk(po + n), out=ps[:, po + n, :],
                             lhsT=mT[:, kc, :], rhs=wo[:, kc, n * 512:(n + 1) * 512], start=(kc == 0),
                             stop=(kc == KC - 1))
                P.op("dve", "tensor_tensor", reads=self.pk(po, 2) + ["xt%d" % sl], writes=[K_("xo")], out=xq[:],
                     in0=ps[:, po:po + 2, :].rearrange("p b n -> p (b n)"), in1=self.xt[sl][:], op=ALU.add)
                if not last:
                    P.op("pool", "dma_start", reads=[K_("xo")], writes=["x1_%d_%d" % (s, c)], dma=K_("xo"),
                         out=self.X1[s, c * 128:(c + 1) * 128, :], in_=xq[:])
                else:
                    P.op("act", "activation", reads=[K_("xo")], writes=["junk%d" % sl, K_("fs")], out=self.junk[sl][:],
                         in_=xq[:], func=AF.Square, accum_out=fs[:, 0:1])
                    P.op("act", "activation", reads=[K_("fs"), "epsb"], writes=[K_("fs")], out=fs[:, 1:2], in_=fs[:, 0:1],
                         func=AF.Ln, scale=1.0 / D, bias=self.epsb[:])
                    P.op("act", "activation", reads=[K_("fs")], writes=[K_("fs")], out=fs[:, 2:3], in_=fs[:, 1:2],
                         func=AF.Exp, scale=-0.5)
                    P.op("dve", "scalar_tensor_tensor", reads=[K_("xo"), K_("fs"), "fnw"], writes=[K_("xo")],
                         out=xq[:], in0=xq[:], scalar=fs[:, 2:3], in1=fnw[:], op0=ALU.mult, op1=ALU.mult)
                    P.op("pool", "dma_start", reads=[K_("xo")], writes=["out_%d_%d" % (s, c)], dma=K_("xo"),
                         out=self.out[s, c * 128:(c + 1) * 128, :], in_=xq[:])

            if NS >= 2:
                items = [[(s, c, s) for s in range(NS)] for c in range(self.NCH)]
            else:
                items = [[(0, c + e, e) for e in range(2) if c + e < self.NCH] for c in range(0, self.NCH, 2)]
            for group in items:
                alive = [chunk_gen(*it) for it in group]
                while alive:
                    for g in list(alive):
                        try:
                            next(g)
                        except StopIteration:
                            alive.remove(g)
            P.emit()


def host_layout(S, norm_w, w_in, conv_w, conv_b, dt_bias, a_log, d_skip, ssm_norm_w, sinks, f_bias, gate_bias,
                w_proj, w_out, final_norm_w):
    L = DEPTH
    cols = _col_order()
    win = np.ascontiguousarray(
        np.asarray(w_in, np.float32)[:, :, cols].reshape(L, KC, 128, NT).transpose(0, 2, 1, 3))
    wproj = np.ascontiguousarray(np.asarray(w_proj, np.float32).reshape(L, 3, KC, 128, D).transpose(0, 1, 3, 2, 4))
    wout = np.ascontiguousarray(np.asarray(w_out, np.float32).reshape(L, KC, 128, D).transpose(0, 2, 1, 3))
    ppart = np.zeros((128, L * 88), np.float32)
    prow = np.zeros((L * 4176 + 1024,), np.float32)
    for l in range(L):
        ppart[:, l * 88:l * 88 + 8] = np.asarray(norm_w[l]).reshape(KC, 128).T
        cw = np.asarray(conv_w[l]).reshape(4, 16, 128)
        ppart[:, l * 88 + 8:l * 88 + 72] = cw.transpose(2, 1, 0).reshape(128, 64)
        ppart[:, l * 88 + 72:l * 88 + 88] = np.asarray(conv_b[l]).reshape(16, 128).T
        o = l * 4176
        prow[o:o + 16] = dt_bias[l]
        prow[o + 16:o + 32] = a_log[l]
        prow[o + 32:o + 48] = d_skip[l]
        prow[o + 48:o + 64] = sinks[l]
        prow[o + 64:o + 80] = f_bias[l]
        prow[o + 80:o + 1104] = ssm_norm_w[l]
        prow[o + 1104:o + 4176] = np.asarray(gate_bias[l]).reshape(-1)
    prow[L * 4176:] = final_norm_w
    pos = np.arange(S, dtype=np.float32)
    inv = (np.float32(10000.0) ** (-np.arange(0, 64, 2, dtype=np.float32) / np.float32(64))).astype(np.float32)
    ang = (pos[:, None] * inv[None, :]).astype(np.float32)
    cos, sin = np.cos(ang).astype(np.float32), np.sin(ang).astype(np.float32)
    rope = np.zeros((2, 128, S), np.float32)
    for p in range(128):
        rope[0, p] = cos[:, p % 32]
        rope[1, p] = sin[:, p % 32] * (-1.0 if (p % 64) < 32 else 1.0)
    return dict(win=win, wproj=wproj, wout=wout, ppart=ppart, prow=prow, rope=rope)


_NC_CACHE = {}


def kernel(x, norm_w, w_in, conv_w, conv_b, dt_bias, a_log, d_skip, ssm_norm_w, sinks, f_bias, gate_bias,
           w_proj, w_out, final_norm_w):
    x = np.asarray(x, np.float32)
    B, S, _ = x.shape
    nseq = B // NCORES
    shared = host_layout(S, norm_w, w_in, conv_w, conv_b, dt_bias, a_log, d_skip, ssm_norm_w, sinks, f_bias,
                         gate_bias, w_proj, w_out, final_norm_w)
    key = (S, nseq)
    if key not in _NC_CACHE:
        _NC_CACHE[key] = Builder(S, nseq).build()
    nc = _NC_CACHE[key]
    in_maps = []
    for c in range(NCORES):
        m = dict(shared)
        m["x"] = np.ascontiguousarray(x[c * nseq:(c + 1) * nseq])
        in_maps.append(m)
    res = run_bass_kernel_spmd(nc, in_maps, core_ids=list(range(NCORES)))
    return np.concatenate([r["out"] for r in res.results], axis=0).astype(np.float32)
```

```python
import math
from contextlib import ExitStack

import numpy as np
import concourse.bass as bass
import concourse.mybir as mybir
from concourse.bass_utils import run_bass_kernel_spmd

F32 = mybir.dt.float32
BF16 = mybir.dt.bfloat16
AF = mybir.ActivationFunctionType
ALU = mybir.AluOpType

D = 1024
KC = 8
DEPTH = 2
NCORES = 8
EPS = 1e-6
ENGS = ("pe", "act", "dve", "pool", "sp")

A0, A_N = 0, 3088
B0, B_N = 3088, 3840
C0, C_G = 6928, 2056
G0, G_N = 6928 + 2 * 2056, 3072
NT = G0 + G_N
SWA_QORDER = [0, 4, 1, 5, 2, 6, 3, 7, 8, 12, 9, 13, 10, 14, 11, 15]


def _col_order():
    o = {}
    off = 0
    names = [("a_xbc", 2048), ("a_z", 1024), ("a_dt", 16), ("b_q", 1024), ("b_k", 256), ("b_v", 256),
             ("b_z", 1024), ("c_q", 1024), ("c_k", 1024), ("c_v", 1024), ("c_f", 16), ("c_z", 1024),
             ("gates", 3072)]
    for n, s in names:
        o[n] = off
        off += s
    cols = []
    cols += list(range(o["a_xbc"], o["a_xbc"] + 2048))
    cols += list(range(o["a_z"], o["a_z"] + 1024))
    cols += list(range(o["a_dt"], o["a_dt"] + 16))
    assert len(cols) == A_N
    q = [o["b_q"] + h * 64 + d for h in SWA_QORDER for d in range(64)]
    qs = [o["b_q"] + h * 64 + (d + 32) % 64 for h in SWA_QORDER for d in range(64)]
    k = [o["b_k"] + h * 64 + d for h in range(4) for d in range(64)]
    ks = [o["b_k"] + h * 64 + (d + 32) % 64 for h in range(4) for d in range(64)]
    cols += q + k + qs + ks
    cols += list(range(o["b_v"], o["b_v"] + 256))
    cols += list(range(o["b_z"], o["b_z"] + 1024))
    assert len(cols) == B0 + B_N
    for hg in range(2):
        for nm in ("c_q", "c_k", "c_v", "c_z"):
            cols += list(range(o[nm] + hg * 512, o[nm] + hg * 512 + 512))
        cols += list(range(o["c_f"] + hg * 8, o["c_f"] + hg * 8 + 8))
    assert len(cols) == G0
    cols += list(range(o["gates"], o["gates"] + 3072))
    assert len(cols) == NT
    return np.array(cols, dtype=np.int64)


class Prog:
    def __init__(self, nc, stack, same_engine_sync=True):
        self.nc = nc
        self.stack = stack
        self.same = same_engine_sync
        self.ops = []
        self.last_w = {}
        self.readers = {}
        self.eng_sem = {e: stack.enter_context(nc.semaphore("s_" + e)) for e in ENGS}
        self.cnt = {e: 0 for e in ENGS}
        self.dsem = {}
        self.dcnt = {}
        self.known = {e: {} for e in ENGS}
        self.done_ops = 0

    def op(self, eng, meth, reads=(), writes=(), dma=None, **kw):
        idx = len(self.ops)
        deps = set()
        for k in reads:
            if k in self.last_w:
                deps.add(self.last_w[k])
        for k in writes:
            if k in self.last_w:
                deps.add(self.last_w[k])
            for r in self.readers.get(k, ()):
                deps.add(r)
        deps.discard(idx)
        best = {}
        for d in deps:
            od = self.ops[d]
            kk = ("d", od["dma"]) if od["dma"] is not None else ("e", od["eng"])
            if kk not in best or best[kk] < d:
                best[kk] = d
        deps = set(best.values())
        for k in reads:
            self.readers.setdefault(k, []).append(idx)
        for k in writes:
            self.last_w[k] = idx
            self.readers[k] = []
        self.ops.append(dict(eng=eng, meth=meth, kw=kw, deps=deps, dma=dma, sig=False, ev=None))
        return idx

    def emit(self, final=False):
        nc, ops = self.nc, self.ops
        import os as _os
        if _os.environ.get("OPS_LIMIT"):
            del ops[int(_os.environ["OPS_LIMIT"]):]
        lo = self.done_ops
        new = range(lo, len(ops))
        for i in new:
            o = ops[i]
            if o["dma"] is not None:
                o["sig"] = True
            for d in o["deps"]:
                od = ops[d]
                if d < lo:
                    continue
                if od["dma"] is not None or od["eng"] != o["eng"] or o["dma"] is not None:
                    od["sig"] = True
                elif self.same and o["eng"] != "pe":
                    od["sig"] = True
        for i in new:
            o = ops[i]
            if o["dma"] is not None:
                k = o["dma"]
                if k not in self.dsem:
                    self.dsem[k] = self.stack.enter_context(nc.semaphore("d_" + str(k)))
                    self.dcnt[k] = 0
                self.dcnt[k] += 16
                o["ev"] = (self.dsem[k], self.dcnt[k], "d_" + str(k))
            elif o["sig"]:
                self.cnt[o["eng"]] += 1
                o["ev"] = (self.eng_sem[o["eng"]], self.cnt[o["eng"]], o["eng"])
        per_eng = {e: [] for e in ENGS}
        for i in new:
            per_eng[ops[i]["eng"]].append(i)
        same = self.same

        def body(ename):
            def f(eng):
                kn = self.known[ename]
                for i in per_eng[ename]:
                    o = ops[i]
                    need = {}
                    for d in o["deps"]:
                        if d < lo:
                            continue
                        od = ops[d]
                        ev = od["ev"]
                        if ev is None:
                            continue
                        sem, val, name = ev
                        if (od["dma"] is None and od["eng"] == ename and o["dma"] is None
                                and (ename == "pe" or not same)):
                            continue
                        if kn.get(name, 0) >= val:
                            continue
                        if name not in need or need[name][1] < val:
                            need[name] = (sem, val)
                    for name, (sem, val) in need.items():
                        eng.wait_ge(sem, val)
                        kn[name] = val
                    ins = getattr(eng, o["meth"])(**o["kw"])
                    if o["ev"] is not None:
                        sem, val, name = o["ev"]
                        ins.then_inc(sem, 16 if o["dma"] is not None else 1)
                if ename == "sp":
                    for k, s in self.dsem.items():
                        if kn.get("d_" + str(k), 0) < self.dcnt[k]:
                            eng.wait_ge(s, self.dcnt[k])
                            kn["d_" + str(k)] = self.dcnt[k]
            return f

        with nc.Block() as block:
            block.tensor(body("pe"))
            block.scalar(body("act"))
            block.vector(body("dve"))
            block.gpsimd(body("pool"))
            block.sync(body("sp"))
        self.done_ops = len(ops)


class Builder:
    def __init__(self, S, NSEQ, debug=False, layers=DEPTH, phases="ABCD"):
        self.S, self.NSEQ, self.debug, self.layers, self.phases = S, NSEQ, debug, layers, phases
        self.NCH = S // 128
        self.NTL = S // 512
        nc = self.nc = bass.Bass("TRN2", target_bir_lowering=False)
        L = DEPTH
        okind = "ExternalOutput" if debug else "Internal"
        self.x = nc.dram_tensor("x", [NSEQ, S, D], F32, kind="ExternalInput").ap()
        self.win = nc.dram_tensor("win", [L, 128, KC, NT], F32, kind="ExternalInput").ap()
        self.wproj = nc.dram_tensor("wproj", [L, 3, 128, KC, D], F32, kind="ExternalInput").ap()
        self.wout = nc.dram_tensor("wout", [L, 128, KC, D], F32, kind="ExternalInput").ap()
        self.ppart = nc.dram_tensor("ppart", [128, L * (8 + 64 + 16)], F32, kind="ExternalInput").ap()
        self.prow = nc.dram_tensor("prow", [L * (80 + 1024 + 3072) + 1024], F32, kind="ExternalInput").ap()
        self.rope = nc.dram_tensor("rope", [2, 128, S], F32, kind="ExternalInput").ap()
        self.out = nc.dram_tensor("out", [NSEQ, S, D], F32, kind="ExternalOutput").ap()
        self.Y = nc.dram_tensor("ybr", [3, NSEQ, S, D], BF16, kind=okind).ap()
        self.X1 = nc.dram_tensor("x1", [NSEQ, S, D], F32, kind=okind).ap()
        self.Y2T = nc.dram_tensor("y2t", [NSEQ, D, S], BF16, kind=okind).ap()
        self.pools = {"gen": list(range(8))}
        self.prr = {}

    def psb(self, n=1, pool="gen"):
        banks = self.pools[pool]
        r = self.prr.get(pool, 0)
        if n == 2:
            assert len(banks) % 2 == 0
            if r % 2:
                r += 1
            b = banks[r % len(banks)]
            self.prr[pool] = r + 2
            return b
        b = banks[r % len(banks)]
        self.prr[pool] = r + 1
        return b

    def pk(self, b, n=1):
        return ["ps%d" % (b + i) for i in range(n)]

    def build(self):
        nc = self.nc
        with ExitStack() as gst:
            self.P = P = Prog(nc, gst)
            sb = lambda name, shape, dt: gst.enter_context(nc.sbuf_tensor(name, shape, dt))
            self.ps = gst.enter_context(nc.psum_tensor("ps", [128, 8, 512], F32))
            self.ident = sb("ident", [128, 128], BF16)
            self.identf = sb("identf", [128, 128], F32)
            self.tri = sb("tri", [128, 128], F32)
            self.elast = sb("elast", [128, 128], F32)
            self.onesf = sb("onesf", [128, 128], F32)
            self.onesb = sb("onesb", [128, 64], BF16)
            self.mle = sb("mle", [128, 128], BF16)
            self.mgt = sb("mgt", [128, 128], BF16)
            self.mlef = sb("mlef", [128, 128], F32)
            self.sel = sb("sel", [8, 8, 65], BF16)
            self.ppt = sb("ppt", [128, DEPTH * 88], F32)
            self.prw = sb("prw", [128, DEPTH * 80], F32)
            self.abc = sb("abc", [128, DEPTH * 16], F32)
            self.esink = sb("esink", [128, DEPTH * 16], F32)
            self.xt = [sb("xt%d" % i, [128, D], F32) for i in range(2)]
            self.xn = [sb("xn%d" % i, [128, D], BF16) for i in range(2)]
            self.junk = [sb("junk%d" % i, [128, D], BF16) for i in range(2)]
            self.st4 = [sb("st4_%d" % i, [128, 4], F32) for i in range(2)]
            self.xslot = 0
            self.setup_consts()
            import os as _os
            if _os.environ.get("SETUP_LIMIT"):
                lim = int(_os.environ["SETUP_LIMIT"])
                del P.ops[lim:]
            P.emit()
            for l in range(self.layers):
                if "A" in self.phases:
                    self.phase_A(l)
                if "B" in self.phases:
                    self.phase_B(l)
                if "C" in self.phases:
                    for hg in range(2):
                        self.phase_C(l, hg)
                if "D" in self.phases:
                    self.phase_D(l)
        return nc

    def setup_consts(self):
        P = self.P
        P.op("pool", "memset", writes=["ident"], ap=self.ident[:], constant=1.0)
        P.op("pool", "affine_select", reads=["ident"], writes=["ident"], out=self.ident[:], in_=self.ident[:],
             pattern=[[-1, 128]], compare_op=ALU.is_equal, fill=0.0, base=0, channel_multiplier=1)
        P.op("pool", "memset", writes=["identf"], ap=self.identf[:], constant=1.0)
        P.op("pool", "affine_select", reads=["identf"], writes=["identf"], out=self.identf[:], in_=self.identf[:],
             pattern=[[-1, 128]], compare_op=ALU.is_equal, fill=0.0, base=0, channel_multiplier=1)
        P.op("pool", "memset", writes=["tri"], ap=self.tri[:], constant=1.0)
        P.op("pool", "affine_select", reads=["tri"], writes=["tri"], out=self.tri[:], in_=self.tri[:],
             pattern=[[1, 128]], compare_op=ALU.is_ge, fill=0.0, base=0, channel_multiplier=-1)
        P.op("pool", "memset", writes=["mlef"], ap=self.mlef[:], constant=1.0)
        P.op("pool", "affine_select", reads=["mlef"], writes=["mlef"], out=self.mlef[:], in_=self.mlef[:],
             pattern=[[1, 128]], compare_op=ALU.is_ge, fill=0.0, base=0, channel_multiplier=-1)
        P.op("pool", "memset", writes=["mle"], ap=self.mle[:], constant=1.0)
        P.op("pool", "affine_select", reads=["mle"], writes=["mle"], out=self.mle[:], in_=self.mle[:],
             pattern=[[1, 128]], compare_op=ALU.is_ge, fill=0.0, base=0, channel_multiplier=-1)
        P.op("pool", "memset", writes=["mgt"], ap=self.mgt[:], constant=1.0)
        P.op("pool", "affine_select", reads=["mgt"], writes=["mgt"], out=self.mgt[:], in_=self.mgt[:],
             pattern=[[-1, 128]], compare_op=ALU.is_gt, fill=0.0, base=0, channel_multiplier=1)
        P.op("pool", "memset", writes=["elast"], ap=self.elast[:], constant=1.0)
        P.op("pool", "affine_select", reads=["elast"], writes=["elast"], out=self.elast[:], in_=self.elast[:],
             pattern=[[0, 128]], compare_op=ALU.is_equal, fill=0.0, base=-127, channel_multiplier=1)
        P.op("pool", "memset", writes=["onesf"], ap=self.onesf[:], constant=1.0)
        P.op("pool", "memset", writes=["onesb"], ap=self.onesb[:], constant=1.0)
        P.op("pool", "memset", writes=["sel"], ap=self.sel[:], constant=8.0)
        P.op("pool", "affine_select", reads=["sel"], writes=["sel"], out=self.sel[:], in_=self.sel[:],
             pattern=[[1, 8], [0, 65]], compare_op=ALU.is_equal, fill=0.0, base=0, channel_multiplier=-1)
        P.op("pool", "affine_select", reads=["sel"], writes=["sel"], out=self.sel[:], in_=self.sel[:],
             pattern=[[0, 8], [1, 65]], compare_op=ALU.is_equal, fill=0.0, base=-64, channel_multiplier=0)
        P.op("sp", "dma_start", writes=["ppt"], dma="ppt", out=self.ppt[:], in_=self.ppart)
        for l in range(DEPTH):
            P.op("sp", "dma_start", writes=["prw"], dma="prw", out=self.prw[:, l * 80:(l + 1) * 80],
                 in_=self.prow[l * 4176:l * 4176 + 80].partition_broadcast(128))
        for l in range(DEPTH):
            P.op("act", "activation", reads=["prw"], writes=["abc"], out=self.abc[:, l * 16:(l + 1) * 16],
                 in_=self.prw[:, l * 80 + 16:l * 80 + 32], func=AF.Exp)
            P.op("dve", "tensor_scalar", reads=["abc"], writes=["abc"], out=self.abc[:, l * 16:(l + 1) * 16],
                 in0=self.abc[:, l * 16:(l + 1) * 16], scalar1=-1.0, scalar2=None, op0=ALU.mult)
            P.op("act", "activation", reads=["prw"], writes=["esink"], out=self.esink[:, l * 16:(l + 1) * 16],
                 in_=self.prw[:, l * 80 + 48:l * 80 + 64], func=AF.Exp)

    def nw(self, l):
        return self.ppt[:, l * 88:l * 88 + 8]

    def convw(self, l, b, k):
        o = l * 88 + 8 + b * 4 + k
        return self.ppt[:, o:o + 1]

    def convb(self, l, b):
        o = l * 88 + 72 + b
        return self.ppt[:, o:o + 1]

    def rowp(self, l, i):
        return self.prw[:, l * 80 + i * 16:l * 80 + (i + 1) * 16]

    def xsrc(self, l, s, c):
        src = self.x if l == 0 else self.X1
        return src[s, c * 128:(c + 1) * 128, :], ("xin" if l == 0 else "x1_%d_%d" % (s, c))

    def h_chunk(self, l, s, c, hT, hkey, col0, slot=None):
        P = self.P
        if slot is None:
            sl = self.xslot
            self.xslot ^= 1
        else:
            sl = slot
        xt, xn, junk, st4 = self.xt[sl], self.xn[sl], self.junk[sl], self.st4[sl]
        src, skey = self.xsrc(l, s, c)
        P.op("sp", "dma_start", reads=[skey], writes=["xt%d" % sl], dma="xt%d" % sl, out=xt[:], in_=src)
        P.op("act", "activation", reads=["xt%d" % sl], writes=["junk%d" % sl, "st4_%d" % sl],
             out=junk[:], in_=xt[:], func=AF.Square, accum_out=st4[:, 0:1])
        P.op("act", "activation", reads=["st4_%d" % sl, "epsb"], writes=["st4_%d" % sl], out=st4[:, 1:2], in_=st4[:, 0:1],
             func=AF.Ln, scale=1.0 / D, bias=self.epsb[:])
        P.op("act", "activation", reads=["st4_%d" % sl], writes=["st4_%d" % sl], out=st4[:, 2:3], in_=st4[:, 1:2],
             func=AF.Exp, scale=-0.5)
        P.op("act", "activation", reads=["xt%d" % sl, "st4_%d" % sl], writes=["xn%d" % sl], out=xn[:], in_=xt[:],
             func=AF.Identity, scale=st4[:, 2:3])
        b = self.psb()
        ptb = self.ps[:, b, :].bitcast(BF16).rearrange("p (k t) -> p k t", k=8)
        for kc in range(KC):
            P.op("pe", "transpose", reads=["xn%d" % sl, "ident"], writes=self.pk(b), out=ptb[:, kc, :],
                 in_=xn[:, kc * 128:(kc + 1) * 128], identity=self.ident[:])
        P.op("dve", "tensor_tensor", reads=self.pk(b) + ["ppt"], writes=[hkey],
             out=hT[:, :, col0:col0 + 128], in0=ptb,
             in1=self.nw(l).unsqueeze(2).broadcast_to([128, 8, 128]), op=ALU.mult)
        return sl

    def load_w(self, wt, key, l, c0, n, src=None):
        P = self.P
        src = self.win[l] if src is None else src
        step = 1024
        for kc in range(KC):
            for o in range(0, n, step):
                m = min(step, n - o)
                P.op("pool", "dma_start", writes=[key], dma=key, out=wt[:, kc, o:o + m],
                     in_=src[:, kc, c0 + o:c0 + o + m])

    def proj_tm(self, dst_bank, hT, hkey, col0, wt, wkey, wc0, n, poff=0):
        P = self.P
        for kc in range(KC):
            P.op("pe", "matmul", reads=[hkey, wkey], writes=self.pk(dst_bank),
                 out=self.ps[:, dst_bank, poff:poff + n], lhsT=hT[:, kc, col0:col0 + 128],
                 rhs=wt[:, kc, wc0:wc0 + n], start=(kc == 0), stop=(kc == KC - 1))

    def proj_fm(self, dst_bank, hT, hkey, ntok, wt, wkey, wc0, m, first=True):
        P = self.P
        for kc in range(KC):
            P.op("pe", "matmul", reads=[hkey, wkey], writes=self.pk(dst_bank),
                 out=self.ps[0:m, dst_bank, 0:ntok], lhsT=wt[:, kc, wc0:wc0 + m],
                 rhs=hT[:, kc, 0:ntok], start=(first and kc == 0), stop=(kc == KC - 1))

    def phase_A(self, l):
        nc, P, S = self.nc, self.P, self.S
        with ExitStack() as st:
            self.uid = getattr(self, "uid", 0) + 1
            sb = lambda name, shape, dt, _u=self.uid: st.enter_context(nc.sbuf_tensor("%s_u%d" % (name, _u), shape, dt))
            wa = sb("wa", [128, KC, A_N], BF16)
            hTs = [sb("hT%d" % i, [128, KC, 512], BF16) for i in range(1)]
            Ub = [sb("Ub%d" % i, [128, 515], F32) for i in range(2)]
            Ucar = sb("Ucar", [128, 16, 3], F32)
            acc = [sb("acc%d" % i, [128, 512], F32) for i in range(2)]
            xsTs = [sb("xsT%d" % i, [128, 8, 512], BF16) for i in range(1)]
            BTs = [sb("BT%d" % i, [128, 4, 512], BF16) for i in range(1)]
            CTs = [sb("CT%d" % i, [128, 4, 512], BF16) for i in range(1)]
            H = sb("H", [128, 16, 64], F32)
            Hb = sb("Hb", [128, 16, 64], BF16)
            Htmp = sb("Htmp", [128, 16, 64], F32)
            bufsets = []
            for k in range(2):
                bufsets.append(dict(
                    sm=sb("sm%d" % k, [128, 12, 16], F32), rhsall=sb("rhsall%d" % k, [128, 16, 128], F32),
                    cbm=sb("cbm%d" % k, [128, 4, 128], F32), MT=sb("MT%d" % k, [128, 16, 128], BF16),
                    xstm=sb("xstm%d" % k, [128, 16, 64], F32), xdt=sb("xdt%d" % k, [128, 16, 64], BF16),
                    xw=sb("xw%d" % k, [128, 16, 64], BF16), Btm=sb("Btm%d" % k, [128, 4, 128], BF16),
                    sz=sb("sz%d" % k, [128, D], F32), y1=sb("y1%d" % k, [128, 16, 64], F32),
                    y2=sb("y2%d" % k, [128, 16, 64], F32), junkA=sb("junkA%d" % k, [128, 256], BF16)))
            yo = [sb("yo%d" % i, [128, D], BF16) for i in range(2)]
            snw = sb("snw", [128, D], F32)
            self.epsb = sb("epsb", [128, 1], F32)
            P.op("dve", "memset", writes=["epsb"], ap=self.epsb[:], constant=EPS)
            self.load_w(wa, "wa", l, A0, A_N)
            P.op("sp", "dma_start", writes=["snw"], dma="snw", out=snw[:],
                 in_=self.prow[l * 4176 + 80:l * 4176 + 80 + 1024].partition_broadcast(128))
            ps = self.ps
            def prologue(s, t, par):
                hT, xsT, BT, CT = hTs[par], xsTs[par], BTs[par], CTs[par]
                hk, xk, bk, ck = "hT%d" % par, "xsT%d" % par, "BT%d" % par, "CT%d" % par
                if t == 0:
                    P.op("pool", "memset", writes=["Ucar%d" % b for b in range(16)], ap=Ucar[:], constant=0.0)
                for c4 in range(4):
                    self.h_chunk(l, s, t * 4 + c4, hT, hk, c4 * 128)
                    yield
                for b in range(16):
                    pb = self.psb()
                    self.proj_fm(pb, hT, hk, 512, wa, "wa", b * 128, 128)
                    ukey = "Ub%d" % (b % 2)
                    U_ = Ub[b % 2]
                    P.op("pool", "tensor_copy", reads=["Ucar%d" % b], writes=[ukey], out=U_[:, 0:3], in_=Ucar[:, b, :])
                    P.op("act", "activation", reads=self.pk(pb), writes=[ukey], out=U_[:, 3:515],
                         in_=ps[:, pb, :], func=AF.Copy)
                    a = acc[b % 2]
                    akey = "acc%d" % (b % 2)
                    P.op("dve", "tensor_scalar", reads=[ukey, "ppt"], writes=[akey], out=a[:], in0=U_[:, 0:512],
                         scalar1=self.convw(l, b, 0), scalar2=None, op0=ALU.mult)
                    for k in range(1, 4):
                        P.op("dve", "scalar_tensor_tensor", reads=[ukey, akey, "ppt"], writes=[akey], out=a[:],
                             in0=U_[:, k:k + 512], scalar=self.convw(l, b, k), in1=a[:],
                             op0=ALU.mult, op1=ALU.add)
                    if b < 8:
                        dst, dkey = xsT[:, b, :], xk
                    elif b < 12:
                        dst, dkey = BT[:, b - 8, :], bk
                    else:
                        dst, dkey = CT[:, b - 12, :], ck
                    P.op("act", "activation", reads=[akey, "ppt"], writes=[dkey], out=dst, in_=a[:],
                         func=AF.Silu, bias=self.convb(l, b))
                    P.op("pool", "tensor_copy", reads=[ukey], writes=["Ucar%d" % b], out=Ucar[:, b, :], in_=U_[:, 512:515])
                    yield

            def chunk(s, t, par, c4, k):
                hT, xsT, BT, CT = hTs[par], xsTs[par], BTs[par], CTs[par]
                hk, xk, bk, ck = "hT%d" % par, "xsT%d" % par, "BT%d" % par, "CT%d" % par
                bs = bufsets[k]
                sm, rhsall, cbm, MT, xstm, xdt, xw, Btm, sz, y1, y2, junkA = [bs[n] for n in (
                    "sm", "rhsall", "cbm", "MT", "xstm", "xdt", "xw", "Btm", "sz", "y1", "y2", "junkA")]
                Eh = rhsall
                dec = rhsall
                K_ = lambda n: "%s_s%d" % (n, k)
                if True:
                    c = t * 4 + c4
                    cs = slice(c4 * 128, (c4 + 1) * 128)
                    pd = self.psb()
                    self.proj_tm(pd, hT, hk, c4 * 128, wa, "wa", 3072, 16)
                    dtr, dt_, adt, acum, nacum, lastbc, dS, ea, cd, dtS, e1 = [sm[:, i, :] for i in range(11)]
                    P.op("dve", "tensor_tensor", reads=self.pk(pd) + ["prw"], writes=[K_("sm0")], out=dtr,
                         in0=ps[:, pd, 0:16], in1=self.rowp(l, 0), op=ALU.add)
                    P.op("act", "activation", reads=[K_("sm0")], writes=[K_("sm10")], out=e1, in_=dtr, func=AF.Exp)
                    P.op("act", "activation", reads=[K_("sm10")], writes=[K_("sm1")], out=dt_, in_=e1, func=AF.Ln, bias=1.0)
                    P.op("dve", "tensor_tensor", reads=[K_("sm1"), "abc"], writes=[K_("sm2")], out=adt, in0=dt_,
                         in1=self.abc[:, l * 16:(l + 1) * 16], op=ALU.mult)
                    pa = self.psb()
                    P.op("pe", "matmul", reads=["tri", K_("sm2")], writes=self.pk(pa), out=ps[:, pa, 0:16],
                         lhsT=self.tri[:], rhs=adt, start=True, stop=True)
                    P.op("dve", "tensor_copy", reads=self.pk(pa), writes=[K_("sm3")], out=acum, in_=ps[:, pa, 0:16])
                    P.op("pe", "matmul", reads=["elast", K_("sm3")], writes=self.pk(pa), out=ps[:, pa, 16:32],
                         lhsT=self.elast[:], rhs=acum, start=True, stop=True)
                    P.op("dve", "tensor_copy", reads=self.pk(pa), writes=[K_("sm5")], out=lastbc, in_=ps[:, pa, 16:32])
                    P.op("dve", "tensor_tensor", reads=[K_("sm5"), K_("sm3")], writes=[K_("sm6")], out=dS, in0=lastbc, in1=acum,
                         op=ALU.subtract)
                    P.op("act", "activation", reads=[K_("sm6")], writes=[K_("sm6")], out=dS, in_=dS, func=AF.Exp)
                    P.op("act", "activation", reads=[K_("sm3")], writes=[K_("sm7")], out=ea, in_=acum, func=AF.Exp)
                    P.op("act", "activation", reads=[K_("sm5")], writes=[K_("sm8")], out=cd, in_=lastbc, func=AF.Exp)
                    P.op("dve", "tensor_tensor", reads=[K_("sm1"), K_("sm6")], writes=[K_("sm9")], out=dtS, in0=dt_, in1=dS,
                         op=ALU.mult)
                    yield
                    P.op("dve", "tensor_tensor", reads=["tri", K_("sm2")], writes=[K_("rhsall")] + [K_("Eh%d" % g_) for g_ in range(4)], out=rhsall[:],
                         in0=self.tri[:].unsqueeze(1).broadcast_to([128, 16, 128]),
                         in1=adt.unsqueeze(2).broadcast_to([128, 16, 128]), op=ALU.mult)
                    for g in range(4):
                        pg = self.psb()
                        P.op("pe", "matmul", reads=["onesf", K_("rhsall")], writes=self.pk(pg),
                             out=ps[:, pg, :], lhsT=self.onesf[:],
                             rhs=rhsall[:, 4 * g:4 * g + 4, :], start=True, stop=True)
                        for r in range(4):
                            h = 4 * g + r
                            P.op("dve", "tensor_scalar", reads=self.pk(pg) + [K_("sm3")], writes=[K_("Eh%d" % g)],
                                 out=Eh[:, h, :], in0=ps[:, pg, r * 128:(r + 1) * 128],
                                 scalar1=acum[:, h:h + 1], scalar2=0.0, op0=ALU.subtract, op1=ALU.min)
                        P.op("act", "activation", reads=[K_("Eh%d" % g)], writes=[K_("Eh%d" % g), K_("dec%d" % g)],
                             out=dec[:, 4 * g:4 * g + 4, :], in_=Eh[:, 4 * g:4 * g + 4, :], func=AF.Exp)
                    pc = self.psb()
                    for g in range(4):
                        P.op("pe", "matmul", reads=[bk, ck], writes=self.pk(pc),
                             out=ps[:, pc, g * 128:(g + 1) * 128], lhsT=BT[:, g, cs], rhs=CT[:, g, cs],
                             start=True, stop=True)
                    P.op("dve", "tensor_tensor", reads=self.pk(pc) + ["mlef"], writes=[K_("cbm")], out=cbm[:],
                         in0=ps[:, pc, :].rearrange("p (g l) -> p g l", g=4),
                         in1=self.mlef[:].unsqueeze(1).broadcast_to([128, 4, 128]), op=ALU.mult)
                    for g in range(4):
                        P.op("pool", "tensor_tensor", reads=[K_("dec%d" % g), K_("Eh%d" % g), K_("cbm")], writes=[K_("MT%d" % g)],
                             out=MT[:, 4 * g:4 * g + 4, :], in0=dec[:, 4 * g:4 * g + 4, :],
                             in1=cbm[:, g, :].unsqueeze(1).broadcast_to([128, 4, 128]), op=ALU.mult)
                    yield
                    px = self.psb()
                    pxb = ps[:, px, :].bitcast(BF16).rearrange("p (k t) -> p k t", k=8)
                    for b in range(8):
                        P.op("pe", "transpose", reads=[xk, "ident"], writes=self.pk(px), out=pxb[:, b, :],
                             in_=xsT[:, b, cs], identity=self.ident[:])
                    pxv = ps[:, px, :].bitcast(BF16).rearrange("p (h d) -> p h d", h=16)
                    P.op("act", "activation", reads=self.pk(px), writes=[K_("xstm")], out=xstm[:], in_=pxv, func=AF.Copy)
                    P.op("dve", "tensor_tensor", reads=[K_("xstm"), K_("sm1")], writes=[K_("xdt")], out=xdt[:], in0=xstm[:],
                         in1=dt_.unsqueeze(2).broadcast_to([128, 16, 64]), op=ALU.mult)
                    P.op("pool", "tensor_tensor", reads=[K_("xstm"), K_("sm9")], writes=[K_("xw")], out=xw[:], in0=xstm[:],
                         in1=dtS.unsqueeze(2).broadcast_to([128, 16, 64]), op=ALU.mult)
                    pbt = self.psb()
                    pbb = ps[:, pbt, 0:256].bitcast(BF16).rearrange("p (k t) -> p k t", k=4)
                    for g in range(4):
                        P.op("pe", "transpose", reads=[bk, "ident"], writes=self.pk(pbt), out=pbb[:, g, :],
                             in_=BT[:, g, cs], identity=self.ident[:])
                    P.op("act", "activation", reads=self.pk(pbt), writes=[K_("Btm")], out=Btm[:], in_=pbb, func=AF.Copy)
                    yield
                    P.op("act", "activation", reads=["H"], writes=["Hb"], out=Hb[:], in_=H[:], func=AF.Copy)
                    po = self.psb(2)
                    for g in range(4):
                        P.op("pe", "matmul", reads=[ck, "Hb"], writes=self.pk(po, 2),
                             out=ps[:, po + g // 2, (g % 2) * 256:(g % 2) * 256 + 256],
                             lhsT=CT[:, g, cs], rhs=Hb[:, 4 * g:4 * g + 4, :], start=True, stop=True)
                    poall = ps[:, po:po + 2, :].rearrange("p b (h d) -> p (b h) d", d=64)
                    P.op("dve", "tensor_tensor", reads=self.pk(po, 2) + [K_("sm7")], writes=[K_("y1")], out=y1[:], in0=poall,
                         in1=ea.unsqueeze(2).broadcast_to([128, 16, 64]), op=ALU.mult)
                    pst = self.psb(2)
                    for g in range(4):
                        P.op("pe", "matmul", reads=[K_("Btm"), K_("xw")], writes=self.pk(pst, 2),
                             out=ps[:, pst + g // 2, (g % 2) * 256:(g % 2) * 256 + 256],
                             lhsT=Btm[:, g, :], rhs=xw[:, 4 * g:4 * g + 4, :], start=True, stop=True)
                    pstall = ps[:, pst:pst + 2, :].rearrange("p b (h d) -> p (b h) d", d=64)
                    P.op("dve", "tensor_tensor", reads=["H", K_("sm8")], writes=["Htmp"], out=Htmp[:], in0=H[:],
                         in1=cd.unsqueeze(2).broadcast_to([128, 16, 64]), op=ALU.mult)
                    P.op("dve", "tensor_tensor", reads=self.pk(pst, 2) + ["Htmp"], writes=["H"], out=H[:], in0=pstall,
                         in1=Htmp[:], op=ALU.add)
                    yield
                    pyd = self.psb(2)
                    for h in range(16):
                        P.op("pe", "matmul", reads=[K_("MT%d" % (h // 4)), K_("xdt")], writes=self.pk(pyd, 2),
                             out=ps[:, pyd + h // 8, (h % 8) * 64:(h % 8) * 64 + 64],
                             lhsT=MT[:, h, :], rhs=xdt[:, h, :], start=True, stop=True)
                    pydall = ps[:, pyd:pyd + 2, :].rearrange("p b (h d) -> p (b h) d", d=64)
                    P.op("dve", "tensor_tensor", reads=self.pk(pyd, 2) + [K_("y1")], writes=[K_("y1")], out=y1[:], in0=pydall,
                         in1=y1[:], op=ALU.add)
                    P.op("pool", "tensor_tensor", reads=[K_("xstm"), "prw"], writes=[K_("y2")], out=y2[:], in0=xstm[:],
                         in1=self.rowp(l, 2).unsqueeze(2).broadcast_to([128, 16, 64]), op=ALU.mult)
                    P.op("pool", "tensor_tensor", reads=[K_("y1"), K_("y2")], writes=[K_("y2")], out=y2[:], in0=y1[:], in1=y2[:],
                         op=ALU.add)
                    yield
                    pz = self.psb(2)
                    for n in range(2):
                        self.proj_tm(pz + n, hT, hk, c4 * 128, wa, "wa", 2048 + n * 512, 512)
                    P.op("act", "activation", reads=self.pk(pz, 2), writes=[K_("sz")], out=sz[:],
                         in_=ps[:, pz:pz + 2, :].rearrange("p b n -> p (b n)"), func=AF.Silu)
                    y2f = y2[:].rearrange("p h d -> p (h d)")
                    P.op("dve", "tensor_tensor", reads=[K_("y2"), K_("sz")], writes=[K_("y2")], out=y2f, in0=y2f, in1=sz[:],
                         op=ALU.mult)
                    ss = sm[:, 11, 0:4]
                    for g in range(4):
                        P.op("act", "activation", reads=[K_("y2")], writes=[K_("junkA"), K_("sm11")], out=junkA[:],
                             in_=y2f[:, g * 256:(g + 1) * 256], func=AF.Square, accum_out=sm[:, 11, g:g + 1])
                    P.op("act", "activation", reads=[K_("sm11"), "epsb"], writes=[K_("sm11")], out=sm[:, 11, 4:8], in_=ss, func=AF.Ln,
                         scale=1.0 / 256, bias=self.epsb[:])
                    P.op("act", "activation", reads=[K_("sm11")], writes=[K_("sm11")], out=sm[:, 11, 8:12], in_=sm[:, 11, 4:8],
                         func=AF.Exp, scale=-0.5)
                    P.op("dve", "tensor_tensor", reads=[K_("y2"), K_("sm11")], writes=[K_("y2")],
                         out=y2[:].rearrange("p (g r) d -> p g (r d)", g=4),
                         in0=y2[:].rearrange("p (g r) d -> p g (r d)", g=4),
                         in1=sm[:, 11, 8:12].unsqueeze(2).broadcast_to([128, 4, 256]), op=ALU.mult)
                    yq = yo[c % 2]
                    P.op("pool", "tensor_tensor", reads=[K_("y2"), "snw"], writes=["yo%d" % (c % 2)], out=yq[:], in0=y2f,
                         in1=snw[:], op=ALU.mult)
                    P.op("pool", "dma_start", reads=["yo%d" % (c % 2)], writes=["Y0_%d_%d" % (s, c)],
                         dma="yo%d" % (c % 2), out=self.Y[0, s, c * 128:(c + 1) * 128, :], in_=yq[:])
                    yield

            items = [(s, t) for s in range(self.NSEQ) for t in range(self.NTL)]

            def run(gens):
                alive = list(gens)
                while alive:
                    for g in list(alive):
                        try:
                            next(g)
                        except StopIteration:
                            alive.remove(g)

            for i, (s, t) in enumerate(items):
                par = 0
                if t == 0:
                    P.op("dve", "memset", writes=["H"], ap=H[:], constant=0.0)
                run([prologue(s, t, par)])
                for c4 in (0, 2):
                    run([chunk(s, t, par, c4, 0), chunk(s, t, par, c4 + 1, 1)])
            P.emit()

    def phase_B(self, l):
        nc, P, S, NCH = self.nc, self.P, self.S, self.NCH
        with ExitStack() as st:
            self.uid = getattr(self, "uid", 0) + 1
            sb = lambda name, shape, dt, _u=self.uid: st.enter_context(nc.sbuf_tensor("%s_u%d" % (name, _u), shape, dt))
            wb = sb("wb", [128, KC, B_N], BF16)
            hT = sb("hT", [128, KC, 512], BF16)
            cosT = sb("cosT", [128, 512], F32)
            sinS = sb("sinS", [128, 512], F32)
            t1 = [sb("t1_%d" % i, [128, 512], F32) for i in range(2)]
            t2 = [sb("t2_%d" % i, [128, 512], F32) for i in range(2)]
            qrT = sb("qrT", [128, 8, 512], BF16)
            krT = sb("krT", [128, 2, S], BF16)
            V = sb("V", [128, NCH, 4, 65], BF16)
            sz = sb("sz", [128, D], F32)
            Pc = [sb("Pc%d" % i, [128, 512], BF16) for i in range(2)]
            Pp = [sb("Pp%d" % i, [128, 512], BF16) for i in range(2)]
            den = sb("den", [128, 16], F32)
            yf = sb("yf", [128, 16, 64], F32)
            yo = [sb("yo%d" % i, [128, D], BF16) for i in range(2)]
            self.epsb = sb("epsb", [128, 1], F32)
            P.op("dve", "memset", writes=["epsb"], ap=self.epsb[:], constant=EPS)
            self.load_w(wb, "wb", l, B0, B_N)
            P.op("pool", "memset", writes=["V%d" % i for i in range(NCH)], ap=V[:], constant=1.0)
            ps = self.ps
            for s in range(self.NSEQ):
                for t in range(self.NTL):
                    ts = slice(t * 512, (t + 1) * 512)
                    for c4 in range(4):
                        self.h_chunk(l, s, t * 4 + c4, hT, "hT", c4 * 128)
                    P.op("sp", "dma_start", writes=["cosT"], dma="cosT", out=cosT[:], in_=self.rope[0, :, ts])
                    P.op("sp", "dma_start", writes=["sinS"], dma="sinS", out=sinS[:], in_=self.rope[1, :, ts])
                    for b in range(10):
                        c0 = b * 128 if b < 8 else 1024 + (b - 8) * 128
                        c1 = 1280 + c0
                        pq = self.psb()
                        self.proj_fm(pq, hT, "hT", 512, wb, "wb", c0, 128)
                        pqs = self.psb()
                        self.proj_fm(pqs, hT, "hT", 512, wb, "wb", c1, 128)
                        i2 = b % 2
                        P.op("dve", "tensor_tensor", reads=self.pk(pq) + ["cosT"], writes=["t1_%d" % i2], out=t1[i2][:],
                             in0=ps[:, pq, :], in1=cosT[:], op=ALU.mult)
                        P.op("dve", "tensor_tensor", reads=self.pk(pqs) + ["sinS"], writes=["t2_%d" % i2], out=t2[i2][:],
                             in0=ps[:, pqs, :], in1=sinS[:], op=ALU.mult)
                        if b < 8:
                            dst, dkey = qrT[:, b, :], "qrT"
                        else:
                            dst, dkey = krT[:, b - 8, ts], "krT%d" % t
                        P.op("pool", "tensor_tensor", reads=["t1_%d" % i2, "t2_%d" % i2], writes=[dkey], out=dst,
                             in0=t1[i2][:], in1=t2[i2][:], op=ALU.add)
                    for c4 in range(4):
                        c = t * 4 + c4
                        cs = slice(c4 * 128, (c4 + 1) * 128)
                        pv = self.psb()
                        self.proj_tm(pv, hT, "hT", c4 * 128, wb, "wb", 2560, 256)
                        P.op("act", "activation", reads=self.pk(pv), writes=["V%d" % c], out=V[:, c, :, 0:64],
                             in_=ps[:, pv, 0:256].rearrange("p (h d) -> p h d", h=4), func=AF.Copy)
                        pz = self.psb(2)
                        for n in range(2):
                            self.proj_tm(pz + n, hT, "hT", c4 * 128, wb, "wb", 2816 + n * 512, 512)
                        P.op("act", "activation", reads=self.pk(pz, 2), writes=["sz"], out=sz[:],
                             in_=ps[:, pz:pz + 2, :].rearrange("p b n -> p (b n)"), func=AF.Silu)
                        pos = []
                        for kv in range(4):
                            half = slice((kv % 2) * 64, (kv % 2) * 64 + 64)
                            blk0 = (kv // 2) * 4
                            qv = qrT[half, blk0:blk0 + 4, cs]
                            i2 = kv % 2
                            psc = self.psb()
                            P.op("pe", "matmul", reads=["krT%d" % t, "qrT"], writes=self.pk(psc),
                                 out=ps[:, psc, :].rearrange("p (a q) -> p a q", a=4),
                                 lhsT=krT[half, kv // 2, c * 128:(c + 1) * 128], rhs=qv, start=True, stop=True)
                            P.op("act", "activation", reads=self.pk(psc), writes=["Pc%d" % i2], out=Pc[i2][:],
                                 in_=ps[:, psc, :], func=AF.Exp, scale=0.125)
                            P.op("pool", "tensor_tensor", reads=["Pc%d" % i2, "mle"], writes=["Pc%d" % i2],
                                 out=Pc[i2][:].rearrange("p (a q) -> p a q", a=4),
                                 in0=Pc[i2][:].rearrange("p (a q) -> p a q", a=4),
                                 in1=self.mle[:].unsqueeze(1).broadcast_to([128, 4, 128]), op=ALU.mult)
                            if c > 0:
                                psp = self.psb()
                                P.op("pe", "matmul", reads=["krT%d" % ((c - 1) // 4), "qrT"], writes=self.pk(psp),
                                     out=ps[:, psp, :].rearrange("p (a q) -> p a q", a=4),
                                     lhsT=krT[half, kv // 2, (c - 1) * 128:c * 128], rhs=qv, start=True, stop=True)
                                P.op("act", "activation", reads=self.pk(psp), writes=["Pp%d" % i2], out=Pp[i2][:],
                                     in_=ps[:, psp, :], func=AF.Exp, scale=0.125)
                                P.op("pool", "tensor_tensor", reads=["Pp%d" % i2, "mgt"], writes=["Pp%d" % i2],
                                     out=Pp[i2][:].rearrange("p (a q) -> p a q", a=4),
                                     in0=Pp[i2][:].rearrange("p (a q) -> p a q", a=4),
                                     in1=self.mgt[:].unsqueeze(1).broadcast_to([128, 4, 128]), op=ALU.mult)
                            po = self.psb()
                            pos.append(po)
                            for a in range(4):
                                if c > 0:
                                    P.op("pe", "matmul", reads=["Pp%d" % i2, "V%d" % (c - 1)], writes=self.pk(po),
                                         out=ps[:, po, a * 65:(a + 1) * 65], lhsT=Pp[i2][:, a * 128:(a + 1) * 128],
                                         rhs=V[:, c - 1, kv, :], start=True, stop=False)
                                P.op("pe", "matmul", reads=["Pc%d" % i2, "V%d" % c], writes=self.pk(po),
                                     out=ps[:, po, a * 65:(a + 1) * 65], lhsT=Pc[i2][:, a * 128:(a + 1) * 128],
                                     rhs=V[:, c, kv, :], start=(c == 0), stop=True)
                            pov = ps[:, po, 0:260].rearrange("p (a e) -> p a e", a=4)
                            P.op("dve", "tensor_tensor", reads=self.pk(po) + ["esink"], writes=["den%d" % kv],
                                 out=den[:, 4 * kv:4 * kv + 4].unsqueeze(2), in0=pov[:, :, 64:65],
                                 in1=self.esink[:, l * 16 + 4 * kv:l * 16 + 4 * kv + 4].unsqueeze(2), op=ALU.add)
                            P.op("dve", "reciprocal", reads=["den%d" % kv], writes=["den%d" % kv],
                                 out=den[:, 4 * kv:4 * kv + 4], in_=den[:, 4 * kv:4 * kv + 4])
                            P.op("dve", "tensor_tensor", reads=self.pk(po) + ["den%d" % kv], writes=["yf%d" % kv],
                                 out=yf[:, 4 * kv:4 * kv + 4, :], in0=pov[:, :, 0:64],
                                 in1=den[:, 4 * kv:4 * kv + 4].unsqueeze(2).broadcast_to([128, 4, 64]), op=ALU.mult)
                        yq = yo[c % 2]
                        P.op("pool", "tensor_tensor", reads=["yf%d" % k for k in range(4)] + ["sz"],
                             writes=["yo%d" % (c % 2)], out=yq[:], in0=yf[:].rearrange("p h d -> p (h d)"), in1=sz[:],
                             op=ALU.mult)
                        P.op("pool", "dma_start", reads=["yo%d" % (c % 2)], writes=["Y1_%d_%d" % (s, c)],
                             dma="yo%d" % (c % 2), out=self.Y[1, s, c * 128:(c + 1) * 128, :], in_=yq[:])
            P.emit()

    def phase_C(self, l, hg):
        nc, P, S, NCH = self.nc, self.P, self.S, self.NCH
        with ExitStack() as st:
            self.uid = getattr(self, "uid", 0) + 1
            sb = lambda name, shape, dt, _u=self.uid: st.enter_context(nc.sbuf_tensor("%s_u%d" % (name, _u), shape, dt))
            wc = sb("wc", [128, KC, C_G], BF16)
            hT = sb("hT", [128, KC, 512], BF16)
            KT = sb("KT", [72, 8, S], BF16)
            V = sb("V", [128, NCH, 8, 65], BF16)
            QT = sb("QT", [72, 8, 512], BF16)
            NC_ = sb("NC", [128, NCH, 8], F32)
            fsm = sb("fsm", [128, 4, 8], F32)
            cumT = sb("cumT", [8, 512], BF16)
            szT = sb("szT", [64, 8, 512], F32)
            PT = [sb("PT%d" % i, [128, 512], BF16) for i in range(4)]
            rd = [sb("rd%d" % i, [65, 512], BF16) for i in range(2)]
            rdf = [sb("rdf%d" % i, [65, 512], F32) for i in range(2)]
            yn = [sb("yn%d" % i, [64, 512], F32) for i in range(2)]
            yo = [sb("yo0", [64, 8, 512], BF16)] * 2
            self.epsb = sb("epsb", [128, 1], F32)
            P.op("dve", "memset", writes=["epsb"], ap=self.epsb[:], constant=EPS)
            self.load_w(wc, "wc", l, C0 + hg * C_G, C_G)
            P.op("pool", "memset", writes=["V%d" % i for i in range(NCH)], ap=V[:], constant=1.0)
            P.op("pool", "memset", writes=["KT%d" % i for i in range(self.NTL)], ap=KT[64:72, :, :], constant=1.0)
            P.op("pool", "affine_select", reads=["KT%d" % i for i in range(self.NTL)],
                 writes=["KT%d" % i for i in range(self.NTL)], out=KT[64:72, :, :], in_=KT[64:72, :, :],
                 pattern=[[1, 8], [0, S]], compare_op=ALU.is_equal, fill=0.0, base=0, channel_multiplier=-1)
            ps = self.ps
            self.pools = {"gen": [0, 1], "ct": [2], "acc": [2, 3, 4], "sc": [5, 6, 7]}
            self.prr = {}
            fb = self.rowp(l, 4)[:, hg * 8:hg * 8 + 8]
            pti = 0
            for s in range(self.NSEQ):
                for t in range(self.NTL):
                    ts = slice(t * 512, (t + 1) * 512)
                    for c4 in range(4):
                        self.h_chunk(l, s, t * 4 + c4, hT, "hT", c4 * 128)
                    for hp in range(4):
                        pkb = self.psb()
                        self.proj_fm(pkb, hT, "hT", 512, wc, "wc", 512 + hp * 128, 128)
                        for e in range(2):
                            P.op("dve", "tensor_copy", reads=self.pk(pkb), writes=["KT%d" % t], out=KT[0:64, 2 * hp + e, ts],
                                 in_=ps[64 * e:64 * e + 64, pkb, :])
                    pct = self.psb(pool="ct")
                    for c4 in range(4):
                        c = t * 4 + c4
                        pf = self.psb()
                        self.proj_tm(pf, hT, "hT", c4 * 128, wc, "wc", 2048, 8)
                        f0, f1, f2 = fsm[:, 0, :], fsm[:, 1, :], fsm[:, 2, :]
                        P.op("dve", "tensor_tensor", reads=self.pk(pf) + ["prw"], writes=["f0"], out=f0,
                             in0=ps[:, pf, 0:8], in1=fb, op=ALU.add)
                        P.op("act", "activation", reads=["f0"], writes=["f1"], out=f1, in_=f0, func=AF.Exp, scale=-1.0)
                        P.op("act", "activation", reads=["f1"], writes=["f2"], out=f2, in_=f1, func=AF.Ln, bias=1.0)
                        P.op("pe", "matmul", reads=["tri", "f2"], writes=self.pk(pf), out=ps[:, pf, 8:16],
                             lhsT=self.tri[:], rhs=f2, start=True, stop=(c == 0))
                        if c > 0:
                            P.op("pe", "matmul", reads=["elast", "NC%d" % (c - 1)], writes=self.pk(pf), out=ps[:, pf, 8:16],
                                 lhsT=self.elast[:], rhs=NC_[:, c - 1, :], start=False, stop=True)
                        P.op("dve", "tensor_copy", reads=self.pk(pf), writes=["NC%d" % c], out=NC_[:, c, :], in_=ps[:, pf, 8:16])
                        P.op("pe", "transpose", reads=["NC%d" % c, "identf"], writes=self.pk(pct),
                             out=ps[0:8, pct, c4 * 128:(c4 + 1) * 128], in_=NC_[:, c, :], identity=self.identf[:])
                        pv = self.psb()
                        self.proj_tm(pv, hT, "hT", c4 * 128, wc, "wc", 1024, 512)
                        P.op("act", "activation", reads=self.pk(pv), writes=["V%d" % c], out=V[:, c, :, 0:64],
                             in_=ps[:, pv, :].rearrange("p (h d) -> p h d", h=8), func=AF.Copy)
                    P.op("dve", "tensor_scalar", reads=self.pk(pct), writes=["QT%d" % h for h in range(8)],
                         out=QT[64:72, :, :], in0=ps[0:8, pct, :].unsqueeze(1).broadcast_to([8, 8, 512]),
                         scalar1=-8.0, scalar2=None, op0=ALU.mult)
                    yq = yo[0]
                    ykey = "yo0"
                    for hp in range(4):
                        pz = self.psb()
                        self.proj_fm(pz, hT, "hT", 512, wc, "wc", 1536 + hp * 128, 128)
                        for e in range(2):
                            P.op("act", "activation", reads=self.pk(pz), writes=["szT%d" % (2 * hp + e)],
                                 out=szT[:, 2 * hp + e, :], in_=ps[64 * e:64 * e + 64, pz, :], func=AF.Silu)
                    for hp in range(4):
                        pqb = self.psb()
                        self.proj_fm(pqb, hT, "hT", 512, wc, "wc", hp * 128, 128)
                        for e in range(2):
                            P.op("dve", "tensor_copy", reads=self.pk(pqb), writes=["QT%d" % (2 * hp + e)],
                                 out=QT[0:64, 2 * hp + e, :], in_=ps[64 * e:64 * e + 64, pqb, :])
                    nkb = 4 * t + 4

                    def qk(h, j):
                        q0 = max(j - 4 * t, 0)
                        nq = 4 - q0
                        psc = self.psb(pool="sc")
                        P.op("pe", "matmul", reads=["KT%d" % (j // 4), "QT%d" % h], writes=self.pk(psc),
                             out=ps[:, psc, 0:nq * 128], lhsT=KT[:, h, j * 128:(j + 1) * 128],
                             rhs=QT[:, h, q0 * 128:512], start=True, stop=True)
                        return psc

                    seq = [(2 * hp + e, j) for hp in range(4) for j in range(nkb) for e in range(2)]
                    DIST = 2
                    pend = [qk(*seq[i]) for i in range(min(DIST, len(seq)))]
                    pos = {}
                    for i, (h, j) in enumerate(seq):
                        psc = pend.pop(0)
                        if i + DIST < len(seq):
                            pend.append(qk(*seq[i + DIST]))
                        if j == 0:
                            pos[h] = self.psb(pool="acc")
                        po = pos[h]
                        jj = j - 4 * t
                        q0 = max(jj, 0)
                        nq = 4 - q0
                        pt = PT[pti % 4]
                        ptk = "PT%d" % (pti % 4)
                        pti += 1
                        P.op("act", "activation", reads=self.pk(psc) + ["NC%d" % j], writes=[ptk], out=pt[:, 0:nq * 128],
                             in_=ps[:, psc, 0:nq * 128], func=AF.Exp, scale=0.125, bias=NC_[:, j, h:h + 1])
                        if jj >= 0:
                            P.op("pool", "tensor_tensor", reads=[ptk, "mle"], writes=[ptk], out=pt[:, 0:128],
                                 in0=pt[:, 0:128], in1=self.mle[:], op=ALU.mult)
                        P.op("pe", "matmul", reads=[ptk, "V%d" % j], writes=self.pk(po), out=ps[0:65, po, q0 * 128:512],
                             lhsT=V[:, j, h, :], rhs=pt[:, 0:nq * 128], start=(j == 0), stop=(j == nkb - 1))
                        if j == nkb - 1:
                            r_, y_ = rd[h % 2], yn[h % 2]
                            rk, yk = "rd%d" % (h % 2), "yn%d" % (h % 2)
                            rf_ = rdf[h % 2]
                            P.op("act", "activation", reads=self.pk(po), writes=[rk + "f"], out=rf_[64:65, :],
                                 in_=ps[64:65, po, :], func=AF.Ln)
                            P.op("act", "activation", reads=[rk + "f"], writes=[rk], out=r_[64:65, :],
                                 in_=rf_[64:65, :], func=AF.Exp, scale=-1.0)
                            pbc = self.psb()
                            P.op("pe", "matmul", reads=[rk, "onesb"], writes=self.pk(pbc), out=ps[0:64, pbc, :],
                                 lhsT=self.onesb[64:65, 0:64], rhs=r_[64:65, :], start=True, stop=True)
                            P.op("dve", "tensor_tensor", reads=self.pk(pbc) + ["szT%d" % h], writes=[yk], out=y_[:],
                                 in0=ps[0:64, pbc, :], in1=szT[:, h, :], op=ALU.mult)
                            P.op("dve", "tensor_tensor", reads=self.pk(po) + [yk], writes=[ykey], out=yq[:, h, :],
                                 in0=ps[0:64, po, :], in1=y_[:], op=ALU.mult)
                    P.op("pool", "dma_start", reads=[ykey], writes=["Y2T_%d_%d_%d" % (s, t, hg)], dma=ykey,
                         out=self.Y2T[s, hg * 512:(hg + 1) * 512, ts].rearrange("(h d) t -> d h t", d=64), in_=yq[:])
            P.emit()
            self.pools = {"gen": list(range(8))}
            self.prr = {}

    def phase_D(self, l):
        nc, P, S = self.nc, self.P, self.S
        last = (l == self.layers - 1)
        NS = self.NSEQ
        with ExitStack() as st:
            self.uid = getattr(self, "uid", 0) + 1
            sb = lambda name, shape, dt, _u=self.uid: st.enter_context(nc.sbuf_tensor("%s_u%d" % (name, _u), shape, dt))
            wg = sb("wg", [128, KC, G_N], BF16)
            wp = [sb("wp%d" % i, [128, KC, D], BF16) for i in range(3)]
            wo = sb("wo", [128, KC, D], BF16)
            gbb = sb("gbb", [128, 3 * D], F32)
            fnw = sb("fnw", [128, D], F32) if last else None
            nb = max(NS, 2)
            yt = [[sb("yt%d_%d" % (b, i), [128, D], BF16) for i in range(nb)] for b in range(2)]
            ybTc = [sb("ybTc%d" % i, [128, KC, 128], BF16) for i in range(nb)]
            hTs = [sb("hT%d" % i, [128, KC, 128], BF16) for i in range(nb)]
            ybTs = [sb("ybT%d" % i, [128, KC, 128], BF16) for i in range(nb)]
            gss = [sb("gs%d" % i, [128, D], F32) for i in range(nb)]
            mgs = [sb("mg%d" % i, [128, D], F32) for i in range(nb)]
            mgbs = [sb("mgb%d" % i, [128, D], BF16) for i in range(nb)]
            mTs = [sb("mT%d" % i, [128, KC, 128], BF16) for i in range(nb)]
            xo = [sb("xo%d" % i, [128, D], F32) for i in range(nb)]
            fss = [sb("fs%d" % i, [128, 4], F32) for i in range(nb)]
            self.epsb = sb("epsb", [128, 1], F32)
            P.op("dve", "memset", writes=["epsb"], ap=self.epsb[:], constant=EPS)
            self.load_w(wg, "wg", l, G0, G_N)
            for b in range(3):
                self.load_w(wp[b], "wp%d" % b, l, 0, D, src=self.wproj[l, b])
            self.load_w(wo, "wo", l, 0, D, src=self.wout[l])
            P.op("sp", "dma_start", writes=["gbb"], dma="gbb", out=gbb[:],
                 in_=self.prow[l * 4176 + 1104:l * 4176 + 1104 + 3072].partition_broadcast(128))
            if last:
                P.op("sp", "dma_start", writes=["fnw"], dma="fnw", out=fnw[:],
                     in_=self.prow[DEPTH * 4176:DEPTH * 4176 + 1024].partition_broadcast(128))
            ps = self.ps

            def chunk_gen(s, c, k):
                hT, ybT, gs, mg, mgb, mT, xq, fs = hTs[k], ybTs[k], gss[k], mgs[k], mgbs[k], mTs[k], xo[k], fss[k]
                K_ = lambda n: "%s%d" % (n, k)
                for b in range(2):
                    P.op("sp", "dma_start", reads=["Y%d_%d_%d" % (b, s, c)], writes=["yt%d_%d" % (b, k)],
                         dma="yt%d_%d" % (b, k), out=yt[b][k][:], in_=self.Y[b, s, c * 128:(c + 1) * 128, :])
                P.op("sp", "dma_start", reads=["Y2T_%d_%d_%d" % (s, c // 4, g) for g in range(2)],
                     writes=[K_("ybTc")], dma=K_("ybTc"), out=ybTc[k][:],
                     in_=self.Y2T[s, :, c * 128:(c + 1) * 128].rearrange("(kc p) t -> p kc t", p=128))
                sl = self.h_chunk(l, s, c, hT, K_("hT"), 0, slot=k % 2)
                yield
                for b in range(3):
                    pg = self.psb(2)
                    for n in range(2):
                        self.proj_tm(pg + n, hT, K_("hT"), 0, wg, "wg", b * D + n * 512, 512)
                    P.op("dve", "tensor_tensor", reads=self.pk(pg, 2) + ["gbb"], writes=[K_("gs")], out=gs[:],
                         in0=ps[:, pg:pg + 2, :].rearrange("p b n -> p (b n)"), in1=gbb[:, b * D:(b + 1) * D],
                         op=ALU.add)
                    P.op("act", "activation", reads=[K_("gs")], writes=[K_("gs")], out=gs[:], in_=gs[:], func=AF.Sigmoid)
                    if b < 2:
                        pt_ = self.psb()
                        ptb = ps[:, pt_, :].bitcast(BF16).rearrange("p (k t) -> p k t", k=8)
                        for kc in range(KC):
                            P.op("pe", "transpose", reads=["yt%d_%d" % (b, k), "ident"], writes=self.pk(pt_),
                                 out=ptb[:, kc, :], in_=yt[b][k][:, kc * 128:(kc + 1) * 128], identity=self.ident[:])
                        P.op("act", "activation", reads=self.pk(pt_), writes=[K_("ybT")], out=ybT[:], in_=ptb, func=AF.Copy)
                        ysrc, ykey_ = ybT, K_("ybT")
                    else:
                        ysrc, ykey_ = ybTc[k], K_("ybTc")
                    yield
                    pb = self.psb(2)
                    for n in range(2):
                        for kc in range(KC):
                            P.op("pe", "matmul", reads=[ykey_, "wp%d" % b], writes=self.pk(pb + n),
                                 out=ps[:, pb + n, :], lhsT=ysrc[:, kc, :], rhs=wp[b][:, kc, n * 512:(n + 1) * 512],
                                 start=(kc == 0), stop=(kc == KC - 1))
                    pball = ps[:, pb:pb + 2, :].rearrange("p b n -> p (b n)")
                    if b == 0:
                        P.op("dve", "tensor_tensor", reads=self.pk(pb, 2) + [K_("gs")], writes=[K_("mg")], out=mg[:],
                             in0=pball, in1=gs[:], op=ALU.mult)
                    else:
                        P.op("dve", "tensor_tensor", reads=self.pk(pb, 2) + [K_("gs")], writes=[K_("gs")], out=gs[:],
                             in0=pball, in1=gs[:], op=ALU.mult)
                        P.op("pool", "tensor_tensor", reads=[K_("gs"), K_("mg")], writes=[K_("mg")], out=mg[:],
                             in0=gs[:], in1=mg[:], op=ALU.add)
                    yield
                P.op("act", "activation", reads=[K_("mg")], writes=[K_("mgb")], out=mgb[:], in_=mg[:], func=AF.Copy)
                pt_ = self.psb()
                ptb = ps[:, pt_, :].bitcast(BF16).rearrange("p (k t) -> p k t", k=8)
                for kc in range(KC):
                    P.op("pe", "transpose", reads=[K_("mgb"), "ident"], writes=self.pk(pt_), out=ptb[:, kc, :],
                         in_=mgb[:, kc * 128:(kc + 1) * 128], identity=self.ident[:])
                P.op("dve", "tensor_copy", reads=self.pk(pt_), writes=[K_("mT")], out=mT[:], in_=ptb)
                yield
                po = self.psb(2)
                for n in range(2):
                    for kc in range(KC):
                        P.op("pe", "matmul", reads=[K_("mT"), "wo"], writes=self.pk(po + n), out=ps[:, po + n, :],
                             lhsT=mT[:, kc, :], rhs=wo[:, kc, n * 512:(n + 1) * 512], start=(kc == 0),
                             stop=(kc == KC - 1))
                P.op("dve", "tensor_tensor", reads=self.pk(po, 2) + ["xt%d" % sl], writes=[K_("xo")], out=xq[:],
                     in0=ps[:, po:po + 2, :].rearrange("p b n -> p (b n)"), in1=self.xt[sl][:], op=ALU.add)
                if not last:
                    P.op("pool", "dma_start", reads=[K_("xo")], writes=["x1_%d_%d" % (s, c)], dma=K_("xo"),
                         out=self.X1[s, c * 128:(c + 1) * 128, :], in_=xq[:])
                else:
                    P.op("act", "activation", reads=[K_("xo")], writes=["junk%d" % sl, K_("fs")], out=self.junk[sl][:],
                         in_=xq[:], func=AF.Square, accum_out=fs[:, 0:1])
                    P.op("act", "activation", reads=[K_("fs"), "epsb"], writes=[K_("fs")], out=fs[:, 1:2], in_=fs[:, 0:1],
                         func=AF.Ln, scale=1.0 / D, bias=self.epsb[:])
                    P.op("act", "activation", reads=[K_("fs")], writes=[K_("fs")], out=fs[:, 2:3], in_=fs[:, 1:2],
                         func=AF.Exp, scale=-0.5)
                    P.op("dve", "scalar_tensor_tensor", reads=[K_("xo"), K_("fs"), "fnw"], writes=[K_("xo")],
                         out=xq[:], in0=xq[:], scalar=fs[:, 2:3], in1=fnw[:], op0=ALU.mult, op1=ALU.mult)
                    P.op("pool", "dma_start", reads=[K_("xo")], writes=["out_%d_%d" % (s, c)], dma=K_("xo"),
                         out=self.out[s, c * 128:(c + 1) * 128, :], in_=xq[:])

            if NS >= 2:
                items = [[(s, c, s) for s in range(NS)] for c in range(self.NCH)]
            else:
                items = [[(0, c + e, e) for e in range(2) if c + e < self.NCH] for c in range(0, self.NCH, 2)]
            for group in items:
                alive = [chunk_gen(*it) for it in group]
                while alive:
                    for g in list(alive):
                        try:
                            next(g)
                        except StopIteration:
                            alive.remove(g)
            P.emit()


def host_layout(S, norm_w, w_in, conv_w, conv_b, dt_bias, a_log, d_skip, ssm_norm_w, sinks, f_bias, gate_bias,
                w_proj, w_out, final_norm_w):
    L = DEPTH
    cols = _col_order()
    win = np.ascontiguousarray(
        np.asarray(w_in, np.float32)[:, :, cols].reshape(L, KC, 128, NT).transpose(0, 2, 1, 3))
    wproj = np.ascontiguousarray(np.asarray(w_proj, np.float32).reshape(L, 3, KC, 128, D).transpose(0, 1, 3, 2, 4))
    wout = np.ascontiguousarray(np.asarray(w_out, np.float32).reshape(L, KC, 128, D).transpose(0, 2, 1, 3))
    ppart = np.zeros((128, L * 88), np.float32)
    prow = np.zeros((L * 4176 + 1024,), np.float32)
    for l in range(L):
        ppart[:, l * 88:l * 88 + 8] = np.asarray(norm_w[l]).reshape(KC, 128).T
        cw = np.asarray(conv_w[l]).reshape(4, 16, 128)
        ppart[:, l * 88 + 8:l * 88 + 72] = cw.transpose(2, 1, 0).reshape(128, 64)
        ppart[:, l * 88 + 72:l * 88 + 88] = np.asarray(conv_b[l]).reshape(16, 128).T
        o = l * 4176
        prow[o:o + 16] = dt_bias[l]
        prow[o + 16:o + 32] = a_log[l]
        prow[o + 32:o + 48] = d_skip[l]
        prow[o + 48:o + 64] = sinks[l]
        prow[o + 64:o + 80] = f_bias[l]
        prow[o + 80:o + 1104] = ssm_norm_w[l]
        prow[o + 1104:o + 4176] = np.asarray(gate_bias[l]).reshape(-1)
    prow[L * 4176:] = final_norm_w
    pos = np.arange(S, dtype=np.float32)
    inv = (np.float32(10000.0) ** (-np.arange(0, 64, 2, dtype=np.float32) / np.float32(64))).astype(np.float32)
    ang = (pos[:, None] * inv[None, :]).astype(np.float32)
    cos, sin = np.cos(ang).astype(np.float32), np.sin(ang).astype(np.float32)
    rope = np.zeros((2, 128, S), np.float32)
    for p in range(128):
        rope[0, p] = cos[:, p % 32]
        rope[1, p] = sin[:, p % 32] * (-1.0 if (p % 64) < 32 else 1.0)
    return dict(win=win, wproj=wproj, wout=wout, ppart=ppart, prow=prow, rope=rope)


_NC_CACHE = {}


def kernel(x, norm_w, w_in, conv_w, conv_b, dt_bias, a_log, d_skip, ssm_norm_w, sinks, f_bias, gate_bias,
           w_proj, w_out, final_norm_w):
    x = np.asarray(x, np.float32)
    B, S, _ = x.shape
    nseq = B // NCORES
    shared = host_layout(S, norm_w, w_in, conv_w, conv_b, dt_bias, a_log, d_skip, ssm_norm_w, sinks, f_bias,
                         gate_bias, w_proj, w_out, final_norm_w)
    key = (S, nseq)
    if key not in _NC_CACHE:
        _NC_CACHE[key] = Builder(S, nseq).build()
    nc = _NC_CACHE[key]
    in_maps = []
    for c in range(NCORES):
        m = dict(shared)
        m["x"] = np.ascontiguousarray(x[c * nseq:(c + 1) * nseq])
        in_maps.append(m)
    res = run_bass_kernel_spmd(nc, in_maps, core_ids=list(range(NCORES)))
    return np.concatenate([r["out"] for r in res.results], axis=0).astype(np.float32)
```

```python
import math
from contextlib import ExitStack

import numpy as np
import concourse.bass as bass
import concourse.mybir as mybir
from concourse.bass_utils import run_bass_kernel_spmd

F32 = mybir.dt.float32
BF16 = mybir.dt.bfloat16
AF = mybir.ActivationFunctionType
ALU = mybir.AluOpType

D = 1024
KC = 8
DEPTH = 2
NCORES = 8
EPS = 1e-6
ENGS = ("pe", "act", "dve", "pool", "sp")

A0, A_N = 0, 3088
B0, B_N = 3088, 3840
C0, C_G = 6928, 2056
G0, G_N = 6928 + 2 * 2056, 3072
NT = G0 + G_N
SWA_QORDER = [0, 4, 1, 5, 2, 6, 3, 7, 8, 12, 9, 13, 10, 14, 11, 15]


def _col_order():
    o = {}
    off = 0
    names = [("a_xbc", 2048), ("a_z", 1024), ("a_dt", 16), ("b_q", 1024), ("b_k", 256), ("b_v", 256),
             ("b_z", 1024), ("c_q", 1024), ("c_k", 1024), ("c_v", 1024), ("c_f", 16), ("c_z", 1024),
             ("gates", 3072)]
    for n, s in names:
        o[n] = off
        off += s
    cols = []
    cols += list(range(o["a_xbc"], o["a_xbc"] + 2048))
    cols += list(range(o["a_z"], o["a_z"] + 1024))
    cols += list(range(o["a_dt"], o["a_dt"] + 16))
    assert len(cols) == A_N
    q = [o["b_q"] + h * 64 + d for h in SWA_QORDER for d in range(64)]
    qs = [o["b_q"] + h * 64 + (d + 32) % 64 for h in SWA_QORDER for d in range(64)]
    k = [o["b_k"] + h * 64 + d for h in range(4) for d in range(64)]
    ks = [o["b_k"] + h * 64 + (d + 32) % 64 for h in range(4) for d in range(64)]
    cols += q + k + qs + ks
    cols += list(range(o["b_v"], o["b_v"] + 256))
    cols += list(range(o["b_z"], o["b_z"] + 1024))
    assert len(cols) == B0 + B_N
    for hg in range(2):
        for nm in ("c_q", "c_k", "c_v", "c_z"):
            cols += list(range(o[nm] + hg * 512, o[nm] + hg * 512 + 512))
        cols += list(range(o["c_f"] + hg * 8, o["c_f"] + hg * 8 + 8))
    assert len(cols) == G0
    cols += list(range(o["gates"], o["gates"] + 3072))
    assert len(cols) == NT
    return np.array(cols, dtype=np.int64)


class Prog:
    def __init__(self, nc, stack, same_engine_sync=True):
        self.nc = nc
        self.stack = stack
        self.same = same_engine_sync
        self.ops = []
        self.last_w = {}
        self.readers = {}
        self.eng_sem = {e: stack.enter_context(nc.semaphore("s_" + e)) for e in ENGS}
        self.cnt = {e: 0 for e in ENGS}
        self.dsem = {}
        self.dcnt = {}
        self.known = {e: {} for e in ENGS}
        self.done_ops = 0

    def op(self, eng, meth, reads=(), writes=(), dma=None, **kw):
        idx = len(self.ops)
        deps = set()
        for k in reads:
            if k in self.last_w:
                deps.add(self.last_w[k])
        for k in writes:
            if k in self.last_w:
                deps.add(self.last_w[k])
            for r in self.readers.get(k, ()):
                deps.add(r)
        deps.discard(idx)
        best = {}
        for d in deps:
            od = self.ops[d]
            kk = ("d", od["dma"]) if od["dma"] is not None else ("e", od["eng"])
            if kk not in best or best[kk] < d:
                best[kk] = d
        deps = set(best.values())
        for k in reads:
            self.readers.setdefault(k, []).append(idx)
        for k in writes:
            self.last_w[k] = idx
            self.readers[k] = []
        self.ops.append(dict(eng=eng, meth=meth, kw=kw, deps=deps, dma=dma, sig=False, ev=None))
        return idx

    def emit(self, final=False):
        nc, ops = self.nc, self.ops
        import os as _os
        if _os.environ.get("OPS_LIMIT"):
            del ops[int(_os.environ["OPS_LIMIT"]):]
        lo = self.done_ops
        new = range(lo, len(ops))
        for i in new:
            o = ops[i]
            if o["dma"] is not None:
                o["sig"] = True
            for d in o["deps"]:
                od = ops[d]
                if d < lo:
                    continue
                if od["dma"] is not None or od["eng"] != o["eng"] or o["dma"] is not None:
                    od["sig"] = True
                elif self.same and o["eng"] != "pe":
                    od["sig"] = True
        for i in new:
            o = ops[i]
            if o["dma"] is not None:
                k = o["dma"]
                if k not in self.dsem:
                    self.dsem[k] = self.stack.enter_context(nc.semaphore("d_" + str(k)))
                    self.dcnt[k] = 0
                self.dcnt[k] += 16
                o["ev"] = (self.dsem[k], self.dcnt[k], "d_" + str(k))
            elif o["sig"]:
                self.cnt[o["eng"]] += 1
                o["ev"] = (self.eng_sem[o["eng"]], self.cnt[o["eng"]], o["eng"])
        per_eng = {e: [] for e in ENGS}
        for i in new:
            per_eng[ops[i]["eng"]].append(i)
        same = self.same

        def body(ename):
            def f(eng):
                kn = self.known[ename]
                for i in per_eng[ename]:
                    o = ops[i]
                    need = {}
                    for d in o["deps"]:
                        if d < lo:
                            continue
                        od = ops[d]
                        ev = od["ev"]
                        if ev is None:
                            continue
                        sem, val, name = ev
                        if (od["dma"] is None and od["eng"] == ename and o["dma"] is None
                                and (ename == "pe" or not same)):
                            continue
                        if kn.get(name, 0) >= val:
                            continue
                        if name not in need or need[name][1] < val:
                            need[name] = (sem, val)
                    for name, (sem, val) in need.items():
                        eng.wait_ge(sem, val)
                        kn[name] = val
                    ins = getattr(eng, o["meth"])(**o["kw"])
                    if o["ev"] is not None:
                        sem, val, name = o["ev"]
                        ins.then_inc(sem, 16 if o["dma"] is not None else 1)
                if ename == "sp":
                    for k, s in self.dsem.items():
                        if kn.get("d_" + str(k), 0) < self.dcnt[k]:
                            eng.wait_ge(s, self.dcnt[k])
                            kn["d_" + str(k)] = self.dcnt[k]
            return f

        with nc.Block() as block:
            block.tensor(body("pe"))
            block.scalar(body("act"))
            block.vector(body("dve"))
            block.gpsimd(body("pool"))
            block.sync(body("sp"))
        self.done_ops = len(ops)


class Builder:
    def __init__(self, S, NSEQ, debug=False, layers=DEPTH, phases="ABCD"):
        self.S, self.NSEQ, self.debug, self.layers, self.phases = S, NSEQ, debug, layers, phases
        self.NCH = S // 128
        self.NTL = S // 512
        nc = self.nc = bass.Bass("TRN2", target_bir_lowering=False)
        L = DEPTH
        okind = "ExternalOutput" if debug else "Internal"
        self.x = nc.dram_tensor("x", [NSEQ, S, D], F32, kind="ExternalInput").ap()
        self.win = nc.dram_tensor("win", [L, 128, KC, NT], F32, kind="ExternalInput").ap()
        self.wproj = nc.dram_tensor("wproj", [L, 3, 128, KC, D], F32, kind="ExternalInput").ap()
        self.wout = nc.dram_tensor("wout", [L, 128, KC, D], F32, kind="ExternalInput").ap()
        self.ppart = nc.dram_tensor("ppart", [128, L * (8 + 64 + 16)], F32, kind="ExternalInput").ap()
        self.prow = nc.dram_tensor("prow", [L * (80 + 1024 + 3072) + 1024], F32, kind="ExternalInput").ap()
        self.rope = nc.dram_tensor("rope", [2, 128, S], F32, kind="ExternalInput").ap()
        self.out = nc.dram_tensor("out", [NSEQ, S, D], F32, kind="ExternalOutput").ap()
        self.Y = nc.dram_tensor("ybr", [3, NSEQ, S, D], BF16, kind=okind).ap()
        self.X1 = nc.dram_tensor("x1", [NSEQ, S, D], F32, kind=okind).ap()
        self.Y2T = nc.dram_tensor("y2t", [NSEQ, D, S], BF16, kind=okind).ap()
        self.pools = {"gen": list(range(8))}
        self.prr = {}

    def psb(self, n=1, pool="gen"):
        banks = self.pools[pool]
        r = self.prr.get(pool, 0)
        if n == 2:
            assert len(banks) % 2 == 0
            if r % 2:
                r += 1
            b = banks[r % len(banks)]
            self.prr[pool] = r + 2
            return b
        b = banks[r % len(banks)]
        self.prr[pool] = r + 1
        return b

    def pk(self, b, n=1):
        return ["ps%d" % (b + i) for i in range(n)]

    def build(self):
        nc = self.nc
        with ExitStack() as gst:
            self.P = P = Prog(nc, gst)
            sb = lambda name, shape, dt: gst.enter_context(nc.sbuf_tensor(name, shape, dt))
            self.ps = gst.enter_context(nc.psum_tensor("ps", [128, 8, 512], F32))
            self.ident = sb("ident", [128, 128], BF16)
            self.identf = sb("identf", [128, 128], F32)
            self.tri = sb("tri", [128, 128], F32)
            self.elast = sb("elast", [128, 128], F32)
            self.onesf = sb("onesf", [128, 128], F32)
            self.onesb = sb("onesb", [128, 64], BF16)
            self.mle = sb("mle", [128, 128], BF16)
            self.mgt = sb("mgt", [128, 128], BF16)
            self.mlef = sb("mlef", [128, 128], F32)
            self.sel = sb("sel", [8, 8, 65], BF16)
            self.ppt = sb("ppt", [128, DEPTH * 88], F32)
            self.prw = sb("prw", [128, DEPTH * 80], F32)
            self.abc = sb("abc", [128, DEPTH * 16], F32)
            self.esink = sb("esink", [128, DEPTH * 16], F32)
            self.xt = [sb("xt%d" % i, [128, D], F32) for i in range(2)]
            self.xn = [sb("xn%d" % i, [128, D], BF16) for i in range(2)]
            self.junk = [sb("junk%d" % i, [128, D], BF16) for i in range(2)]
            self.st4 = [sb("st4_%d" % i, [128, 4], F32) for i in range(2)]
            self.xslot = 0
            self.setup_consts()
            import os as _os
            if _os.environ.get("SETUP_LIMIT"):
                lim = int(_os.environ["SETUP_LIMIT"])
                del P.ops[lim:]
            P.emit()
            for l in range(self.layers):
                if "A" in self.phases:
                    self.phase_A(l)
                if "B" in self.phases:
                    self.phase_B(l)
                if "C" in self.phases:
                    for hg in range(2):
                        self.phase_C(l, hg)
                if "D" in self.phases:
                    self.phase_D(l)
        return nc

    def setup_consts(self):
        P = self.P
        P.op("pool", "memset", writes=["ident"], ap=self.ident[:], constant=1.0)
        P.op("pool", "affine_select", reads=["ident"], writes=["ident"], out=self.ident[:], in_=self.ident[:],
             pattern=[[-1, 128]], compare_op=ALU.is_equal, fill=0.0, base=0, channel_multiplier=1)
        P.op("pool", "memset", writes=["identf"], ap=self.identf[:], constant=1.0)
        P.op("pool", "affine_select", reads=["identf"], writes=["identf"], out=self.identf[:], in_=self.identf[:],
             pattern=[[-1, 128]], compare_op=ALU.is_equal, fill=0.0, base=0, channel_multiplier=1)
        P.op("pool", "memset", writes=["tri"], ap=self.tri[:], constant=1.0)
        P.op("pool", "affine_select", reads=["tri"], writes=["tri"], out=self.tri[:], in_=self.tri[:],
             pattern=[[1, 128]], compare_op=ALU.is_ge, fill=0.0, base=0, channel_multiplier=-1)
        P.op("pool", "memset", writes=["mlef"], ap=self.mlef[:], constant=1.0)
        P.op("pool", "affine_select", reads=["mlef"], writes=["mlef"], out=self.mlef[:], in_=self.mlef[:],
             pattern=[[1, 128]], compare_op=ALU.is_ge, fill=0.0, base=0, channel_multiplier=-1)
        P.op("pool", "memset", writes=["mle"], ap=self.mle[:], constant=1.0)
        P.op("pool", "affine_select", reads=["mle"], writes=["mle"], out=self.mle[:], in_=self.mle[:],
             pattern=[[1, 128]], compare_op=ALU.is_ge, fill=0.0, base=0, channel_multiplier=-1)
        P.op("pool", "memset", writes=["mgt"], ap=self.mgt[:], constant=1.0)
        P.op("pool", "affine_select", reads=["mgt"], writes=["mgt"], out=self.mgt[:], in_=self.mgt[:],
             pattern=[[-1, 128]], compare_op=ALU.is_gt, fill=0.0, base=0, channel_multiplier=1)
        P.op("pool", "memset", writes=["elast"], ap=self.elast[:], constant=1.0)
        P.op("pool", "affine_select", reads=["elast"], writes=["elast"], out=self.elast[:], in_=self.elast[:],
             pattern=[[0, 128]], compare_op=ALU.is_equal, fill=0.0, base=-127, channel_multiplier=1)
        P.op("pool", "memset", writes=["onesf"], ap=self.onesf[:], constant=1.0)
        P.op("pool", "memset", writes=["onesb"], ap=self.onesb[:], constant=1.0)
        P.op("pool", "memset", writes=["sel"], ap=self.sel[:], constant=8.0)
        P.op("pool", "affine_select", reads=["sel"], writes=["sel"], out=self.sel[:], in_=self.sel[:],
             pattern=[[1, 8], [0, 65]], compare_op=ALU.is_equal, fill=0.0, base=0, channel_multiplier=-1)
        P.op("pool", "affine_select", reads=["sel"], writes=["sel"], out=self.sel[:], in_=self.sel[:],
             pattern=[[0, 8], [1, 65]], compare_op=ALU.is_equal, fill=0.0, base=-64, channel_multiplier=0)
        P.op("sp", "dma_start", writes=["ppt"], dma="ppt", out=self.ppt[:], in_=self.ppart)
        for l in range(DEPTH):
            P.op("sp", "dma_start", writes=["prw"], dma="prw", out=self.prw[:, l * 80:(l + 1) * 80],
                 in_=self.prow[l * 4176:l * 4176 + 80].partition_broadcast(128))
        for l in range(DEPTH):
            P.op("act", "activation", reads=["prw"], writes=["abc"], out=self.abc[:, l * 16:(l + 1) * 16],
                 in_=self.prw[:, l * 80 + 16:l * 80 + 32], func=AF.Exp)
            P.op("dve", "tensor_scalar", reads=["abc"], writes=["abc"], out=self.abc[:, l * 16:(l + 1) * 16],
                 in0=self.abc[:, l * 16:(l + 1) * 16], scalar1=-1.0, scalar2=None, op0=ALU.mult)
            P.op("act", "activation", reads=["prw"], writes=["esink"], out=self.esink[:, l * 16:(l + 1) * 16],
                 in_=self.prw[:, l * 80 + 48:l * 80 + 64], func=AF.Exp)

    def nw(self, l):
        return self.ppt[:, l * 88:l * 88 + 8]

    def convw(self, l, b, k):
        o = l * 88 + 8 + b * 4 + k
        return self.ppt[:, o:o + 1]

    def convb(self, l, b):
        o = l * 88 + 72 + b
        return self.ppt[:, o:o + 1]

    def rowp(self, l, i):
        return self.prw[:, l * 80 + i * 16:l * 80 + (i + 1) * 16]

    def xsrc(self, l, s, c):
        src = self.x if l == 0 else self.X1
        return src[s, c * 128:(c + 1) * 128, :], ("xin" if l == 0 else "x1_%d_%d" % (s, c))

    def h_chunk(self, l, s, c, hT, hkey, col0, slot=None):
        P = self.P
        if slot is None:
            sl = self.xslot
            self.xslot ^= 1
        else:
            sl = slot
        xt, xn, junk, st4 = self.xt[sl], self.xn[sl], self.junk[sl], self.st4[sl]
        src, skey = self.xsrc(l, s, c)
        P.op("sp", "dma_start", reads=[skey], writes=["xt%d" % sl], dma="xt%d" % sl, out=xt[:], in_=src)
        P.op("act", "activation", reads=["xt%d" % sl], writes=["junk%d" % sl, "st4_%d" % sl],
             out=junk[:], in_=xt[:], func=AF.Square, accum_out=st4[:, 0:1])
        P.op("act", "activation", reads=["st4_%d" % sl, "epsb"], writes=["st4_%d" % sl], out=st4[:, 1:2], in_=st4[:, 0:1],
             func=AF.Ln, scale=1.0 / D, bias=self.epsb[:])
        P.op("act", "activation", reads=["st4_%d" % sl], writes=["st4_%d" % sl], out=st4[:, 2:3], in_=st4[:, 1:2],
             func=AF.Exp, scale=-0.5)
        P.op("act", "activation", reads=["xt%d" % sl, "st4_%d" % sl], writes=["xn%d" % sl], out=xn[:], in_=xt[:],
             func=AF.Identity, scale=st4[:, 2:3])
        b = self.psb()
        ptb = self.ps[:, b, :].bitcast(BF16).rearrange("p (k t) -> p k t", k=8)
        for kc in range(KC):
            P.op("pe", "transpose", reads=["xn%d" % sl, "ident"], writes=self.pk(b), out=ptb[:, kc, :],
                 in_=xn[:, kc * 128:(kc + 1) * 128], identity=self.ident[:])
        P.op("dve", "tensor_tensor", reads=self.pk(b) + ["ppt"], writes=[hkey],
             out=hT[:, :, col0:col0 + 128], in0=ptb,
             in1=self.nw(l).unsqueeze(2).broadcast_to([128, 8, 128]), op=ALU.mult)
        return sl

    def load_w(self, wt, key, l, c0, n, src=None):
        P = self.P
        src = self.win[l] if src is None else src
        step = 1024
        for kc in range(KC):
            for o in range(0, n, step):
                m = min(step, n - o)
                P.op("pool", "dma_start", writes=[key], dma=key, out=wt[:, kc, o:o + m],
                     in_=src[:, kc, c0 + o:c0 + o + m])

    def proj_tm(self, dst_bank, hT, hkey, col0, wt, wkey, wc0, n, poff=0):
        P = self.P
        for kc in range(KC):
            P.op("pe", "matmul", reads=[hkey, wkey], writes=self.pk(dst_bank),
                 out=self.ps[:, dst_bank, poff:poff + n], lhsT=hT[:, kc, col0:col0 + 128],
                 rhs=wt[:, kc, wc0:wc0 + n], start=(kc == 0), stop=(kc == KC - 1))

    def proj_fm(self, dst_bank, hT, hkey, ntok, wt, wkey, wc0, m, first=True):
        P = self.P
        for kc in range(KC):
            P.op("pe", "matmul", reads=[hkey, wkey], writes=self.pk(dst_bank),
                 out=self.ps[0:m, dst_bank, 0:ntok], lhsT=wt[:, kc, wc0:wc0 + m],
                 rhs=hT[:, kc, 0:ntok], start=(first and kc == 0), stop=(kc == KC - 1))

    def phase_A(self, l):
        nc, P, S = self.nc, self.P, self.S
        with ExitStack() as st:
            self.uid = getattr(self, "uid", 0) + 1
            sb = lambda name, shape, dt, _u=self.uid: st.enter_context(nc.sbuf_tensor("%s_u%d" % (name, _u), shape, dt))
            wa = sb("wa", [128, KC, A_N], BF16)
            hTs = [sb("hT%d" % i, [128, KC, 512], BF16) for i in range(1)]
            Ub = [sb("Ub%d" % i, [128, 515], F32) for i in range(2)]
            Ucar = sb("Ucar", [128, 16, 3], F32)
            acc = [sb("acc%d" % i, [128, 512], F32) for i in range(2)]
            xsTs = [sb("xsT%d" % i, [128, 8, 512], BF16) for i in range(1)]
            BTs = [sb("BT%d" % i, [128, 4, 512], BF16) for i in range(1)]
            CTs = [sb("CT%d" % i, [128, 4, 512], BF16) for i in range(1)]
            H = sb("H", [128, 16, 64], F32)
            Hb = sb("Hb", [128, 16, 64], BF16)
            Htmp = sb("Htmp", [128, 16, 64], F32)
            bufsets = []
            for k in range(2):
                bufsets.append(dict(
                    sm=sb("sm%d" % k, [128, 12, 16], F32), rhsall=sb("rhsall%d" % k, [128, 16, 128], F32),
                    cbm=sb("cbm%d" % k, [128, 4, 128], F32), MT=sb("MT%d" % k, [128, 16, 128], BF16),
                    xstm=sb("xstm%d" % k, [128, 16, 64], F32), xdt=sb("xdt%d" % k, [128, 16, 64], BF16),
                    xw=sb("xw%d" % k, [128, 16, 64], BF16), Btm=sb("Btm%d" % k, [128, 4, 128], BF16),
                    sz=sb("sz%d" % k, [128, D], F32), y1=sb("y1%d" % k, [128, 16, 64], F32),
                    y2=sb("y2%d" % k, [128, 16, 64], F32), junkA=sb("junkA%d" % k, [128, 256], BF16)))
            yo = [sb("yo%d" % i, [128, D], BF16) for i in range(2)]
            snw = sb("snw", [128, D], F32)
            self.epsb = sb("epsb", [128, 1], F32)
            P.op("dve", "memset", writes=["epsb"], ap=self.epsb[:], constant=EPS)
            self.load_w(wa, "wa", l, A0, A_N)
            P.op("sp", "dma_start", writes=["snw"], dma="snw", out=snw[:],
                 in_=self.prow[l * 4176 + 80:l * 4176 + 80 + 1024].partition_broadcast(128))
            ps = self.ps
            def prologue(s, t, par):
                hT, xsT, BT, CT = hTs[par], xsTs[par], BTs[par], CTs[par]
                hk, xk, bk, ck = "hT%d" % par, "xsT%d" % par, "BT%d" % par, "CT%d" % par
                if t == 0:
                    P.op("pool", "memset", writes=["Ucar%d" % b for b in range(16)], ap=Ucar[:], constant=0.0)
                for c4 in range(4):
                    self.h_chunk(l, s, t * 4 + c4, hT, hk, c4 * 128)
                    yield
                for b in range(16):
                    pb = self.psb()
                    self.proj_fm(pb, hT, hk, 512, wa, "wa", b * 128, 128)
                    ukey = "Ub%d" % (b % 2)
                    U_ = Ub[b % 2]
                    P.op("pool", "tensor_copy", reads=["Ucar%d" % b], writes=[ukey], out=U_[:, 0:3], in_=Ucar[:, b, :])
                    P.op("act", "activation", reads=self.pk(pb), writes=[ukey], out=U_[:, 3:515],
                         in_=ps[:, pb, :], func=AF.Copy)
                    a = acc[b % 2]
                    akey = "acc%d" % (b % 2)
                    P.op("dve", "tensor_scalar", reads=[ukey, "ppt"], writes=[akey], out=a[:], in0=U_[:, 0:512],
                         scalar1=self.convw(l, b, 0), scalar2=None, op0=ALU.mult)
                    for k in range(1, 4):
                        P.op("dve", "scalar_tensor_tensor", reads=[ukey, akey, "ppt"], writes=[akey], out=a[:],
                             in0=U_[:, k:k + 512], scalar=self.convw(l, b, k), in1=a[:],
                             op0=ALU.mult, op1=ALU.add)
                    if b < 8:
                        dst, dkey = xsT[:, b, :], xk
                    elif b < 12:
                        dst, dkey = BT[:, b - 8, :], bk
                    else:
                        dst, dkey = CT[:, b - 12, :], ck
                    P.op("act", "activation", reads=[akey, "ppt"], writes=[dkey], out=dst, in_=a[:],
                         func=AF.Silu, bias=self.convb(l, b))
                    P.op("pool", "tensor_copy", reads=[ukey], writes=["Ucar%d" % b], out=Ucar[:, b, :], in_=U_[:, 512:515])
                    yield

            def chunk(s, t, par, c4, k):
                hT, xsT, BT, CT = hTs[par], xsTs[par], BTs[par], CTs[par]
                hk, xk, bk, ck = "hT%d" % par, "xsT%d" % par, "BT%d" % par, "CT%d" % par
                bs = bufsets[k]
                sm, rhsall, cbm, MT, xstm, xdt, xw, Btm, sz, y1, y2, junkA = [bs[n] for n in (
                    "sm", "rhsall", "cbm", "MT", "xstm", "xdt", "xw", "Btm", "sz", "y1", "y2", "junkA")]
                Eh = rhsall
                dec = rhsall
                K_ = lambda n: "%s_s%d" % (n, k)
                if True:
                    c = t * 4 + c4
                    cs = slice(c4 * 128, (c4 + 1) * 128)
                    pd = self.psb()
                    self.proj_tm(pd, hT, hk, c4 * 128, wa, "wa", 3072, 16)
                    dtr, dt_, adt, acum, nacum, lastbc, dS, ea, cd, dtS, e1 = [sm[:, i, :] for i in range(11)]
                    P.op("dve", "tensor_tensor", reads=self.pk(pd) + ["prw"], writes=[K_("sm0")], out=dtr,
                         in0=ps[:, pd, 0:16], in1=self.rowp(l, 0), op=ALU.add)
                    P.op("act", "activation", reads=[K_("sm0")], writes=[K_("sm10")], out=e1, in_=dtr, func=AF.Exp)
                    P.op("act", "activation", reads=[K_("sm10")], writes=[K_("sm1")], out=dt_, in_=e1, func=AF.Ln, bias=1.0)
                    P.op("dve", "tensor_tensor", reads=[K_("sm1"), "abc"], writes=[K_("sm2")], out=adt, in0=dt_,
                         in1=self.abc[:, l * 16:(l + 1) * 16], op=ALU.mult)
                    pa = self.psb()
                    P.op("pe", "matmul", reads=["tri", K_("sm2")], writes=self.pk(pa), out=ps[:, pa, 0:16],
                         lhsT=self.tri[:], rhs=adt, start=True, stop=True)
                    P.op("dve", "tensor_copy", reads=self.pk(pa), writes=[K_("sm3")], out=acum, in_=ps[:, pa, 0:16])
                    P.op("pe", "matmul", reads=["elast", K_("sm3")], writes=self.pk(pa), out=ps[:, pa, 16:32],
                         lhsT=self.elast[:], rhs=acum, start=True, stop=True)
                    P.op("dve", "tensor_copy", reads=self.pk(pa), writes=[K_("sm5")], out=lastbc, in_=ps[:, pa, 16:32])
                    P.op("dve", "tensor_tensor", reads=[K_("sm5"), K_("sm3")], writes=[K_("sm6")], out=dS, in0=lastbc, in1=acum,
                         op=ALU.subtract)
                    P.op("act", "activation", reads=[K_("sm6")], writes=[K_("sm6")], out=dS, in_=dS, func=AF.Exp)
                    P.op("act", "activation", reads=[K_("sm3")], writes=[K_("sm7")], out=ea, in_=acum, func=AF.Exp)
                    P.op("act", "activation", reads=[K_("sm5")], writes=[K_("sm8")], out=cd, in_=lastbc, func=AF.Exp)
                    P.op("dve", "tensor_tensor", reads=[K_("sm1"), K_("sm6")], writes=[K_("sm9")], out=dtS, in0=dt_, in1=dS,
                         op=ALU.mult)
                    yield
                    P.op("dve", "tensor_tensor", reads=["tri", K_("sm2")], writes=[K_("rhsall")] + [K_("Eh%d" % g_) for g_ in range(4)], out=rhsall[:],
                         in0=self.tri[:].unsqueeze(1).broadcast_to([128, 16, 128]),
                         in1=adt.unsqueeze(2).broadcast_to([128, 16, 128]), op=ALU.mult)
                    for g in range(4):
                        pg = self.psb()
                        P.op("pe", "matmul", reads=["onesf", K_("rhsall")], writes=self.pk(pg),
                             out=ps[:, pg, :], lhsT=self.onesf[:],
                             rhs=rhsall[:, 4 * g:4 * g + 4, :], start=True, stop=True)
                        for r in range(4):
                            h = 4 * g + r
                            P.op("dve", "tensor_scalar", reads=self.pk(pg) + [K_("sm3")], writes=[K_("Eh%d" % g)],
                                 out=Eh[:, h, :], in0=ps[:, pg, r * 128:(r + 1) * 128],
                                 scalar1=acum[:, h:h + 1], scalar2=0.0, op0=ALU.subtract, op1=ALU.min)
                        P.op("act", "activation", reads=[K_("Eh%d" % g)], writes=[K_("Eh%d" % g), K_("dec%d" % g)],
                             out=dec[:, 4 * g:4 * g + 4, :], in_=Eh[:, 4 * g:4 * g + 4, :], func=AF.Exp)
                    pc = self.psb()
                    for g in range(4):
                        P.op("pe", "matmul", reads=[bk, ck], writes=self.pk(pc),
                             out=ps[:, pc, g * 128:(g + 1) * 128], lhsT=BT[:, g, cs], rhs=CT[:, g, cs],
                             start=True, stop=True)
                    P.op("dve", "tensor_tensor", reads=self.pk(pc) + ["mlef"], writes=[K_("cbm")], out=cbm[:],
                         in0=ps[:, pc, :].rearrange("p (g l) -> p g l", g=4),
                         in1=self.mlef[:].unsqueeze(1).broadcast_to([128, 4, 128]), op=ALU.mult)
                    for g in range(4):
                        P.op("pool", "tensor_tensor", reads=[K_("dec%d" % g), K_("Eh%d" % g), K_("cbm")], writes=[K_("MT%d" % g)],
                             out=MT[:, 4 * g:4 * g + 4, :], in0=dec[:, 4 * g:4 * g + 4, :],
                             in1=cbm[:, g, :].unsqueeze(1).broadcast_to([128, 4, 128]), op=ALU.mult)
                    yield
                    px = self.psb()
                    pxb = ps[:, px, :].bitcast(BF16).rearrange("p (k t) -> p k t", k=8)
                    for b in range(8):
                        P.op("pe", "transpose", reads=[xk, "ident"], writes=self.pk(px), out=pxb[:, b, :],
                             in_=xsT[:, b, cs], identity=self.ident[:])
                    pxv = ps[:, px, :].bitcast(BF16).rearrange("p (h d) -> p h d", h=16)
                    P.op("act", "activation", reads=self.pk(px), writes=[K_("xstm")], out=xstm[:], in_=pxv, func=AF.Copy)
                    P.op("dve", "tensor_tensor", reads=[K_("xstm"), K_("sm1")], writes=[K_("xdt")], out=xdt[:], in0=xstm[:],
                         in1=dt_.unsqueeze(2).broadcast_to([128, 16, 64]), op=ALU.mult)
                    P.op("pool", "tensor_tensor", reads=[K_("xstm"), K_("sm9")], writes=[K_("xw")], out=xw[:], in0=xstm[:],
                         in1=dtS.unsqueeze(2).broadcast_to([128, 16, 64]), op=ALU.mult)
                    pbt = self.psb()
                    pbb = ps[:, pbt, 0:256].bitcast(BF16).rearrange("p (k t) -> p k t", k=4)
                    for g in range(4):
                        P.op("pe", "transpose", reads=[bk, "ident"], writes=self.pk(pbt), out=pbb[:, g, :],
                             in_=BT[:, g, cs], identity=self.ident[:])
                    P.op("act", "activation", reads=self.pk(pbt), writes=[K_("Btm")], out=Btm[:], in_=pbb, func=AF.Copy)
                    yield
                    P.op("act", "activation", reads=["H"], writes=["Hb"], out=Hb[:], in_=H[:], func=AF.Copy)
                    po = self.psb(2)
                    for g in range(4):
                        P.op("pe", "matmul", reads=[ck, "Hb"], writes=self.pk(po, 2),
                             out=ps[:, po + g // 2, (g % 2) * 256:(g % 2) * 256 + 256],
                             lhsT=CT[:, g, cs], rhs=Hb[:, 4 * g:4 * g + 4, :], start=True, stop=True)
                    poall = ps[:, po:po + 2, :].rearrange("p b (h d) -> p (b h) d", d=64)
                    P.op("dve", "tensor_tensor", reads=self.pk(po, 2) + [K_("sm7")], writes=[K_("y1")], out=y1[:], in0=poall,
                         in1=ea.unsqueeze(2).broadcast_to([128, 16, 64]), op=ALU.mult)
                    pst = self.psb(2)
                    for g in range(4):
                        P.op("pe", "matmul", reads=[K_("Btm"), K_("xw")], writes=self.pk(pst, 2),
                             out=ps[:, pst + g // 2, (g % 2) * 256:(g % 2) * 256 + 256],
                             lhsT=Btm[:, g, :], rhs=xw[:, 4 * g:4 * g + 4, :], start=True, stop=True)
                    pstall = ps[:, pst:pst + 2, :].rearrange("p b (h d) -> p (b h) d", d=64)
                    P.op("dve", "tensor_tensor", reads=["H", K_("sm8")], writes=["Htmp"], out=Htmp[:], in0=H[:],
                         in1=cd.unsqueeze(2).broadcast_to([128, 16, 64]), op=ALU.mult)
                    P.op("dve", "tensor_tensor", reads=self.pk(pst, 2) + ["Htmp"], writes=["H"], out=H[:], in0=pstall,
                         in1=Htmp[:], op=ALU.add)
                    yield
                    pyd = self.psb(2)
                    for h in range(16):
                        P.op("pe", "matmul", reads=[K_("MT%d" % (h // 4)), K_("xdt")], writes=self.pk(pyd, 2),
                             out=ps[:, pyd + h // 8, (h % 8) * 64:(h % 8) * 64 + 64],
                             lhsT=MT[:, h, :], rhs=xdt[:, h, :], start=True, stop=True)
                    pydall = ps[:, pyd:pyd + 2, :].rearrange("p b (h d) -> p (b h) d", d=64)
                    P.op("dve", "tensor_tensor", reads=self.pk(pyd, 2) + [K_("y1")], writes=[K_("y1")], out=y1[:], in0=pydall,
                         in1=y1[:], op=ALU.add)
                    P.op("pool", "tensor_tensor", reads=[K_("xstm"), "prw"], writes=[K_("y2")], out=y2[:], in0=xstm[:],
                         in1=self.rowp(l, 2).unsqueeze(2).broadcast_to([128, 16, 64]), op=ALU.mult)
                    P.op("pool", "tensor_tensor", reads=[K_("y1"), K_("y2")], writes=[K_("y2")], out=y2[:], in0=y1[:], in1=y2[:],
                         op=ALU.add)
                    yield
                    pz = self.psb(2)
                    for n in range(2):
                        self.proj_tm(pz + n, hT, hk, c4 * 128, wa, "wa", 2048 + n * 512, 512)
                    P.op("act", "activation", reads=self.pk(pz, 2), writes=[K_("sz")], out=sz[:],
                         in_=ps[:, pz:pz + 2, :].rearrange("p b n -> p (b n)"), func=AF.Silu)
                    y2f = y2[:].rearrange("p h d -> p (h d)")
                    P.op("dve", "tensor_tensor", reads=[K_("y2"), K_("sz")], writes=[K_("y2")], out=y2f, in0=y2f, in1=sz[:],
                         op=ALU.mult)
                    ss = sm[:, 11, 0:4]
                    for g in range(4):
                        P.op("act", "activation", reads=[K_("y2")], writes=[K_("junkA"), K_("sm11")], out=junkA[:],
                             in_=y2f[:, g * 256:(g + 1) * 256], func=AF.Square, accum_out=sm[:, 11, g:g + 1])
                    P.op("act", "activation", reads=[K_("sm11"), "epsb"], writes=[K_("sm11")], out=sm[:, 11, 4:8], in_=ss, func=AF.Ln,
                         scale=1.0 / 256, bias=self.epsb[:])
                    P.op("act", "activation", reads=[K_("sm11")], writes=[K_("sm11")], out=sm[:, 11, 8:12], in_=sm[:, 11, 4:8],
                         func=AF.Exp, scale=-0.5)
                    P.op("dve", "tensor_tensor", reads=[K_("y2"), K_("sm11")], writes=[K_("y2")],
                         out=y2[:].rearrange("p (g r) d -> p g (r d)", g=4),
                         in0=y2[:].rearrange("p (g r) d -> p g (r d)", g=4),
                         in1=sm[:, 11, 8:12].unsqueeze(2).broadcast_to([128, 4, 256]), op=ALU.mult)
                    yq = yo[c % 2]
                    P.op("pool", "tensor_tensor", reads=[K_("y2"), "snw"], writes=["yo%d" % (c % 2)], out=yq[:], in0=y2f,
                         in1=snw[:], op=ALU.mult)
                    P.op("pool", "dma_start", reads=["yo%d" % (c % 2)], writes=["Y0_%d_%d" % (s, c)],
                         dma="yo%d" % (c % 2), out=self.Y[0, s, c * 128:(c + 1) * 128, :], in_=yq[:])
                    yield

            items = [(s, t) for s in range(self.NSEQ) for t in range(self.NTL)]

            def run(gens):
                alive = list(gens)
                while alive:
                    for g in list(alive):
                        try:
                            next(g)
                        except StopIteration:
                            alive.remove(g)

            for i, (s, t) in enumerate(items):
                par = 0
                if t == 0:
                    P.op("dve", "memset", writes=["H"], ap=H[:], constant=0.0)
                run([prologue(s, t, par)])
                for c4 in (0, 2):
                    run([chunk(s, t, par, c4, 0), chunk(s, t, par, c4 + 1, 1)])
            P.emit()

    def phase_B(self, l):
        nc, P, S, NCH = self.nc, self.P, self.S, self.NCH
        with ExitStack() as st:
            self.uid = getattr(self, "uid", 0) + 1
            sb = lambda name, shape, dt, _u=self.uid: st.enter_context(nc.sbuf_tensor("%s_u%d" % (name, _u), shape, dt))
            wb = sb("wb", [128, KC, B_N], BF16)
            hT = sb("hT", [128, KC, 512], BF16)
            cosT = sb("cosT", [128, 512], F32)
            sinS = sb("sinS", [128, 512], F32)
            t1 = [sb("t1_%d" % i, [128, 512], F32) for i in range(2)]
            t2 = [sb("t2_%d" % i, [128, 512], F32) for i in range(2)]
            qrT = sb("qrT", [128, 8, 512], BF16)
            krT = sb("krT", [128, 2, S], BF16)
            V = sb("V", [128, NCH, 4, 65], BF16)
            sz = sb("sz", [128, D], F32)
            Pc = [sb("Pc%d" % i, [128, 512], BF16) for i in range(2)]
            Pp = [sb("Pp%d" % i, [128, 512], BF16) for i in range(2)]
            den = sb("den", [128, 16], F32)
            yf = sb("yf", [128, 16, 64], F32)
            yo = [sb("yo%d" % i, [128, D], BF16) for i in range(2)]
            self.epsb = sb("epsb", [128, 1], F32)
            P.op("dve", "memset", writes=["epsb"], ap=self.epsb[:], constant=EPS)
            nmc = sb("nmc", [128, 512], BF16)
            nmp = sb("nmp", [128, 512], BF16)
            P.op("dve", "tensor_scalar", reads=["mle"], writes=["nmc"], out=nmc[:].rearrange("p (a q) -> p a q", a=4),
                 in0=self.mle[:].unsqueeze(1).broadcast_to([128, 4, 128]), scalar1=-1.0, scalar2=30000.0,
                 op0=ALU.add, op1=ALU.mult)
            P.op("dve", "tensor_scalar", reads=["mgt"], writes=["nmp"], out=nmp[:].rearrange("p (a q) -> p a q", a=4),
                 in0=self.mgt[:].unsqueeze(1).broadcast_to([128, 4, 128]), scalar1=-1.0, scalar2=30000.0,
                 op0=ALU.add, op1=ALU.mult)
            self.load_w(wb, "wb", l, B0, B_N)
            P.op("pool", "memset", writes=["V%d" % i for i in range(NCH)], ap=V[:], constant=1.0)
            ps = self.ps
            for s in range(self.NSEQ):
                for t in range(self.NTL):
                    ts = slice(t * 512, (t + 1) * 512)
                    for c4 in range(4):
                        self.h_chunk(l, s, t * 4 + c4, hT, "hT", c4 * 128)
                    P.op("sp", "dma_start", writes=["cosT"], dma="cosT", out=cosT[:], in_=self.rope[0, :, ts])
                    P.op("sp", "dma_start", writes=["sinS"], dma="sinS", out=sinS[:], in_=self.rope[1, :, ts])
                    for b in range(10):
                        c0 = b * 128 if b < 8 else 1024 + (b - 8) * 128
                        c1 = 1280 + c0
                        pq = self.psb()
                        self.proj_fm(pq, hT, "hT", 512, wb, "wb", c0, 128)
                        pqs = self.psb()
                        self.proj_fm(pqs, hT, "hT", 512, wb, "wb", c1, 128)
                        i2 = b % 2
                        P.op("dve", "tensor_tensor", reads=self.pk(pq) + ["cosT"], writes=["t1_%d" % i2], out=t1[i2][:],
                             in0=ps[:, pq, :], in1=cosT[:], op=ALU.mult)
                        P.op("dve", "tensor_tensor", reads=self.pk(pqs) + ["sinS"], writes=["t2_%d" % i2], out=t2[i2][:],
                             in0=ps[:, pqs, :], in1=sinS[:], op=ALU.mult)
                        if b < 8:
                            dst, dkey = qrT[:, b, :], "qrT"
                        else:
                            dst, dkey = krT[:, b - 8, ts], "krT%d" % t
                        P.op("pool", "tensor_tensor", reads=["t1_%d" % i2, "t2_%d" % i2], writes=[dkey], out=dst,
                             in0=t1[i2][:], in1=t2[i2][:], op=ALU.add)
                    for c4 in range(4):
                        c = t * 4 + c4
                        cs = slice(c4 * 128, (c4 + 1) * 128)
                        pv = self.psb()
                        self.proj_tm(pv, hT, "hT", c4 * 128, wb, "wb", 2560, 256)
                        P.op("act", "activation", reads=self.pk(pv), writes=["V%d" % c], out=V[:, c, :, 0:64],
                             in_=ps[:, pv, 0:256].rearrange("p (h d) -> p h d", h=4), func=AF.Copy)
                        pz = self.psb(2)
                        for n in range(2):
                            self.proj_tm(pz + n, hT, "hT", c4 * 128, wb, "wb", 2816 + n * 512, 512)
                        P.op("act", "activation", reads=self.pk(pz, 2), writes=["sz"], out=sz[:],
                             in_=ps[:, pz:pz + 2, :].rearrange("p b n -> p (b n)"), func=AF.Silu)
                        def qk_stage(kv):
                            half = slice((kv % 2) * 64, (kv % 2) * 64 + 64)
                            blk0 = (kv // 2) * 4
                            qv = qrT[half, blk0:blk0 + 4, cs]
                            psc = self.psb()
                            P.op("pe", "matmul", reads=["krT%d" % t, "qrT"], writes=self.pk(psc),
                                 out=ps[:, psc, :].rearrange("p (a q) -> p a q", a=4),
                                 lhsT=krT[half, kv // 2, c * 128:(c + 1) * 128], rhs=qv, start=True, stop=False)
                            P.op("pe", "matmul", reads=["ident", "nmc"], writes=self.pk(psc), out=ps[:, psc, :],
                                 lhsT=self.ident[:], rhs=nmc[:], start=False, stop=True)
                            psp = None
                            if c > 0:
                                psp = self.psb()
                                P.op("pe", "matmul", reads=["krT%d" % ((c - 1) // 4), "qrT"], writes=self.pk(psp),
                                     out=ps[:, psp, :].rearrange("p (a q) -> p a q", a=4),
                                     lhsT=krT[half, kv // 2, (c - 1) * 128:c * 128], rhs=qv, start=True, stop=False)
                                P.op("pe", "matmul", reads=["ident", "nmp"], writes=self.pk(psp), out=ps[:, psp, :],
                                     lhsT=self.ident[:], rhs=nmp[:], start=False, stop=True)
                            return psc, psp

                        pend = qk_stage(0)
                        for kv in range(4):
                            i2 = kv % 2
                            psc, psp = pend
                            if kv + 1 < 4:
                                pend = qk_stage(kv + 1)
                            P.op("act", "activation", reads=self.pk(psc), writes=["Pc%d" % i2], out=Pc[i2][:],
                                 in_=ps[:, psc, :], func=AF.Exp, scale=0.125)
                            if c > 0:
                                P.op("act", "activation", reads=self.pk(psp), writes=["Pp%d" % i2], out=Pp[i2][:],
                                     in_=ps[:, psp, :], func=AF.Exp, scale=0.125)
                            po = self.psb()
                            for a in range(4):
                                if c > 0:
                                    P.op("pe", "matmul", reads=["Pp%d" % i2, "V%d" % (c - 1)], writes=self.pk(po),
                                         out=ps[:, po, a * 65:(a + 1) * 65], lhsT=Pp[i2][:, a * 128:(a + 1) * 128],
                                         rhs=V[:, c - 1, kv, :], start=True, stop=False)
                                P.op("pe", "matmul", reads=["Pc%d" % i2, "V%d" % c], writes=self.pk(po),
                                     out=ps[:, po, a * 65:(a + 1) * 65], lhsT=Pc[i2][:, a * 128:(a + 1) * 128],
                                     rhs=V[:, c, kv, :], start=(c == 0), stop=True)
                            pov = ps[:, po, 0:260].rearrange("p (a e) -> p a e", a=4)
                            P.op("dve", "tensor_tensor", reads=self.pk(po) + ["esink"], writes=["den%d" % kv],
                                 out=den[:, 4 * kv:4 * kv + 4].unsqueeze(2), in0=pov[:, :, 64:65],
                                 in1=self.esink[:, l * 16 + 4 * kv:l * 16 + 4 * kv + 4].unsqueeze(2), op=ALU.add)
                            P.op("dve", "reciprocal", reads=["den%d" % kv], writes=["den%d" % kv],
                                 out=den[:, 4 * kv:4 * kv + 4], in_=den[:, 4 * kv:4 * kv + 4])
                            P.op("dve", "tensor_tensor", reads=self.pk(po) + ["den%d" % kv], writes=["yf%d" % kv],
                                 out=yf[:, 4 * kv:4 * kv + 4, :], in0=pov[:, :, 0:64],
                                 in1=den[:, 4 * kv:4 * kv + 4].unsqueeze(2).broadcast_to([128, 4, 64]), op=ALU.mult)
                        yq = yo[c % 2]
                        P.op("pool", "tensor_tensor", reads=["yf%d" % k for k in range(4)] + ["sz"],
                             writes=["yo%d" % (c % 2)], out=yq[:], in0=yf[:].rearrange("p h d -> p (h d)"), in1=sz[:],
                             op=ALU.mult)
                        P.op("pool", "dma_start", reads=["yo%d" % (c % 2)], writes=["Y1_%d_%d" % (s, c)],
                             dma="yo%d" % (c % 2), out=self.Y[1, s, c * 128:(c + 1) * 128, :], in_=yq[:])
            P.emit()

    def phase_C(self, l, hg):
        nc, P, S, NCH = self.nc, self.P, self.S, self.NCH
        with ExitStack() as st:
            self.uid = getattr(self, "uid", 0) + 1
            sb = lambda name, shape, dt, _u=self.uid: st.enter_context(nc.sbuf_tensor("%s_u%d" % (name, _u), shape, dt))
            wc = sb("wc", [128, KC, C_G], BF16)
            hT = sb("hT", [128, KC, 512], BF16)
            KT = sb("KT", [72, 8, S], BF16)
            V = sb("V", [128, NCH, 8, 65], BF16)
            QT = sb("QT", [72, 8, 512], BF16)
            NC_ = sb("NC", [128, NCH, 8], F32)
            fsm = sb("fsm", [128, 4, 8], F32)
            cumT = sb("cumT", [8, 512], BF16)
            szT = sb("szT", [64, 8, 512], F32)
            PT = [sb("PT%d" % i, [128, 512], BF16) for i in range(4)]
            rd = [sb("rd%d" % i, [65, 512], BF16) for i in range(2)]
            rdf = [sb("rdf%d" % i, [65, 512], F32) for i in range(2)]
            yn = [sb("yn%d" % i, [64, 512], F32) for i in range(2)]
            yo = [sb("yo0", [64, 8, 512], BF16)] * 2
            self.epsb = sb("epsb", [128, 1], F32)
            P.op("dve", "memset", writes=["epsb"], ap=self.epsb[:], constant=EPS)
            self.load_w(wc, "wc", l, C0 + hg * C_G, C_G)
            P.op("pool", "memset", writes=["V%d" % i for i in range(NCH)], ap=V[:], constant=1.0)
            P.op("pool", "memset", writes=["KT%d" % i for i in range(self.NTL)], ap=KT[64:72, :, :], constant=1.0)
            P.op("pool", "affine_select", reads=["KT%d" % i for i in range(self.NTL)],
                 writes=["KT%d" % i for i in range(self.NTL)], out=KT[64:72, :, :], in_=KT[64:72, :, :],
                 pattern=[[1, 8], [0, S]], compare_op=ALU.is_equal, fill=0.0, base=0, channel_multiplier=-1)
            ps = self.ps
            self.pools = {"gen": [0, 1], "ct": [2], "acc": [2, 3, 4], "sc": [5, 6, 7]}
            self.prr = {}
            fb = self.rowp(l, 4)[:, hg * 8:hg * 8 + 8]
            pti = 0
            for s in range(self.NSEQ):
                for t in range(self.NTL):
                    ts = slice(t * 512, (t + 1) * 512)
                    for c4 in range(4):
                        self.h_chunk(l, s, t * 4 + c4, hT, "hT", c4 * 128)
                    for hp in range(4):
                        pkb = self.psb()
                        self.proj_fm(pkb, hT, "hT", 512, wc, "wc", 512 + hp * 128, 128)
                        for e in range(2):
                            P.op("dve", "tensor_copy", reads=self.pk(pkb), writes=["KT%d" % t], out=KT[0:64, 2 * hp + e, ts],
                                 in_=ps[64 * e:64 * e + 64, pkb, :])
                    pct = self.psb(pool="ct")
                    for c4 in range(4):
                        c = t * 4 + c4
                        pf = self.psb()
                        self.proj_tm(pf, hT, "hT", c4 * 128, wc, "wc", 2048, 8)
                        f0, f1, f2 = fsm[:, 0, :], fsm[:, 1, :], fsm[:, 2, :]
                        P.op("dve", "tensor_tensor", reads=self.pk(pf) + ["prw"], writes=["f0"], out=f0,
                             in0=ps[:, pf, 0:8], in1=fb, op=ALU.add)
                        P.op("act", "activation", reads=["f0"], writes=["f1"], out=f1, in_=f0, func=AF.Exp, scale=-1.0)
                        P.op("act", "activation", reads=["f1"], writes=["f2"], out=f2, in_=f1, func=AF.Ln, bias=1.0)
                        P.op("pe", "matmul", reads=["tri", "f2"], writes=self.pk(pf), out=ps[:, pf, 8:16],
                             lhsT=self.tri[:], rhs=f2, start=True, stop=(c == 0))
                        if c > 0:
                            P.op("pe", "matmul", reads=["elast", "NC%d" % (c - 1)], writes=self.pk(pf), out=ps[:, pf, 8:16],
                                 lhsT=self.elast[:], rhs=NC_[:, c - 1, :], start=False, stop=True)
                        P.op("dve", "tensor_copy", reads=self.pk(pf), writes=["NC%d" % c], out=NC_[:, c, :], in_=ps[:, pf, 8:16])
                        P.op("pe", "transpose", reads=["NC%d" % c, "identf"], writes=self.pk(pct),
                             out=ps[0:8, pct, c4 * 128:(c4 + 1) * 128], in_=NC_[:, c, :], identity=self.identf[:])
                        pv = self.psb()
                        self.proj_tm(pv, hT, "hT", c4 * 128, wc, "wc", 1024, 512)
                        P.op("act", "activation", reads=self.pk(pv), writes=["V%d" % c], out=V[:, c, :, 0:64],
                             in_=ps[:, pv, :].rearrange("p (h d) -> p h d", h=8), func=AF.Copy)
                    P.op("dve", "tensor_scalar", reads=self.pk(pct), writes=["QT%d" % h for h in range(8)],
                         out=QT[64:72, :, :], in0=ps[0:8, pct, :].unsqueeze(1).broadcast_to([8, 8, 512]),
                         scalar1=-8.0, scalar2=None, op0=ALU.mult)
                    yq = yo[0]
                    ykey = "yo0"
                    for hp in range(4):
                        pz = self.psb()
                        self.proj_fm(pz, hT, "hT", 512, wc, "wc", 1536 + hp * 128, 128)
                        for e in range(2):
                            P.op("act", "activation", reads=self.pk(pz), writes=["szT%d" % (2 * hp + e)],
                                 out=szT[:, 2 * hp + e, :], in_=ps[64 * e:64 * e + 64, pz, :], func=AF.Silu)
                    for hp in range(4):
                        pqb = self.psb()
                        self.proj_fm(pqb, hT, "hT", 512, wc, "wc", hp * 128, 128)
                        for e in range(2):
                            P.op("dve", "tensor_copy", reads=self.pk(pqb), writes=["QT%d" % (2 * hp + e)],
                                 out=QT[0:64, 2 * hp + e, :], in_=ps[64 * e:64 * e + 64, pqb, :])
                    nkb = 4 * t + 4

                    def qk(h, j):
                        q0 = max(j - 4 * t, 0)
                        nq = 4 - q0
                        psc = self.psb(pool="sc")
                        P.op("pe", "matmul", reads=["KT%d" % (j // 4), "QT%d" % h], writes=self.pk(psc),
                             out=ps[:, psc, 0:nq * 128], lhsT=KT[:, h, j * 128:(j + 1) * 128],
                             rhs=QT[:, h, q0 * 128:512], start=True, stop=True)
                        return psc

                    seq = [(2 * hp + e, j) for hp in range(4) for j in range(nkb) for e in range(2)]
                    DIST = 2
                    pend = [qk(*seq[i]) for i in range(min(DIST, len(seq)))]
                    pos = {}
                    for i, (h, j) in enumerate(seq):
                        psc = pend.pop(0)
                        if i + DIST < len(seq):
                            pend.append(qk(*seq[i + DIST]))
                        if j == 0:
                            pos[h] = self.psb(pool="acc")
                        po = pos[h]
                        jj = j - 4 * t
                        q0 = max(jj, 0)
                        nq = 4 - q0
                        pt = PT[pti % 4]
                        ptk = "PT%d" % (pti % 4)
                        pti += 1
                        P.op("act", "activation", reads=self.pk(psc) + ["NC%d" % j], writes=[ptk], out=pt[:, 0:nq * 128],
                             in_=ps[:, psc, 0:nq * 128], func=AF.Exp, scale=0.125, bias=NC_[:, j, h:h + 1])
                        if jj >= 0:
                            P.op("pool", "tensor_tensor", reads=[ptk, "mle"], writes=[ptk], out=pt[:, 0:128],
                                 in0=pt[:, 0:128], in1=self.mle[:], op=ALU.mult)
                        P.op("pe", "matmul", reads=[ptk, "V%d" % j], writes=self.pk(po), out=ps[0:65, po, q0 * 128:512],
                             lhsT=V[:, j, h, :], rhs=pt[:, 0:nq * 128], start=(j == 0), stop=(j == nkb - 1))
                        if j == nkb - 1:
                            r_, y_ = rd[h % 2], yn[h % 2]
                            rk, yk = "rd%d" % (h % 2), "yn%d" % (h % 2)
                            rf_ = rdf[h % 2]
                            P.op("act", "activation", reads=self.pk(po), writes=[rk + "f"], out=rf_[64:65, :],
                                 in_=ps[64:65, po, :], func=AF.Ln)
                            P.op("act", "activation", reads=[rk + "f"], writes=[rk], out=r_[64:65, :],
                                 in_=rf_[64:65, :], func=AF.Exp, scale=-1.0)
                            pbc = self.psb()
                            P.op("pe", "matmul", reads=[rk, "onesb"], writes=self.pk(pbc), out=ps[0:64, pbc, :],
                                 lhsT=self.onesb[64:65, 0:64], rhs=r_[64:65, :], start=True, stop=True)
                            P.op("dve", "tensor_tensor", reads=self.pk(pbc) + ["szT%d" % h], writes=[yk], out=y_[:],
                                 in0=ps[0:64, pbc, :], in1=szT[:, h, :], op=ALU.mult)
                            P.op("dve", "tensor_tensor", reads=self.pk(po) + [yk], writes=[ykey], out=yq[:, h, :],
                                 in0=ps[0:64, po, :], in1=y_[:], op=ALU.mult)
                    P.op("pool", "dma_start", reads=[ykey], writes=["Y2T_%d_%d_%d" % (s, t, hg)], dma=ykey,
                         out=self.Y2T[s, hg * 512:(hg + 1) * 512, ts].rearrange("(h d) t -> d h t", d=64), in_=yq[:])
            P.emit()
            self.pools = {"gen": list(range(8))}
            self.prr = {}

    def phase_D(self, l):
        nc, P, S = self.nc, self.P, self.S
        last = (l == self.layers - 1)
        NS = self.NSEQ
        with ExitStack() as st:
            self.uid = getattr(self, "uid", 0) + 1
            sb = lambda name, shape, dt, _u=self.uid: st.enter_context(nc.sbuf_tensor("%s_u%d" % (name, _u), shape, dt))
            wg = sb("wg", [128, KC, G_N], BF16)
            wp = [sb("wp%d" % i, [128, KC, D], BF16) for i in range(3)]
            wo = sb("wo", [128, KC, D], BF16)
            gbb = sb("gbb", [128, 3 * D], F32)
            fnw = sb("fnw", [128, D], F32) if last else None
            nb = max(NS, 2)
            yt = [[sb("yt%d_%d" % (b, i), [128, D], BF16) for i in range(nb)] for b in range(2)]
            ybTc = [sb("ybTc%d" % i, [128, KC, 128], BF16) for i in range(nb)]
            hTs = [sb("hT%d" % i, [128, KC, 128], BF16) for i in range(nb)]
            ybTs = [sb("ybT%d" % i, [128, KC, 128], BF16) for i in range(nb)]
            gss = [sb("gs%d" % i, [128, D], F32) for i in range(nb)]
            mgs = [sb("mg%d" % i, [128, D], F32) for i in range(nb)]
            mgbs = [sb("mgb%d" % i, [128, D], BF16) for i in range(nb)]
            mTs = [sb("mT%d" % i, [128, KC, 128], BF16) for i in range(nb)]
            xo = [sb("xo%d" % i, [128, D], F32) for i in range(nb)]
            fss = [sb("fs%d" % i, [128, 4], F32) for i in range(nb)]
            self.epsb = sb("epsb", [128, 1], F32)
            P.op("dve", "memset", writes=["epsb"], ap=self.epsb[:], constant=EPS)
            self.load_w(wg, "wg", l, G0, G_N)
            for b in range(3):
                self.load_w(wp[b], "wp%d" % b, l, 0, D, src=self.wproj[l, b])
            self.load_w(wo, "wo", l, 0, D, src=self.wout[l])
            P.op("sp", "dma_start", writes=["gbb"], dma="gbb", out=gbb[:],
                 in_=self.prow[l * 4176 + 1104:l * 4176 + 1104 + 3072].partition_broadcast(128))
            if last:
                P.op("sp", "dma_start", writes=["fnw"], dma="fnw", out=fnw[:],
                     in_=self.prow[DEPTH * 4176:DEPTH * 4176 + 1024].partition_broadcast(128))
            ps = self.ps

            def chunk_gen(s, c, k):
                hT, ybT, gs, mg, mgb, mT, xq, fs = hTs[k], ybTs[k], gss[k], mgs[k], mgbs[k], mTs[k], xo[k], fss[k]
                K_ = lambda n: "%s%d" % (n, k)
                for b in range(2):
                    P.op("sp", "dma_start", reads=["Y%d_%d_%d" % (b, s, c)], writes=["yt%d_%d" % (b, k)],
                         dma="yt%d_%d" % (b, k), out=yt[b][k][:], in_=self.Y[b, s, c * 128:(c + 1) * 128, :])
                P.op("sp", "dma_start", reads=["Y2T_%d_%d_%d" % (s, c // 4, g) for g in range(2)],
                     writes=[K_("ybTc")], dma=K_("ybTc"), out=ybTc[k][:],
                     in_=self.Y2T[s, :, c * 128:(c + 1) * 128].rearrange("(kc p) t -> p kc t", p=128))
                sl = self.h_chunk(l, s, c, hT, K_("hT"), 0, slot=k % 2)
                yield
                for b in range(3):
                    pg = self.psb(2)
                    for n in range(2):
                        self.proj_tm(pg + n, hT, K_("hT"), 0, wg, "wg", b * D + n * 512, 512)
                    P.op("dve", "tensor_tensor", reads=self.pk(pg, 2) + ["gbb"], writes=[K_("gs")], out=gs[:],
                         in0=ps[:, pg:pg + 2, :].rearrange("p b n -> p (b n)"), in1=gbb[:, b * D:(b + 1) * D],
                         op=ALU.add)
                    P.op("act", "activation", reads=[K_("gs")], writes=[K_("gs")], out=gs[:], in_=gs[:], func=AF.Sigmoid)
                    if b < 2:
                        pt_ = self.psb()
                        ptb = ps[:, pt_, :].bitcast(BF16).rearrange("p (k t) -> p k t", k=8)
                        for kc in range(KC):
                            P.op("pe", "transpose", reads=["yt%d_%d" % (b, k), "ident"], writes=self.pk(pt_),
                                 out=ptb[:, kc, :], in_=yt[b][k][:, kc * 128:(kc + 1) * 128], identity=self.ident[:])
                        P.op("act", "activation", reads=self.pk(pt_), writes=[K_("ybT")], out=ybT[:], in_=ptb, func=AF.Copy)
                        ysrc, ykey_ = ybT, K_("ybT")
                    else:
                        ysrc, ykey_ = ybTc[k], K_("ybTc")
                    yield
                    pb = self.psb(2)
                    for n in range(2):
                        for kc in range(KC):
                            P.op("pe", "matmul", reads=[ykey_, "wp%d" % b], writes=self.pk(pb + n),
                                 out=ps[:, pb + n, :], lhsT=ysrc[:, kc, :], rhs=wp[b][:, kc, n * 512:(n + 1) * 512],
                                 start=(kc == 0), stop=(kc == KC - 1))
                    pball = ps[:, pb:pb + 2, :].rearrange("p b n -> p (b n)")
                    if b == 0:
                        P.op("dve", "tensor_tensor", reads=self.pk(pb, 2) + [K_("gs")], writes=[K_("mg")], out=mg[:],
                             in0=pball, in1=gs[:], op=ALU.mult)
                    else:
                        P.op("dve", "tensor_tensor", reads=self.pk(pb, 2) + [K_("gs")], writes=[K_("gs")], out=gs[:],
                             in0=pball, in1=gs[:], op=ALU.mult)
                        P.op("pool", "tensor_tensor", reads=[K_("gs"), K_("mg")], writes=[K_("mg")], out=mg[:],
                             in0=gs[:], in1=mg[:], op=ALU.add)
                    yield
                P.op("act", "activation", reads=[K_("mg")], writes=[K_("mgb")], out=mgb[:], in_=mg[:], func=AF.Copy)
                pt_ = self.psb()
                ptb = ps[:, pt_, :].bitcast(BF16).rearrange("p (k t) -> p k t", k=8)
                for kc in range(KC):
                    P.op("pe", "transpose", reads=[K_("mgb"), "ident"], writes=self.pk(pt_), out=ptb[:, kc, :],
                         in_=mgb[:, kc * 128:(kc + 1) * 128], identity=self.ident[:])
                P.op("dve", "tensor_copy", reads=self.pk(pt_), writes=[K_("mT")], out=mT[:], in_=ptb)
                yield
                po = self.psb(2)
                for n in range(2):
                    for kc in range(KC):
                        P.op("pe", "matmul", reads=[K_("mT"), "wo"], writes=self.pk(po + n), out=ps[:, po + n, :],
                             lhsT=mT[:, kc, :], rhs=wo[:, kc, n * 512:(n + 1) * 512], start=(kc == 0),
                             stop=(kc == KC - 1))
                P.op("dve", "tensor_tensor", reads=self.pk(po, 2) + ["xt%d" % sl], writes=[K_("xo")], out=xq[:],
                     in0=ps[:, po:po + 2, :].rearrange("p b n -> p (b n)"), in1=self.xt[sl][:], op=ALU.add)
                if not last:
                    P.op("pool", "dma_start", reads=[K_("xo")], writes=["x1_%d_%d" % (s, c)], dma=K_("xo"),
                         out=self.X1[s, c * 128:(c + 1) * 128, :], in_=xq[:])
                else:
                    P.op("act", "activation", reads=[K_("xo")], writes=["junk%d" % sl, K_("fs")], out=self.junk[sl][:],
                         in_=xq[:], func=AF.Square, accum_out=fs[:, 0:1])
                    P.op("act", "activation", reads=[K_("fs"), "epsb"], writes=[K_("fs")], out=fs[:, 1:2], in_=fs[:, 0:1],
                         func=AF.Ln, scale=1.0 / D, bias=self.epsb[:])
                    P.op("act", "activation", reads=[K_("fs")], writes=[K_("fs")], out=fs[:, 2:3], in_=fs[:, 1:2],
                         func=AF.Exp, scale=-0.5)
                    P.op("dve", "scalar_tensor_tensor", reads=[K_("xo"), K_("fs"), "fnw"], writes=[K_("xo")],
                         out=xq[:], in0=xq[:], scalar=fs[:, 2:3], in1=fnw[:], op0=ALU.mult, op1=ALU.mult)
                    P.op("pool", "dma_start", reads=[K_("xo")], writes=["out_%d_%d" % (s, c)], dma=K_("xo"),
                         out=self.out[s, c * 128:(c + 1) * 128, :], in_=xq[:])

            if NS >= 2:
                items = [[(s, c, s) for s in range(NS)] for c in range(self.NCH)]
            else:
                items = [[(0, c + e, e) for e in range(2) if c + e < self.NCH] for c in range(0, self.NCH, 2)]
            for group in items:
                alive = [chunk_gen(*it) for it in group]
                while alive:
                    for g in list(alive):
                        try:
                            next(g)
                        except StopIteration:
                            alive.remove(g)
            P.emit()


def host_layout(S, norm_w, w_in, conv_w, conv_b, dt_bias, a_log, d_skip, ssm_norm_w, sinks, f_bias, gate_bias,
                w_proj, w_out, final_norm_w):
    L = DEPTH
    cols = _col_order()
    win = np.ascontiguousarray(
        np.asarray(w_in, np.float32)[:, :, cols].reshape(L, KC, 128, NT).transpose(0, 2, 1, 3))
    wproj = np.ascontiguousarray(np.asarray(w_proj, np.float32).reshape(L, 3, KC, 128, D).transpose(0, 1, 3, 2, 4))
    wout = np.ascontiguousarray(np.asarray(w_out, np.float32).reshape(L, KC, 128, D).transpose(0, 2, 1, 3))
    ppart = np.zeros((128, L * 88), np.float32)
    prow = np.zeros((L * 4176 + 1024,), np.float32)
    for l in range(L):
        ppart[:, l * 88:l * 88 + 8] = np.asarray(norm_w[l]).reshape(KC, 128).T
        cw = np.asarray(conv_w[l]).reshape(4, 16, 128)
        ppart[:, l * 88 + 8:l * 88 + 72] = cw.transpose(2, 1, 0).reshape(128, 64)
        ppart[:, l * 88 + 72:l * 88 + 88] = np.asarray(conv_b[l]).reshape(16, 128).T
        o = l * 4176
        prow[o:o + 16] = dt_bias[l]
        prow[o + 16:o + 32] = a_log[l]
        prow[o + 32:o + 48] = d_skip[l]
        prow[o + 48:o + 64] = sinks[l]
        prow[o + 64:o + 80] = f_bias[l]
        prow[o + 80:o + 1104] = ssm_norm_w[l]
        prow[o + 1104:o + 4176] = np.asarray(gate_bias[l]).reshape(-1)
    prow[L * 4176:] = final_norm_w
    pos = np.arange(S, dtype=np.float32)
    inv = (np.float32(10000.0) ** (-np.arange(0, 64, 2, dtype=np.float32) / np.float32(64))).astype(np.float32)
    ang = (pos[:, None] * inv[None, :]).astype(np.float32)
    cos, sin = np.cos(ang).astype(np.float32), np.sin(ang).astype(np.float32)
    rope = np.zeros((2, 128, S), np.float32)
    for p in range(128):
        rope[0, p] = cos[:, p % 32]
        rope[1, p] = sin[:, p % 32] * (-1.0 if (p % 64) < 32 else 1.0)
    return dict(win=win, wproj=wproj, wout=wout, ppart=ppart, prow=prow, rope=rope)


_NC_CACHE = {}


def kernel(x, norm_w, w_in, conv_w, conv_b, dt_bias, a_log, d_skip, ssm_norm_w, sinks, f_bias, gate_bias,
           w_proj, w_out, final_norm_w):
    x = np.asarray(x, np.float32)
    B, S, _ = x.shape
    nseq = B // NCORES
    shared = host_layout(S, norm_w, w_in, conv_w, conv_b, dt_bias, a_log, d_skip, ssm_norm_w, sinks, f_bias,
                         gate_bias, w_proj, w_out, final_norm_w)
    key = (S, nseq)
    if key not in _NC_CACHE:
        _NC_CACHE[key] = Builder(S, nseq).build()
    nc = _NC_CACHE[key]
    in_maps = []
    for c in range(NCORES):
        m = dict(shared)
        m["x"] = np.ascontiguousarray(x[c * nseq:(c + 1) * nseq])
        in_maps.append(m)
    res = run_bass_kernel_spmd(nc, in_maps, core_ids=list(range(NCORES)))
    return np.concatenate([r["out"] for r in res.results], axis=0).astype(np.float32)
```

```python
import math
from contextlib import ExitStack

import numpy as np
import concourse.bass as bass
import concourse.mybir as mybir
from concourse.bass_utils import run_bass_kernel_spmd

F32 = mybir.dt.float32
BF16 = mybir.dt.bfloat16
AF = mybir.ActivationFunctionType
ALU = mybir.AluOpType

D = 1024
KC = 8
DEPTH = 2
NCORES = 8
EPS = 1e-6
ENGS = ("pe", "act", "dve", "pool", "sp")

A0, A_N = 0, 3088
B0, B_N = 3088, 3840
C0, C_G = 6928, 2056
G0, G_N = 6928 + 2 * 2056, 3072
NT = G0 + G_N
SWA_QORDER = [0, 4, 1, 5, 2, 6, 3, 7, 8, 12, 9, 13, 10, 14, 11, 15]


def _col_order():
    o = {}
    off = 0
    names = [("a_xbc", 2048), ("a_z", 1024), ("a_dt", 16), ("b_q", 1024), ("b_k", 256), ("b_v", 256),
             ("b_z", 1024), ("c_q", 1024), ("c_k", 1024), ("c_v", 1024), ("c_f", 16), ("c_z", 1024),
             ("gates", 3072)]
    for n, s in names:
        o[n] = off
        off += s
    cols = []
    cols += list(range(o["a_xbc"], o["a_xbc"] + 2048))
    cols += list(range(o["a_z"], o["a_z"] + 1024))
    cols += list(range(o["a_dt"], o["a_dt"] + 16))
    assert len(cols) == A_N
    q = [o["b_q"] + h * 64 + d for h in SWA_QORDER for d in range(64)]
    qs = [o["b_q"] + h * 64 + (d + 32) % 64 for h in SWA_QORDER for d in range(64)]
    k = [o["b_k"] + h * 64 + d for h in range(4) for d in range(64)]
    ks = [o["b_k"] + h * 64 + (d + 32) % 64 for h in range(4) for d in range(64)]
    cols += q + k + qs + ks
    cols += list(range(o["b_v"], o["b_v"] + 256))
    cols += list(range(o["b_z"], o["b_z"] + 1024))
    assert len(cols) == B0 + B_N
    for hg in range(2):
        for nm in ("c_q", "c_k", "c_v", "c_z"):
            cols += list(range(o[nm] + hg * 512, o[nm] + hg * 512 + 512))
        cols += list(range(o["c_f"] + hg * 8, o["c_f"] + hg * 8 + 8))
    assert len(cols) == G0
    cols += list(range(o["gates"], o["gates"] + 3072))
    assert len(cols) == NT
    return np.array(cols, dtype=np.int64)


class Prog:
    def __init__(self, nc, stack, same_engine_sync=True):
        self.nc = nc
        self.stack = stack
        self.same = same_engine_sync
        self.ops = []
        self.last_w = {}
        self.readers = {}
        self.eng_sem = {e: stack.enter_context(nc.semaphore("s_" + e)) for e in ENGS}
        self.cnt = {e: 0 for e in ENGS}
        self.dsem = {}
        self.dcnt = {}
        self.known = {e: {} for e in ENGS}
        self.done_ops = 0

    def op(self, eng, meth, reads=(), writes=(), dma=None, **kw):
        idx = len(self.ops)
        deps = set()
        for k in reads:
            if k in self.last_w:
                deps.add(self.last_w[k])
        for k in writes:
            if k in self.last_w:
                deps.add(self.last_w[k])
            for r in self.readers.get(k, ()):
                deps.add(r)
        deps.discard(idx)
        best = {}
        for d in deps:
            od = self.ops[d]
            kk = ("d", od["dma"]) if od["dma"] is not None else ("e", od["eng"])
            if kk not in best or best[kk] < d:
                best[kk] = d
        deps = set(best.values())
        for k in reads:
            self.readers.setdefault(k, []).append(idx)
        for k in writes:
            self.last_w[k] = idx
            self.readers[k] = []
        self.ops.append(dict(eng=eng, meth=meth, kw=kw, deps=deps, dma=dma, sig=False, ev=None))
        return idx

    def emit(self, final=False):
        nc, ops = self.nc, self.ops
        import os as _os
        if _os.environ.get("OPS_LIMIT"):
            del ops[int(_os.environ["OPS_LIMIT"]):]
        lo = self.done_ops
        new = range(lo, len(ops))
        for i in new:
            o = ops[i]
            if o["dma"] is not None:
                o["sig"] = True
            for d in o["deps"]:
                od = ops[d]
                if d < lo:
                    continue
                if od["dma"] is not None or od["eng"] != o["eng"] or o["dma"] is not None:
                    od["sig"] = True
                elif self.same and o["eng"] != "pe":
                    od["sig"] = True
        for i in new:
            o = ops[i]
            if o["dma"] is not None:
                k = o["dma"]
                if k not in self.dsem:
                    self.dsem[k] = self.stack.enter_context(nc.semaphore("d_" + str(k)))
                    self.dcnt[k] = 0
                self.dcnt[k] += 16
                o["ev"] = (self.dsem[k], self.dcnt[k], "d_" + str(k))
            elif o["sig"]:
                self.cnt[o["eng"]] += 1
                o["ev"] = (self.eng_sem[o["eng"]], self.cnt[o["eng"]], o["eng"])
        per_eng = {e: [] for e in ENGS}
        for i in new:
            per_eng[ops[i]["eng"]].append(i)
        same = self.same

        def body(ename):
            def f(eng):
                kn = self.known[ename]
                for i in per_eng[ename]:
                    o = ops[i]
                    need = {}
                    for d in o["deps"]:
                        if d < lo:
                            continue
                        od = ops[d]
                        ev = od["ev"]
                        if ev is None:
                            continue
                        sem, val, name = ev
                        if (od["dma"] is None and od["eng"] == ename and o["dma"] is None
                                and (ename == "pe" or not same)):
                            continue
                        if kn.get(name, 0) >= val:
                            continue
                        if name not in need or need[name][1] < val:
                            need[name] = (sem, val)
                    for name, (sem, val) in need.items():
                        eng.wait_ge(sem, val)
                        kn[name] = val
                    ins = getattr(eng, o["meth"])(**o["kw"])
                    if o["ev"] is not None:
                        sem, val, name = o["ev"]
                        ins.then_inc(sem, 16 if o["dma"] is not None else 1)
                if ename == "sp":
                    for k, s in self.dsem.items():
                        if kn.get("d_" + str(k), 0) < self.dcnt[k]:
                            eng.wait_ge(s, self.dcnt[k])
                            kn["d_" + str(k)] = self.dcnt[k]
            return f

        with nc.Block() as block:
            block.tensor(body("pe"))
            block.scalar(body("act"))
            block.vector(body("dve"))
            block.gpsimd(body("pool"))
            block.sync(body("sp"))
        self.done_ops = len(ops)


class Builder:
    def __init__(self, S, NSEQ, debug=False, layers=DEPTH, phases="ABCD"):
        self.S, self.NSEQ, self.debug, self.layers, self.phases = S, NSEQ, debug, layers, phases
        self.NCH = S // 128
        self.NTL = S // 512
        nc = self.nc = bass.Bass("TRN2", target_bir_lowering=False)
        L = DEPTH
        okind = "ExternalOutput" if debug else "Internal"
        self.x = nc.dram_tensor("x", [NSEQ, S, D], F32, kind="ExternalInput").ap()
        self.win = nc.dram_tensor("win", [L, 128, KC, NT], F32, kind="ExternalInput").ap()
        self.wproj = nc.dram_tensor("wproj", [L, 3, 128, KC, D], F32, kind="ExternalInput").ap()
        self.wout = nc.dram_tensor("wout", [L, 128, KC, D], F32, kind="ExternalInput").ap()
        self.ppart = nc.dram_tensor("ppart", [128, L * (8 + 64 + 16)], F32, kind="ExternalInput").ap()
        self.prow = nc.dram_tensor("prow", [L * (80 + 1024 + 3072) + 1024], F32, kind="ExternalInput").ap()
        self.rope = nc.dram_tensor("rope", [2, 128, S], F32, kind="ExternalInput").ap()
        self.out = nc.dram_tensor("out", [NSEQ, S, D], F32, kind="ExternalOutput").ap()
        self.Y = nc.dram_tensor("ybr", [3, NSEQ, S, D], BF16, kind=okind).ap()
        self.X1 = nc.dram_tensor("x1", [NSEQ, S, D], F32, kind=okind).ap()
        self.Y2T = nc.dram_tensor("y2t", [NSEQ, D, S], BF16, kind=okind).ap()
        self.pools = {"gen": list(range(8))}
        self.prr = {}

    def psb(self, n=1, pool="gen"):
        banks = self.pools[pool]
        r = self.prr.get(pool, 0)
        if n == 2:
            assert len(banks) % 2 == 0
            if r % 2:
                r += 1
            b = banks[r % len(banks)]
            self.prr[pool] = r + 2
            return b
        b = banks[r % len(banks)]
        self.prr[pool] = r + 1
        return b

    def pk(self, b, n=1):
        return ["ps%d" % (b + i) for i in range(n)]

    def build(self):
        nc = self.nc
        with ExitStack() as gst:
            self.P = P = Prog(nc, gst)
            sb = lambda name, shape, dt: gst.enter_context(nc.sbuf_tensor(name, shape, dt))
            self.ps = gst.enter_context(nc.psum_tensor("ps", [128, 8, 512], F32))
            self.ident = sb("ident", [128, 128], BF16)
            self.identf = sb("identf", [128, 128], F32)
            self.tri = sb("tri", [128, 128], F32)
            self.elast = sb("elast", [128, 128], F32)
            self.onesf = sb("onesf", [128, 128], F32)
            self.onesb = sb("onesb", [128, 64], BF16)
            self.mle = sb("mle", [128, 128], BF16)
            self.mgt = sb("mgt", [128, 128], BF16)
            self.mlef = sb("mlef", [128, 128], F32)
            self.sel = sb("sel", [8, 8, 65], BF16)
            self.ppt = sb("ppt", [128, DEPTH * 88], F32)
            self.prw = sb("prw", [128, DEPTH * 80], F32)
            self.abc = sb("abc", [128, DEPTH * 16], F32)
            self.esink = sb("esink", [128, DEPTH * 16], F32)
            self.xt = [sb("xt%d" % i, [128, D], F32) for i in range(2)]
            self.xn = [sb("xn%d" % i, [128, D], BF16) for i in range(2)]
            self.junk = [sb("junk%d" % i, [128, D], BF16) for i in range(2)]
            self.st4 = [sb("st4_%d" % i, [128, 4], F32) for i in range(2)]
            self.xslot = 0
            self.setup_consts()
            import os as _os
            if _os.environ.get("SETUP_LIMIT"):
                lim = int(_os.environ["SETUP_LIMIT"])
                del P.ops[lim:]
            P.emit()
            for l in range(self.layers):
                if "A" in self.phases:
                    self.phase_A(l)
                if "B" in self.phases:
                    self.phase_B(l)
                if "C" in self.phases:
                    for hg in range(2):
                        self.phase_C(l, hg)
                if "D" in self.phases:
                    self.phase_D(l)
        return nc

    def setup_consts(self):
        P = self.P
        P.op("pool", "memset", writes=["ident"], ap=self.ident[:], constant=1.0)
        P.op("pool", "affine_select", reads=["ident"], writes=["ident"], out=self.ident[:], in_=self.ident[:],
             pattern=[[-1, 128]], compare_op=ALU.is_equal, fill=0.0, base=0, channel_multiplier=1)
        P.op("pool", "memset", writes=["identf"], ap=self.identf[:], constant=1.0)
        P.op("pool", "affine_select", reads=["identf"], writes=["identf"], out=self.identf[:], in_=self.identf[:],
             pattern=[[-1, 128]], compare_op=ALU.is_equal, fill=0.0, base=0, channel_multiplier=1)
        P.op("pool", "memset", writes=["tri"], ap=self.tri[:], constant=1.0)
        P.op("pool", "affine_select", reads=["tri"], writes=["tri"], out=self.tri[:], in_=self.tri[:],
             pattern=[[1, 128]], compare_op=ALU.is_ge, fill=0.0, base=0, channel_multiplier=-1)
        P.op("pool", "memset", writes=["mlef"], ap=self.mlef[:], constant=1.0)
        P.op("pool", "affine_select", reads=["mlef"], writes=["mlef"], out=self.mlef[:], in_=self.mlef[:],
             pattern=[[1, 128]], compare_op=ALU.is_ge, fill=0.0, base=0, channel_multiplier=-1)
        P.op("pool", "memset", writes=["mle"], ap=self.mle[:], constant=1.0)
        P.op("pool", "affine_select", reads=["mle"], writes=["mle"], out=self.mle[:], in_=self.mle[:],
             pattern=[[1, 128]], compare_op=ALU.is_ge, fill=0.0, base=0, channel_multiplier=-1)
        P.op("pool", "memset", writes=["mgt"], ap=self.mgt[:], constant=1.0)
        P.op("pool", "affine_select", reads=["mgt"], writes=["mgt"], out=self.mgt[:], in_=self.mgt[:],
             pattern=[[-1, 128]], compare_op=ALU.is_gt, fill=0.0, base=0, channel_multiplier=1)
        P.op("pool", "memset", writes=["elast"], ap=self.elast[:], constant=1.0)
        P.op("pool", "affine_select", reads=["elast"], writes=["elast"], out=self.elast[:], in_=self.elast[:],
             pattern=[[0, 128]], compare_op=ALU.is_equal, fill=0.0, base=-127, channel_multiplier=1)
        P.op("pool", "memset", writes=["onesf"], ap=self.onesf[:], constant=1.0)
        P.op("pool", "memset", writes=["onesb"], ap=self.onesb[:], constant=1.0)
        P.op("pool", "memset", writes=["sel"], ap=self.sel[:], constant=8.0)
        P.op("pool", "affine_select", reads=["sel"], writes=["sel"], out=self.sel[:], in_=self.sel[:],
             pattern=[[1, 8], [0, 65]], compare_op=ALU.is_equal, fill=0.0, base=0, channel_multiplier=-1)
        P.op("pool", "affine_select", reads=["sel"], writes=["sel"], out=self.sel[:], in_=self.sel[:],
             pattern=[[0, 8], [1, 65]], compare_op=ALU.is_equal, fill=0.0, base=-64, channel_multiplier=0)
        P.op("sp", "dma_start", writes=["ppt"], dma="ppt", out=self.ppt[:], in_=self.ppart)
        for l in range(DEPTH):
            P.op("sp", "dma_start", writes=["prw"], dma="prw", out=self.prw[:, l * 80:(l + 1) * 80],
                 in_=self.prow[l * 4176:l * 4176 + 80].partition_broadcast(128))
        for l in range(DEPTH):
            P.op("act", "activation", reads=["prw"], writes=["abc"], out=self.abc[:, l * 16:(l + 1) * 16],
                 in_=self.prw[:, l * 80 + 16:l * 80 + 32], func=AF.Exp)
            P.op("dve", "tensor_scalar", reads=["abc"], writes=["abc"], out=self.abc[:, l * 16:(l + 1) * 16],
                 in0=self.abc[:, l * 16:(l + 1) * 16], scalar1=-1.0, scalar2=None, op0=ALU.mult)
            P.op("act", "activation", reads=["prw"], writes=["esink"], out=self.esink[:, l * 16:(l + 1) * 16],
                 in_=self.prw[:, l * 80 + 48:l * 80 + 64], func=AF.Exp)

    def nw(self, l):
        return self.ppt[:, l * 88:l * 88 + 8]

    def convw(self, l, b, k):
        o = l * 88 + 8 + b * 4 + k
        return self.ppt[:, o:o + 1]

    def convb(self, l, b):
        o = l * 88 + 72 + b
        return self.ppt[:, o:o + 1]

    def rowp(self, l, i):
        return self.prw[:, l * 80 + i * 16:l * 80 + (i + 1) * 16]

    def xsrc(self, l, s, c):
        src = self.x if l == 0 else self.X1
        return src[s, c * 128:(c + 1) * 128, :], ("xin" if l == 0 else "x1_%d_%d" % (s, c))

    def h_chunk(self, l, s, c, hT, hkey, col0, slot=None):
        P = self.P
        if slot is None:
            sl = self.xslot
            self.xslot ^= 1
        else:
            sl = slot
        xt, xn, junk, st4 = self.xt[sl], self.xn[sl], self.junk[sl], self.st4[sl]
        src, skey = self.xsrc(l, s, c)
        P.op("sp", "dma_start", reads=[skey], writes=["xt%d" % sl], dma="xt%d" % sl, out=xt[:], in_=src)
        P.op("act", "activation", reads=["xt%d" % sl], writes=["junk%d" % sl, "st4_%d" % sl],
             out=junk[:], in_=xt[:], func=AF.Square, accum_out=st4[:, 0:1])
        P.op("act", "activation", reads=["st4_%d" % sl, "epsb"], writes=["st4_%d" % sl], out=st4[:, 1:2], in_=st4[:, 0:1],
             func=AF.Ln, scale=1.0 / D, bias=self.epsb[:])
        P.op("act", "activation", reads=["st4_%d" % sl], writes=["st4_%d" % sl], out=st4[:, 2:3], in_=st4[:, 1:2],
             func=AF.Exp, scale=-0.5)
        P.op("act", "activation", reads=["xt%d" % sl, "st4_%d" % sl], writes=["xn%d" % sl], out=xn[:], in_=xt[:],
             func=AF.Identity, scale=st4[:, 2:3])
        b = self.psb()
        ptb = self.ps[:, b, :].bitcast(BF16).rearrange("p (k t) -> p k t", k=8)
        for kc in range(KC):
            P.op("pe", "transpose", reads=["xn%d" % sl, "ident"], writes=self.pk(b), out=ptb[:, kc, :],
                 in_=xn[:, kc * 128:(kc + 1) * 128], identity=self.ident[:])
        P.op("dve", "tensor_tensor", reads=self.pk(b) + ["ppt"], writes=[hkey],
             out=hT[:, :, col0:col0 + 128], in0=ptb,
             in1=self.nw(l).unsqueeze(2).broadcast_to([128, 8, 128]), op=ALU.mult)
        return sl

    def load_w(self, wt, key, l, c0, n, src=None):
        P = self.P
        src = self.win[l] if src is None else src
        step = 1024
        for kc in range(KC):
            for o in range(0, n, step):
                m = min(step, n - o)
                P.op("pool", "dma_start", writes=[key], dma=key, out=wt[:, kc, o:o + m],
                     in_=src[:, kc, c0 + o:c0 + o + m])

    def proj_tm(self, dst_bank, hT, hkey, col0, wt, wkey, wc0, n, poff=0):
        P = self.P
        for kc in range(KC):
            P.op("pe", "matmul", reads=[hkey, wkey], writes=self.pk(dst_bank),
                 out=self.ps[:, dst_bank, poff:poff + n], lhsT=hT[:, kc, col0:col0 + 128],
                 rhs=wt[:, kc, wc0:wc0 + n], start=(kc == 0), stop=(kc == KC - 1))

    def proj_fm(self, dst_bank, hT, hkey, ntok, wt, wkey, wc0, m, first=True):
        P = self.P
        for kc in range(KC):
            P.op("pe", "matmul", reads=[hkey, wkey], writes=self.pk(dst_bank),
                 out=self.ps[0:m, dst_bank, 0:ntok], lhsT=wt[:, kc, wc0:wc0 + m],
                 rhs=hT[:, kc, 0:ntok], start=(first and kc == 0), stop=(kc == KC - 1))

    def phase_A(self, l):
        nc, P, S = self.nc, self.P, self.S
        with ExitStack() as st:
            self.uid = getattr(self, "uid", 0) + 1
            sb = lambda name, shape, dt, _u=self.uid: st.enter_context(nc.sbuf_tensor("%s_u%d" % (name, _u), shape, dt))
            wa = sb("wa", [128, KC, A_N], BF16)
            hTs = [sb("hT%d" % i, [128, KC, 512], BF16) for i in range(1)]
            Ub = [sb("Ub%d" % i, [128, 515], F32) for i in range(2)]
            Ucar = sb("Ucar", [128, 16, 3], F32)
            acc = [sb("acc%d" % i, [128, 512], F32) for i in range(2)]
            xsTs = [sb("xsT%d" % i, [128, 8, 512], BF16) for i in range(1)]
            BTs = [sb("BT%d" % i, [128, 4, 512], BF16) for i in range(1)]
            CTs = [sb("CT%d" % i, [128, 4, 512], BF16) for i in range(1)]
            H = sb("H", [128, 16, 64], F32)
            Hb = sb("Hb", [128, 16, 64], BF16)
            Htmp = sb("Htmp", [128, 16, 64], F32)
            bufsets = []
            for k in range(2):
                bufsets.append(dict(
                    sm=sb("sm%d" % k, [128, 12, 16], F32), rhsall=sb("rhsall%d" % k, [128, 16, 128], F32),
                    cbm=sb("cbm%d" % k, [128, 4, 128], F32), MT=sb("MT%d" % k, [128, 16, 128], BF16),
                    xstm=sb("xstm%d" % k, [128, 16, 64], F32), xdt=sb("xdt%d" % k, [128, 16, 64], BF16),
                    xw=sb("xw%d" % k, [128, 16, 64], BF16), Btm=sb("Btm%d" % k, [128, 4, 128], BF16),
                    sz=sb("sz%d" % k, [128, D], F32), y1=sb("y1%d" % k, [128, 16, 64], F32),
                    y2=sb("y2%d" % k, [128, 16, 64], F32), junkA=sb("junkA%d" % k, [128, 256], BF16)))
            yo = [sb("yo%d" % i, [128, D], BF16) for i in range(2)]
            snw = sb("snw", [128, D], F32)
            self.epsb = sb("epsb", [128, 1], F32)
            P.op("dve", "memset", writes=["epsb"], ap=self.epsb[:], constant=EPS)
            self.load_w(wa, "wa", l, A0, A_N)
            P.op("sp", "dma_start", writes=["snw"], dma="snw", out=snw[:],
                 in_=self.prow[l * 4176 + 80:l * 4176 + 80 + 1024].partition_broadcast(128))
            ps = self.ps
            def prologue(s, t, par):
                hT, xsT, BT, CT = hTs[par], xsTs[par], BTs[par], CTs[par]
                hk, xk, bk, ck = "hT%d" % par, "xsT%d" % par, "BT%d" % par, "CT%d" % par
                if t == 0:
                    P.op("pool", "memset", writes=["Ucar%d" % b for b in range(16)], ap=Ucar[:], constant=0.0)
                for c4 in range(4):
                    self.h_chunk(l, s, t * 4 + c4, hT, hk, c4 * 128)
                    yield
                for b in range(16):
                    pb = self.psb()
                    self.proj_fm(pb, hT, hk, 512, wa, "wa", b * 128, 128)
                    ukey = "Ub%d" % (b % 2)
                    U_ = Ub[b % 2]
                    P.op("pool", "tensor_copy", reads=["Ucar%d" % b], writes=[ukey], out=U_[:, 0:3], in_=Ucar[:, b, :])
                    P.op("act", "activation", reads=self.pk(pb), writes=[ukey], out=U_[:, 3:515],
                         in_=ps[:, pb, :], func=AF.Copy)
                    a = acc[b % 2]
                    akey = "acc%d" % (b % 2)
                    P.op("dve", "tensor_scalar", reads=[ukey, "ppt"], writes=[akey], out=a[:], in0=U_[:, 0:512],
                         scalar1=self.convw(l, b, 0), scalar2=None, op0=ALU.mult)
                    for k in range(1, 4):
                        P.op("dve", "scalar_tensor_tensor", reads=[ukey, akey, "ppt"], writes=[akey], out=a[:],
                             in0=U_[:, k:k + 512], scalar=self.convw(l, b, k), in1=a[:],
                             op0=ALU.mult, op1=ALU.add)
                    if b < 8:
                        dst, dkey = xsT[:, b, :], xk
                    elif b < 12:
                        dst, dkey = BT[:, b - 8, :], bk
                    else:
                        dst, dkey = CT[:, b - 12, :], ck
                    P.op("act", "activation", reads=[akey, "ppt"], writes=[dkey], out=dst, in_=a[:],
                         func=AF.Silu, bias=self.convb(l, b))
                    P.op("pool", "tensor_copy", reads=[ukey], writes=["Ucar%d" % b], out=Ucar[:, b, :], in_=U_[:, 512:515])
                    yield

            def chunk(s, t, par, c4, k):
                hT, xsT, BT, CT = hTs[par], xsTs[par], BTs[par], CTs[par]
                hk, xk, bk, ck = "hT%d" % par, "xsT%d" % par, "BT%d" % par, "CT%d" % par
                bs = bufsets[k]
                sm, rhsall, cbm, MT, xstm, xdt, xw, Btm, sz, y1, y2, junkA = [bs[n] for n in (
                    "sm", "rhsall", "cbm", "MT", "xstm", "xdt", "xw", "Btm", "sz", "y1", "y2", "junkA")]
                Eh = rhsall
                dec = rhsall
                K_ = lambda n: "%s_s%d" % (n, k)
                if True:
                    c = t * 4 + c4
                    cs = slice(c4 * 128, (c4 + 1) * 128)
                    pd = self.psb()
                    self.proj_tm(pd, hT, hk, c4 * 128, wa, "wa", 3072, 16)
                    dtr, dt_, adt, acum, nacum, lastbc, dS, ea, cd, dtS, e1 = [sm[:, i, :] for i in range(11)]
                    P.op("dve", "tensor_tensor", reads=self.pk(pd) + ["prw"], writes=[K_("sm0")], out=dtr,
                         in0=ps[:, pd, 0:16], in1=self.rowp(l, 0), op=ALU.add)
                    P.op("act", "activation", reads=[K_("sm0")], writes=[K_("sm10")], out=e1, in_=dtr, func=AF.Exp)
                    P.op("act", "activation", reads=[K_("sm10")], writes=[K_("sm1")], out=dt_, in_=e1, func=AF.Ln, bias=1.0)
                    P.op("dve", "tensor_tensor", reads=[K_("sm1"), "abc"], writes=[K_("sm2")], out=adt, in0=dt_,
                         in1=self.abc[:, l * 16:(l + 1) * 16], op=ALU.mult)
                    pa = self.psb()
                    P.op("pe", "matmul", reads=["tri", K_("sm2")], writes=self.pk(pa), out=ps[:, pa, 0:16],
                         lhsT=self.tri[:], rhs=adt, start=True, stop=True)
                    P.op("dve", "tensor_copy", reads=self.pk(pa), writes=[K_("sm3")], out=acum, in_=ps[:, pa, 0:16])
                    P.op("pe", "matmul", reads=["elast", K_("sm3")], writes=self.pk(pa), out=ps[:, pa, 16:32],
                         lhsT=self.elast[:], rhs=acum, start=True, stop=True)
                    P.op("dve", "tensor_copy", reads=self.pk(pa), writes=[K_("sm5")], out=lastbc, in_=ps[:, pa, 16:32])
                    P.op("dve", "tensor_tensor", reads=[K_("sm5"), K_("sm3")], writes=[K_("sm6")], out=dS, in0=lastbc, in1=acum,
                         op=ALU.subtract)
                    P.op("act", "activation", reads=[K_("sm6")], writes=[K_("sm6")], out=dS, in_=dS, func=AF.Exp)
                    P.op("act", "activation", reads=[K_("sm3")], writes=[K_("sm7")], out=ea, in_=acum, func=AF.Exp)
                    P.op("act", "activation", reads=[K_("sm5")], writes=[K_("sm8")], out=cd, in_=lastbc, func=AF.Exp)
                    P.op("dve", "tensor_tensor", reads=[K_("sm1"), K_("sm6")], writes=[K_("sm9")], out=dtS, in0=dt_, in1=dS,
                         op=ALU.mult)
                    yield
                    P.op("dve", "tensor_tensor", reads=["tri", K_("sm2")], writes=[K_("rhsall")] + [K_("Eh%d" % g_) for g_ in range(4)], out=rhsall[:],
                         in0=self.tri[:].unsqueeze(1).broadcast_to([128, 16, 128]),
                         in1=adt.unsqueeze(2).broadcast_to([128, 16, 128]), op=ALU.mult)
                    for g in range(4):
                        pg = self.psb()
                        P.op("pe", "matmul", reads=["onesf", K_("rhsall")], writes=self.pk(pg),
                             out=ps[:, pg, :], lhsT=self.onesf[:],
                             rhs=rhsall[:, 4 * g:4 * g + 4, :], start=True, stop=True)
                        for r in range(4):
                            h = 4 * g + r
                            P.op("dve", "tensor_scalar", reads=self.pk(pg) + [K_("sm3")], writes=[K_("Eh%d" % g)],
                                 out=Eh[:, h, :], in0=ps[:, pg, r * 128:(r + 1) * 128],
                                 scalar1=acum[:, h:h + 1], scalar2=0.0, op0=ALU.subtract, op1=ALU.min)
                        P.op("act", "activation", reads=[K_("Eh%d" % g)], writes=[K_("Eh%d" % g), K_("dec%d" % g)],
                             out=dec[:, 4 * g:4 * g + 4, :], in_=Eh[:, 4 * g:4 * g + 4, :], func=AF.Exp)
                    pc = self.psb()
                    for g in range(4):
                        P.op("pe", "matmul", reads=[bk, ck], writes=self.pk(pc),
                             out=ps[:, pc, g * 128:(g + 1) * 128], lhsT=BT[:, g, cs], rhs=CT[:, g, cs],
                             start=True, stop=True)
                    P.op("dve", "tensor_tensor", reads=self.pk(pc) + ["mlef"], writes=[K_("cbm")], out=cbm[:],
                         in0=ps[:, pc, :].rearrange("p (g l) -> p g l", g=4),
                         in1=self.mlef[:].unsqueeze(1).broadcast_to([128, 4, 128]), op=ALU.mult)
                    for g in range(4):
                        P.op("pool", "tensor_tensor", reads=[K_("dec%d" % g), K_("Eh%d" % g), K_("cbm")], writes=[K_("MT%d" % g)],
                             out=MT[:, 4 * g:4 * g + 4, :], in0=dec[:, 4 * g:4 * g + 4, :],
                             in1=cbm[:, g, :].unsqueeze(1).broadcast_to([128, 4, 128]), op=ALU.mult)
                    yield
                    px = self.psb()
                    pxb = ps[:, px, :].bitcast(BF16).rearrange("p (k t) -> p k t", k=8)
                    for b in range(8):
                        P.op("pe", "transpose", reads=[xk, "ident"], writes=self.pk(px), out=pxb[:, b, :],
                             in_=xsT[:, b, cs], identity=self.ident[:])
                    pxv = ps[:, px, :].bitcast(BF16).rearrange("p (h d) -> p h d", h=16)
                    P.op("act", "activation", reads=self.pk(px), writes=[K_("xstm")], out=xstm[:], in_=pxv, func=AF.Copy)
                    P.op("dve", "tensor_tensor", reads=[K_("xstm"), K_("sm1")], writes=[K_("xdt")], out=xdt[:], in0=xstm[:],
                         in1=dt_.unsqueeze(2).broadcast_to([128, 16, 64]), op=ALU.mult)
                    P.op("pool", "tensor_tensor", reads=[K_("xstm"), K_("sm9")], writes=[K_("xw")], out=xw[:], in0=xstm[:],
                         in1=dtS.unsqueeze(2).broadcast_to([128, 16, 64]), op=ALU.mult)
                    pbt = self.psb()
                    pbb = ps[:, pbt, 0:256].bitcast(BF16).rearrange("p (k t) -> p k t", k=4)
                    for g in range(4):
                        P.op("pe", "transpose", reads=[bk, "ident"], writes=self.pk(pbt), out=pbb[:, g, :],
                             in_=BT[:, g, cs], identity=self.ident[:])
                    P.op("act", "activation", reads=self.pk(pbt), writes=[K_("Btm")], out=Btm[:], in_=pbb, func=AF.Copy)
                    yield
                    P.op("act", "activation", reads=["H"], writes=["Hb"], out=Hb[:], in_=H[:], func=AF.Copy)
                    po = self.psb(2)
                    for g in range(4):
                        P.op("pe", "matmul", reads=[ck, "Hb"], writes=self.pk(po, 2),
                             out=ps[:, po + g // 2, (g % 2) * 256:(g % 2) * 256 + 256],
                             lhsT=CT[:, g, cs], rhs=Hb[:, 4 * g:4 * g + 4, :], start=True, stop=True)
                    poall = ps[:, po:po + 2, :].rearrange("p b (h d) -> p (b h) d", d=64)
                    P.op("dve", "tensor_tensor", reads=self.pk(po, 2) + [K_("sm7")], writes=[K_("y1")], out=y1[:], in0=poall,
                         in1=ea.unsqueeze(2).broadcast_to([128, 16, 64]), op=ALU.mult)
                    pst = self.psb(2)
                    for g in range(4):
                        P.op("pe", "matmul", reads=[K_("Btm"), K_("xw")], writes=self.pk(pst, 2),
                             out=ps[:, pst + g // 2, (g % 2) * 256:(g % 2) * 256 + 256],
                             lhsT=Btm[:, g, :], rhs=xw[:, 4 * g:4 * g + 4, :], start=True, stop=True)
                    pstall = ps[:, pst:pst + 2, :].rearrange("p b (h d) -> p (b h) d", d=64)
                    P.op("dve", "tensor_tensor", reads=["H", K_("sm8")], writes=["Htmp"], out=Htmp[:], in0=H[:],
                         in1=cd.unsqueeze(2).broadcast_to([128, 16, 64]), op=ALU.mult)
                    P.op("dve", "tensor_tensor", reads=self.pk(pst, 2) + ["Htmp"], writes=["H"], out=H[:], in0=pstall,
                         in1=Htmp[:], op=ALU.add)
                    yield
                    pyd = self.psb(2)
                    for h in range(16):
                        P.op("pe", "matmul", reads=[K_("MT%d" % (h // 4)), K_("xdt")], writes=self.pk(pyd, 2),
                             out=ps[:, pyd + h // 8, (h % 8) * 64:(h % 8) * 64 + 64],
                             lhsT=MT[:, h, :], rhs=xdt[:, h, :], start=True, stop=True)
                    pydall = ps[:, pyd:pyd + 2, :].rearrange("p b (h d) -> p (b h) d", d=64)
                    P.op("dve", "tensor_tensor", reads=self.pk(pyd, 2) + [K_("y1")], writes=[K_("y1")], out=y1[:], in0=pydall,
                         in1=y1[:], op=ALU.add)
                    P.op("pool", "tensor_tensor", reads=[K_("xstm"), "prw"], writes=[K_("y2")], out=y2[:], in0=xstm[:],
                         in1=self.rowp(l, 2).unsqueeze(2).broadcast_to([128, 16, 64]), op=ALU.mult)
                    P.op("pool", "tensor_tensor", reads=[K_("y1"), K_("y2")], writes=[K_("y2")], out=y2[:], in0=y1[:], in1=y2[:],
                         op=ALU.add)
                    yield
                    pz = self.psb(2)
                    for n in range(2):
                        self.proj_tm(pz + n, hT, hk, c4 * 128, wa, "wa", 2048 + n * 512, 512)
                    P.op("act", "activation", reads=self.pk(pz, 2), writes=[K_("sz")], out=sz[:],
                         in_=ps[:, pz:pz + 2, :].rearrange("p b n -> p (b n)"), func=AF.Silu)
                    y2f = y2[:].rearrange("p h d -> p (h d)")
                    P.op("dve", "tensor_tensor", reads=[K_("y2"), K_("sz")], writes=[K_("y2")], out=y2f, in0=y2f, in1=sz[:],
                         op=ALU.mult)
                    ss = sm[:, 11, 0:4]
                    for g in range(4):
                        P.op("act", "activation", reads=[K_("y2")], writes=[K_("junkA"), K_("sm11")], out=junkA[:],
                             in_=y2f[:, g * 256:(g + 1) * 256], func=AF.Square, accum_out=sm[:, 11, g:g + 1])
                    P.op("act", "activation", reads=[K_("sm11"), "epsb"], writes=[K_("sm11")], out=sm[:, 11, 4:8], in_=ss, func=AF.Ln,
                         scale=1.0 / 256, bias=self.epsb[:])
                    P.op("act", "activation", reads=[K_("sm11")], writes=[K_("sm11")], out=sm[:, 11, 8:12], in_=sm[:, 11, 4:8],
                         func=AF.Exp, scale=-0.5)
                    P.op("dve", "tensor_tensor", reads=[K_("y2"), K_("sm11")], writes=[K_("y2")],
                         out=y2[:].rearrange("p (g r) d -> p g (r d)", g=4),
                         in0=y2[:].rearrange("p (g r) d -> p g (r d)", g=4),
                         in1=sm[:, 11, 8:12].unsqueeze(2).broadcast_to([128, 4, 256]), op=ALU.mult)
                    yq = yo[c % 2]
                    P.op("pool", "tensor_tensor", reads=[K_("y2"), "snw"], writes=["yo%d" % (c % 2)], out=yq[:], in0=y2f,
                         in1=snw[:], op=ALU.mult)
                    P.op("pool", "dma_start", reads=["yo%d" % (c % 2)], writes=["Y0_%d_%d" % (s, c)],
                         dma="yo%d" % (c % 2), out=self.Y[0, s, c * 128:(c + 1) * 128, :], in_=yq[:])
                    yield

            items = [(s, t) for s in range(self.NSEQ) for t in range(self.NTL)]

            def run(gens):
                alive = list(gens)
                while alive:
                    for g in list(alive):
                        try:
                            next(g)
                        except StopIteration:
                            alive.remove(g)

            for i, (s, t) in enumerate(items):
                par = 0
                if t == 0:
                    P.op("dve", "memset", writes=["H"], ap=H[:], constant=0.0)
                run([prologue(s, t, par)])
                for c4 in (0, 2):
                    run([chunk(s, t, par, c4, 0), chunk(s, t, par, c4 + 1, 1)])
            P.emit()

    def phase_B(self, l):
        nc, P, S, NCH = self.nc, self.P, self.S, self.NCH
        with ExitStack() as st:
            self.uid = getattr(self, "uid", 0) + 1
            sb = lambda name, shape, dt, _u=self.uid: st.enter_context(nc.sbuf_tensor("%s_u%d" % (name, _u), shape, dt))
            wb = sb("wb", [128, KC, B_N], BF16)
            hT = sb("hT", [128, KC, 512], BF16)
            cosT = sb("cosT", [128, 512], F32)
            sinS = sb("sinS", [128, 512], F32)
            t1 = [sb("t1_%d" % i, [128, 512], F32) for i in range(2)]
            t2 = [sb("t2_%d" % i, [128, 512], F32) for i in range(2)]
            qrT = sb("qrT", [128, 8, 512], BF16)
            krT = sb("krT", [128, 2, S], BF16)
            V = sb("V", [128, NCH, 4, 65], BF16)
            sz = sb("sz", [128, D], F32)
            Pc = [sb("Pc%d" % i, [128, 512], BF16) for i in range(2)]
            Pp = [sb("Pp%d" % i, [128, 512], BF16) for i in range(2)]
            den = sb("den", [128, 16], F32)
            yf = sb("yf", [128, 16, 64], F32)
            yo = [sb("yo%d" % i, [128, D], BF16) for i in range(2)]
            self.epsb = sb("epsb", [128, 1], F32)
            P.op("dve", "memset", writes=["epsb"], ap=self.epsb[:], constant=EPS)
            nmc = sb("nmc", [128, 512], BF16)
            nmp = sb("nmp", [128, 512], BF16)
            P.op("dve", "tensor_scalar", reads=["mle"], writes=["nmc"], out=nmc[:].rearrange("p (a q) -> p a q", a=4),
                 in0=self.mle[:].unsqueeze(1).broadcast_to([128, 4, 128]), scalar1=-1.0, scalar2=30000.0,
                 op0=ALU.add, op1=ALU.mult)
            P.op("dve", "tensor_scalar", reads=["mgt"], writes=["nmp"], out=nmp[:].rearrange("p (a q) -> p a q", a=4),
                 in0=self.mgt[:].unsqueeze(1).broadcast_to([128, 4, 128]), scalar1=-1.0, scalar2=30000.0,
                 op0=ALU.add, op1=ALU.mult)
            self.load_w(wb, "wb", l, B0, B_N)
            P.op("pool", "memset", writes=["V%d" % i for i in range(NCH)], ap=V[:], constant=1.0)
            ps = self.ps
            for s in range(self.NSEQ):
                for t in range(self.NTL):
                    ts = slice(t * 512, (t + 1) * 512)
                    for c4 in range(4):
                        self.h_chunk(l, s, t * 4 + c4, hT, "hT", c4 * 128)
                    P.op("sp", "dma_start", writes=["cosT"], dma="cosT", out=cosT[:], in_=self.rope[0, :, ts])
                    P.op("sp", "dma_start", writes=["sinS"], dma="sinS", out=sinS[:], in_=self.rope[1, :, ts])
                    for b in range(10):
                        c0 = b * 128 if b < 8 else 1024 + (b - 8) * 128
                        c1 = 1280 + c0
                        pq = self.psb()
                        self.proj_fm(pq, hT, "hT", 512, wb, "wb", c0, 128)
                        pqs = self.psb()
                        self.proj_fm(pqs, hT, "hT", 512, wb, "wb", c1, 128)
                        i2 = b % 2
                        P.op("dve", "tensor_tensor", reads=self.pk(pq) + ["cosT"], writes=["t1_%d" % i2], out=t1[i2][:],
                             in0=ps[:, pq, :], in1=cosT[:], op=ALU.mult)
                        P.op("dve", "tensor_tensor", reads=self.pk(pqs) + ["sinS"], writes=["t2_%d" % i2], out=t2[i2][:],
                             in0=ps[:, pqs, :], in1=sinS[:], op=ALU.mult)
                        if b < 8:
                            dst, dkey = qrT[:, b, :], "qrT"
                        else:
                            dst, dkey = krT[:, b - 8, ts], "krT%d" % t
                        P.op("pool", "tensor_tensor", reads=["t1_%d" % i2, "t2_%d" % i2], writes=[dkey], out=dst,
                             in0=t1[i2][:], in1=t2[i2][:], op=ALU.add)
                    for c4 in range(4):
                        c = t * 4 + c4
                        cs = slice(c4 * 128, (c4 + 1) * 128)
                        pv = self.psb()
                        self.proj_tm(pv, hT, "hT", c4 * 128, wb, "wb", 2560, 256)
                        P.op("act", "activation", reads=self.pk(pv), writes=["V%d" % c], out=V[:, c, :, 0:64],
                             in_=ps[:, pv, 0:256].rearrange("p (h d) -> p h d", h=4), func=AF.Copy)
                        pz = self.psb(2)
                        for n in range(2):
                            self.proj_tm(pz + n, hT, "hT", c4 * 128, wb, "wb", 2816 + n * 512, 512)
                        P.op("act", "activation", reads=self.pk(pz, 2), writes=["sz"], out=sz[:],
                             in_=ps[:, pz:pz + 2, :].rearrange("p b n -> p (b n)"), func=AF.Silu)
                        def qk_stage(kv):
                            half = slice((kv % 2) * 64, (kv % 2) * 64 + 64)
                            blk0 = (kv // 2) * 4
                            qv = qrT[half, blk0:blk0 + 4, cs]
                            psc = self.psb()
                            P.op("pe", "matmul", reads=["krT%d" % t, "qrT"], writes=self.pk(psc),
                                 out=ps[:, psc, :].rearrange("p (a q) -> p a q", a=4),
                                 lhsT=krT[half, kv // 2, c * 128:(c + 1) * 128], rhs=qv, start=True, stop=False)
                            P.op("pe", "matmul", reads=["ident", "nmc"], writes=self.pk(psc), out=ps[:, psc, :],
                                 lhsT=self.ident[:], rhs=nmc[:], start=False, stop=True)
                            psp = None
                            if c > 0:
                                psp = self.psb()
                                P.op("pe", "matmul", reads=["krT%d" % ((c - 1) // 4), "qrT"], writes=self.pk(psp),
                                     out=ps[:, psp, :].rearrange("p (a q) -> p a q", a=4),
                                     lhsT=krT[half, kv // 2, (c - 1) * 128:c * 128], rhs=qv, start=True, stop=False)
                                P.op("pe", "matmul", reads=["ident", "nmp"], writes=self.pk(psp), out=ps[:, psp, :],
                                     lhsT=self.ident[:], rhs=nmp[:], start=False, stop=True)
                            return psc, psp

                        pend = qk_stage(0)
                        for kv in range(4):
                            i2 = kv % 2
                            psc, psp = pend
                            if kv + 1 < 4:
                                pend = qk_stage(kv + 1)
                            P.op("act", "activation", reads=self.pk(psc), writes=["Pc%d" % i2], out=Pc[i2][:],
                                 in_=ps[:, psc, :], func=AF.Exp, scale=0.125)
                            if c > 0:
                                P.op("act", "activation", reads=self.pk(psp), writes=["Pp%d" % i2], out=Pp[i2][:],
                                     in_=ps[:, psp, :], func=AF.Exp, scale=0.125)
                            po = self.psb()
                            for a in range(4):
                                if c > 0:
                                    P.op("pe", "matmul", reads=["Pp%d" % i2, "V%d" % (c - 1)], writes=self.pk(po),
                                         out=ps[:, po, a * 65:(a + 1) * 65], lhsT=Pp[i2][:, a * 128:(a + 1) * 128],
                                         rhs=V[:, c - 1, kv, :], start=True, stop=False)
                                P.op("pe", "matmul", reads=["Pc%d" % i2, "V%d" % c], writes=self.pk(po),
                                     out=ps[:, po, a * 65:(a + 1) * 65], lhsT=Pc[i2][:, a * 128:(a + 1) * 128],
                                     rhs=V[:, c, kv, :], start=(c == 0), stop=True)
                            pov = ps[:, po, 0:260].rearrange("p (a e) -> p a e", a=4)
                            P.op("dve", "tensor_tensor", reads=self.pk(po) + ["esink"], writes=["den%d" % kv],
                                 out=den[:, 4 * kv:4 * kv + 4].unsqueeze(2), in0=pov[:, :, 64:65],
                                 in1=self.esink[:, l * 16 + 4 * kv:l * 16 + 4 * kv + 4].unsqueeze(2), op=ALU.add)
                            P.op("dve", "reciprocal", reads=["den%d" % kv], writes=["den%d" % kv],
                                 out=den[:, 4 * kv:4 * kv + 4], in_=den[:, 4 * kv:4 * kv + 4])
                            P.op("dve", "tensor_tensor", reads=self.pk(po) + ["den%d" % kv], writes=["yf%d" % kv],
                                 out=yf[:, 4 * kv:4 * kv + 4, :], in0=pov[:, :, 0:64],
                                 in1=den[:, 4 * kv:4 * kv + 4].unsqueeze(2).broadcast_to([128, 4, 64]), op=ALU.mult)
                        yq = yo[c % 2]
                        P.op("pool", "tensor_tensor", reads=["yf%d" % k for k in range(4)] + ["sz"],
                             writes=["yo%d" % (c % 2)], out=yq[:], in0=yf[:].rearrange("p h d -> p (h d)"), in1=sz[:],
                             op=ALU.mult)
                        P.op("pool", "dma_start", reads=["yo%d" % (c % 2)], writes=["Y1_%d_%d" % (s, c)],
                             dma="yo%d" % (c % 2), out=self.Y[1, s, c * 128:(c + 1) * 128, :], in_=yq[:])
            P.emit()

    def phase_C(self, l, hg):
        nc, P, S, NCH = self.nc, self.P, self.S, self.NCH
        with ExitStack() as st:
            self.uid = getattr(self, "uid", 0) + 1
            sb = lambda name, shape, dt, _u=self.uid: st.enter_context(nc.sbuf_tensor("%s_u%d" % (name, _u), shape, dt))
            wc = sb("wc", [128, KC, C_G], BF16)
            hT = sb("hT", [128, KC, 512], BF16)
            KT = sb("KT", [72, 8, S], BF16)
            V = sb("V", [128, NCH, 8, 65], BF16)
            QT = sb("QT", [72, 8, 512], BF16)
            NC_ = sb("NC", [128, NCH, 8], F32)
            fsm = sb("fsm", [128, 4, 8], F32)
            cumT = sb("cumT", [8, 512], BF16)
            szT = sb("szT", [64, 8, 512], F32)
            PT = [sb("PT%d" % i, [128, 512], BF16) for i in range(4)]
            rd = [sb("rd%d" % i, [65, 512], BF16) for i in range(2)]
            rdf = [sb("rdf%d" % i, [65, 512], F32) for i in range(2)]
            yn = [sb("yn%d" % i, [64, 512], F32) for i in range(2)]
            yo = [sb("yo0", [64, 8, 512], BF16)] * 2
            self.epsb = sb("epsb", [128, 1], F32)
            P.op("dve", "memset", writes=["epsb"], ap=self.epsb[:], constant=EPS)
            nmd = sb("nmd", [128, 128], BF16)
            P.op("dve", "tensor_scalar", reads=["mle"], writes=["nmd"], out=nmd[:], in0=self.mle[:], scalar1=-1.0,
                 scalar2=30000.0, op0=ALU.add, op1=ALU.mult)
            self.load_w(wc, "wc", l, C0 + hg * C_G, C_G)
            P.op("pool", "memset", writes=["V%d" % i for i in range(NCH)], ap=V[:], constant=1.0)
            P.op("pool", "memset", writes=["KT%d" % i for i in range(self.NTL)], ap=KT[64:72, :, :], constant=1.0)
            P.op("pool", "affine_select", reads=["KT%d" % i for i in range(self.NTL)],
                 writes=["KT%d" % i for i in range(self.NTL)], out=KT[64:72, :, :], in_=KT[64:72, :, :],
                 pattern=[[1, 8], [0, S]], compare_op=ALU.is_equal, fill=0.0, base=0, channel_multiplier=-1)
            ps = self.ps
            self.pools = {"gen": [0, 1], "ct": [2], "acc": [2, 3, 4], "sc": [5, 6, 7]}
            self.prr = {}
            fb = self.rowp(l, 4)[:, hg * 8:hg * 8 + 8]
            pti = 0
            for s in range(self.NSEQ):
                for t in range(self.NTL):
                    ts = slice(t * 512, (t + 1) * 512)
                    for c4 in range(4):
                        self.h_chunk(l, s, t * 4 + c4, hT, "hT", c4 * 128)
                    for hp in range(4):
                        pkb = self.psb()
                        self.proj_fm(pkb, hT, "hT", 512, wc, "wc", 512 + hp * 128, 128)
                        for e in range(2):
                            P.op("dve", "tensor_copy", reads=self.pk(pkb), writes=["KT%d" % t], out=KT[0:64, 2 * hp + e, ts],
                                 in_=ps[64 * e:64 * e + 64, pkb, :])
                    pct = self.psb(pool="ct")
                    for c4 in range(4):
                        c = t * 4 + c4
                        pf = self.psb()
                        self.proj_tm(pf, hT, "hT", c4 * 128, wc, "wc", 2048, 8)
                        f0, f1, f2 = fsm[:, 0, :], fsm[:, 1, :], fsm[:, 2, :]
                        P.op("dve", "tensor_tensor", reads=self.pk(pf) + ["prw"], writes=["f0"], out=f0,
                             in0=ps[:, pf, 0:8], in1=fb, op=ALU.add)
                        P.op("act", "activation", reads=["f0"], writes=["f1"], out=f1, in_=f0, func=AF.Exp, scale=-1.0)
                        P.op("act", "activation", reads=["f1"], writes=["f2"], out=f2, in_=f1, func=AF.Ln, bias=1.0)
                        P.op("pe", "matmul", reads=["tri", "f2"], writes=self.pk(pf), out=ps[:, pf, 8:16],
                             lhsT=self.tri[:], rhs=f2, start=True, stop=(c == 0))
                        if c > 0:
                            P.op("pe", "matmul", reads=["elast", "NC%d" % (c - 1)], writes=self.pk(pf), out=ps[:, pf, 8:16],
                                 lhsT=self.elast[:], rhs=NC_[:, c - 1, :], start=False, stop=True)
                        P.op("dve", "tensor_copy", reads=self.pk(pf), writes=["NC%d" % c], out=NC_[:, c, :], in_=ps[:, pf, 8:16])
                        P.op("pe", "transpose", reads=["NC%d" % c, "identf"], writes=self.pk(pct),
                             out=ps[0:8, pct, c4 * 128:(c4 + 1) * 128], in_=NC_[:, c, :], identity=self.identf[:])
                        pv = self.psb()
                        self.proj_tm(pv, hT, "hT", c4 * 128, wc, "wc", 1024, 512)
                        P.op("act", "activation", reads=self.pk(pv), writes=["V%d" % c], out=V[:, c, :, 0:64],
                             in_=ps[:, pv, :].rearrange("p (h d) -> p h d", h=8), func=AF.Copy)
                    P.op("dve", "tensor_scalar", reads=self.pk(pct), writes=["QT%d" % h for h in range(8)],
                         out=QT[64:72, :, :], in0=ps[0:8, pct, :].unsqueeze(1).broadcast_to([8, 8, 512]),
                         scalar1=-8.0, scalar2=None, op0=ALU.mult)
                    yq = yo[0]
                    ykey = "yo0"
                    for hp in range(4):
                        pz = self.psb()
                        self.proj_fm(pz, hT, "hT", 512, wc, "wc", 1536 + hp * 128, 128)
                        for e in range(2):
                            P.op("act", "activation", reads=self.pk(pz), writes=["szT%d" % (2 * hp + e)],
                                 out=szT[:, 2 * hp + e, :], in_=ps[64 * e:64 * e + 64, pz, :], func=AF.Silu)
                    for hp in range(4):
                        pqb = self.psb()
                        self.proj_fm(pqb, hT, "hT", 512, wc, "wc", hp * 128, 128)
                        for e in range(2):
                            P.op("dve", "tensor_copy", reads=self.pk(pqb), writes=["QT%d" % (2 * hp + e)],
                                 out=QT[0:64, 2 * hp + e, :], in_=ps[64 * e:64 * e + 64, pqb, :])
                    nkb = 4 * t + 4

                    def qk(h, j):
                        q0 = max(j - 4 * t, 0)
                        nq = 4 - q0
                        psc = self.psb(pool="sc")
                        diag = (j - 4 * t) >= 0
                        P.op("pe", "matmul", reads=["KT%d" % (j // 4), "QT%d" % h], writes=self.pk(psc),
                             out=ps[:, psc, 0:nq * 128], lhsT=KT[:, h, j * 128:(j + 1) * 128],
                             rhs=QT[:, h, q0 * 128:512], start=True, stop=not diag)
                        if diag:
                            P.op("pe", "matmul", reads=["ident", "nmd"], writes=self.pk(psc), out=ps[:, psc, 0:128],
                                 lhsT=self.ident[:], rhs=nmd[:], start=False, stop=True)
                        return psc

                    seq = [(2 * hp + e, j) for hp in range(4) for j in range(nkb) for e in range(2)]
                    DIST = 2
                    pend = [qk(*seq[i]) for i in range(min(DIST, len(seq)))]
                    pos = {}
                    for i, (h, j) in enumerate(seq):
                        psc = pend.pop(0)
                        if i + DIST < len(seq):
                            pend.append(qk(*seq[i + DIST]))
                        if j == 0:
                            pos[h] = self.psb(pool="acc")
                        po = pos[h]
                        jj = j - 4 * t
                        q0 = max(jj, 0)
                        nq = 4 - q0
                        pt = PT[pti % 4]
                        ptk = "PT%d" % (pti % 4)
                        pti += 1
                        P.op("act", "activation", reads=self.pk(psc) + ["NC%d" % j], writes=[ptk], out=pt[:, 0:nq * 128],
                             in_=ps[:, psc, 0:nq * 128], func=AF.Exp, scale=0.125, bias=NC_[:, j, h:h + 1])
                        P.op("pe", "matmul", reads=[ptk, "V%d" % j], writes=self.pk(po), out=ps[0:65, po, q0 * 128:512],
                             lhsT=V[:, j, h, :], rhs=pt[:, 0:nq * 128], start=(j == 0), stop=(j == nkb - 1))
                        if j == nkb - 1:
                            r_, y_ = rd[h % 2], yn[h % 2]
                            rk, yk = "rd%d" % (h % 2), "yn%d" % (h % 2)
                            rf_ = rdf[h % 2]
                            P.op("act", "activation", reads=self.pk(po), writes=[rk + "f"], out=rf_[64:65, :],
                                 in_=ps[64:65, po, :], func=AF.Ln)
                            P.op("act", "activation", reads=[rk + "f"], writes=[rk], out=r_[64:65, :],
                                 in_=rf_[64:65, :], func=AF.Exp, scale=-1.0)
                            pbc = self.psb()
                            P.op("pe", "matmul", reads=[rk, "onesb"], writes=self.pk(pbc), out=ps[0:64, pbc, :],
                                 lhsT=self.onesb[64:65, 0:64], rhs=r_[64:65, :], start=True, stop=True)
                            P.op("dve", "tensor_tensor", reads=self.pk(pbc) + ["szT%d" % h], writes=[yk], out=y_[:],
                                 in0=ps[0:64, pbc, :], in1=szT[:, h, :], op=ALU.mult)
                            P.op("dve", "tensor_tensor", reads=self.pk(po) + [yk], writes=[ykey], out=yq[:, h, :],
                                 in0=ps[0:64, po, :], in1=y_[:], op=ALU.mult)
                    P.op("pool", "dma_start", reads=[ykey], writes=["Y2T_%d_%d_%d" % (s, t, hg)], dma=ykey,
                         out=self.Y2T[s, hg * 512:(hg + 1) * 512, ts].rearrange("(h d) t -> d h t", d=64), in_=yq[:])
            P.emit()
            self.pools = {"gen": list(range(8))}
            self.prr = {}

    def phase_D(self, l):
        nc, P, S = self.nc, self.P, self.S
        last = (l == self.layers - 1)
        NS = self.NSEQ
        with ExitStack() as st:
            self.uid = getattr(self, "uid", 0) + 1
            sb = lambda name, shape, dt, _u=self.uid: st.enter_context(nc.sbuf_tensor("%s_u%d" % (name, _u), shape, dt))
            wg = sb("wg", [128, KC, G_N], BF16)
            wp = [sb("wp%d" % i, [128, KC, D], BF16) for i in range(3)]
            wo = sb("wo", [128, KC, D], BF16)
            gbb = sb("gbb", [128, 3 * D], F32)
            fnw = sb("fnw", [128, D], F32) if last else None
            nb = max(NS, 2)
            yt = [[sb("yt%d_%d" % (b, i), [128, D], BF16) for i in range(nb)] for b in range(2)]
            ybTc = [sb("ybTc%d" % i, [128, KC, 128], BF16) for i in range(nb)]
            hTs = [sb("hT%d" % i, [128, KC, 128], BF16) for i in range(nb)]
            ybTs = [sb("ybT%d" % i, [128, KC, 128], BF16) for i in range(nb)]
            gss = [sb("gs%d" % i, [128, D], F32) for i in range(nb)]
            mgs = [sb("mg%d" % i, [128, D], F32) for i in range(nb)]
            mgbs = [sb("mgb%d" % i, [128, D], BF16) for i in range(nb)]
            mTs = [sb("mT%d" % i, [128, KC, 128], BF16) for i in range(nb)]
            xo = [sb("xo%d" % i, [128, D], F32) for i in range(nb)]
            fss = [sb("fs%d" % i, [128, 4], F32) for i in range(nb)]
            self.epsb = sb("epsb", [128, 1], F32)
            P.op("dve", "memset", writes=["epsb"], ap=self.epsb[:], constant=EPS)
            self.load_w(wg, "wg", l, G0, G_N)
            for b in range(3):
                self.load_w(wp[b], "wp%d" % b, l, 0, D, src=self.wproj[l, b])
            self.load_w(wo, "wo", l, 0, D, src=self.wout[l])
            P.op("sp", "dma_start", writes=["gbb"], dma="gbb", out=gbb[:],
                 in_=self.prow[l * 4176 + 1104:l * 4176 + 1104 + 3072].partition_broadcast(128))
            if last:
                P.op("sp", "dma_start", writes=["fnw"], dma="fnw", out=fnw[:],
                     in_=self.prow[DEPTH * 4176:DEPTH * 4176 + 1024].partition_broadcast(128))
            ps = self.ps

            def chunk_gen(s, c, k):
                hT, ybT, gs, mg, mgb, mT, xq, fs = hTs[k], ybTs[k], gss[k], mgs[k], mgbs[k], mTs[k], xo[k], fss[k]
                K_ = lambda n: "%s%d" % (n, k)
                for b in range(2):
                    P.op("sp", "dma_start", reads=["Y%d_%d_%d" % (b, s, c)], writes=["yt%d_%d" % (b, k)],
                         dma="yt%d_%d" % (b, k), out=yt[b][k][:], in_=self.Y[b, s, c * 128:(c + 1) * 128, :])
                P.op("sp", "dma_start", reads=["Y2T_%d_%d_%d" % (s, c // 4, g) for g in range(2)],
                     writes=[K_("ybTc")], dma=K_("ybTc"), out=ybTc[k][:],
                     in_=self.Y2T[s, :, c * 128:(c + 1) * 128].rearrange("(kc p) t -> p kc t", p=128))
                sl = self.h_chunk(l, s, c, hT, K_("hT"), 0, slot=k % 2)
                yield
                for b in range(3):
                    pg = self.psb(2)
                    for n in range(2):
                        self.proj_tm(pg + n, hT, K_("hT"), 0, wg, "wg", b * D + n * 512, 512)
                    P.op("dve", "tensor_tensor", reads=self.pk(pg, 2) + ["gbb"], writes=[K_("gs")], out=gs[:],
                         in0=ps[:, pg:pg + 2, :].rearrange("p b n -> p (b n)"), in1=gbb[:, b * D:(b + 1) * D],
                         op=ALU.add)
                    P.op("act", "activation", reads=[K_("gs")], writes=[K_("gs")], out=gs[:], in_=gs[:], func=AF.Sigmoid)
                    if b < 2:
                        pt_ = self.psb()
                        ptb = ps[:, pt_, :].bitcast(BF16).rearrange("p (k t) -> p k t", k=8)
                        for kc in range(KC):
                            P.op("pe", "transpose", reads=["yt%d_%d" % (b, k), "ident"], writes=self.pk(pt_),
                                 out=ptb[:, kc, :], in_=yt[b][k][:, kc * 128:(kc + 1) * 128], identity=self.ident[:])
                        P.op("act", "activation", reads=self.pk(pt_), writes=[K_("ybT")], out=ybT[:], in_=ptb, func=AF.Copy)
                        ysrc, ykey_ = ybT, K_("ybT")
                    else:
                        ysrc, ykey_ = ybTc[k], K_("ybTc")
                    yield
                    pb = self.psb(2)
                    for n in range(2):
                        for kc in range(KC):
                            P.op("pe", "matmul", reads=[ykey_, "wp%d" % b], writes=self.pk(pb + n),
                                 out=ps[:, pb + n, :], lhsT=ysrc[:, kc, :], rhs=wp[b][:, kc, n * 512:(n + 1) * 512],
                                 start=(kc == 0), stop=(kc == KC - 1))
                    pball = ps[:, pb:pb + 2, :].rearrange("p b n -> p (b n)")
                    if b == 0:
                        P.op("dve", "tensor_tensor", reads=self.pk(pb, 2) + [K_("gs")], writes=[K_("mg")], out=mg[:],
                             in0=pball, in1=gs[:], op=ALU.mult)
                    else:
                        P.op("dve", "tensor_tensor", reads=self.pk(pb, 2) + [K_("gs")], writes=[K_("gs")], out=gs[:],
                             in0=pball, in1=gs[:], op=ALU.mult)
                        P.op("pool", "tensor_tensor", reads=[K_("gs"), K_("mg")], writes=[K_("mg")], out=mg[:],
                             in0=gs[:], in1=mg[:], op=ALU.add)
                    yield
                P.op("act", "activation", reads=[K_("mg")], writes=[K_("mgb")], out=mgb[:], in_=mg[:], func=AF.Copy)
                pt_ = self.psb()
                ptb = ps[:, pt_, :].bitcast(BF16).rearrange("p (k t) -> p k t", k=8)
                for kc in range(KC):
                    P.op("pe", "transpose", reads=[K_("mgb"), "ident"], writes=self.pk(pt_), out=ptb[:, kc, :],
                         in_=mgb[:, kc * 128:(kc + 1) * 128], identity=self.ident[:])
                P.op("dve", "tensor_copy", reads=self.pk(pt_), writes=[K_("mT")], out=mT[:], in_=ptb)
                yield
                po = self.psb(2)
                for n in range(2):
                    for kc in range(KC):
                        P.op("pe", "matmul", reads=[K_("mT"), "wo"], writes=self.pk(po + n), out=ps[:, po + n, :],
                             lhsT=mT[:, kc, :], rhs=wo[:, kc, n * 512:(n + 1) * 512], start=(kc == 0),
                             stop=(kc == KC - 1))
                P.op("dve", "tensor_tensor", reads=self.pk(po, 2) + ["xt%d" % sl], writes=[K_("xo")], out=xq[:],
                     in0=ps[:, po:po + 2, :].rearrange("p b n -> p (b n)"), in1=self.xt[sl][:], op=ALU.add)
                if not last:
                    P.op("pool", "dma_start", reads=[K_("xo")], writes=["x1_%d_%d" % (s, c)], dma=K_("xo"),
                         out=self.X1[s, c * 128:(c + 1) * 128, :], in_=xq[:])
                else:
                    P.op("act", "activation", reads=[K_("xo")], writes=["junk%d" % sl, K_("fs")], out=self.junk[sl][:],
                         in_=xq[:], func=AF.Square, accum_out=fs[:, 0:1])
                    P.op("act", "activation", reads=[K_("fs"), "epsb"], writes=[K_("fs")], out=fs[:, 1:2], in_=fs[:, 0:1],
                         func=AF.Ln, scale=1.0 / D, bias=self.epsb[:])
                    P.op("act", "activation", reads=[K_("fs")], writes=[K_("fs")], out=fs[:, 2:3], in_=fs[:, 1:2],
                         func=AF.Exp, scale=-0.5)
                    P.op("dve", "scalar_tensor_tensor", reads=[K_("xo"), K_("fs"), "fnw"], writes=[K_("xo")],
                         out=xq[:], in0=xq[:], scalar=fs[:, 2:3], in1=fnw[:], op0=ALU.mult, op1=ALU.mult)
                    P.op("pool", "dma_start", reads=[K_("xo")], writes=["out_%d_%d" % (s, c)], dma=K_("xo"),
                         out=self.out[s, c * 128:(c + 1) * 128, :], in_=xq[:])

            if NS >= 2:
                items = [[(s, c, s) for s in range(NS)] for c in range(self.NCH)]
            else:
                items = [[(0, c + e, e) for e in range(2) if c + e < self.NCH] for c in range(0, self.NCH, 2)]
            for group in items:
                alive = [chunk_gen(*it) for it in group]
                while alive:
                    for g in list(alive):
                        try:
                            next(g)
                        except StopIteration:
                            alive.remove(g)
            P.emit()


def host_layout(S, norm_w, w_in, conv_w, conv_b, dt_bias, a_log, d_skip, ssm_norm_w, sinks, f_bias, gate_bias,
                w_proj, w_out, final_norm_w):
    L = DEPTH
    cols = _col_order()
    win = np.ascontiguousarray(
        np.asarray(w_in, np.float32)[:, :, cols].reshape(L, KC, 128, NT).transpose(0, 2, 1, 3))
    wproj = np.ascontiguousarray(np.asarray(w_proj, np.float32).reshape(L, 3, KC, 128, D).transpose(0, 1, 3, 2, 4))
    wout = np.ascontiguousarray(np.asarray(w_out, np.float32).reshape(L, KC, 128, D).transpose(0, 2, 1, 3))
    ppart = np.zeros((128, L * 88), np.float32)
    prow = np.zeros((L * 4176 + 1024,), np.float32)
    for l in range(L):
        ppart[:, l * 88:l * 88 + 8] = np.asarray(norm_w[l]).reshape(KC, 128).T
        cw = np.asarray(conv_w[l]).reshape(4, 16, 128)
        ppart[:, l * 88 + 8:l * 88 + 72] = cw.transpose(2, 1, 0).reshape(128, 64)
        ppart[:, l * 88 + 72:l * 88 + 88] = np.asarray(conv_b[l]).reshape(16, 128).T
        o = l * 4176
        prow[o:o + 16] = dt_bias[l]
        prow[o + 16:o + 32] = a_log[l]
        prow[o + 32:o + 48] = d_skip[l]
        prow[o + 48:o + 64] = sinks[l]
        prow[o + 64:o + 80] = f_bias[l]
        prow[o + 80:o + 1104] = ssm_norm_w[l]
        prow[o + 1104:o + 4176] = np.asarray(gate_bias[l]).reshape(-1)
    prow[L * 4176:] = final_norm_w
    pos = np.arange(S, dtype=np.float32)
    inv = (np.float32(10000.0) ** (-np.arange(0, 64, 2, dtype=np.float32) / np.float32(64))).astype(np.float32)
    ang = (pos[:, None] * inv[None, :]).astype(np.float32)
    cos, sin = np.cos(ang).astype(np.float32), np.sin(ang).astype(np.float32)
    rope = np.zeros((2, 128, S), np.float32)
    for p in range(128):
        rope[0, p] = cos[:, p % 32]
        rope[1, p] = sin[:, p % 32] * (-1.0 if (p % 64) < 32 else 1.0)
    return dict(win=win, wproj=wproj, wout=wout, ppart=ppart, prow=prow, rope=rope)


_NC_CACHE = {}


def kernel(x, norm_w, w_in, conv_w, conv_b, dt_bias, a_log, d_skip, ssm_norm_w, sinks, f_bias, gate_bias,
           w_proj, w_out, final_norm_w):
    x = np.asarray(x, np.float32)
    B, S, _ = x.shape
    nseq = B // NCORES
    shared = host_layout(S, norm_w, w_in, conv_w, conv_b, dt_bias, a_log, d_skip, ssm_norm_w, sinks, f_bias,
                         gate_bias, w_proj, w_out, final_norm_w)
    key = (S, nseq)
    if key not in _NC_CACHE:
        _NC_CACHE[key] = Builder(S, nseq).build()
    nc = _NC_CACHE[key]
    in_maps = []
    for c in range(NCORES):
        m = dict(shared)
        m["x"] = np.ascontiguousarray(x[c * nseq:(c + 1) * nseq])
        in_maps.append(m)
    res = run_bass_kernel_spmd(nc, in_maps, core_ids=list(range(NCORES)))
    return np.concatenate([r["out"] for r in res.results], axis=0).astype(np.float32)
```
